# Optimizing a Trainium2 kernel written in Bass

```python
import math
import jax, jax.numpy as jnp
from jax import lax
import numpy as np

D_MODEL = 1024
BATCH = 8
SEQ = 4096
DEPTH = 4

N_MIXERS = 3
D_FF = 4 * D_MODEL
D_MIX = D_MODEL
EPS = 1e-6
CONV_WIDTH = 3
S5_GROUP = 16
S5_GROUPS = D_MIX // S5_GROUP
S5_STATE = 64
DT_MIN = 1e-3
DT_MAX = 1e-1
CHUNK = 128
SG_HEADS = 8
SG_HEAD_DIM = D_MIX // SG_HEADS
N_A = (DEPTH + 2) // 3
N_B = (DEPTH + 1) // 3
N_C = DEPTH // 3

kernel_name = "hybrid_conv_s5_sgmlp_trunk"


def rmsnorm(x, g):
    xf = x.astype(jnp.float32)
    y = xf * lax.rsqrt(jnp.mean(xf * xf, axis=-1, keepdims=True) + EPS)
    return (y * g.astype(jnp.float32)).astype(x.dtype)


def short_conv_mixer(h, w_in, conv_w, conv_b, w_out):
    bcx = h @ w_in
    b_gate, c_gate, xh = jnp.split(bcx, 3, axis=-1)
    z = c_gate * xh
    conv = lax.conv_general_dilated(
        z, conv_w[:, None, :].astype(z.dtype), window_strides=(1,),
        padding=[(CONV_WIDTH - 1, 0)], dimension_numbers=("NWC", "WIO", "NWC"),
        feature_group_count=D_MIX) + conv_b
    return (b_gate * conv) @ w_out


def _ssm_combine(e1, e2):
    a1r, a1i, b1r, b1i = e1
    a2r, a2i, b2r, b2i = e2
    ar = a2r * a1r - a2i * a1i
    ai = a2r * a1i + a2i * a1r
    br = a2r * b1r - a2i * b1i + b2r
    bi = a2r * b1i + a2i * b1r + b2i
    return (ar, ai, br, bi)


def s5_mixer(h, w_in, a_re, a_im, log_dt, b_re, b_im, c_re, c_im, d_skip, glu_w, glu_b, w_out):
    bsz, seq_len, _ = h.shape
    f32 = jnp.float32
    u = (h @ w_in).astype(f32).reshape(bsz, seq_len, S5_GROUPS, S5_GROUP)
    a_re = a_re.astype(f32); a_im = a_im.astype(f32)
    dt = jnp.exp(log_dt.astype(f32))[:, None]
    mag = jnp.exp(a_re * dt)
    abar_re = mag * jnp.cos(a_im * dt)
    abar_im = mag * jnp.sin(a_im * dt)
    den = a_re * a_re + a_im * a_im
    nr = abar_re - 1.0
    ni = abar_im
    f_re = ((nr * a_re + ni * a_im) / den)[..., None]
    f_im = ((ni * a_re - nr * a_im) / den)[..., None]
    b_re = b_re.astype(f32); b_im = b_im.astype(f32)
    bbar_re = f_re * b_re - f_im * b_im
    bbar_im = f_re * b_im + f_im * b_re
    bu_re = jnp.einsum("blgh,gph->blgp", u, bbar_re)
    bu_im = jnp.einsum("blgh,gph->blgp", u, bbar_im)
    a_seq_re = jnp.broadcast_to(abar_re, (1, seq_len, S5_GROUPS, S5_STATE))
    a_seq_im = jnp.broadcast_to(abar_im, (1, seq_len, S5_GROUPS, S5_STATE))
    _, _, s_re, s_im = lax.associative_scan(
        _ssm_combine, (a_seq_re, a_seq_im, bu_re, bu_im), axis=1)
    y = (jnp.einsum("blgp,ghp->blgh", s_re, c_re.astype(f32))
         - jnp.einsum("blgp,ghp->blgh", s_im, c_im.astype(f32)))
    y = y + d_skip.astype(f32).reshape(S5_GROUPS, S5_GROUP) * u
    y = jax.nn.gelu(y.reshape(bsz, seq_len, D_MIX))
    y = y * jax.nn.sigmoid(y @ glu_w.astype(f32) + glu_b.astype(f32))
    return y.astype(h.dtype) @ w_out


def spatial_gating_mixer(h, w_in, v_gain, w_s, b_s, w_out):
    bsz, seq_len, _ = h.shape
    u, v = jnp.split(h @ w_in, 2, axis=-1)
    v = rmsnorm(v, v_gain)
    vc = v.reshape(bsz, seq_len // CHUNK, CHUNK, SG_HEADS, SG_HEAD_DIM)
    causal = jnp.tril(jnp.ones((CHUNK, CHUNK), dtype=bool))
    ws = jnp.where(causal[None], w_s, jnp.zeros_like(w_s))
    vm = jnp.einsum("hts,bnshd->bnthd", ws, vc) + b_s.T[:, :, None]
    return (u * vm.reshape(bsz, seq_len, D_MIX)) @ w_out


def squared_relu_mlp(h, w1, w2):
    return jnp.square(jax.nn.relu(h @ w1)) @ w2


def setup_inputs(seed: int = 0) -> dict:
    key = jax.random.key(seed)
    ks = iter(jax.random.split(key, 40))
    f32 = jnp.float32

    def nrm(shape, std):
        return std * jax.random.normal(next(ks), shape, f32)

    D = D_MODEL
    G, P = S5_GROUPS, S5_STATE
    x = nrm((BATCH, SEQ, D), 1.0)
    c = nrm((BATCH, D), 1.0)
    ada_w = nrm((DEPTH, D, 6 * D), 0.5 * D ** -0.5)
    ada_b = nrm((DEPTH, 6 * D), 0.02)
    norm1_g = 1.0 + nrm((DEPTH, D), 0.02)
    norm2_g = 1.0 + nrm((DEPTH, D), 0.02)
    ff_w1 = nrm((DEPTH, D, D_FF), D ** -0.5)
    ff_w2 = nrm((DEPTH, D_FF, D), D_FF ** -0.5)
    final_g = 1.0 + nrm((D,), 0.02)
    conv_w_in = nrm((N_A, D, 3 * D_MIX), D ** -0.5)
    conv_w = nrm((N_A, CONV_WIDTH, D_MIX), CONV_WIDTH ** -0.5)
    conv_b = nrm((N_A, D_MIX), 0.02)
    conv_w_out = nrm((N_A, D_MIX, D), D_MIX ** -0.5)
    ssm_w_in = nrm((N_B, D, D_MIX), D ** -0.5)
    ssm_a_re = -0.5 + nrm((N_B, G, P), 0.01)
    ssm_a_im = math.pi * jnp.arange(P, dtype=f32) + nrm((N_B, G, P), 0.01)
    ssm_log_dt = jax.random.uniform(next(ks), (N_B, G), f32,
                                    minval=math.log(DT_MIN), maxval=math.log(DT_MAX))
    ssm_b_re = nrm((N_B, G, P, S5_GROUP), S5_GROUP ** -0.5)
    ssm_b_im = nrm((N_B, G, P, S5_GROUP), S5_GROUP ** -0.5)
    ssm_c_re = nrm((N_B, G, S5_GROUP, P), P ** -0.5)
    ssm_c_im = nrm((N_B, G, S5_GROUP, P), P ** -0.5)
    ssm_d = nrm((N_B, D_MIX), 0.5)
    ssm_glu_w = nrm((N_B, D_MIX, D_MIX), D_MIX ** -0.5)
    ssm_glu_b = nrm((N_B, D_MIX), 0.02)
    ssm_w_out = nrm((N_B, D_MIX, D), D_MIX ** -0.5)
    sg_w_in = nrm((N_C, D, 2 * D_MIX), D ** -0.5)
    sg_v_g = 1.0 + nrm((N_C, D_MIX), 0.02)
    sg_w_s = nrm((N_C, SG_HEADS, CHUNK, CHUNK), CHUNK ** -0.5)
    sg_b_s = 1.0 + nrm((N_C, SG_HEADS, CHUNK), 0.02)
    sg_w_out = nrm((N_C, D_MIX, D), D_MIX ** -0.5)
    return {
        "x": x, "c": c, "ada_w": ada_w, "ada_b": ada_b,
        "norm1_g": norm1_g, "norm2_g": norm2_g, "ff_w1": ff_w1, "ff_w2": ff_w2,
        "final_g": final_g,
        "conv_w_in": conv_w_in, "conv_w": conv_w, "conv_b": conv_b, "conv_w_out": conv_w_out,
        "ssm_w_in": ssm_w_in, "ssm_a_re": ssm_a_re, "ssm_a_im": ssm_a_im,
        "ssm_log_dt": ssm_log_dt, "ssm_b_re": ssm_b_re, "ssm_b_im": ssm_b_im,
        "ssm_c_re": ssm_c_re, "ssm_c_im": ssm_c_im, "ssm_d": ssm_d,
        "ssm_glu_w": ssm_glu_w, "ssm_glu_b": ssm_glu_b, "ssm_w_out": ssm_w_out,
        "sg_w_in": sg_w_in, "sg_v_g": sg_v_g, "sg_w_s": sg_w_s, "sg_b_s": sg_b_s,
        "sg_w_out": sg_w_out,
    }


def reference(x, c, ada_w, ada_b, norm1_g, norm2_g, ff_w1, ff_w2, final_g,
              conv_w_in, conv_w, conv_b, conv_w_out,
              ssm_w_in, ssm_a_re, ssm_a_im, ssm_log_dt, ssm_b_re, ssm_b_im,
              ssm_c_re, ssm_c_im, ssm_d, ssm_glu_w, ssm_glu_b, ssm_w_out,
              sg_w_in, sg_v_g, sg_w_s, sg_b_s, sg_w_out):
    c_act = jax.nn.silu(c)
    for i in range(DEPTH):
        kind = i % N_MIXERS
        j = i // N_MIXERS
        mod = (c_act @ ada_w[i] + ada_b[i])[:, None, :]
        sh1, sc1, g1, sh2, sc2, g2 = jnp.split(mod, 6, axis=-1)
        h = rmsnorm(x, norm1_g[i]) * (1.0 + sc1) + sh1
        if kind == 0:
            y = short_conv_mixer(h, conv_w_in[j], conv_w[j], conv_b[j], conv_w_out[j])
        elif kind == 1:
            y = s5_mixer(h, ssm_w_in[j], ssm_a_re[j], ssm_a_im[j], ssm_log_dt[j],
                         ssm_b_re[j], ssm_b_im[j], ssm_c_re[j], ssm_c_im[j], ssm_d[j],
                         ssm_glu_w[j], ssm_glu_b[j], ssm_w_out[j])
        else:
            y = spatial_gating_mixer(h, sg_w_in[j], sg_v_g[j], sg_w_s[j], sg_b_s[j], sg_w_out[j])
        x = x + g1 * y
        h = rmsnorm(x, norm2_g[i]) * (1.0 + sc2) + sh2
        x = x + g2 * squared_relu_mlp(h, ff_w1[i], ff_w2[i])
    return rmsnorm(x, final_g)
```

```python
import math
from contextlib import ExitStack

import numpy as np
import concourse.bass as bass
import concourse.mybir as mybir
from concourse.bass_utils import run_bass_kernel_spmd

F32 = mybir.dt.float32
BF16 = mybir.dt.bfloat16
I32 = mybir.dt.int32
AF = mybir.ActivationFunctionType
ALU = mybir.AluOpType
AX = mybir.AxisListType

D = 1024
L = 4096
DEPTH = 4
TT = 512
NT = L // TT
KC = 8
EPS = 1e-6
ENGS = ("pe", "act", "dve", "pool", "sp")
N_DMA_SEMS = 14
PI = math.pi


class Prog:
    def __init__(self, nc, stack):
        self.nc = nc
        self.stream = {e: [] for e in ENGS}
        self.count = {e: 0 for e in ENGS}
        self.sem = {e: stack.enter_context(nc.semaphore("s_" + e)) for e in ENGS if e != "sp"}
        self.dsem = [stack.enter_context(nc.semaphore("d%d" % i)) for i in range(N_DMA_SEMS)]
        self.dval = [0] * N_DMA_SEMS
        self.dnext = 0
        self.waited = {e: {} for e in ENGS}
        self.last_w = {}
        self.readers = {}

    def _deps(self, eng, reads, writes, same_engine_ok=False):
        evs = []
        for r in reads:
            ev = self.last_w.get(r)
            if ev is not None:
                evs.append((ev, False))
        for w in writes:
            ev = self.last_w.get(w)
            if ev is not None:
                evs.append((ev, False))
            for ev in self.readers.get(w, {}).values():
                evs.append((ev, True))
        for (ev, is_war) in evs:
            owner, key, sem, val = ev
            if owner == eng and (same_engine_ok or is_war):
                continue
            if self.waited[eng].get(key, 0) >= val:
                continue
            self.waited[eng][key] = val
            self.stream[eng].append(("wait", sem, val))

    def _record(self, rkey, ev, reads, writes):
        for w in writes:
            self.last_w[w] = ev
            self.readers[w] = {}
        for r in reads:
            if r in writes:
                continue
            self.readers.setdefault(r, {})[rkey] = ev

    def op(self, eng, fn, reads=(), writes=()):
        self.group(eng, [fn], reads, writes)

    def group(self, eng, fns, reads=(), writes=()):
        psr = [r for r in reads if r.startswith("ps") and r not in writes]
        if psr:
            writes = list(writes) + psr
        self._deps(eng, reads, writes, same_engine_ok=(eng == "pe"))
        for fn in fns[:-1]:
            self.stream[eng].append(("ins", fn, None))
        self.count[eng] += 1
        ev = (eng, eng, self.sem[eng], self.count[eng])
        self.stream[eng].append(("ins", fns[-1], self.sem[eng]))
        self._record(eng, ev, reads, writes)

    def dma(self, out, in_, reads=(), writes=(), eng="sp"):
        i = self.dnext
        self.dnext = (self.dnext + 1) % N_DMA_SEMS
        key = "dma%d" % i
        sem = self.dsem[i]
        if self.dval[i] > 0 and self.waited[eng].get(key, 0) < self.dval[i]:
            self.waited[eng][key] = self.dval[i]
            self.stream[eng].append(("wait", sem, self.dval[i]))
        self._deps(eng, reads, writes)
        self.dval[i] += 16
        ev = ("dmaq", key, sem, self.dval[i])
        self.stream[eng].append(("dma", out, in_, sem))
        self._record("dmaq_" + key, ev, reads, writes)

    def barrier(self):
        for e in ENGS:
            for o in ENGS:
                if o != e and o != "sp" and self.count[o] > 0 and self.waited[e].get(o, 0) < self.count[o]:
                    self.waited[e][o] = self.count[o]
                    self.stream[e].append(("wait", self.sem[o], self.count[o]))
            for i in range(N_DMA_SEMS):
                key = "dma%d" % i
                if self.dval[i] > 0 and self.waited[e].get(key, 0) < self.dval[i]:
                    self.waited[e][key] = self.dval[i]
                    self.stream[e].append(("wait", self.dsem[i], self.dval[i]))

    def finish(self):
        for i in range(N_DMA_SEMS):
            if self.dval[i] > 0:
                self.stream["sp"].append(("wait", self.dsem[i], self.dval[i]))
        for e in ENGS:
            if e != "sp" and self.count[e] > 0:
                self.stream["sp"].append(("wait", self.sem[e], self.count[e]))

    def emit(self, block):
        def run(engh, items):
            for it in items:
                if it[0] == "wait":
                    engh.wait_ge(it[1], it[2])
                elif it[0] == "ins":
                    ins = it[1](engh)
                    if it[2] is not None:
                        ins.then_inc(it[2], 1)
                else:
                    engh.dma_start(out=it[1], in_=it[2]).then_inc(it[3], 16)

        @block.sync
        def _(e):
            run(e, self.stream["sp"])

        @block.tensor
        def _(e):
            run(e, self.stream["pe"])

        @block.scalar
        def _(e):
            run(e, self.stream["act"])

        @block.vector
        def _(e):
            run(e, self.stream["dve"])

        @block.gpsimd
        def _(e):
            run(e, self.stream["pool"])


class Builder:
    def __init__(self, n_layers=DEPTH):
        self.n_layers = n_layers
        self.nc = bass.Bass("TRN2", target_bir_lowering=False)
        self.dram = {}
        self.rr = 0
        self.cast_rr = 0
        self.stage_rr = 0
        self.wparts = {}

    def din(self, name, shape):
        self.dram[name] = self.nc.dram_tensor(name, list(shape), F32, kind="ExternalInput").ap()

    def declare(self):
        din = self.din
        din("x", [L, D]); din("cT", [128, 8]); din("ada_w", [DEPTH, D, 6 * D]); din("ada_bT", [128, DEPTH, 48])
        din("n1gT", [128, DEPTH, 8]); din("n2gT", [128, DEPTH, 8]); din("fgT", [128, 8])
        din("ff_w1", [DEPTH, D, 4 * D]); din("ff_w2", [DEPTH, 4 * D, D])
        din("conv_w_in", [2, D, 3 * D]); din("conv_wT", [128, 2, 3, 8]); din("conv_bT", [128, 2, 8])
        din("conv_w_out", [2, D, D])
        din("ssm_w_in", [D, D]); din("ssm_glu_w", [D, D]); din("ssm_w_out", [D, D])
        din("glu_bT", [128, 8]); din("dT", [128, 8])
        din("are_c", [128, 32]); din("aim_c", [128, 32]); din("ldt_c", [128, 32])
        din("Bre_z", [128, 32, 128]); din("Bim_z", [128, 32, 128])
        din("Cre_z", [128, 32, 128]); din("Cim_z", [128, 32, 128])
        din("sg_w_in", [D, 2 * D]); din("sg_w_out", [D, D]); din("sg_w_s", [8, 128, 128])
        din("sg_bias_rep", [128, 8, 128]); din("sg_gain_rep", [128, D])
        din("ident", [128, 128]); din("tril", [128, 128])
        self.out = self.nc.dram_tensor("out", [L, D], F32, kind="ExternalOutput").ap()
        self.xT = self.nc.dram_tensor("xT_scr", [D, L], F32, kind="Internal").ap()

    def alloc(self, st):
        nc = self.nc

        def sb(name, shape, dt):
            return st.enter_context(nc.sbuf_tensor("sb_" + name, list(shape), dt))

        self.WA = sb("WA", [128, 32768], BF16)
        self.WB = sb("WB", [128, 32768], BF16)
        self.stage = sb("stage", [128, 3, 1024], F32)
        self.xt = sb("xt", [128, 8, TT], F32)
        self.h = sb("h", [128, 8, TT], BF16)
        self.sq = sb("sq", [128, 2, TT], BF16)
        self.tmpn = sb("tmpn", [128, 2, TT], F32)
        self.rt = sb("rt", [128, 2, TT], F32)
        self.rstd = sb("rstd", [128, TT], F32)
        self.A1 = sb("A1", [128, 16, TT], BF16)
        self.sbf = sb("sbf", [128, 2, 2, TT], BF16)
        self.ident = sb("ident", [128, 128], F32)
        self.tril = sb("tril", [128, 128], F32)
        self.ones_bf = sb("ones_bf", [128, 128], BF16)
        self.ones_f = sb("ones_f", [128, 128], F32)
        self.epsc = sb("epsc", [128, 1], F32)
        self.cT = sb("cT", [128, 8], F32)
        self.cab = sb("cab", [128, 8], BF16)
        self.modT = sb("modT", [128, DEPTH, 48], F32)
        self.adab = sb("adab", [128, DEPTH, 48], F32)
        self.n1g = sb("n1g", [128, DEPTH, 8], F32)
        self.n2g = sb("n2g", [128, DEPTH, 8], F32)
        self.fg = sb("fg", [128, 8], F32)
        self.aT = sb("aT", [128, DEPTH, 2, 8], F32)
        self.cw = sb("cw", [128, 2, 3, 8], F32)
        self.cb = sb("cb", [128, 2, 8], F32)
        self.zc = sb("zc", [128, 8, 2], F32)
        self.glub = sb("glub", [128, 8], F32)
        self.dsk = sb("dsk", [128, 8], F32)
        self.s5c = sb("s5c", [128, 16, 32], F32)
        self.s5i = sb("s5i", [128, 32], I32)
        self.pw = sb("pw", [128, 3, 9, 32], F32)
        self.car = sb("car", [128, 2, 32], F32)
        self.ssv = sb("ssv", [128, 4], F32)
        self.dg = sb("dg", [128, 2, 128], F32)
        self.WAf = self.WA[:, :].bitcast(F32)
        self.A2 = self.WB[:, 16384:24576].bitcast(F32)
        self.A3 = self.WB[:, 24576:32768].bitcast(F32)
        self.KS0 = self.WB[:, 8192:14336].bitcast(F32)
        self.KS1 = self.stage[:, :, :].rearrange("p a b -> p (a b)")
        self.rowtmp = self.rt[0:1, :, :].rearrange("p a b -> p (a b)")
        self.ps = [st.enter_context(nc.psum_tensor("ps%d" % i, [128, 512], F32)) for i in range(8)]

    def next_ps(self):
        i = self.rr
        self.rr = (self.rr + 1) % 6
        return self.ps[i], "ps%d" % i

    def mm(self, out_ps, psname, pairs, reads, flags=None):
        fns = []
        n = len(pairs)
        for i, (a, b) in enumerate(pairs):
            if flags is None:
                o, s0, s1 = out_ps, (i == 0), (i == n - 1)
            else:
                o, s0, s1 = flags[i]
            fns.append(lambda e, o=o, a=a, b=b, s0=s0, s1=s1: e.matmul(o, lhsT=a, rhs=b, start=s0, stop=s1))
        self.P.group("pe", fns, reads=reads, writes=[psname])

    def act(self, out, in_, func, reads, writes, bias=None, scale=1.0):
        if bias is None:
            fn = lambda e: e.activation(out=out, in_=in_, func=func, scale=scale)
        else:
            fn = lambda e: e.activation(out=out, in_=in_, func=func, bias=bias, scale=scale)
        self.P.op("act", fn, reads=reads, writes=writes)

    def stt(self, eng, out, in0, scalar, in1, op0, op1, reads, writes):
        self.P.op(eng, lambda e: e.scalar_tensor_tensor(out=out, in0=in0, scalar=scalar, in1=in1, op0=op0, op1=op1),
                  reads=reads, writes=writes)

    def tt(self, eng, out, in0, in1, op, reads, writes):
        self.P.op(eng, lambda e: e.tensor_tensor(out=out, in0=in0, in1=in1, op=op), reads=reads, writes=writes)

    def ts(self, eng, out, in0, s1, s2, op0, op1, reads, writes):
        if s2 is None:
            fn = lambda e: e.tensor_scalar(out=out, in0=in0, scalar1=s1, scalar2=None, op0=op0)
        else:
            fn = lambda e: e.tensor_scalar(out=out, in0=in0, scalar1=s1, scalar2=s2, op0=op0, op1=op1)
        self.P.op(eng, fn, reads=reads, writes=writes)

    def cp(self, eng, out, in_, reads, writes):
        if eng == "act":
            self.act(out, in_, AF.Copy, reads, writes)
        else:
            self.P.op(eng, lambda e: e.tensor_copy(out=out, in_=in_), reads=reads, writes=writes)

    def memset(self, eng, ap, val, writes):
        self.P.op(eng, lambda e: e.memset(ap, val), writes=writes)

    def load_w(self, name, dst3, src2, K, N, scale=None):
        parts = []
        for k in range(K):
            for c0 in range(0, N, 1024):
                w = min(1024, N - c0)
                b = self.stage_rr
                self.stage_rr = (self.stage_rr + 1) % 3
                sname = "stage%d" % b
                self.P.dma(self.stage[:, b, 0:w], src2[k * 128:(k + 1) * 128, c0:c0 + w], writes=[sname])
                pn = "%s_%d_%d" % (name, k, c0)
                eng = ("act", "dve", "pool")[self.cast_rr % 3]
                self.cast_rr += 1
                if scale is None:
                    self.cp(eng, dst3[:, k, c0:c0 + w], self.stage[:, b, 0:w], [sname], [pn])
                else:
                    self.act(dst3[:, k, c0:c0 + w], self.stage[:, b, 0:w], AF.Copy, [sname], [pn], scale=scale)
                parts.append(pn)
        self.wparts[name] = parts
        return parts

    def wview(self, buf, c0, K, N):
        return buf[:, c0:c0 + K * N].rearrange("p (k n) -> p k n", k=K)

    def prologue(self):
        P = self.P
        d = self.dram
        P.dma(self.ident[:], d["ident"], writes=["ident"])
        P.dma(self.tril[:], d["tril"], writes=["tril"])
        P.dma(self.cT[:], d["cT"], writes=["cT"])
        P.dma(self.adab[:], d["ada_bT"], writes=["adab"])
        P.dma(self.n1g[:], d["n1gT"], writes=["n1g"])
        P.dma(self.n2g[:], d["n2gT"], writes=["n2g"])
        P.dma(self.fg[:], d["fgT"], writes=["fg"])
        P.dma(self.cw[:], d["conv_wT"], writes=["cw"])
        P.dma(self.cb[:], d["conv_bT"], writes=["cb"])
        P.dma(self.glub[:], d["glu_bT"], writes=["glub"])
        P.dma(self.dsk[:], d["dT"], writes=["dsk"])
        self.memset("pool", self.ones_bf[:], 1.0, ["ones_bf"])
        self.memset("pool", self.ones_f[:], 1.0, ["ones_f"])
        self.memset("pool", self.epsc[:], EPS, ["epsc"])
        self.act(self.s5c[:, 0, 0:8], self.cT[:], AF.Sigmoid, ["cT"], ["sgc"])
        self.tt("dve", self.cab[:], self.cT[:], self.s5c[:, 0, 0:8], ALU.mult, ["cT", "sgc"], ["cab"])

    def ada_stage(self, l):
        P = self.P
        wtmp = self.A1
        for cbk in range(6):
            psa, na = self.next_ps()
            psb, nb = self.next_ps()
            for k in range(8):
                b = self.stage_rr
                self.stage_rr = (self.stage_rr + 1) % 3
                sname = "stage%d" % b
                P.dma(self.stage[:, b, :], self.dram["ada_w"][l, k * 128:(k + 1) * 128, cbk * 1024:(cbk + 1) * 1024],
                      writes=[sname])
                wb = (cbk * 8 + k) % 4
                wt = self.A1[:, 2 * wb:2 * wb + 2, :].rearrange("p a b -> p (a b)")
                wn = ["a1_%d" % (2 * wb), "a1_%d" % (2 * wb + 1)]
                eng = ("act", "dve", "pool")[self.cast_rr % 3]
                self.cast_rr += 1
                self.cp(eng, wt, self.stage[:, b, :], [sname], wn)
                lhs = self.cab[:, k:k + 1]
                self.P.group("pe", [
                    (lambda e, o=psa[0:1, :], a=lhs, r=wt[:, 0:512], s0=(k == 0), s1=(k == 7):
                     e.matmul(o, lhsT=a, rhs=r, start=s0, stop=s1)),
                    (lambda e, o=psb[0:1, :], a=lhs, r=wt[:, 512:1024], s0=(k == 0), s1=(k == 7):
                     e.matmul(o, lhsT=a, rhs=r, start=s0, stop=s1)),
                ], reads=wn + ["cab"], writes=[na, nb])
            self.act(self.rowtmp[0:1, 0:512], psa[0:1, :], AF.Copy, [na], ["rt0", "rt1"])
            self.act(self.rowtmp[0:1, 512:1024], psb[0:1, :], AF.Copy, [nb], ["rt0", "rt1"])
            pst, nt = self.next_ps()
            fns = []
            for m in range(8):
                fns.append(lambda e, o=pst[:, m:m + 1], a=self.rowtmp[0:1, m * 128:(m + 1) * 128], r=self.ones_f[0:1, 0:1]:
                           e.matmul(o, lhsT=a, rhs=r, start=True, stop=True))
            P.group("pe", fns, reads=["rt0", "rt1", "ones_f"], writes=[nt])
            self.tt("dve", self.modT[:, l, cbk * 8:(cbk + 1) * 8], pst[:, 0:8], self.adab[:, l, cbk * 8:(cbk + 1) * 8],
                    ALU.add, [nt, "adab"], ["modT%d" % l])
        mn = "modT%d" % l
        self.ts("dve", self.aT[:, l, 0, :], self.modT[:, l, 8:16], 1.0, None, ALU.add, None, [mn], ["aT%d" % l])
        self.tt("dve", self.aT[:, l, 0, :], self.aT[:, l, 0, :], self.n1g[:, l, :], ALU.mult, ["aT%d" % l, "n1g"], ["aT%d" % l])
        self.ts("dve", self.aT[:, l, 1, :], self.modT[:, l, 32:40], 1.0, None, ALU.add, None, [mn], ["aT%d" % l])
        self.tt("dve", self.aT[:, l, 1, :], self.aT[:, l, 1, :], self.n2g[:, l, :], ALU.mult, ["aT%d" % l, "n2g"], ["aT%d" % l])

    def load_x_first(self, ti):
        P = self.P
        for tb in range(4):
            b = self.stage_rr
            self.stage_rr = (self.stage_rr + 1) % 3
            sname = "stage%d" % b
            r0 = (ti * 4 + tb) * 128
            P.dma(self.stage[:, b, :], self.dram["x"][r0:r0 + 128, :], writes=[sname])
            for hf in range(2):
                ps, pn = self.next_ps()
                fns = []
                for kk in range(4):
                    k = hf * 4 + kk
                    fns.append(lambda e, o=ps[:, kk * 128:(kk + 1) * 128], i=self.stage[:, b, k * 128:(k + 1) * 128]:
                               e.transpose(o, i, self.ident[:]))
                P.group("pe", fns, reads=[sname, "ident"], writes=[pn])
                eng = "dve" if hf == 0 else "act"
                self.cp(eng, self.xt[:, hf * 4:hf * 4 + 4, tb * 128:(tb + 1) * 128],
                        ps[:, :].rearrange("p (a b) -> p a b", a=4), [pn], ["xt%d" % k for k in range(hf * 4, hf * 4 + 4)])

    def load_x(self, ti):
        for k in range(8):
            self.P.dma(self.xt[:, k, :], self.xT[k * 128:(k + 1) * 128, ti * TT:(ti + 1) * TT],
                       reads=["xT_%d_%d" % (k, ti)], writes=["xt%d" % k])

    def store_x(self, ti):
        for k in range(8):
            self.P.dma(self.xT[k * 128:(k + 1) * 128, ti * TT:(ti + 1) * TT], self.xt[:, k, :],
                       reads=["xt%d" % k], writes=["xT_%d_%d" % (k, ti)])

    def sumsq_rstd(self):
        P = self.P
        pss, pn = self.ps[6], "ps6"
        for k in range(8):
            b = k % 2
            self.tt("pool", self.sq[:, b, :], self.xt[:, k, :], self.xt[:, k, :], ALU.mult, ["xt%d" % k], ["sq%d" % b])
            P.group("pe", [lambda e, b=b, k=k: e.matmul(pss[:], lhsT=self.ones_bf[:], rhs=self.sq[:, b, :],
                                                         start=(k == 0), stop=(k == 7))],
                    reads=["sq%d" % b, "ones_bf"], writes=[pn])
        self.act(self.rstd[:], pss[:], AF.Sqrt, [pn, "epsc"], ["rstd"], bias=self.epsc[:, 0:1], scale=1.0 / D)
        P.op("dve", lambda e: e.reciprocal(out=self.rstd[:], in_=self.rstd[:]), reads=["rstd"], writes=["rstd"])

    def norm_stage(self, l, which):
        self.sumsq_rstd()
        an = "aT%d" % l
        mn = "modT%d" % l
        sh0 = 0 if which == 0 else 24
        for k in range(8):
            b = k % 2
            self.stt("dve", self.tmpn[:, b, :], self.xt[:, k, :], self.aT[:, l, which, k:k + 1], self.rstd[:],
                     ALU.mult, ALU.mult, ["xt%d" % k, an, "rstd"], ["tmpn%d" % b])
            self.act(self.h[:, k, :], self.tmpn[:, b, :], AF.Identity, ["tmpn%d" % b, mn], ["h%d" % k],
                     bias=self.modT[:, l, sh0 + k:sh0 + k + 1])

    def final_stage(self, ti):
        P = self.P
        self.sumsq_rstd()
        for k in range(8):
            self.stt("dve", self.xt[:, k, :], self.xt[:, k, :], self.fg[:, k:k + 1], self.rstd[:],
                     ALU.mult, ALU.mult, ["xt%d" % k, "fg", "rstd"], ["xt%d" % k])
        for tb in range(4):
            b = self.stage_rr
            self.stage_rr = (self.stage_rr + 1) % 3
            sname = "stage%d" % b
            for hf in range(2):
                ps, pn = self.next_ps()
                fns = []
                for kk in range(4):
                    k = hf * 4 + kk
                    fns.append(lambda e, o=ps[:, kk * 128:(kk + 1) * 128], i=self.xt[:, k, tb * 128:(tb + 1) * 128]:
                               e.transpose(o, i, self.ident[:]))
                P.group("pe", fns, reads=["xt%d" % k for k in range(hf * 4, hf * 4 + 4)] + ["ident"], writes=[pn])
                eng = "dve" if hf == 0 else "act"
                self.cp(eng, self.stage[:, b, hf * 512:(hf + 1) * 512], ps[:, :], [pn], [sname])
            r0 = (ti * 4 + tb) * 128
            P.dma(self.out[r0:r0 + 128, :], self.stage[:, b, :], reads=[sname], writes=["out_%d" % r0])

    def resid_update(self, ps, pn, mo, gate_ap, gname):
        self.stt("dve", self.xt[:, mo, :], ps[:], gate_ap, self.xt[:, mo, :], ALU.mult, ALU.add,
                 [pn, gname, "xt%d" % mo], ["xt%d" % mo])

    def out_proj(self, l, Wout, wname, src, srcnames):
        for mo in range(8):
            ps, pn = self.next_ps()
            self.mm(ps[:], pn, [(Wout[:, k, mo * 128:(mo + 1) * 128], src[:, k, :]) for k in range(8)],
                    reads=self.wparts[wname] + srcnames)
            self.resid_update(ps, pn, mo, self.modT[:, l, 16 + mo:17 + mo], "modT%d" % l)

    def ffn_block(self, l, last):
        self.P.barrier()
        W1 = self.wview(self.WA, 0, 8, 4096)
        W2 = self.wview(self.WB, 0, 32, 1024)
        self.load_w("w1", W1, self.dram["ff_w1"][l], 8, 4096)
        self.load_w("w2", W2, self.dram["ff_w2"][l], 32, 1024)
        r2 = self.A1
        for ti in range(NT):
            self.load_x(ti)
            self.norm_stage(l, 1)
            hn = ["h%d" % k for k in range(8)]
            for half in range(2):
                for jj in range(16):
                    j = half * 16 + jj
                    ps, pn = self.next_ps()
                    self.mm(ps[:], pn, [(W1[:, k, j * 128:(j + 1) * 128], self.h[:, k, :]) for k in range(8)],
                            reads=self.wparts["w1"] + hn)
                    b = jj % 2
                    self.act(self.rt[:, b, :], ps[:], AF.Relu, [pn], ["rt%d" % b])
                    eng = "dve" if jj % 2 == 0 else "pool"
                    self.tt(eng, r2[:, jj, :], self.rt[:, b, :], self.rt[:, b, :], ALU.mult, ["rt%d" % b], ["a1_%d" % jj])
                for mo in range(8):
                    ps, pn = self.next_ps()
                    self.mm(ps[:], pn, [(W2[:, half * 16 + jj, mo * 128:(mo + 1) * 128], r2[:, jj, :]) for jj in range(16)],
                            reads=self.wparts["w2"] + ["a1_%d" % jj for jj in range(16)])
                    self.resid_update(ps, pn, mo, self.modT[:, l, 40 + mo:41 + mo], "modT%d" % l)
            if last:
                self.final_stage(ti)
            else:
                self.store_x(ti)

    def conv_block(self, l, j, first):
        self.P.barrier()
        Win = self.wview(self.WA, 0, 8, 3072)
        Wout = self.wview(self.WB, 0, 8, 1024)
        self.load_w("cwin", Win, self.dram["conv_w_in"][j], 8, 3072)
        self.load_w("cwout", Wout, self.dram["conv_w_out"][j], 8, 1024)
        A2 = self.A2
        cs = [A2[:, 0:512], A2[:, 512:1024]]
        zb = [A2[:, 1024:1538], A2[:, 1538:2052]]
        acc = [A2[:, 2052:2564], A2[:, 2564:3076]]
        q = self.A1
        self.memset("pool", self.zc[:], 0.0, ["zc%d" % m for m in range(8)])
        for ti in range(NT):
            if first:
                self.load_x_first(ti)
            else:
                self.load_x(ti)
            self.norm_stage(l, 0)
            hn = ["h%d" % k for k in range(8)]
            wr = self.wparts["cwin"] + hn
            for m in range(8):
                b = m % 2
                psB, nB = self.next_ps()
                psC, nC = self.next_ps()
                psX, nX = self.next_ps()
                self.mm(psC[:], nC, [(Win[:, k, 1024 + m * 128:1024 + (m + 1) * 128], self.h[:, k, :]) for k in range(8)], wr)
                self.mm(psX[:], nX, [(Win[:, k, 2048 + m * 128:2048 + (m + 1) * 128], self.h[:, k, :]) for k in range(8)], wr)
                self.mm(psB[:], nB, [(Win[:, k, m * 128:(m + 1) * 128], self.h[:, k, :]) for k in range(8)], wr)
                self.act(cs[b], psC[:], AF.Copy, [nC], ["cs%d" % b])
                self.cp("pool", zb[b][:, 0:2], self.zc[:, m, :], ["zc%d" % m], ["zb%d" % b])
                self.tt("dve", zb[b][:, 2:514], psX[:], cs[b], ALU.mult, [nX, "cs%d" % b], ["zb%d" % b])
                self.act(acc[b], zb[b][:, 2:514], AF.Identity, ["zb%d" % b, "cw", "cb"], ["acc%d" % b],
                         bias=self.cb[:, j, m:m + 1], scale=self.cw[:, j, 2, m:m + 1])
                self.stt("dve", acc[b], zb[b][:, 1:513], self.cw[:, j, 1, m:m + 1], acc[b], ALU.mult, ALU.add,
                         ["zb%d" % b, "cw", "acc%d" % b], ["acc%d" % b])
                self.stt("dve", acc[b], zb[b][:, 0:512], self.cw[:, j, 0, m:m + 1], acc[b], ALU.mult, ALU.add,
                         ["zb%d" % b, "cw", "acc%d" % b], ["acc%d" % b])
                self.cp("pool", self.zc[:, m, :], zb[b][:, 512:514], ["zb%d" % b], ["zc%d" % m])
                self.tt("dve", q[:, m, :], psB[:], acc[b], ALU.mult, [nB, "acc%d" % b], ["a1_%d" % m])
            self.out_proj(l, Wout, "cwout", q, ["a1_%d" % m for m in range(8)])
            self.store_x(ti)

    def sg_block(self, l):
        P = self.P
        P.barrier()
        Win = self.wview(self.WA, 0, 8, 2048)
        Wout = self.wview(self.WB, 0, 8, 1024)
        wsT = self.WB[:, 8192:9216].rearrange("p (h t) -> p h t", h=8)
        self.load_w("gwin", Win, self.dram["sg_w_in"], 8, 2048)
        self.load_w("gwout", Wout, self.dram["sg_w_out"], 8, 1024)
        A3 = self.A3
        vsb = A3[:, 0:1024]
        gain = A3[:, 1024:2048]
        bias = A3[:, 2048:3072]
        vsq = A3[:, 3072:3584]
        tmpg = [A3[:, 3584:4096], self.rt[:, 0, :]]
        tmpgn = ["tmpg0", "rt0"]
        vn = self.sbf[:, :, :, :].rearrange("p a b c -> p (a b c)")[:, 0:1024]
        us = self.A2.rearrange("p (k n) -> p k n", k=8)
        gq = self.A1
        P.dma(gain, self.dram["sg_gain_rep"], writes=["gain"])
        P.dma(bias, self.dram["sg_bias_rep"].rearrange("p h t -> p (h t)"), writes=["bias"])
        b = self.stage_rr
        self.stage_rr = (self.stage_rr + 1) % 3
        sname = "stage%d" % b
        stg = self.stage[:, b, :].rearrange("p (h s) -> p h s", h=8)
        P.dma(stg, self.dram["sg_w_s"].rearrange("h t s -> t h s"), writes=[sname])
        self.tt("pool", stg, stg, self.tril[:, :].unsqueeze(1).broadcast_to([128, 8, 128]), ALU.mult,
                [sname, "tril"], [sname])
        for hf in range(2):
            ps, pn = self.next_ps()
            fns = []
            for hh in range(4):
                fns.append(lambda e, o=ps[:, hh * 128:(hh + 1) * 128], i=stg[:, hf * 4 + hh, :]:
                           e.transpose(o, i, self.ident[:]))
            P.group("pe", fns, reads=[sname, "ident"], writes=[pn])
            self.cp("dve", wsT[:, hf * 4:hf * 4 + 4, :], ps[:, :].rearrange("p (a b) -> p a b", a=4), [pn], ["wsT"])
        for ti in range(NT):
            self.load_x(ti)
            self.norm_stage(l, 0)
            hn = ["h%d" % k for k in range(8)]
            wr = self.wparts["gwin"] + hn
            for m in range(8):
                ps, pn = self.next_ps()
                self.mm(ps[:], pn, [(Win[:, k, m * 128:(m + 1) * 128], self.h[:, k, :]) for k in range(8)], wr)
                self.act(us[:, m, :], ps[:], AF.Copy, [pn], ["us%d" % m])
            for n in range(4):
                for hf in range(2):
                    ps, pn = self.next_ps()
                    self.mm(ps[:], pn, [(self.h[:, k, n * 128:(n + 1) * 128], Win[:, k, 1024 + hf * 512:1024 + (hf + 1) * 512])
                                        for k in range(8)], wr)
                    self.act(vsb[:, hf * 512:(hf + 1) * 512], ps[:], AF.Copy, [pn], ["vsb%d" % hf])
                    self.tt("pool", vsq, vsb[:, hf * 512:(hf + 1) * 512], vsb[:, hf * 512:(hf + 1) * 512], ALU.mult,
                            ["vsb%d" % hf], ["vsq"])
                    P.op("dve", lambda e, hf=hf: e.reduce_sum(out=self.ssv[:, hf:hf + 1], in_=vsq, axis=AX.X),
                         reads=["vsq"], writes=["ssv"])
                self.tt("dve", self.ssv[:, 2:3], self.ssv[:, 0:1], self.ssv[:, 1:2], ALU.add, ["ssv"], ["ssv2"])
                self.act(self.ssv[:, 3:4], self.ssv[:, 2:3], AF.Sqrt, ["ssv2", "epsc"], ["rv"], bias=self.epsc[:, 0:1], scale=1.0 / D)
                P.op("dve", lambda e: e.reciprocal(out=self.ssv[:, 3:4], in_=self.ssv[:, 3:4]), reads=["rv"], writes=["rv"])
                self.stt("dve", vn, vsb, self.ssv[:, 3:4], gain, ALU.mult, ALU.mult, ["vsb0", "vsb1", "rv", "gain"], ["vn"])
                for hq in range(2):
                    ps, pn = self.next_ps()
                    pairs, flags = [], []
                    for hh in range(4):
                        hd = hq * 4 + hh
                        pairs.append((vn[:, hd * 128:(hd + 1) * 128], wsT[:, hd, :]))
                        flags.append((ps[:, hh * 128:(hh + 1) * 128], True, True))
                    self.mm(None, pn, pairs, ["vn", "wsT"], flags=flags)
                    tb = tmpg[hq]
                    self.tt("dve", tb, ps[:], bias[:, hq * 512:(hq + 1) * 512], ALU.add, [pn, "bias"], [tmpgn[hq]])
                    self.tt("pool", gq[:, hq * 4:hq * 4 + 4, n * 128:(n + 1) * 128],
                            tb.rearrange("p (a b) -> p a b", a=4), us[:, hq * 4:hq * 4 + 4, n * 128:(n + 1) * 128], ALU.mult,
                            [tmpgn[hq]] + ["us%d" % m for m in range(hq * 4, hq * 4 + 4)],
                            ["a1_%d" % m for m in range(hq * 4, hq * 4 + 4)])
            self.out_proj(l, Wout, "gwout", gq, ["a1_%d" % m for m in range(8)])
            self.store_x(ti)

    def s5_prep(self):
        P = self.P
        d = self.dram
        c = self.s5c
        ARE, AIM, LDT, DT, MAG, ANG, SN, CS, AR, AI, T1, T2, T3, T4, FRE, FIM = [c[:, i, :] for i in range(16)]
        P.dma(ARE, d["are_c"], writes=["s5c"])
        P.dma(AIM, d["aim_c"], writes=["s5c"])
        P.dma(LDT, d["ldt_c"], writes=["s5c"])
        R = ["s5c"]
        self.act(DT, LDT, AF.Exp, R, R)
        self.tt("dve", T1, ARE, DT, ALU.mult, R, R)
        self.act(MAG, T1, AF.Exp, R, R)
        self.tt("dve", ANG, AIM, DT, ALU.mult, R, R)

        def sin_of(dst, src, shift):
            self.ts("dve", T2, src, shift, None, ALU.add, None, R, R)
            self.ts("dve", T3, T2, 1.0 / (2 * PI), None, ALU.mult, None, R, R)
            self.cp("dve", self.s5i[:], T3, R, ["s5i"])
            self.cp("dve", T3, self.s5i[:], ["s5i"], R)
            self.stt("dve", T2, T3, -2 * PI, T2, ALU.mult, ALU.add, R, R)
            self.ts("dve", T3, T2, PI, -2 * PI, ALU.is_gt, ALU.mult, R, R)
            self.tt("dve", T2, T2, T3, ALU.add, R, R)
            self.ts("dve", T3, T2, -PI, 2 * PI, ALU.is_lt, ALU.mult, R, R)
            self.tt("dve", T2, T2, T3, ALU.add, R, R)
            self.ts("dve", T2, T2, -3.141592, 3.141592, ALU.max, ALU.min, R, R)
            self.act(dst, T2, AF.Sin, R, R)

        sin_of(SN, ANG, 0.0)
        sin_of(CS, ANG, PI / 2)
        self.tt("dve", AR, MAG, CS, ALU.mult, R, R)
        self.tt("dve", AI, MAG, SN, ALU.mult, R, R)
        self.tt("dve", T1, ARE, ARE, ALU.mult, R, R)
        self.tt("dve", T2, AIM, AIM, ALU.mult, R, R)
        self.tt("dve", T1, T1, T2, ALU.add, R, R)
        P.op("dve", lambda e: e.reciprocal(out=T4, in_=T1), reads=R, writes=R)
        self.ts("dve", T3, AR, -1.0, None, ALU.add, None, R, R)
        self.tt("dve", T1, T3, ARE, ALU.mult, R, R)
        self.tt("dve", T2, AI, AIM, ALU.mult, R, R)
        self.tt("dve", T1, T1, T2, ALU.add, R, R)
        self.tt("dve", FRE, T1, T4, ALU.mult, R, R)
        self.tt("dve", T1, AI, ARE, ALU.mult, R, R)
        self.tt("dve", T2, T3, AIM, ALU.mult, R, R)
        self.tt("dve", T1, T1, T2, ALU.subtract, R, R)
        self.tt("dve", FIM, T1, T4, ALU.mult, R, R)
        pr, pi_, npi = self.pw[:, 0], self.pw[:, 1], self.pw[:, 2]
        W = ["pw"]
        self.cp("dve", pr[:, 0, :], AR, R, W)
        self.cp("dve", pi_[:, 0, :], AI, R, W)
        for k in range(1, 9):
            self.tt("dve", T1, pr[:, k - 1, :], pr[:, k - 1, :], ALU.mult, W + R, R)
            self.tt("dve", T2, pi_[:, k - 1, :], pi_[:, k - 1, :], ALU.mult, W + R, R)
            self.tt("dve", pr[:, k, :], T1, T2, ALU.subtract, R, W)
            self.tt("dve", T1, pr[:, k - 1, :], pi_[:, k - 1, :], ALU.mult, W + R, R)
            self.ts("dve", pi_[:, k, :], T1, 2.0, None, ALU.mult, None, R, W)
        self.ts("dve", npi, pi_, -1.0, None, ALU.mult, None, W, W)

    def s5_block(self, l):
        P = self.P
        d = self.dram
        P.barrier()
        self.s5_prep()
        Win = self.wview(self.WA, 0, 8, 1024)
        Wglu = self.wview(self.WA, 8192, 8, 1024)
        BW = self.WA[:, 16384:24576].rearrange("p (g r n) -> p g r n", g=32, r=2)
        CW = self.WA[:, 24576:32768].rearrange("p (g r n) -> p g r n", g=32, r=2)
        Wout = self.wview(self.WB, 0, 8, 1024)
        self.load_w("swin", Win, d["ssm_w_in"], 8, 1024)
        self.load_w("sglu", Wglu, d["ssm_glu_w"], 8, 1024)
        self.load_w("swout", Wout, d["ssm_w_out"], 8, 1024)
        c = self.s5c
        FRE, FIM = c[:, 14, :], c[:, 15, :]
        bwparts, cwparts = [], []
        for pc in range(8):
            psr, nr_ = self.next_ps()
            psi, ni_ = self.next_ps()
            for g4 in range(4):
                gp = pc * 4 + g4
                self.ts("dve", self.dg[:, 0, :], self.ident[:], FRE[:, gp:gp + 1], None, ALU.mult, None, ["s5c", "ident"], ["dg0"])
                self.ts("dve", self.dg[:, 1, :], self.ident[:], FIM[:, gp:gp + 1], None, ALU.mult, None, ["s5c", "ident"], ["dg1"])
                self.mm(psr[:, g4 * 128:(g4 + 1) * 128], nr_, [(self.ones_f[:], self.dg[:, 0, :])], ["dg0", "ones_f"],
                        flags=[(psr[:, g4 * 128:(g4 + 1) * 128], True, True)])
                self.mm(psi[:, g4 * 128:(g4 + 1) * 128], ni_, [(self.ones_f[:], self.dg[:, 1, :])], ["dg1", "ones_f"],
                        flags=[(psi[:, g4 * 128:(g4 + 1) * 128], True, True)])
            b1 = self.stage_rr
            b2 = (b1 + 1) % 3
            self.stage_rr = (b1 + 2) % 3
            s1, s2 = "stage%d" % b1, "stage%d" % b2
            bre = self.stage[:, b1, 0:512]
            bim = self.stage[:, b2, 0:512]
            P.dma(bre, d["Bre_z"][:, pc * 4:(pc + 1) * 4, :].rearrange("p g n -> p (g n)"), writes=[s1])
            P.dma(bim, d["Bim_z"][:, pc * 4:(pc + 1) * 4, :].rearrange("p g n -> p (g n)"), writes=[s2])
            t1, t2 = self.tmpn[:, 0, :], self.tmpn[:, 1, :]
            pn = "bw%d" % pc
            self.tt("dve", t1, psr[:], bre, ALU.mult, [nr_, s1], ["tmpn0"])
            self.tt("dve", t2, psi[:], bim, ALU.mult, [ni_, s2], ["tmpn1"])
            self.tt("dve", BW[:, pc * 4:(pc + 1) * 4, 0, :], t1.rearrange("p (g n) -> p g n", g=4),
                    t2.rearrange("p (g n) -> p g n", g=4), ALU.subtract, ["tmpn0", "tmpn1"], [pn + "r"])
            self.tt("dve", t1, psr[:], bim, ALU.mult, [nr_, s2], ["tmpn0"])
            self.tt("dve", t2, psi[:], bre, ALU.mult, [ni_, s1], ["tmpn1"])
            self.tt("dve", BW[:, pc * 4:(pc + 1) * 4, 1, :], t1.rearrange("p (g n) -> p g n", g=4),
                    t2.rearrange("p (g n) -> p g n", g=4), ALU.add, ["tmpn0", "tmpn1"], [pn + "i"])
            bwparts += [pn + "r", pn + "i"]
        for pc in range(8):
            for ri, key, sc in ((0, "Cre_z", 1.0), (1, "Cim_z", -1.0)):
                b1 = self.stage_rr
                self.stage_rr = (b1 + 1) % 3
                s1 = "stage%d" % b1
                P.dma(self.stage[:, b1, 0:512], d[key][:, pc * 4:(pc + 1) * 4, :].rearrange("p g n -> p (g n)"), writes=[s1])
                pn = "cwp%d_%d" % (pc, ri)
                self.act(CW[:, pc * 4:(pc + 1) * 4, ri, :], self.stage[:, b1, 0:512].rearrange("p (g n) -> p g n", g=4),
                         AF.Copy, [s1], [pn], scale=sc)
                cwparts.append(pn)
        KS = [self.KS0, self.KS1]
        ksn = [["ks0_R0", "ks0_I0", "ks0_R1", "ks0_I1"], ["ks1_R0", "ks1_I0", "ks1_R1", "ks1_I1"]]
        self.memset("dve", KS[0], 0.0, ksn[0])
        self.memset("pool", KS[1], 0.0, ksn[1] + ["stage0", "stage1", "stage2"])
        self.memset("dve", self.car[:], 0.0, ["car%d" % g for g in range(32)])
        us = self.A2.rearrange("p (k n) -> p k n", k=8)
        yg = self.A3.rearrange("p (k n) -> p k n", k=8)
        ubf = self.A1[:, 0:8, :]
        ygb = self.A1[:, 8:16, :]
        qv = self.A1[:, 0:8, :]
        pr, pi_, npi = self.pw[:, 0], self.pw[:, 1], self.pw[:, 2]

        def buf(s, i):
            return KS[s][:, i * 768:(i + 1) * 768]

        for ti in range(NT):
            self.load_x(ti)
            self.norm_stage(l, 0)
            hn = ["h%d" % k for k in range(8)]
            for m in range(8):
                ps, pn = self.next_ps()
                self.mm(ps[:], pn, [(Win[:, k, m * 128:(m + 1) * 128], self.h[:, k, :]) for k in range(8)],
                        self.wparts["swin"] + hn)
                self.act(us[:, m, :], ps[:], AF.Copy, [pn], ["us%d" % m])
                self.cp("dve", ubf[:, m, :], ps[:], [pn], ["a1_%d" % m])

            def bu(gp):
                q = gp // 4
                s = gp % 2
                eng = "dve"
                psr, nr_ = self.next_ps()
                psi, ni_ = self.next_ps()
                self.mm(psr[:], nr_, [(BW[:, gp, 0, :], ubf[:, q, :])], bwparts + ["a1_%d" % q])
                self.mm(psi[:], ni_, [(BW[:, gp, 1, :], ubf[:, q, :])], bwparts + ["a1_%d" % q])
                R0, I0, R1, I1 = [buf(s, i) for i in range(4)]
                n = ksn[s]
                cin = "act"
                self.cp(cin, R0[:, 256:768], psr[:], [nr_], [n[0]])
                self.cp(cin, I0[:, 256:768], psi[:], [ni_], [n[1]])
                cr, ci = self.car[:, 0, gp:gp + 1], self.car[:, 1, gp:gp + 1]
                carn = "car%d" % gp
                self.stt(eng, R0[:, 256:257], cr, pr[:, 0, gp:gp + 1], R0[:, 256:257], ALU.mult, ALU.add, [carn, "pw", n[0]], [n[0]])
                self.stt(eng, R0[:, 256:257], ci, npi[:, 0, gp:gp + 1], R0[:, 256:257], ALU.mult, ALU.add, [carn, "pw", n[0]], [n[0]])
                self.stt(eng, I0[:, 256:257], ci, pr[:, 0, gp:gp + 1], I0[:, 256:257], ALU.mult, ALU.add, [carn, "pw", n[1]], [n[1]])
                self.stt(eng, I0[:, 256:257], cr, pi_[:, 0, gp:gp + 1], I0[:, 256:257], ALU.mult, ALU.add, [carn, "pw", n[1]], [n[1]])
                src, dst = (0, 1), (2, 3)
                for k in range(9):
                    dd = 1 << k
                    sR, sI = buf(s, src[0]), buf(s, src[1])
                    dR, dI = buf(s, dst[0]), buf(s, dst[1])
                    nsR, nsI, ndR, ndI = n[src[0]], n[src[1]], n[dst[0]], n[dst[1]]
                    a_r, a_i, a_n = pr[:, k, gp:gp + 1], pi_[:, k, gp:gp + 1], npi[:, k, gp:gp + 1]
                    self.stt(eng, dR[:, 256:768], sR[:, 256 - dd:768 - dd], a_r, sR[:, 256:768], ALU.mult, ALU.add, [nsR, "pw"], [ndR])
                    self.stt(eng, dR[:, 256:768], sI[:, 256 - dd:768 - dd], a_n, dR[:, 256:768], ALU.mult, ALU.add, [nsI, ndR, "pw"], [ndR])
                    self.stt(eng, dI[:, 256:768], sI[:, 256 - dd:768 - dd], a_r, sI[:, 256:768], ALU.mult, ALU.add, [nsI, "pw"], [ndI])
                    self.stt(eng, dI[:, 256:768], sR[:, 256 - dd:768 - dd], a_i, dI[:, 256:768], ALU.mult, ALU.add, [nsR, ndI, "pw"], [ndI])
                    src, dst = dst, src
                fR, fI = buf(s, src[0]), buf(s, src[1])
                nfR, nfI = n[src[0]], n[src[1]]
                self.cp(eng, cr, fR[:, 767:768], [nfR], [carn])
                self.cp(eng, ci, fI[:, 767:768], [nfI], [carn])
                self.cp(eng, self.sbf[:, s, 0, :], fR[:, 256:768], [nfR], ["sbf%d" % s])
                self.cp(eng, self.sbf[:, s, 1, :], fI[:, 256:768], [nfI], ["sbf%d" % s])

            def outp(gp):
                q = gp // 4
                s = gp % 2
                psY, nY = self.ps[6 + (q % 2)], "ps%d" % (6 + (q % 2))
                self.mm(None, nY, [(CW[:, gp, 0, :], self.sbf[:, s, 0, :]), (CW[:, gp, 1, :], self.sbf[:, s, 1, :])],
                        cwparts + ["sbf%d" % s],
                        flags=[(psY[:], gp % 4 == 0, False), (psY[:], False, gp % 4 == 3)])
                if gp % 4 == 3:
                    b = q % 2
                    self.stt("dve", self.tmpn[:, b, :], us[:, q, :], self.dsk[:, q:q + 1], psY[:], ALU.mult, ALU.add,
                             ["us%d" % q, "dsk", nY], ["tmpn%d" % b])
                    self.act(yg[:, q, :], self.tmpn[:, b, :], AF.Gelu_apprx_tanh, ["tmpn%d" % b], ["yg%d" % q])
                    self.cp("act", ygb[:, q, :], yg[:, q, :], ["yg%d" % q], ["a1_%d" % (8 + q)])

            LOOK = 1
            for sidx in range(32 + LOOK):
                if sidx < 32:
                    bu(sidx)
                if sidx - LOOK >= 0:
                    outp(sidx - LOOK)
            ygn = ["a1_%d" % (8 + k) for k in range(8)]
            for mo in range(8):
                ps, pn = self.next_ps()
                self.mm(ps[:], pn, [(Wglu[:, k, mo * 128:(mo + 1) * 128], ygb[:, k, :]) for k in range(8)],
                        self.wparts["sglu"] + ygn)
                b = mo % 2
                self.act(self.rt[:, b, :], ps[:], AF.Sigmoid, [pn, "glub"], ["rt%d" % b], bias=self.glub[:, mo:mo + 1])
                self.tt("dve", qv[:, mo, :], yg[:, mo, :], self.rt[:, b, :], ALU.mult, ["yg%d" % mo, "rt%d" % b], ["a1_%d" % mo])
            self.out_proj(l, Wout, "swout", qv, ["a1_%d" % m for m in range(8)])
            self.store_x(ti)

    def build(self):
        nc = self.nc
        self.declare()
        with ExitStack() as st:
            self.alloc(st)
            self.P = Prog(nc, st)
            block = st.enter_context(nc.Block())
            self.prologue()
            for l in range(self.n_layers):
                self.ada_stage(l)
            for l in range(self.n_layers):
                kind = l % 3
                if kind == 0:
                    self.conv_block(l, l // 3, first=(l == 0))
                elif kind == 1:
                    self.s5_block(l)
                else:
                    self.sg_block(l)
                self.ffn_block(l, last=(l == self.n_layers - 1))
            self.P.finish()
            self.P.emit(block)
        return nc


def tT(v):
    v = np.asarray(v, np.float32)
    lead = v.shape[:-1]
    r = v.reshape(lead + (8, 128))
    return np.ascontiguousarray(np.moveaxis(r, -1, 0))


def host_layout(inp, b):
    f32 = np.float32
    m = {}
    m["x"] = np.ascontiguousarray(inp["x"][b], f32)
    m["cT"] = tT(inp["c"][b])
    m["ada_w"] = np.ascontiguousarray(inp["ada_w"], f32)
    ab = np.asarray(inp["ada_b"], f32).reshape(DEPTH, 48, 128)
    m["ada_bT"] = np.ascontiguousarray(ab.transpose(2, 0, 1))
    m["n1gT"] = tT(inp["norm1_g"])
    m["n2gT"] = tT(inp["norm2_g"])
    m["fgT"] = tT(inp["final_g"])
    m["ff_w1"] = np.ascontiguousarray(inp["ff_w1"], f32)
    m["ff_w2"] = np.ascontiguousarray(inp["ff_w2"], f32)
    m["conv_w_in"] = np.ascontiguousarray(inp["conv_w_in"], f32)
    m["conv_wT"] = tT(inp["conv_w"])
    m["conv_bT"] = tT(inp["conv_b"])
    m["conv_w_out"] = np.ascontiguousarray(inp["conv_w_out"], f32)
    m["ssm_w_in"] = np.ascontiguousarray(inp["ssm_w_in"][0], f32)
    m["ssm_glu_w"] = np.ascontiguousarray(inp["ssm_glu_w"][0], f32)
    m["ssm_w_out"] = np.ascontiguousarray(inp["ssm_w_out"][0], f32)
    m["glu_bT"] = tT(inp["ssm_glu_b"][0])
    m["dT"] = tT(inp["ssm_d"][0])
    def compact(a):
        a = np.asarray(a, f32).reshape(32, 2, 64)
        return np.ascontiguousarray(a.transpose(1, 2, 0).reshape(128, 32))
    m["are_c"] = compact(inp["ssm_a_re"][0])
    m["aim_c"] = compact(inp["ssm_a_im"][0])
    m["ldt_c"] = compact(np.broadcast_to(np.asarray(inp["ssm_log_dt"][0], f32)[:, None], (64, 64)))
    bre = np.asarray(inp["ssm_b_re"][0], f32)
    bim = np.asarray(inp["ssm_b_im"][0], f32)
    cre = np.asarray(inp["ssm_c_re"][0], f32)
    cim = np.asarray(inp["ssm_c_im"][0], f32)
    Bre_z = np.zeros((128, 32, 128), f32); Bim_z = np.zeros((128, 32, 128), f32)
    Cre_z = np.zeros((128, 32, 128), f32); Cim_z = np.zeros((128, 32, 128), f32)
    for g in range(64):
        gp, g2, g8 = g // 2, g % 2, g % 8
        Bre_z[g8 * 16:(g8 + 1) * 16, gp, g2 * 64:(g2 + 1) * 64] = bre[g].T
        Bim_z[g8 * 16:(g8 + 1) * 16, gp, g2 * 64:(g2 + 1) * 64] = bim[g].T
        Cre_z[g2 * 64:(g2 + 1) * 64, gp, g8 * 16:(g8 + 1) * 16] = cre[g].T
        Cim_z[g2 * 64:(g2 + 1) * 64, gp, g8 * 16:(g8 + 1) * 16] = cim[g].T
    m["Bre_z"], m["Bim_z"], m["Cre_z"], m["Cim_z"] = Bre_z, Bim_z, Cre_z, Cim_z
    m["sg_w_in"] = np.ascontiguousarray(inp["sg_w_in"][0], f32)
    m["sg_w_out"] = np.ascontiguousarray(inp["sg_w_out"][0], f32)
    m["sg_w_s"] = np.ascontiguousarray(inp["sg_w_s"][0], f32)
    m["sg_bias_rep"] = np.ascontiguousarray(np.broadcast_to(np.asarray(inp["sg_b_s"][0], f32)[None], (128, 8, 128)))
    m["sg_gain_rep"] = np.ascontiguousarray(np.broadcast_to(np.asarray(inp["sg_v_g"][0], f32)[None], (128, D)))
    m["ident"] = np.eye(128, dtype=f32)
    m["tril"] = np.tril(np.ones((128, 128), f32))
    return m


_NC_CACHE = {}


def kernel(_n_layers=DEPTH, **inputs):
    inp = {k: np.asarray(v) for k, v in inputs.items()}
    if _n_layers not in _NC_CACHE:
        _NC_CACHE[_n_layers] = Builder(_n_layers).build()
    nc = _NC_CACHE[_n_layers]
    in_maps = [host_layout(inp, b) for b in range(8)]
    res = run_bass_kernel_spmd(nc, in_maps, core_ids=list(range(8)))
    out = np.stack([np.asarray(r["out"], np.float32) for r in res.results], axis=0)
    return out
```

```python
import math
from contextlib import ExitStack

import numpy as np
import concourse.bass as bass
import concourse.mybir as mybir
from concourse.bass_utils import run_bass_kernel_spmd

F32 = mybir.dt.float32
BF16 = mybir.dt.bfloat16
I32 = mybir.dt.int32
AF = mybir.ActivationFunctionType
ALU = mybir.AluOpType
AX = mybir.AxisListType

D = 1024
L = 4096
DEPTH = 4
TT = 512
NT = L // TT
KC = 8
EPS = 1e-6
ENGS = ("pe", "act", "dve", "pool", "sp")
N_DMA_SEMS = 14
PI = math.pi


class Prog:
    def __init__(self, nc, stack):
        self.nc = nc
        self.stream = {e: [] for e in ENGS}
        self.count = {e: 0 for e in ENGS}
        self.sem = {e: stack.enter_context(nc.semaphore("s_" + e)) for e in ENGS if e != "sp"}
        self.dsem = [stack.enter_context(nc.semaphore("d%d" % i)) for i in range(N_DMA_SEMS)]
        self.dval = [0] * N_DMA_SEMS
        self.dnext = 0
        self.waited = {e: {} for e in ENGS}
        self.last_w = {}
        self.readers = {}

    def _deps(self, eng, reads, writes, same_engine_ok=False):
        evs = []
        for r in reads:
            ev = self.last_w.get(r)
            if ev is not None:
                evs.append((ev, False))
        for w in writes:
            ev = self.last_w.get(w)
            if ev is not None:
                evs.append((ev, False))
            for ev in self.readers.get(w, {}).values():
                evs.append((ev, True))
        for (ev, is_war) in evs:
            owner, key, sem, val = ev
            if owner == eng and (same_engine_ok or is_war):
                continue
            if self.waited[eng].get(key, 0) >= val:
                continue
            self.waited[eng][key] = val
            self.stream[eng].append(("wait", sem, val))

    def _record(self, rkey, ev, reads, writes):
        for w in writes:
            self.last_w[w] = ev
            self.readers[w] = {}
        for r in reads:
            if r in writes:
                continue
            self.readers.setdefault(r, {})[rkey] = ev

    def op(self, eng, fn, reads=(), writes=()):
        self.group(eng, [fn], reads, writes)

    def group(self, eng, fns, reads=(), writes=()):
        psr = [r for r in reads if r.startswith("ps") and r not in writes]
        if psr:
            writes = list(writes) + psr
        self._deps(eng, reads, writes, same_engine_ok=(eng == "pe"))
        for fn in fns[:-1]:
            self.stream[eng].append(("ins", fn, None))
        self.count[eng] += 1
        ev = (eng, eng, self.sem[eng], self.count[eng])
        self.stream[eng].append(("ins", fns[-1], self.sem[eng]))
        self._record(eng, ev, reads, writes)

    def dma(self, out, in_, reads=(), writes=(), eng="sp"):
        i = self.dnext
        self.dnext = (self.dnext + 1) % N_DMA_SEMS
        key = "dma%d" % i
        sem = self.dsem[i]
        if self.dval[i] > 0 and self.waited[eng].get(key, 0) < self.dval[i]:
            self.waited[eng][key] = self.dval[i]
            self.stream[eng].append(("wait", sem, self.dval[i]))
        self._deps(eng, reads, writes)
        self.dval[i] += 16
        ev = ("dmaq", key, sem, self.dval[i])
        self.stream[eng].append(("dma", out, in_, sem))
        self._record("dmaq_" + key, ev, reads, writes)

    def barrier(self):
        for e in ENGS:
            for o in ENGS:
                if o != e and o != "sp" and self.count[o] > 0 and self.waited[e].get(o, 0) < self.count[o]:
                    self.waited[e][o] = self.count[o]
                    self.stream[e].append(("wait", self.sem[o], self.count[o]))
            for i in range(N_DMA_SEMS):
                key = "dma%d" % i
                if self.dval[i] > 0 and self.waited[e].get(key, 0) < self.dval[i]:
                    self.waited[e][key] = self.dval[i]
                    self.stream[e].append(("wait", self.dsem[i], self.dval[i]))

    def finish(self):
        for i in range(N_DMA_SEMS):
            if self.dval[i] > 0:
                self.stream["sp"].append(("wait", self.dsem[i], self.dval[i]))
        for e in ENGS:
            if e != "sp" and self.count[e] > 0:
                self.stream["sp"].append(("wait", self.sem[e], self.count[e]))

    def emit(self, block):
        def run(engh, items):
            for it in items:
                if it[0] == "wait":
                    engh.wait_ge(it[1], it[2])
                elif it[0] == "ins":
                    ins = it[1](engh)
                    if it[2] is not None:
                        ins.then_inc(it[2], 1)
                else:
                    engh.dma_start(out=it[1], in_=it[2]).then_inc(it[3], 16)

        @block.sync
        def _(e):
            run(e, self.stream["sp"])

        @block.tensor
        def _(e):
            run(e, self.stream["pe"])

        @block.scalar
        def _(e):
            run(e, self.stream["act"])

        @block.vector
        def _(e):
            run(e, self.stream["dve"])

        @block.gpsimd
        def _(e):
            run(e, self.stream["pool"])


class Builder:
    def __init__(self, n_layers=DEPTH):
        self.n_layers = n_layers
        self.nc = bass.Bass("TRN2", target_bir_lowering=False)
        self.dram = {}
        self.rr = 0
        self.cast_rr = 0
        self.stage_rr = 0
        self.wparts = {}

    def din(self, name, shape):
        self.dram[name] = self.nc.dram_tensor(name, list(shape), F32, kind="ExternalInput").ap()

    def declare(self):
        din = self.din
        din("x", [L, D]); din("cT", [128, 8]); din("ada_w", [DEPTH, D, 6 * D]); din("ada_bT", [128, DEPTH, 48])
        din("n1gT", [128, DEPTH, 8]); din("n2gT", [128, DEPTH, 8]); din("fgT", [128, 8])
        din("ff_w1", [DEPTH, D, 4 * D]); din("ff_w2", [DEPTH, 4 * D, D])
        din("conv_w_in", [2, D, 3 * D]); din("conv_wT", [128, 2, 3, 8]); din("conv_bT", [128, 2, 8])
        din("conv_w_out", [2, D, D])
        din("ssm_w_in", [D, D]); din("ssm_glu_w", [D, D]); din("ssm_w_out", [D, D])
        din("glu_bT", [128, 8]); din("dT", [128, 8])
        din("are_c", [128, 32]); din("aim_c", [128, 32]); din("ldt_c", [128, 32])
        din("Bre_z", [128, 32, 128]); din("Bim_z", [128, 32, 128])
        din("Cre_z", [128, 32, 128]); din("Cim_z", [128, 32, 128])
        din("sg_w_in", [D, 2 * D]); din("sg_w_out", [D, D]); din("sg_w_s", [8, 128, 128])
        din("sg_bias_rep", [128, 8, 128]); din("sg_gain_rep", [128, D])
        din("ident", [128, 128]); din("tril", [128, 128])
        self.out = self.nc.dram_tensor("out", [L, D], F32, kind="ExternalOutput").ap()
        self.xT = self.nc.dram_tensor("xT_scr", [D, L], F32, kind="Internal").ap()
        self.tabC = self.nc.dram_tensor("tabC_scr", [32, 128, TT], F32, kind="Internal").ap()
        self.tabS = self.nc.dram_tensor("tabS_scr", [32, 128, TT], F32, kind="Internal").ap()

    def alloc(self, st):
        nc = self.nc

        def sb(name, shape, dt):
            return st.enter_context(nc.sbuf_tensor("sb_" + name, list(shape), dt))

        self.WA = sb("WA", [128, 32768], BF16)
        self.WB = sb("WB", [128, 32768], BF16)
        self.stage = sb("stage", [128, 3, 1024], F32)
        self.xt = sb("xt", [128, 8, TT], F32)
        self.h = sb("h", [128, 8, TT], BF16)
        self.sq = sb("sq", [128, 2, TT], BF16)
        self.tmpn = sb("tmpn", [128, 2, TT], F32)
        self.rt = sb("rt", [128, 2, TT], F32)
        self.rstd = sb("rstd", [128, TT], F32)
        self.A1 = sb("A1", [128, 16, TT], BF16)
        self.sbf = sb("sbf", [128, 2, 2, TT], BF16)
        self.ident = sb("ident", [128, 128], F32)
        self.tril = sb("tril", [128, 128], F32)
        self.ones_bf = sb("ones_bf", [128, 128], BF16)
        self.ones_f = sb("ones_f", [128, 128], F32)
        self.epsc = sb("epsc", [128, 1], F32)
        self.cT = sb("cT", [128, 8], F32)
        self.cab = sb("cab", [128, 8], BF16)
        self.modT = sb("modT", [128, DEPTH, 48], F32)
        self.adab = sb("adab", [128, DEPTH, 48], F32)
        self.n1g = sb("n1g", [128, DEPTH, 8], F32)
        self.n2g = sb("n2g", [128, DEPTH, 8], F32)
        self.fg = sb("fg", [128, 8], F32)
        self.aT = sb("aT", [128, DEPTH, 2, 8], F32)
        self.cw = sb("cw", [128, 2, 3, 8], F32)
        self.cb = sb("cb", [128, 2, 8], F32)
        self.zc = sb("zc", [128, 8, 2], F32)
        self.glub = sb("glub", [128, 8], F32)
        self.dsk = sb("dsk", [128, 8], F32)
        self.s5c = sb("s5c", [128, 16, 32], F32)
        self.s5i = sb("s5i", [128, 32], I32)
        self.pw = sb("pw", [128, 3, 9, 32], F32)
        self.car = sb("car", [128, 2, 32], F32)
        self.ssv = sb("ssv", [128, 4], F32)
        self.dg = sb("dg", [128, 2, 128], F32)
        self.WAf = self.WA[:, :].bitcast(F32)
        self.A2 = self.WB[:, 16384:24576].bitcast(F32)
        self.A3 = self.WB[:, 24576:32768].bitcast(F32)
        self.KS0 = self.WB[:, 8192:14336].bitcast(F32)
        self.KS1 = self.stage[:, :, :].rearrange("p a b -> p (a b)")
        self.rowtmp = self.rt[0:1, :, :].rearrange("p a b -> p (a b)")
        self.ps = [st.enter_context(nc.psum_tensor("ps%d" % i, [128, 512], F32)) for i in range(8)]

    def next_ps(self):
        i = self.rr
        self.rr = (self.rr + 1) % 6
        return self.ps[i], "ps%d" % i

    def mm(self, out_ps, psname, pairs, reads, flags=None):
        fns = []
        n = len(pairs)
        for i, (a, b) in enumerate(pairs):
            if flags is None:
                o, s0, s1 = out_ps, (i == 0), (i == n - 1)
            else:
                o, s0, s1 = flags[i]
            fns.append(lambda e, o=o, a=a, b=b, s0=s0, s1=s1: e.matmul(o, lhsT=a, rhs=b, start=s0, stop=s1))
        self.P.group("pe", fns, reads=reads, writes=[psname])

    def act(self, out, in_, func, reads, writes, bias=None, scale=1.0):
        if bias is None:
            fn = lambda e: e.activation(out=out, in_=in_, func=func, scale=scale)
        else:
            fn = lambda e: e.activation(out=out, in_=in_, func=func, bias=bias, scale=scale)
        self.P.op("act", fn, reads=reads, writes=writes)

    def stt(self, eng, out, in0, scalar, in1, op0, op1, reads, writes):
        self.P.op(eng, lambda e: e.scalar_tensor_tensor(out=out, in0=in0, scalar=scalar, in1=in1, op0=op0, op1=op1),
                  reads=reads, writes=writes)

    def tt(self, eng, out, in0, in1, op, reads, writes):
        self.P.op(eng, lambda e: e.tensor_tensor(out=out, in0=in0, in1=in1, op=op), reads=reads, writes=writes)

    def ts(self, eng, out, in0, s1, s2, op0, op1, reads, writes):
        if s2 is None:
            fn = lambda e: e.tensor_scalar(out=out, in0=in0, scalar1=s1, scalar2=None, op0=op0)
        else:
            fn = lambda e: e.tensor_scalar(out=out, in0=in0, scalar1=s1, scalar2=s2, op0=op0, op1=op1)
        self.P.op(eng, fn, reads=reads, writes=writes)

    def cp(self, eng, out, in_, reads, writes):
        if eng == "act":
            self.act(out, in_, AF.Copy, reads, writes)
        else:
            self.P.op(eng, lambda e: e.tensor_copy(out=out, in_=in_), reads=reads, writes=writes)

    def memset(self, eng, ap, val, writes):
        self.P.op(eng, lambda e: e.memset(ap, val), writes=writes)

    def load_w(self, name, dst3, src2, K, N, scale=None):
        parts = []
        for k in range(K):
            for c0 in range(0, N, 1024):
                w = min(1024, N - c0)
                b = self.stage_rr
                self.stage_rr = (self.stage_rr + 1) % 3
                sname = "stage%d" % b
                self.P.dma(self.stage[:, b, 0:w], src2[k * 128:(k + 1) * 128, c0:c0 + w], writes=[sname])
                pn = "%s_%d_%d" % (name, k, c0)
                eng = ("act", "dve")[self.cast_rr % 2]
                self.cast_rr += 1
                if scale is None:
                    self.cp(eng, dst3[:, k, c0:c0 + w], self.stage[:, b, 0:w], [sname], [pn])
                else:
                    self.act(dst3[:, k, c0:c0 + w], self.stage[:, b, 0:w], AF.Copy, [sname], [pn], scale=scale)
                parts.append(pn)
        self.wparts[name] = parts
        return parts

    def wview(self, buf, c0, K, N):
        return buf[:, c0:c0 + K * N].rearrange("p (k n) -> p k n", k=K)

    def prologue(self):
        P = self.P
        d = self.dram
        P.dma(self.ident[:], d["ident"], writes=["ident"])
        P.dma(self.tril[:], d["tril"], writes=["tril"])
        P.dma(self.cT[:], d["cT"], writes=["cT"])
        P.dma(self.adab[:], d["ada_bT"], writes=["adab"])
        P.dma(self.n1g[:], d["n1gT"], writes=["n1g"])
        P.dma(self.n2g[:], d["n2gT"], writes=["n2g"])
        P.dma(self.fg[:], d["fgT"], writes=["fg"])
        P.dma(self.cw[:], d["conv_wT"], writes=["cw"])
        P.dma(self.cb[:], d["conv_bT"], writes=["cb"])
        P.dma(self.glub[:], d["glu_bT"], writes=["glub"])
        P.dma(self.dsk[:], d["dT"], writes=["dsk"])
        self.memset("pool", self.ones_bf[:], 1.0, ["ones_bf"])
        self.memset("pool", self.ones_f[:], 1.0, ["ones_f"])
        self.memset("pool", self.epsc[:], EPS, ["epsc"])
        self.act(self.s5c[:, 0, 0:8], self.cT[:], AF.Sigmoid, ["cT"], ["sgc"])
        self.tt("dve", self.cab[:], self.cT[:], self.s5c[:, 0, 0:8], ALU.mult, ["cT", "sgc"], ["cab"])

    def ada_stage(self, l):
        P = self.P
        wtmp = self.A1
        for cbk in range(6):
            psa, na = self.next_ps()
            psb, nb = self.next_ps()
            for k in range(8):
                b = self.stage_rr
                self.stage_rr = (self.stage_rr + 1) % 3
                sname = "stage%d" % b
                P.dma(self.stage[:, b, :], self.dram["ada_w"][l, k * 128:(k + 1) * 128, cbk * 1024:(cbk + 1) * 1024],
                      writes=[sname])
                wb = (cbk * 8 + k) % 4
                wt = self.A1[:, 2 * wb:2 * wb + 2, :].rearrange("p a b -> p (a b)")
                wn = ["a1_%d" % (2 * wb), "a1_%d" % (2 * wb + 1)]
                eng = ("act", "dve")[self.cast_rr % 2]
                self.cast_rr += 1
                self.cp(eng, wt, self.stage[:, b, :], [sname], wn)
                lhs = self.cab[:, k:k + 1]
                self.P.group("pe", [
                    (lambda e, o=psa[0:1, :], a=lhs, r=wt[:, 0:512], s0=(k == 0), s1=(k == 7):
                     e.matmul(o, lhsT=a, rhs=r, start=s0, stop=s1)),
                    (lambda e, o=psb[0:1, :], a=lhs, r=wt[:, 512:1024], s0=(k == 0), s1=(k == 7):
                     e.matmul(o, lhsT=a, rhs=r, start=s0, stop=s1)),
                ], reads=wn + ["cab"], writes=[na, nb])
            self.act(self.rowtmp[0:1, 0:512], psa[0:1, :], AF.Copy, [na], ["rt0", "rt1"])
            self.act(self.rowtmp[0:1, 512:1024], psb[0:1, :], AF.Copy, [nb], ["rt0", "rt1"])
            pst, nt = self.next_ps()
            fns = []
            for m in range(8):
                fns.append(lambda e, o=pst[:, m:m + 1], a=self.rowtmp[0:1, m * 128:(m + 1) * 128], r=self.ones_f[0:1, 0:1]:
                           e.matmul(o, lhsT=a, rhs=r, start=True, stop=True))
            P.group("pe", fns, reads=["rt0", "rt1", "ones_f"], writes=[nt])
            self.tt("dve", self.modT[:, l, cbk * 8:(cbk + 1) * 8], pst[:, 0:8], self.adab[:, l, cbk * 8:(cbk + 1) * 8],
                    ALU.add, [nt, "adab"], ["modT%d" % l])
        mn = "modT%d" % l
        self.ts("dve", self.aT[:, l, 0, :], self.modT[:, l, 8:16], 1.0, None, ALU.add, None, [mn], ["aT%d" % l])
        self.tt("dve", self.aT[:, l, 0, :], self.aT[:, l, 0, :], self.n1g[:, l, :], ALU.mult, ["aT%d" % l, "n1g"], ["aT%d" % l])
        self.ts("dve", self.aT[:, l, 1, :], self.modT[:, l, 32:40], 1.0, None, ALU.add, None, [mn], ["aT%d" % l])
        self.tt("dve", self.aT[:, l, 1, :], self.aT[:, l, 1, :], self.n2g[:, l, :], ALU.mult, ["aT%d" % l, "n2g"], ["aT%d" % l])

    def load_x_first(self, ti):
        P = self.P
        for tb in range(4):
            b = self.stage_rr
            self.stage_rr = (self.stage_rr + 1) % 3
            sname = "stage%d" % b
            r0 = (ti * 4 + tb) * 128
            P.dma(self.stage[:, b, :], self.dram["x"][r0:r0 + 128, :], writes=[sname])
            for hf in range(2):
                ps, pn = self.next_ps()
                fns = []
                for kk in range(4):
                    k = hf * 4 + kk
                    fns.append(lambda e, o=ps[:, kk * 128:(kk + 1) * 128], i=self.stage[:, b, k * 128:(k + 1) * 128]:
                               e.transpose(o, i, self.ident[:]))
                P.group("pe", fns, reads=[sname, "ident"], writes=[pn])
                eng = "dve" if hf == 0 else "act"
                self.cp(eng, self.xt[:, hf * 4:hf * 4 + 4, tb * 128:(tb + 1) * 128],
                        ps[:, :].rearrange("p (a b) -> p a b", a=4), [pn], ["xt%d" % k for k in range(hf * 4, hf * 4 + 4)])

    def load_x(self, ti):
        for k in range(8):
            self.P.dma(self.xt[:, k, :], self.xT[k * 128:(k + 1) * 128, ti * TT:(ti + 1) * TT],
                       reads=["xT_%d_%d" % (k, ti)], writes=["xt%d" % k])

    def store_x(self, ti):
        for k in range(8):
            self.P.dma(self.xT[k * 128:(k + 1) * 128, ti * TT:(ti + 1) * TT], self.xt[:, k, :],
                       reads=["xt%d" % k], writes=["xT_%d_%d" % (k, ti)])

    def sumsq_rstd(self):
        P = self.P
        pss, pn = self.ps[6], "ps6"
        for k in range(8):
            b = k % 2
            self.tt("pool", self.sq[:, b, :], self.xt[:, k, :], self.xt[:, k, :], ALU.mult, ["xt%d" % k], ["sq%d" % b])
            P.group("pe", [lambda e, b=b, k=k: e.matmul(pss[:], lhsT=self.ones_bf[:], rhs=self.sq[:, b, :],
                                                         start=(k == 0), stop=(k == 7))],
                    reads=["sq%d" % b, "ones_bf"], writes=[pn])
        self.act(self.rstd[:], pss[:], AF.Sqrt, [pn, "epsc"], ["rstd"], bias=self.epsc[:, 0:1], scale=1.0 / D)
        P.op("dve", lambda e: e.reciprocal(out=self.rstd[:], in_=self.rstd[:]), reads=["rstd"], writes=["rstd"])

    def norm_stage(self, l, which):
        self.sumsq_rstd()
        an = "aT%d" % l
        mn = "modT%d" % l
        sh0 = 0 if which == 0 else 24
        for k in range(8):
            b = k % 2
            self.stt("dve", self.tmpn[:, b, :], self.xt[:, k, :], self.aT[:, l, which, k:k + 1], self.rstd[:],
                     ALU.mult, ALU.mult, ["xt%d" % k, an, "rstd"], ["tmpn%d" % b])
            self.act(self.h[:, k, :], self.tmpn[:, b, :], AF.Identity, ["tmpn%d" % b, mn], ["h%d" % k],
                     bias=self.modT[:, l, sh0 + k:sh0 + k + 1])

    def final_stage(self, ti):
        P = self.P
        self.sumsq_rstd()
        for k in range(8):
            self.stt("dve", self.xt[:, k, :], self.xt[:, k, :], self.fg[:, k:k + 1], self.rstd[:],
                     ALU.mult, ALU.mult, ["xt%d" % k, "fg", "rstd"], ["xt%d" % k])
        for tb in range(4):
            b = self.stage_rr
            self.stage_rr = (self.stage_rr + 1) % 3
            sname = "stage%d" % b
            for hf in range(2):
                ps, pn = self.next_ps()
                fns = []
                for kk in range(4):
                    k = hf * 4 + kk
                    fns.append(lambda e, o=ps[:, kk * 128:(kk + 1) * 128], i=self.xt[:, k, tb * 128:(tb + 1) * 128]:
                               e.transpose(o, i, self.ident[:]))
                P.group("pe", fns, reads=["xt%d" % k for k in range(hf * 4, hf * 4 + 4)] + ["ident"], writes=[pn])
                eng = "dve" if hf == 0 else "act"
                self.cp(eng, self.stage[:, b, hf * 512:(hf + 1) * 512], ps[:, :], [pn], [sname])
            r0 = (ti * 4 + tb) * 128
            P.dma(self.out[r0:r0 + 128, :], self.stage[:, b, :], reads=[sname], writes=["out_%d" % r0])

    def resid_update(self, ps, pn, mo, gate_ap, gname):
        self.stt("dve", self.xt[:, mo, :], ps[:], gate_ap, self.xt[:, mo, :], ALU.mult, ALU.add,
                 [pn, gname, "xt%d" % mo], ["xt%d" % mo])

    def out_proj(self, l, Wout, wname, src, srcnames):
        for mo in range(8):
            ps, pn = self.next_ps()
            self.mm(ps[:], pn, [(Wout[:, k, mo * 128:(mo + 1) * 128], src[:, k, :]) for k in range(8)],
                    reads=self.wparts[wname] + srcnames)
            self.resid_update(ps, pn, mo, self.modT[:, l, 16 + mo:17 + mo], "modT%d" % l)

    def ffn_block(self, l, last):
        self.P.barrier()
        W1 = self.wview(self.WA, 0, 8, 4096)
        W2 = self.wview(self.WB, 0, 32, 1024)
        self.load_w("w1", W1, self.dram["ff_w1"][l], 8, 4096)
        self.load_w("w2", W2, self.dram["ff_w2"][l], 32, 1024)
        r2 = self.A1
        for ti in range(NT):
            self.load_x(ti)
            self.norm_stage(l, 1)
            hn = ["h%d" % k for k in range(8)]
            for half in range(2):
                for jj in range(16):
                    j = half * 16 + jj
                    ps, pn = self.next_ps()
                    self.mm(ps[:], pn, [(W1[:, k, j * 128:(j + 1) * 128], self.h[:, k, :]) for k in range(8)],
                            reads=self.wparts["w1"] + hn)
                    b = jj % 2
                    self.act(self.rt[:, b, :], ps[:], AF.Relu, [pn], ["rt%d" % b])
                    eng = "dve" if jj % 2 == 0 else "pool"
                    self.tt(eng, r2[:, jj, :], self.rt[:, b, :], self.rt[:, b, :], ALU.mult, ["rt%d" % b], ["a1_%d" % jj])
                for mo in range(8):
                    ps, pn = self.next_ps()
                    self.mm(ps[:], pn, [(W2[:, half * 16 + jj, mo * 128:(mo + 1) * 128], r2[:, jj, :]) for jj in range(16)],
                            reads=self.wparts["w2"] + ["a1_%d" % jj for jj in range(16)])
                    self.resid_update(ps, pn, mo, self.modT[:, l, 40 + mo:41 + mo], "modT%d" % l)
            if last:
                self.final_stage(ti)
            else:
                self.store_x(ti)

    def conv_block(self, l, j, first):
        self.P.barrier()
        Win = self.wview(self.WA, 0, 8, 3072)
        Wout = self.wview(self.WB, 0, 8, 1024)
        self.load_w("cwin", Win, self.dram["conv_w_in"][j], 8, 3072)
        self.load_w("cwout", Wout, self.dram["conv_w_out"][j], 8, 1024)
        A2 = self.A2
        cs = [A2[:, 0:512], A2[:, 512:1024]]
        zb = [A2[:, 1024:1538], A2[:, 1538:2052]]
        acc = [A2[:, 2052:2564], A2[:, 2564:3076]]
        q = self.A1
        self.memset("pool", self.zc[:], 0.0, ["zc%d" % m for m in range(8)])
        for ti in range(NT):
            if first:
                self.load_x_first(ti)
            else:
                self.load_x(ti)
            self.norm_stage(l, 0)
            hn = ["h%d" % k for k in range(8)]
            wr = self.wparts["cwin"] + hn
            for m in range(8):
                b = m % 2
                psB, nB = self.next_ps()
                psC, nC = self.next_ps()
                psX, nX = self.next_ps()
                self.mm(psC[:], nC, [(Win[:, k, 1024 + m * 128:1024 + (m + 1) * 128], self.h[:, k, :]) for k in range(8)], wr)
                self.mm(psX[:], nX, [(Win[:, k, 2048 + m * 128:2048 + (m + 1) * 128], self.h[:, k, :]) for k in range(8)], wr)
                self.mm(psB[:], nB, [(Win[:, k, m * 128:(m + 1) * 128], self.h[:, k, :]) for k in range(8)], wr)
                self.act(cs[b], psC[:], AF.Copy, [nC], ["cs%d" % b])
                self.cp("pool", zb[b][:, 0:2], self.zc[:, m, :], ["zc%d" % m], ["zb%d" % b])
                self.tt("dve", zb[b][:, 2:514], psX[:], cs[b], ALU.mult, [nX, "cs%d" % b], ["zb%d" % b])
                self.act(acc[b], zb[b][:, 2:514], AF.Identity, ["zb%d" % b, "cw", "cb"], ["acc%d" % b],
                         bias=self.cb[:, j, m:m + 1], scale=self.cw[:, j, 2, m:m + 1])
                self.stt("dve", acc[b], zb[b][:, 1:513], self.cw[:, j, 1, m:m + 1], acc[b], ALU.mult, ALU.add,
                         ["zb%d" % b, "cw", "acc%d" % b], ["acc%d" % b])
                self.stt("dve", acc[b], zb[b][:, 0:512], self.cw[:, j, 0, m:m + 1], acc[b], ALU.mult, ALU.add,
                         ["zb%d" % b, "cw", "acc%d" % b], ["acc%d" % b])
                self.cp("pool", self.zc[:, m, :], zb[b][:, 512:514], ["zb%d" % b], ["zc%d" % m])
                self.tt("dve", q[:, m, :], psB[:], acc[b], ALU.mult, [nB, "acc%d" % b], ["a1_%d" % m])
            self.out_proj(l, Wout, "cwout", q, ["a1_%d" % m for m in range(8)])
            self.store_x(ti)

    def sg_block(self, l):
        P = self.P
        P.barrier()
        Win = self.wview(self.WA, 0, 8, 2048)
        Wout = self.wview(self.WB, 0, 8, 1024)
        wsT = self.WB[:, 8192:9216].rearrange("p (h t) -> p h t", h=8)
        self.load_w("gwin", Win, self.dram["sg_w_in"], 8, 2048)
        self.load_w("gwout", Wout, self.dram["sg_w_out"], 8, 1024)
        A3 = self.A3
        vsb = A3[:, 0:1024]
        gain = A3[:, 1024:2048]
        bias = A3[:, 2048:3072]
        vsq = A3[:, 3072:3584]
        tmpg = [A3[:, 3584:4096], self.rt[:, 0, :]]
        tmpgn = ["tmpg0", "rt0"]
        vn = self.sbf[:, :, :, :].rearrange("p a b c -> p (a b c)")[:, 0:1024]
        us = self.A2.rearrange("p (k n) -> p k n", k=8)
        gq = self.A1
        P.dma(gain, self.dram["sg_gain_rep"], writes=["gain"])
        P.dma(bias, self.dram["sg_bias_rep"].rearrange("p h t -> p (h t)"), writes=["bias"])
        b = self.stage_rr
        self.stage_rr = (self.stage_rr + 1) % 3
        sname = "stage%d" % b
        stg = self.stage[:, b, :].rearrange("p (h s) -> p h s", h=8)
        P.dma(stg, self.dram["sg_w_s"].rearrange("h t s -> t h s"), writes=[sname])
        self.tt("pool", stg, stg, self.tril[:, :].unsqueeze(1).broadcast_to([128, 8, 128]), ALU.mult,
                [sname, "tril"], [sname])
        for hf in range(2):
            ps, pn = self.next_ps()
            fns = []
            for hh in range(4):
                fns.append(lambda e, o=ps[:, hh * 128:(hh + 1) * 128], i=stg[:, hf * 4 + hh, :]:
                           e.transpose(o, i, self.ident[:]))
            P.group("pe", fns, reads=[sname, "ident"], writes=[pn])
            self.cp("dve", wsT[:, hf * 4:hf * 4 + 4, :], ps[:, :].rearrange("p (a b) -> p a b", a=4), [pn], ["wsT"])
        for ti in range(NT):
            self.load_x(ti)
            self.norm_stage(l, 0)
            hn = ["h%d" % k for k in range(8)]
            wr = self.wparts["gwin"] + hn
            for m in range(8):
                ps, pn = self.next_ps()
                self.mm(ps[:], pn, [(Win[:, k, m * 128:(m + 1) * 128], self.h[:, k, :]) for k in range(8)], wr)
                self.act(us[:, m, :], ps[:], AF.Copy, [pn], ["us%d" % m])
            for n in range(4):
                for hf in range(2):
                    ps, pn = self.next_ps()
                    self.mm(ps[:], pn, [(self.h[:, k, n * 128:(n + 1) * 128], Win[:, k, 1024 + hf * 512:1024 + (hf + 1) * 512])
                                        for k in range(8)], wr)
                    self.act(vsb[:, hf * 512:(hf + 1) * 512], ps[:], AF.Copy, [pn], ["vsb%d" % hf])
                    self.tt("pool", vsq, vsb[:, hf * 512:(hf + 1) * 512], vsb[:, hf * 512:(hf + 1) * 512], ALU.mult,
                            ["vsb%d" % hf], ["vsq"])
                    P.op("dve", lambda e, hf=hf: e.reduce_sum(out=self.ssv[:, hf:hf + 1], in_=vsq, axis=AX.X),
                         reads=["vsq"], writes=["ssv"])
                self.tt("dve", self.ssv[:, 2:3], self.ssv[:, 0:1], self.ssv[:, 1:2], ALU.add, ["ssv"], ["ssv2"])
                self.act(self.ssv[:, 3:4], self.ssv[:, 2:3], AF.Sqrt, ["ssv2", "epsc"], ["rv"], bias=self.epsc[:, 0:1], scale=1.0 / D)
                P.op("dve", lambda e: e.reciprocal(out=self.ssv[:, 3:4], in_=self.ssv[:, 3:4]), reads=["rv"], writes=["rv"])
                self.stt("dve", vn, vsb, self.ssv[:, 3:4], gain, ALU.mult, ALU.mult, ["vsb0", "vsb1", "rv", "gain"], ["vn"])
                for hq in range(2):
                    ps, pn = self.next_ps()
                    pairs, flags = [], []
                    for hh in range(4):
                        hd = hq * 4 + hh
                        pairs.append((vn[:, hd * 128:(hd + 1) * 128], wsT[:, hd, :]))
                        flags.append((ps[:, hh * 128:(hh + 1) * 128], True, True))
                    self.mm(None, pn, pairs, ["vn", "wsT"], flags=flags)
                    tb = tmpg[hq]
                    self.tt("dve", tb, ps[:], bias[:, hq * 512:(hq + 1) * 512], ALU.add, [pn, "bias"], [tmpgn[hq]])
                    self.tt("pool", gq[:, hq * 4:hq * 4 + 4, n * 128:(n + 1) * 128],
                            tb.rearrange("p (a b) -> p a b", a=4), us[:, hq * 4:hq * 4 + 4, n * 128:(n + 1) * 128], ALU.mult,
                            [tmpgn[hq]] + ["us%d" % m for m in range(hq * 4, hq * 4 + 4)],
                            ["a1_%d" % m for m in range(hq * 4, hq * 4 + 4)])
            self.out_proj(l, Wout, "gwout", gq, ["a1_%d" % m for m in range(8)])
            self.store_x(ti)

    def s5_prep(self):
        P = self.P
        d = self.dram
        c = self.s5c
        ARE, AIM, LDT, DT, MAG, ANG, SN, CS, AR, AI, T1, T2, T3, T4, FRE, FIM = [c[:, i, :] for i in range(16)]
        P.dma(ARE, d["are_c"], writes=["s5c"])
        P.dma(AIM, d["aim_c"], writes=["s5c"])
        P.dma(LDT, d["ldt_c"], writes=["s5c"])
        R = ["s5c"]
        self.act(DT, LDT, AF.Exp, R, R)
        self.tt("dve", T1, ARE, DT, ALU.mult, R, R)
        self.act(MAG, T1, AF.Exp, R, R)
        self.tt("dve", ANG, AIM, DT, ALU.mult, R, R)

        def sin_of(dst, src, shift):
            self.ts("dve", T2, src, shift, None, ALU.add, None, R, R)
            self.ts("dve", T3, T2, 1.0 / (2 * PI), None, ALU.mult, None, R, R)
            self.cp("dve", self.s5i[:], T3, R, ["s5i"])
            self.cp("dve", T3, self.s5i[:], ["s5i"], R)
            self.stt("dve", T2, T3, -2 * PI, T2, ALU.mult, ALU.add, R, R)
            self.ts("dve", T3, T2, PI, -2 * PI, ALU.is_gt, ALU.mult, R, R)
            self.tt("dve", T2, T2, T3, ALU.add, R, R)
            self.ts("dve", T3, T2, -PI, 2 * PI, ALU.is_lt, ALU.mult, R, R)
            self.tt("dve", T2, T2, T3, ALU.add, R, R)
            self.ts("dve", T2, T2, -3.141592, 3.141592, ALU.max, ALU.min, R, R)
            self.act(dst, T2, AF.Sin, R, R)

        sin_of(SN, ANG, 0.0)
        sin_of(CS, ANG, PI / 2)
        self.tt("dve", AR, MAG, CS, ALU.mult, R, R)
        self.tt("dve", AI, MAG, SN, ALU.mult, R, R)
        self.tt("dve", T1, ARE, ARE, ALU.mult, R, R)
        self.tt("dve", T2, AIM, AIM, ALU.mult, R, R)
        self.tt("dve", T1, T1, T2, ALU.add, R, R)
        P.op("dve", lambda e: e.reciprocal(out=T4, in_=T1), reads=R, writes=R)
        self.ts("dve", T3, AR, -1.0, None, ALU.add, None, R, R)
        self.tt("dve", T1, T3, ARE, ALU.mult, R, R)
        self.tt("dve", T2, AI, AIM, ALU.mult, R, R)
        self.tt("dve", T1, T1, T2, ALU.add, R, R)
        self.tt("dve", FRE, T1, T4, ALU.mult, R, R)
        self.tt("dve", T1, AI, ARE, ALU.mult, R, R)
        self.tt("dve", T2, T3, AIM, ALU.mult, R, R)
        self.tt("dve", T1, T1, T2, ALU.subtract, R, R)
        self.tt("dve", FIM, T1, T4, ALU.mult, R, R)
        pr, pi_, npi = self.pw[:, 0], self.pw[:, 1], self.pw[:, 2]
        W = ["pw"]
        self.cp("dve", pr[:, 0, :], CS, R, W)
        self.cp("dve", pi_[:, 0, :], SN, R, W)
        for k in range(1, 9):
            self.tt("dve", T1, pr[:, k - 1, :], pr[:, k - 1, :], ALU.mult, W + R, R)
            self.tt("dve", T2, pi_[:, k - 1, :], pi_[:, k - 1, :], ALU.mult, W + R, R)
            self.tt("dve", pr[:, k, :], T1, T2, ALU.subtract, R, W)
            self.tt("dve", T1, pr[:, k - 1, :], pi_[:, k - 1, :], ALU.mult, W + R, R)
            self.ts("dve", pi_[:, k, :], T1, 2.0, None, ALU.mult, None, R, W)
        self.ts("dve", npi, pi_, -1.0, None, ALU.mult, None, W, W)

    def s5_block(self, l):
        P = self.P
        d = self.dram
        P.barrier()
        self.s5_prep()
        Win = self.wview(self.WA, 0, 8, 1024)
        Wglu = self.wview(self.WA, 8192, 8, 1024)
        BW = self.WA[:, 16384:24576].rearrange("p (g r n) -> p g r n", g=32, r=2)
        CW = self.WA[:, 24576:32768].rearrange("p (g r n) -> p g r n", g=32, r=2)
        Wout = self.wview(self.WB, 0, 8, 1024)
        self.load_w("swin", Win, d["ssm_w_in"], 8, 1024)
        self.load_w("sglu", Wglu, d["ssm_glu_w"], 8, 1024)
        self.load_w("swout", Wout, d["ssm_w_out"], 8, 1024)
        c = self.s5c
        FRE, FIM = c[:, 14, :], c[:, 15, :]
        bwparts, cwparts = [], []
        for pc in range(8):
            psr, nr_ = self.next_ps()
            psi, ni_ = self.next_ps()
            for g4 in range(4):
                gp = pc * 4 + g4
                self.ts("dve", self.dg[:, 0, :], self.ident[:], FRE[:, gp:gp + 1], None, ALU.mult, None, ["s5c", "ident"], ["dg0"])
                self.ts("dve", self.dg[:, 1, :], self.ident[:], FIM[:, gp:gp + 1], None, ALU.mult, None, ["s5c", "ident"], ["dg1"])
                self.mm(psr[:, g4 * 128:(g4 + 1) * 128], nr_, [(self.ones_f[:], self.dg[:, 0, :])], ["dg0", "ones_f"],
                        flags=[(psr[:, g4 * 128:(g4 + 1) * 128], True, True)])
                self.mm(psi[:, g4 * 128:(g4 + 1) * 128], ni_, [(self.ones_f[:], self.dg[:, 1, :])], ["dg1", "ones_f"],
                        flags=[(psi[:, g4 * 128:(g4 + 1) * 128], True, True)])
            b1 = self.stage_rr
            b2 = (b1 + 1) % 3
            self.stage_rr = (b1 + 2) % 3
            s1, s2 = "stage%d" % b1, "stage%d" % b2
            bre = self.stage[:, b1, 0:512]
            bim = self.stage[:, b2, 0:512]
            P.dma(bre, d["Bre_z"][:, pc * 4:(pc + 1) * 4, :].rearrange("p g n -> p (g n)"), writes=[s1])
            P.dma(bim, d["Bim_z"][:, pc * 4:(pc + 1) * 4, :].rearrange("p g n -> p (g n)"), writes=[s2])
            t1, t2 = self.tmpn[:, 0, :], self.tmpn[:, 1, :]
            pn = "bw%d" % pc
            self.tt("dve", t1, psr[:], bre, ALU.mult, [nr_, s1], ["tmpn0"])
            self.tt("dve", t2, psi[:], bim, ALU.mult, [ni_, s2], ["tmpn1"])
            self.tt("dve", BW[:, pc * 4:(pc + 1) * 4, 0, :], t1.rearrange("p (g n) -> p g n", g=4),
                    t2.rearrange("p (g n) -> p g n", g=4), ALU.subtract, ["tmpn0", "tmpn1"], [pn + "r"])
            self.tt("dve", t1, psr[:], bim, ALU.mult, [nr_, s2], ["tmpn0"])
            self.tt("dve", t2, psi[:], bre, ALU.mult, [ni_, s1], ["tmpn1"])
            self.tt("dve", BW[:, pc * 4:(pc + 1) * 4, 1, :], t1.rearrange("p (g n) -> p g n", g=4),
                    t2.rearrange("p (g n) -> p g n", g=4), ALU.add, ["tmpn0", "tmpn1"], [pn + "i"])
            bwparts += [pn + "r", pn + "i"]
        for pc in range(8):
            for ri, key, sc in ((0, "Cre_z", 1.0), (1, "Cim_z", -1.0)):
                b1 = self.stage_rr
                self.stage_rr = (b1 + 1) % 3
                s1 = "stage%d" % b1
                P.dma(self.stage[:, b1, 0:512], d[key][:, pc * 4:(pc + 1) * 4, :].rearrange("p g n -> p (g n)"), writes=[s1])
                pn = "cwp%d_%d" % (pc, ri)
                self.act(CW[:, pc * 4:(pc + 1) * 4, ri, :], self.stage[:, b1, 0:512].rearrange("p (g n) -> p g n", g=4),
                         AF.Copy, [s1], [pn], scale=sc)
                cwparts.append(pn)
        K0 = self.KS0
        K1 = self.KS1
        SRI = self.WB[:, 14336:16384].bitcast(F32)
        BR = [K0[:, 0:512], K0[:, 1024:1536]]
        BI = [K0[:, 512:1024], K0[:, 1536:2048]]
        A2f = self.A2
        PRs = [K0[:, 2048:2560], A2f[:, 0:512]]
        PIs = [K0[:, 2560:3072], A2f[:, 512:1024]]
        TC = [K1[:, 0:512], K1[:, 1024:1536], A2f[:, 1024:1536]]
        TS = [K1[:, 512:1024], K1[:, 1536:2048], A2f[:, 1536:2048]]
        RR, RI = K1[:, 2048:2560], K1[:, 2560:3072]
        Dd = self.WB[:, 16384 + 4096:16384 + 4096 + 1024].rearrange("p (q n) -> p q n", q=8)
        sbf3 = self.WB[:, 16384 + 5120:16384 + 5120 + 1024].rearrange("p (r n) -> p r n", r=2)
        sbfs = [self.sbf[:, 0], self.sbf[:, 1], sbf3]
        for q_ in range(8):
            self.ts("dve", Dd[:, q_, :], self.ident[:], self.dsk[:, q_:q_ + 1], None, ALU.mult, None, ["ident", "dsk"], ["Dd"])
        SR, SI = SRI[:, 0:512], SRI[:, 512:1024]
        M1, M2 = self.rt[:, 0, :], self.rt[:, 1, :]
        scan_names = ["br0", "bi0", "br1", "bi1", "pr", "pi", "tc0", "ts0", "tc1", "ts1", "rr", "ri", "sr", "si"]
        self.memset("pool", K1, 0.0, ["tc0", "ts0", "tc1", "ts1", "rr", "ri", "stage0", "stage1", "stage2"])
        carn_all = ["car%d" % g for g in range(32)]
        self.memset("dve", self.car[:], 0.0, carn_all)
        uc, us_, uns = self.pw[:, 0], self.pw[:, 1], self.pw[:, 2]
        for gp in range(32):
            t = gp % 3
            tcn, tsn = "tc%d" % t, "ts%d" % t
            self.memset("dve", TC[t][:, 0:1], 1.0, [tcn])
            self.memset("dve", TS[t][:, 0:1], 0.0, [tsn])
            for k in range(9):
                n = 1 << k
                self.ts("dve", TC[t][:, n:2 * n], TC[t][:, 0:n], uc[:, k, gp:gp + 1], None, ALU.mult, None, [tcn, "pw"], [tcn])
                self.stt("dve", TC[t][:, n:2 * n], TS[t][:, 0:n], uns[:, k, gp:gp + 1], TC[t][:, n:2 * n], ALU.mult, ALU.add,
                         [tcn, tsn, "pw"], [tcn])
                self.ts("dve", TS[t][:, n:2 * n], TS[t][:, 0:n], uc[:, k, gp:gp + 1], None, ALU.mult, None, [tsn, "pw"], [tsn])
                self.stt("dve", TS[t][:, n:2 * n], TC[t][:, 0:n], us_[:, k, gp:gp + 1], TS[t][:, n:2 * n], ALU.mult, ALU.add,
                         [tcn, tsn, "pw"], [tsn])
            P.dma(self.tabC[gp], TC[t], reads=[tcn], writes=["tabC%d" % gp])
            P.dma(self.tabS[gp], TS[t], reads=[tsn], writes=["tabS%d" % gp])
        us = self.A2.rearrange("p (k n) -> p k n", k=8)
        yg = self.A3.rearrange("p (k n) -> p k n", k=8)
        ubf = self.A1[:, 0:8, :]
        ygb = self.A1[:, 8:16, :]
        qv = self.A1[:, 0:8, :]
        c = self.s5c
        MAG, SNt, CSt = c[:, 4, :], c[:, 6, :], c[:, 7, :]
        X1, X2, INITR, INITI = c[:, 10, :], c[:, 11, :], c[:, 12, :], c[:, 13, :]

        for ti in range(NT):
            self.load_x(ti)
            self.norm_stage(l, 0)
            hn = ["h%d" % k for k in range(8)]
            for m in range(8):
                ps, pn = self.next_ps()
                self.mm(ps[:], pn, [(Win[:, k, m * 128:(m + 1) * 128], self.h[:, k, :]) for k in range(8)],
                        self.wparts["swin"] + hn)
                self.cp("act" if m % 2 == 0 else "dve", ubf[:, m, :], ps[:], [pn], ["a1_%d" % m])
            self.tt("dve", X1, CSt, self.car[:, 0, :], ALU.mult, ["s5c"] + carn_all, ["x1"])
            self.tt("dve", X2, SNt, self.car[:, 1, :], ALU.mult, ["s5c"] + carn_all, ["x2"])
            self.tt("dve", INITR, X1, X2, ALU.subtract, ["x1", "x2"], ["initr"])
            self.tt("dve", X1, CSt, self.car[:, 1, :], ALU.mult, ["s5c"] + carn_all, ["x1"])
            self.tt("dve", X2, SNt, self.car[:, 0, :], ALU.mult, ["s5c"] + carn_all, ["x2"])
            self.tt("dve", INITI, X1, X2, ALU.add, ["x1", "x2"], ["initi"])

            def bu(gp):
                q = gp // 4
                s = gp % 2
                t = gp % 3
                pb = gp % 2
                s3 = gp % 3
                PRb, PIb = PRs[pb], PIs[pb]
                prn, pin = "pr%d" % pb, "pi%d" % pb
                psr, nr_ = self.next_ps()
                psi, ni_ = self.next_ps()
                self.mm(psr[:], nr_, [(BW[:, gp, 0, :], ubf[:, q, :])], bwparts + ["a1_%d" % q])
                self.mm(psi[:], ni_, [(BW[:, gp, 1, :], ubf[:, q, :])], bwparts + ["a1_%d" % q])
                brn, bin_, tcn, tsn = "br%d" % s, "bi%d" % s, "tc%d" % t, "ts%d" % t
                self.cp("act", BR[s], psr[:], [nr_], [brn])
                self.cp("act", BI[s], psi[:], [ni_], [bin_])
                P.dma(TC[t], self.tabC[gp], reads=["tabC%d" % gp], writes=[tcn])
                P.dma(TS[t], self.tabS[gp], reads=["tabS%d" % gp], writes=[tsn])
                self.tt("pool", PRb, BR[s], TC[t], ALU.mult, [brn, tcn], [prn])
                self.tt("pool", M1, BI[s], TS[t], ALU.mult, [bin_, tsn], ["rt0"])
                self.tt("pool", PRb, PRb, M1, ALU.add, [prn, "rt0"], [prn])
                self.tt("pool", PIb, BI[s], TC[t], ALU.mult, [bin_, tcn], [pin])
                self.tt("pool", M1, BR[s], TS[t], ALU.mult, [brn, tsn], ["rt0"])
                self.tt("pool", PIb, PIb, M1, ALU.subtract, [pin, "rt0"], [pin])
                rho = MAG[:, gp:gp + 1].broadcast_to([128, TT])
                P.op("dve", lambda e: e.tensor_tensor_scan(out=RR, data0=rho, data1=PRb, initial=INITR[:, gp:gp + 1],
                                                           op0=ALU.mult, op1=ALU.add),
                     reads=[prn, "s5c", "initr"], writes=["rr"])
                P.op("dve", lambda e: e.tensor_tensor_scan(out=RI, data0=rho, data1=PIb, initial=INITI[:, gp:gp + 1],
                                                           op0=ALU.mult, op1=ALU.add),
                     reads=[pin, "s5c", "initi"], writes=["ri"])
                self.tt("dve", SR, RR, TC[t], ALU.mult, ["rr", tcn], ["sr"])
                self.tt("dve", M2, RI, TS[t], ALU.mult, ["ri", tsn], ["rt1"])
                self.tt("dve", SR, SR, M2, ALU.subtract, ["sr", "rt1"], ["sr"])
                self.tt("dve", SI, RR, TS[t], ALU.mult, ["rr", tsn], ["si"])
                self.tt("dve", M2, RI, TC[t], ALU.mult, ["ri", tcn], ["rt1"])
                self.tt("dve", SI, SI, M2, ALU.add, ["si", "rt1"], ["si"])
                carn = "car%d" % gp
                self.cp("act", sbfs[s3][:, 0, :], SR, ["sr"], ["sbf%d" % s3])
                self.cp("act", sbfs[s3][:, 1, :], SI, ["si"], ["sbf%d" % s3])
                self.cp("act", self.car[:, 0, gp:gp + 1], SR[:, 511:512], ["sr"], [carn])
                self.cp("act", self.car[:, 1, gp:gp + 1], SI[:, 511:512], ["si"], [carn])

            def outp(gp):
                q = gp // 4
                s3 = gp % 3
                psY, nY = self.ps[6 + (q % 2)], "ps%d" % (6 + (q % 2))
                pairs = [(CW[:, gp, 0, :], sbfs[s3][:, 0, :]), (CW[:, gp, 1, :], sbfs[s3][:, 1, :])]
                flags = [(psY[:], False, False), (psY[:], False, gp % 4 == 3)]
                rd = cwparts + ["sbf%d" % s3]
                if gp % 4 == 0:
                    pairs = [(Dd[:, q, :], ubf[:, q, :])] + pairs
                    flags = [(psY[:], True, False)] + flags
                    rd = rd + ["Dd", "a1_%d" % q]
                self.mm(None, nY, pairs, rd, flags=flags)
                if gp % 4 == 3:
                    self.act(yg[:, q, :], psY[:], AF.Gelu_apprx_tanh, [nY], ["yg%d" % q])
                    self.cp("act", ygb[:, q, :], yg[:, q, :], ["yg%d" % q], ["a1_%d" % (8 + q)])

            LOOK = 2
            for sidx in range(32 + LOOK):
                if sidx < 32:
                    bu(sidx)
                if sidx - LOOK >= 0:
                    outp(sidx - LOOK)
            ygn = ["a1_%d" % (8 + k) for k in range(8)]
            for mo in range(8):
                ps, pn = self.next_ps()
                self.mm(ps[:], pn, [(Wglu[:, k, mo * 128:(mo + 1) * 128], ygb[:, k, :]) for k in range(8)],
                        self.wparts["sglu"] + ygn)
                b = mo % 2
                self.act(self.rt[:, b, :], ps[:], AF.Sigmoid, [pn, "glub"], ["rt%d" % b], bias=self.glub[:, mo:mo + 1])
                self.tt("dve", qv[:, mo, :], yg[:, mo, :], self.rt[:, b, :], ALU.mult, ["yg%d" % mo, "rt%d" % b], ["a1_%d" % mo])
            self.out_proj(l, Wout, "swout", qv, ["a1_%d" % m for m in range(8)])
            self.store_x(ti)

    def build(self):
        nc = self.nc
        self.declare()
        with ExitStack() as st:
            self.alloc(st)
            self.P = Prog(nc, st)
            block = st.enter_context(nc.Block())
            self.prologue()
            for l in range(self.n_layers):
                self.ada_stage(l)
            for l in range(self.n_layers):
                kind = l % 3
                if kind == 0:
                    self.conv_block(l, l // 3, first=(l == 0))
                elif kind == 1:
                    self.s5_block(l)
                else:
                    self.sg_block(l)
                self.ffn_block(l, last=(l == self.n_layers - 1))
            self.P.finish()
            self.P.emit(block)
        return nc


def tT(v):
    v = np.asarray(v, np.float32)
    lead = v.shape[:-1]
    r = v.reshape(lead + (8, 128))
    return np.ascontiguousarray(np.moveaxis(r, -1, 0))


def host_layout(inp, b):
    f32 = np.float32
    m = {}
    m["x"] = np.ascontiguousarray(inp["x"][b], f32)
    m["cT"] = tT(inp["c"][b])
    m["ada_w"] = np.ascontiguousarray(inp["ada_w"], f32)
    ab = np.asarray(inp["ada_b"], f32).reshape(DEPTH, 48, 128)
    m["ada_bT"] = np.ascontiguousarray(ab.transpose(2, 0, 1))
    m["n1gT"] = tT(inp["norm1_g"])
    m["n2gT"] = tT(inp["norm2_g"])
    m["fgT"] = tT(inp["final_g"])
    m["ff_w1"] = np.ascontiguousarray(inp["ff_w1"], f32)
    m["ff_w2"] = np.ascontiguousarray(inp["ff_w2"], f32)
    m["conv_w_in"] = np.ascontiguousarray(inp["conv_w_in"], f32)
    m["conv_wT"] = tT(inp["conv_w"])
    m["conv_bT"] = tT(inp["conv_b"])
    m["conv_w_out"] = np.ascontiguousarray(inp["conv_w_out"], f32)
    m["ssm_w_in"] = np.ascontiguousarray(inp["ssm_w_in"][0], f32)
    m["ssm_glu_w"] = np.ascontiguousarray(inp["ssm_glu_w"][0], f32)
    m["ssm_w_out"] = np.ascontiguousarray(inp["ssm_w_out"][0], f32)
    m["glu_bT"] = tT(inp["ssm_glu_b"][0])
    m["dT"] = tT(inp["ssm_d"][0])
    def compact(a):
        a = np.asarray(a, f32).reshape(32, 2, 64)
        return np.ascontiguousarray(a.transpose(1, 2, 0).reshape(128, 32))
    m["are_c"] = compact(inp["ssm_a_re"][0])
    m["aim_c"] = compact(inp["ssm_a_im"][0])
    m["ldt_c"] = compact(np.broadcast_to(np.asarray(inp["ssm_log_dt"][0], f32)[:, None], (64, 64)))
    bre = np.asarray(inp["ssm_b_re"][0], f32)
    bim = np.asarray(inp["ssm_b_im"][0], f32)
    cre = np.asarray(inp["ssm_c_re"][0], f32)
    cim = np.asarray(inp["ssm_c_im"][0], f32)
    Bre_z = np.zeros((128, 32, 128), f32); Bim_z = np.zeros((128, 32, 128), f32)
    Cre_z = np.zeros((128, 32, 128), f32); Cim_z = np.zeros((128, 32, 128), f32)
    for g in range(64):
        gp, g2, g8 = g // 2, g % 2, g % 8
        Bre_z[g8 * 16:(g8 + 1) * 16, gp, g2 * 64:(g2 + 1) * 64] = bre[g].T
        Bim_z[g8 * 16:(g8 + 1) * 16, gp, g2 * 64:(g2 + 1) * 64] = bim[g].T
        Cre_z[g2 * 64:(g2 + 1) * 64, gp, g8 * 16:(g8 + 1) * 16] = cre[g].T
        Cim_z[g2 * 64:(g2 + 1) * 64, gp, g8 * 16:(g8 + 1) * 16] = cim[g].T
    m["Bre_z"], m["Bim_z"], m["Cre_z"], m["Cim_z"] = Bre_z, Bim_z, Cre_z, Cim_z
    m["sg_w_in"] = np.ascontiguousarray(inp["sg_w_in"][0], f32)
    m["sg_w_out"] = np.ascontiguousarray(inp["sg_w_out"][0], f32)
    m["sg_w_s"] = np.ascontiguousarray(inp["sg_w_s"][0], f32)
    m["sg_bias_rep"] = np.ascontiguousarray(np.broadcast_to(np.asarray(inp["sg_b_s"][0], f32)[None], (128, 8, 128)))
    m["sg_gain_rep"] = np.ascontiguousarray(np.broadcast_to(np.asarray(inp["sg_v_g"][0], f32)[None], (128, D)))
    m["ident"] = np.eye(128, dtype=f32)
    m["tril"] = np.tril(np.ones((128, 128), f32))
    return m


_NC_CACHE = {}


def kernel(_n_layers=DEPTH, **inputs):
    inp = {k: np.asarray(v) for k, v in inputs.items()}
    if _n_layers not in _NC_CACHE:
        _NC_CACHE[_n_layers] = Builder(_n_layers).build()
    nc = _NC_CACHE[_n_layers]
    in_maps = [host_layout(inp, b) for b in range(8)]
    res = run_bass_kernel_spmd(nc, in_maps, core_ids=list(range(8)))
    out = np.stack([np.asarray(r["out"], np.float32) for r in res.results], axis=0)
    return out
```

```python
import math
from contextlib import ExitStack

import numpy as np
import concourse.bass as bass
import concourse.mybir as mybir
from concourse.bass_utils import run_bass_kernel_spmd

F32 = mybir.dt.float32
BF16 = mybir.dt.bfloat16
I32 = mybir.dt.int32
AF = mybir.ActivationFunctionType
ALU = mybir.AluOpType
AX = mybir.AxisListType

D = 1024
L = 4096
DEPTH = 4
TT = 512
NT = L // TT
KC = 8
EPS = 1e-6
ENGS = ("pe", "act", "dve", "pool", "sp")
N_DMA_SEMS = 14
PI = math.pi


class Prog:
    def __init__(self, nc, stack):
        self.nc = nc
        self.stream = {e: [] for e in ENGS}
        self.count = {e: 0 for e in ENGS}
        self.sem = {e: stack.enter_context(nc.semaphore("s_" + e)) for e in ENGS if e != "sp"}
        self.dsem = [stack.enter_context(nc.semaphore("d%d" % i)) for i in range(N_DMA_SEMS)]
        self.dval = [0] * N_DMA_SEMS
        self.dnext = 0
        self.waited = {e: {} for e in ENGS}
        self.last_w = {}
        self.readers = {}

    def _deps(self, eng, reads, writes, same_engine_ok=False):
        evs = []
        for r in reads:
            ev = self.last_w.get(r)
            if ev is not None:
                evs.append((ev, False))
        for w in writes:
            ev = self.last_w.get(w)
            if ev is not None:
                evs.append((ev, False))
            for ev in self.readers.get(w, {}).values():
                evs.append((ev, True))
        for (ev, is_war) in evs:
            owner, key, sem, val = ev
            if owner == eng and (same_engine_ok or is_war):
                continue
            if self.waited[eng].get(key, 0) >= val:
                continue
            self.waited[eng][key] = val
            self.stream[eng].append(("wait", sem, val))

    def _record(self, rkey, ev, reads, writes):
        for w in writes:
            self.last_w[w] = ev
            self.readers[w] = {}
        for r in reads:
            if r in writes:
                continue
            self.readers.setdefault(r, {})[rkey] = ev

    def op(self, eng, fn, reads=(), writes=()):
        self.group(eng, [fn], reads, writes)

    def group(self, eng, fns, reads=(), writes=()):
        psr = [r for r in reads if r.startswith("ps") and r not in writes]
        if psr:
            writes = list(writes) + psr
        self._deps(eng, reads, writes, same_engine_ok=(eng == "pe"))
        for fn in fns[:-1]:
            self.stream[eng].append(("ins", fn, None))
        self.count[eng] += 1
        ev = (eng, eng, self.sem[eng], self.count[eng])
        self.stream[eng].append(("ins", fns[-1], self.sem[eng]))
        self._record(eng, ev, reads, writes)

    def dma(self, out, in_, reads=(), writes=(), eng="sp"):
        i = self.dnext
        self.dnext = (self.dnext + 1) % N_DMA_SEMS
        key = "dma%d" % i
        sem = self.dsem[i]
        if self.dval[i] > 0 and self.waited[eng].get(key, 0) < self.dval[i]:
            self.waited[eng][key] = self.dval[i]
            self.stream[eng].append(("wait", sem, self.dval[i]))
        self._deps(eng, reads, writes)
        self.dval[i] += 16
        ev = ("dmaq", key, sem, self.dval[i])
        self.stream[eng].append(("dma", out, in_, sem))
        self._record("dmaq_" + key, ev, reads, writes)

    def barrier(self):
        for e in ENGS:
            for o in ENGS:
                if o != e and o != "sp" and self.count[o] > 0 and self.waited[e].get(o, 0) < self.count[o]:
                    self.waited[e][o] = self.count[o]
                    self.stream[e].append(("wait", self.sem[o], self.count[o]))
            for i in range(N_DMA_SEMS):
                key = "dma%d" % i
                if self.dval[i] > 0 and self.waited[e].get(key, 0) < self.dval[i]:
                    self.waited[e][key] = self.dval[i]
                    self.stream[e].append(("wait", self.dsem[i], self.dval[i]))

    def finish(self):
        for i in range(N_DMA_SEMS):
            if self.dval[i] > 0:
                self.stream["sp"].append(("wait", self.dsem[i], self.dval[i]))
        for e in ENGS:
            if e != "sp" and self.count[e] > 0:
                self.stream["sp"].append(("wait", self.sem[e], self.count[e]))

    def emit(self, block):
        def run(engh, items):
            for it in items:
                if it[0] == "wait":
                    engh.wait_ge(it[1], it[2])
                elif it[0] == "ins":
                    ins = it[1](engh)
                    if it[2] is not None:
                        ins.then_inc(it[2], 1)
                else:
                    engh.dma_start(out=it[1], in_=it[2]).then_inc(it[3], 16)

        @block.sync
        def _(e):
            run(e, self.stream["sp"])

        @block.tensor
        def _(e):
            run(e, self.stream["pe"])

        @block.scalar
        def _(e):
            run(e, self.stream["act"])

        @block.vector
        def _(e):
            run(e, self.stream["dve"])

        @block.gpsimd
        def _(e):
            run(e, self.stream["pool"])


class Builder:
    def __init__(self, n_layers=DEPTH):
        self.n_layers = n_layers
        self.nc = bass.Bass("TRN2", target_bir_lowering=False)
        self.dram = {}
        self.rr = 0
        self.cast_rr = 0
        self.stage_rr = 0
        self.wparts = {}

    def din(self, name, shape):
        self.dram[name] = self.nc.dram_tensor(name, list(shape), F32, kind="ExternalInput").ap()

    def declare(self):
        din = self.din
        din("x", [L, D]); din("cT", [128, 8]); din("ada_w", [DEPTH, D, 6 * D]); din("ada_bT", [128, DEPTH, 48])
        din("n1gT", [128, DEPTH, 8]); din("n2gT", [128, DEPTH, 8]); din("fgT", [128, 8])
        din("ff_w1", [DEPTH, D, 4 * D]); din("ff_w2", [DEPTH, 4 * D, D])
        din("conv_w_in", [2, D, 3 * D]); din("conv_wT", [128, 2, 3, 8]); din("conv_bT", [128, 2, 8])
        din("conv_w_out", [2, D, D])
        din("ssm_w_in", [D, D]); din("ssm_glu_w", [D, D]); din("ssm_w_out", [D, D])
        din("glu_bT", [128, 8]); din("dT", [128, 8])
        din("are_c", [128, 32]); din("aim_c", [128, 32]); din("ldt_c", [128, 32])
        din("Bre_z", [128, 32, 128]); din("Bim_z", [128, 32, 128])
        din("Cre_z", [128, 32, 128]); din("Cim_z", [128, 32, 128])
        din("sg_w_in", [D, 2 * D]); din("sg_w_out", [D, D]); din("sg_w_s", [8, 128, 128])
        din("sg_bias_rep", [128, 8, 128]); din("sg_gain_rep", [128, D])
        din("ident", [128, 128]); din("tril", [128, 128])
        self.out = self.nc.dram_tensor("out", [L, D], F32, kind="ExternalOutput").ap()
        self.xT = self.nc.dram_tensor("xT_scr", [D, L], F32, kind="Internal").ap()
        self.tabC = self.nc.dram_tensor("tabC_scr", [32, 128, TT], F32, kind="Internal").ap()
        self.tabS = self.nc.dram_tensor("tabS_scr", [32, 128, TT], F32, kind="Internal").ap()

    def alloc(self, st):
        nc = self.nc

        def sb(name, shape, dt):
            return st.enter_context(nc.sbuf_tensor("sb_" + name, list(shape), dt))

        self.WA = sb("WA", [128, 32768], BF16)
        self.WB = sb("WB", [128, 32768], BF16)
        self.stage = sb("stage", [128, 3, 1024], F32)
        self.xt = sb("xt", [128, 8, TT], F32)
        self.h = sb("h", [128, 8, TT], BF16)
        self.sq = sb("sq", [128, 2, TT], BF16)
        self.tmpn = sb("tmpn", [128, 2, TT], F32)
        self.rt = sb("rt", [128, 2, TT], F32)
        self.rstd = sb("rstd", [128, TT], F32)
        self.A1 = sb("A1", [128, 16, TT], BF16)
        self.sbf = sb("sbf", [128, 2, 2, TT], BF16)
        self.ident = sb("ident", [128, 128], F32)
        self.tril = sb("tril", [128, 128], F32)
        self.ones_bf = sb("ones_bf", [128, 128], BF16)
        self.ones_f = sb("ones_f", [128, 128], F32)
        self.epsc = sb("epsc", [128, 1], F32)
        self.cT = sb("cT", [128, 8], F32)
        self.cab = sb("cab", [128, 8], BF16)
        self.modT = sb("modT", [128, DEPTH, 48], F32)
        self.adab = sb("adab", [128, DEPTH, 48], F32)
        self.n1g = sb("n1g", [128, DEPTH, 8], F32)
        self.n2g = sb("n2g", [128, DEPTH, 8], F32)
        self.fg = sb("fg", [128, 8], F32)
        self.aT = sb("aT", [128, DEPTH, 2, 8], F32)
        self.cw = sb("cw", [128, 2, 3, 8], F32)
        self.cb = sb("cb", [128, 2, 8], F32)
        self.zc = sb("zc", [128, 8, 2], F32)
        self.glub = sb("glub", [128, 8], F32)
        self.dsk = sb("dsk", [128, 8], F32)
        self.s5c = sb("s5c", [128, 16, 32], F32)
        self.s5i = sb("s5i", [128, 32], I32)
        self.pw = sb("pw", [128, 3, 9, 32], F32)
        self.car = sb("car", [128, 2, 32], F32)
        self.ssv = sb("ssv", [128, 4], F32)
        self.dg = sb("dg", [128, 2, 128], F32)
        self.WAf = self.WA[:, :].bitcast(F32)
        self.A2 = self.WB[:, 16384:24576].bitcast(F32)
        self.A3 = self.WB[:, 24576:32768].bitcast(F32)
        self.KS0 = self.WB[:, 8192:14336].bitcast(F32)
        self.KS1 = self.stage[:, :, :].rearrange("p a b -> p (a b)")
        self.rowtmp = self.rt[0:1, :, :].rearrange("p a b -> p (a b)")
        self.ps = [st.enter_context(nc.psum_tensor("ps%d" % i, [128, 512], F32)) for i in range(8)]

    def next_ps(self):
        i = self.rr
        self.rr = (self.rr + 1) % 6
        return self.ps[i], "ps%d" % i

    def mm(self, out_ps, psname, pairs, reads, flags=None):
        fns = []
        n = len(pairs)
        for i, (a, b) in enumerate(pairs):
            if flags is None:
                o, s0, s1 = out_ps, (i == 0), (i == n - 1)
            else:
                o, s0, s1 = flags[i]
            fns.append(lambda e, o=o, a=a, b=b, s0=s0, s1=s1: e.matmul(o, lhsT=a, rhs=b, start=s0, stop=s1))
        self.P.group("pe", fns, reads=reads, writes=[psname])

    def act(self, out, in_, func, reads, writes, bias=None, scale=1.0):
        if bias is None:
            fn = lambda e: e.activation(out=out, in_=in_, func=func, scale=scale)
        else:
            fn = lambda e: e.activation(out=out, in_=in_, func=func, bias=bias, scale=scale)
        self.P.op("act", fn, reads=reads, writes=writes)

    def stt(self, eng, out, in0, scalar, in1, op0, op1, reads, writes):
        self.P.op(eng, lambda e: e.scalar_tensor_tensor(out=out, in0=in0, scalar=scalar, in1=in1, op0=op0, op1=op1),
                  reads=reads, writes=writes)

    def tt(self, eng, out, in0, in1, op, reads, writes):
        self.P.op(eng, lambda e: e.tensor_tensor(out=out, in0=in0, in1=in1, op=op), reads=reads, writes=writes)

    def ts(self, eng, out, in0, s1, s2, op0, op1, reads, writes):
        if s2 is None:
            fn = lambda e: e.tensor_scalar(out=out, in0=in0, scalar1=s1, scalar2=None, op0=op0)
        else:
            fn = lambda e: e.tensor_scalar(out=out, in0=in0, scalar1=s1, scalar2=s2, op0=op0, op1=op1)
        self.P.op(eng, fn, reads=reads, writes=writes)

    def cp(self, eng, out, in_, reads, writes):
        if eng == "act":
            self.act(out, in_, AF.Copy, reads, writes)
        else:
            self.P.op(eng, lambda e: e.tensor_copy(out=out, in_=in_), reads=reads, writes=writes)

    def memset(self, eng, ap, val, writes):
        self.P.op(eng, lambda e: e.memset(ap, val), writes=writes)

    def load_w(self, name, dst3, src2, K, N, scale=None, xt_stage=True):
        parts = []
        bufs = [(self.stage[:, i, :], ["stage%d" % i]) for i in range(3)]
        if xt_stage:
            for i in range(4):
                bufs.append((self.xt[:, 2 * i:2 * i + 2, :].rearrange("p a b -> p (a b)"), ["xt%d" % (2 * i), "xt%d" % (2 * i + 1)]))
        for k in range(K):
            for c0 in range(0, N, 1024):
                w = min(1024, N - c0)
                self.wl_rr = (getattr(self, "wl_rr", 0) + 1) % len(bufs)
                sb_, snames = bufs[self.wl_rr]
                self.P.dma(sb_[:, 0:w], src2[k * 128:(k + 1) * 128, c0:c0 + w], writes=snames)
                pn = "%s_%d_%d" % (name, k, c0)
                eng = ("act", "dve")[self.cast_rr % 2]
                self.cast_rr += 1
                if scale is None:
                    self.cp(eng, dst3[:, k, c0:c0 + w], sb_[:, 0:w], snames, [pn])
                else:
                    self.act(dst3[:, k, c0:c0 + w], sb_[:, 0:w], AF.Copy, snames, [pn], scale=scale)
                parts.append(pn)
        self.wparts[name] = parts
        return parts

    def wview(self, buf, c0, K, N):
        return buf[:, c0:c0 + K * N].rearrange("p (k n) -> p k n", k=K)

    def prologue(self):
        P = self.P
        d = self.dram
        P.dma(self.ident[:], d["ident"], writes=["ident"])
        P.dma(self.tril[:], d["tril"], writes=["tril"])
        P.dma(self.cT[:], d["cT"], writes=["cT"])
        P.dma(self.adab[:], d["ada_bT"], writes=["adab"])
        P.dma(self.n1g[:], d["n1gT"], writes=["n1g"])
        P.dma(self.n2g[:], d["n2gT"], writes=["n2g"])
        P.dma(self.fg[:], d["fgT"], writes=["fg"])
        P.dma(self.cw[:], d["conv_wT"], writes=["cw"])
        P.dma(self.cb[:], d["conv_bT"], writes=["cb"])
        P.dma(self.glub[:], d["glu_bT"], writes=["glub"])
        P.dma(self.dsk[:], d["dT"], writes=["dsk"])
        self.memset("pool", self.ones_bf[:], 1.0, ["ones_bf"])
        self.memset("pool", self.ones_f[:], 1.0, ["ones_f"])
        self.memset("pool", self.epsc[:], EPS, ["epsc"])
        self.act(self.s5c[:, 0, 0:8], self.cT[:], AF.Sigmoid, ["cT"], ["sgc"])
        self.tt("dve", self.cab[:], self.cT[:], self.s5c[:, 0, 0:8], ALU.mult, ["cT", "sgc"], ["cab"])

    def ada_stage(self, l):
        P = self.P
        wtmp = self.A1
        for cbk in range(6):
            psa, na = self.next_ps()
            psb, nb = self.next_ps()
            for k in range(8):
                b = self.stage_rr
                self.stage_rr = (self.stage_rr + 1) % 3
                sname = "stage%d" % b
                P.dma(self.stage[:, b, :], self.dram["ada_w"][l, k * 128:(k + 1) * 128, cbk * 1024:(cbk + 1) * 1024],
                      writes=[sname])
                wb = (cbk * 8 + k) % 4
                wt = self.A1[:, 2 * wb:2 * wb + 2, :].rearrange("p a b -> p (a b)")
                wn = ["a1_%d" % (2 * wb), "a1_%d" % (2 * wb + 1)]
                eng = ("act", "dve")[self.cast_rr % 2]
                self.cast_rr += 1
                self.cp(eng, wt, self.stage[:, b, :], [sname], wn)
                lhs = self.cab[:, k:k + 1]
                self.P.group("pe", [
                    (lambda e, o=psa[0:1, :], a=lhs, r=wt[:, 0:512], s0=(k == 0), s1=(k == 7):
                     e.matmul(o, lhsT=a, rhs=r, start=s0, stop=s1)),
                    (lambda e, o=psb[0:1, :], a=lhs, r=wt[:, 512:1024], s0=(k == 0), s1=(k == 7):
                     e.matmul(o, lhsT=a, rhs=r, start=s0, stop=s1)),
                ], reads=wn + ["cab"], writes=[na, nb])
            self.act(self.rowtmp[0:1, 0:512], psa[0:1, :], AF.Copy, [na], ["rt0", "rt1"])
            self.act(self.rowtmp[0:1, 512:1024], psb[0:1, :], AF.Copy, [nb], ["rt0", "rt1"])
            pst, nt = self.next_ps()
            fns = []
            for m in range(8):
                fns.append(lambda e, o=pst[:, m:m + 1], a=self.rowtmp[0:1, m * 128:(m + 1) * 128], r=self.ones_f[0:1, 0:1]:
                           e.matmul(o, lhsT=a, rhs=r, start=True, stop=True))
            P.group("pe", fns, reads=["rt0", "rt1", "ones_f"], writes=[nt])
            self.tt("dve", self.modT[:, l, cbk * 8:(cbk + 1) * 8], pst[:, 0:8], self.adab[:, l, cbk * 8:(cbk + 1) * 8],
                    ALU.add, [nt, "adab"], ["modT%d" % l])
        mn = "modT%d" % l
        self.ts("dve", self.aT[:, l, 0, :], self.modT[:, l, 8:16], 1.0, None, ALU.add, None, [mn], ["aT%d" % l])
        self.tt("dve", self.aT[:, l, 0, :], self.aT[:, l, 0, :], self.n1g[:, l, :], ALU.mult, ["aT%d" % l, "n1g"], ["aT%d" % l])
        self.ts("dve", self.aT[:, l, 1, :], self.modT[:, l, 32:40], 1.0, None, ALU.add, None, [mn], ["aT%d" % l])
        self.tt("dve", self.aT[:, l, 1, :], self.aT[:, l, 1, :], self.n2g[:, l, :], ALU.mult, ["aT%d" % l, "n2g"], ["aT%d" % l])

    def load_x_first(self, ti):
        P = self.P
        for tb in range(4):
            b = self.stage_rr
            self.stage_rr = (self.stage_rr + 1) % 3
            sname = "stage%d" % b
            r0 = (ti * 4 + tb) * 128
            P.dma(self.stage[:, b, :], self.dram["x"][r0:r0 + 128, :], writes=[sname])
            for hf in range(2):
                ps, pn = self.next_ps()
                fns = []
                for kk in range(4):
                    k = hf * 4 + kk
                    fns.append(lambda e, o=ps[:, kk * 128:(kk + 1) * 128], i=self.stage[:, b, k * 128:(k + 1) * 128]:
                               e.transpose(o, i, self.ident[:]))
                P.group("pe", fns, reads=[sname, "ident"], writes=[pn])
                eng = "dve" if hf == 0 else "act"
                self.cp(eng, self.xt[:, hf * 4:hf * 4 + 4, tb * 128:(tb + 1) * 128],
                        ps[:, :].rearrange("p (a b) -> p a b", a=4), [pn], ["xt%d" % k for k in range(hf * 4, hf * 4 + 4)])

    def load_x(self, ti):
        for k in range(8):
            self.P.dma(self.xt[:, k, :], self.xT[k * 128:(k + 1) * 128, ti * TT:(ti + 1) * TT],
                       reads=["xT_%d_%d" % (k, ti)], writes=["xt%d" % k])

    def store_x(self, ti):
        for k in range(8):
            self.P.dma(self.xT[k * 128:(k + 1) * 128, ti * TT:(ti + 1) * TT], self.xt[:, k, :],
                       reads=["xt%d" % k], writes=["xT_%d_%d" % (k, ti)])

    def sumsq_rstd(self):
        P = self.P
        pss, pn = self.ps[6], "ps6"
        for k in range(8):
            b = k % 2
            self.tt("pool", self.sq[:, b, :], self.xt[:, k, :], self.xt[:, k, :], ALU.mult, ["xt%d" % k], ["sq%d" % b])
            P.group("pe", [lambda e, b=b, k=k: e.matmul(pss[:], lhsT=self.ones_bf[:], rhs=self.sq[:, b, :],
                                                         start=(k == 0), stop=(k == 7))],
                    reads=["sq%d" % b, "ones_bf"], writes=[pn])
        self.act(self.rstd[:], pss[:], AF.Sqrt, [pn, "epsc"], ["rstd"], bias=self.epsc[:, 0:1], scale=1.0 / D)
        P.op("dve", lambda e: e.reciprocal(out=self.rstd[:], in_=self.rstd[:]), reads=["rstd"], writes=["rstd"])

    def norm_stage(self, l, which):
        self.sumsq_rstd()
        an = "aT%d" % l
        mn = "modT%d" % l
        sh0 = 0 if which == 0 else 24
        for k in range(8):
            b = k % 2
            self.stt("dve", self.tmpn[:, b, :], self.xt[:, k, :], self.aT[:, l, which, k:k + 1], self.rstd[:],
                     ALU.mult, ALU.mult, ["xt%d" % k, an, "rstd"], ["tmpn%d" % b])
            self.act(self.h[:, k, :], self.tmpn[:, b, :], AF.Identity, ["tmpn%d" % b, mn], ["h%d" % k],
                     bias=self.modT[:, l, sh0 + k:sh0 + k + 1])

    def final_stage(self, ti):
        P = self.P
        self.sumsq_rstd()
        for k in range(8):
            self.stt("dve", self.xt[:, k, :], self.xt[:, k, :], self.fg[:, k:k + 1], self.rstd[:],
                     ALU.mult, ALU.mult, ["xt%d" % k, "fg", "rstd"], ["xt%d" % k])
        for tb in range(4):
            b = self.stage_rr
            self.stage_rr = (self.stage_rr + 1) % 3
            sname = "stage%d" % b
            for hf in range(2):
                ps, pn = self.next_ps()
                fns = []
                for kk in range(4):
                    k = hf * 4 + kk
                    fns.append(lambda e, o=ps[:, kk * 128:(kk + 1) * 128], i=self.xt[:, k, tb * 128:(tb + 1) * 128]:
                               e.transpose(o, i, self.ident[:]))
                P.group("pe", fns, reads=["xt%d" % k for k in range(hf * 4, hf * 4 + 4)] + ["ident"], writes=[pn])
                eng = "dve" if hf == 0 else "act"
                self.cp(eng, self.stage[:, b, hf * 512:(hf + 1) * 512], ps[:, :], [pn], [sname])
            r0 = (ti * 4 + tb) * 128
            P.dma(self.out[r0:r0 + 128, :], self.stage[:, b, :], reads=[sname], writes=["out_%d" % r0])

    def resid_update(self, ps, pn, mo, gate_ap, gname):
        self.stt("dve", self.xt[:, mo, :], ps[:], gate_ap, self.xt[:, mo, :], ALU.mult, ALU.add,
                 [pn, gname, "xt%d" % mo], ["xt%d" % mo])

    def out_proj(self, l, Wout, wname, src, srcnames):
        for mo in range(8):
            ps, pn = self.next_ps()
            self.mm(ps[:], pn, [(Wout[:, k, mo * 128:(mo + 1) * 128], src[:, k, :]) for k in range(8)],
                    reads=self.wparts[wname] + srcnames)
            self.resid_update(ps, pn, mo, self.modT[:, l, 16 + mo:17 + mo], "modT%d" % l)

    def ffn_block(self, l, last):
        self.P.barrier()
        W1 = self.wview(self.WA, 0, 8, 4096)
        W2 = self.wview(self.WB, 0, 32, 1024)
        self.load_w("w1", W1, self.dram["ff_w1"][l], 8, 4096)
        self.load_w("w2", W2, self.dram["ff_w2"][l], 32, 1024)
        r2 = self.A1
        for ti in range(NT):
            self.load_x(ti)
            self.norm_stage(l, 1)
            hn = ["h%d" % k for k in range(8)]
            for half in range(2):
                for jj in range(16):
                    j = half * 16 + jj
                    ps, pn = self.next_ps()
                    self.mm(ps[:], pn, [(W1[:, k, j * 128:(j + 1) * 128], self.h[:, k, :]) for k in range(8)],
                            reads=self.wparts["w1"] + hn)
                    b = jj % 2
                    self.act(self.rt[:, b, :], ps[:], AF.Relu, [pn], ["rt%d" % b])
                    eng = "dve" if jj % 2 == 0 else "pool"
                    self.tt(eng, r2[:, jj, :], self.rt[:, b, :], self.rt[:, b, :], ALU.mult, ["rt%d" % b], ["a1_%d" % jj])
                for mo in range(8):
                    ps, pn = self.next_ps()
                    self.mm(ps[:], pn, [(W2[:, half * 16 + jj, mo * 128:(mo + 1) * 128], r2[:, jj, :]) for jj in range(16)],
                            reads=self.wparts["w2"] + ["a1_%d" % jj for jj in range(16)])
                    self.resid_update(ps, pn, mo, self.modT[:, l, 40 + mo:41 + mo], "modT%d" % l)
            if last:
                self.final_stage(ti)
            else:
                self.store_x(ti)

    def conv_block(self, l, j, first):
        self.P.barrier()
        Win = self.wview(self.WA, 0, 8, 3072)
        Wout = self.wview(self.WB, 0, 8, 1024)
        self.load_w("cwin", Win, self.dram["conv_w_in"][j], 8, 3072)
        self.load_w("cwout", Wout, self.dram["conv_w_out"][j], 8, 1024)
        A2 = self.A2
        cs = [A2[:, 0:512], A2[:, 512:1024]]
        zb = [A2[:, 1024:1538], A2[:, 1538:2052]]
        acc = [A2[:, 2052:2564], A2[:, 2564:3076]]
        bsb = [self.A3[:, 0:512], self.A3[:, 512:1024]]
        q = self.A1
        self.memset("pool", self.zc[:], 0.0, ["zc%d" % m for m in range(8)])
        for ti in range(NT):
            if first:
                self.load_x_first(ti)
            else:
                self.load_x(ti)
            self.norm_stage(l, 0)
            hn = ["h%d" % k for k in range(8)]
            wr = self.wparts["cwin"] + hn
            for m in range(8):
                b = m % 2
                psB, nB = self.next_ps()
                psC, nC = self.next_ps()
                psX, nX = self.next_ps()
                self.mm(psC[:], nC, [(Win[:, k, 1024 + m * 128:1024 + (m + 1) * 128], self.h[:, k, :]) for k in range(8)], wr)
                self.mm(psX[:], nX, [(Win[:, k, 2048 + m * 128:2048 + (m + 1) * 128], self.h[:, k, :]) for k in range(8)], wr)
                self.mm(psB[:], nB, [(Win[:, k, m * 128:(m + 1) * 128], self.h[:, k, :]) for k in range(8)], wr)
                self.act(cs[b], psC[:], AF.Copy, [nC], ["cs%d" % b])
                self.act(bsb[b], psB[:], AF.Copy, [nB], ["bsb%d" % b])
                self.cp("pool", zb[b][:, 0:2], self.zc[:, m, :], ["zc%d" % m], ["zb%d" % b])
                self.tt("dve", zb[b][:, 2:514], psX[:], cs[b], ALU.mult, [nX, "cs%d" % b], ["zb%d" % b])
                self.act(acc[b], zb[b][:, 2:514], AF.Identity, ["zb%d" % b, "cw", "cb"], ["acc%d" % b],
                         bias=self.cb[:, j, m:m + 1], scale=self.cw[:, j, 2, m:m + 1])
                self.stt("dve", acc[b], zb[b][:, 1:513], self.cw[:, j, 1, m:m + 1], acc[b], ALU.mult, ALU.add,
                         ["zb%d" % b, "cw", "acc%d" % b], ["acc%d" % b])
                self.stt("dve", acc[b], zb[b][:, 0:512], self.cw[:, j, 0, m:m + 1], acc[b], ALU.mult, ALU.add,
                         ["zb%d" % b, "cw", "acc%d" % b], ["acc%d" % b])
                self.cp("pool", self.zc[:, m, :], zb[b][:, 512:514], ["zb%d" % b], ["zc%d" % m])
                self.tt("pool", q[:, m, :], bsb[b], acc[b], ALU.mult, ["bsb%d" % b, "acc%d" % b], ["a1_%d" % m])
            self.out_proj(l, Wout, "cwout", q, ["a1_%d" % m for m in range(8)])
            self.store_x(ti)

    def sg_block(self, l):
        P = self.P
        P.barrier()
        Win = self.wview(self.WA, 0, 8, 2048)
        Wout = self.wview(self.WB, 0, 8, 1024)
        wsT = self.WB[:, 8192:9216].rearrange("p (h t) -> p h t", h=8)
        self.load_w("gwin", Win, self.dram["sg_w_in"], 8, 2048)
        self.load_w("gwout", Wout, self.dram["sg_w_out"], 8, 1024)
        A3 = self.A3
        vsb = A3[:, 0:1024]
        gain = A3[:, 1024:2048]
        bias = A3[:, 2048:3072]
        vsq = A3[:, 3072:3584]
        tmpg = [A3[:, 3584:4096], self.rt[:, 0, :]]
        tmpgn = ["tmpg0", "rt0"]
        vn = self.sbf[:, :, :, :].rearrange("p a b c -> p (a b c)")[:, 0:1024]
        us = self.A2.rearrange("p (k n) -> p k n", k=8)
        gq = self.A1
        P.dma(gain, self.dram["sg_gain_rep"], writes=["gain"])
        P.dma(bias, self.dram["sg_bias_rep"].rearrange("p h t -> p (h t)"), writes=["bias"])
        b = self.stage_rr
        self.stage_rr = (self.stage_rr + 1) % 3
        sname = "stage%d" % b
        stg = self.stage[:, b, :].rearrange("p (h s) -> p h s", h=8)
        P.dma(stg, self.dram["sg_w_s"].rearrange("h t s -> t h s"), writes=[sname])
        self.tt("pool", stg, stg, self.tril[:, :].unsqueeze(1).broadcast_to([128, 8, 128]), ALU.mult,
                [sname, "tril"], [sname])
        for hf in range(2):
            ps, pn = self.next_ps()
            fns = []
            for hh in range(4):
                fns.append(lambda e, o=ps[:, hh * 128:(hh + 1) * 128], i=stg[:, hf * 4 + hh, :]:
                           e.transpose(o, i, self.ident[:]))
            P.group("pe", fns, reads=[sname, "ident"], writes=[pn])
            self.cp("dve", wsT[:, hf * 4:hf * 4 + 4, :], ps[:, :].rearrange("p (a b) -> p a b", a=4), [pn], ["wsT"])
        for ti in range(NT):
            self.load_x(ti)
            self.norm_stage(l, 0)
            hn = ["h%d" % k for k in range(8)]
            wr = self.wparts["gwin"] + hn
            for m in range(8):
                ps, pn = self.next_ps()
                self.mm(ps[:], pn, [(Win[:, k, m * 128:(m + 1) * 128], self.h[:, k, :]) for k in range(8)], wr)
                self.act(us[:, m, :], ps[:], AF.Copy, [pn], ["us%d" % m])
            for n in range(4):
                for hf in range(2):
                    ps, pn = self.next_ps()
                    self.mm(ps[:], pn, [(self.h[:, k, n * 128:(n + 1) * 128], Win[:, k, 1024 + hf * 512:1024 + (hf + 1) * 512])
                                        for k in range(8)], wr)
                    self.act(vsb[:, hf * 512:(hf + 1) * 512], ps[:], AF.Copy, [pn], ["vsb%d" % hf])
                    self.tt("pool", vsq, vsb[:, hf * 512:(hf + 1) * 512], vsb[:, hf * 512:(hf + 1) * 512], ALU.mult,
                            ["vsb%d" % hf], ["vsq"])
                    P.op("dve", lambda e, hf=hf: e.reduce_sum(out=self.ssv[:, hf:hf + 1], in_=vsq, axis=AX.X),
                         reads=["vsq"], writes=["ssv"])
                self.tt("dve", self.ssv[:, 2:3], self.ssv[:, 0:1], self.ssv[:, 1:2], ALU.add, ["ssv"], ["ssv2"])
                self.act(self.ssv[:, 3:4], self.ssv[:, 2:3], AF.Sqrt, ["ssv2", "epsc"], ["rv"], bias=self.epsc[:, 0:1], scale=1.0 / D)
                P.op("dve", lambda e: e.reciprocal(out=self.ssv[:, 3:4], in_=self.ssv[:, 3:4]), reads=["rv"], writes=["rv"])
                self.stt("dve", vn, vsb, self.ssv[:, 3:4], gain, ALU.mult, ALU.mult, ["vsb0", "vsb1", "rv", "gain"], ["vn"])
                for hq in range(2):
                    ps, pn = self.next_ps()
                    pairs, flags = [], []
                    for hh in range(4):
                        hd = hq * 4 + hh
                        pairs.append((vn[:, hd * 128:(hd + 1) * 128], wsT[:, hd, :]))
                        flags.append((ps[:, hh * 128:(hh + 1) * 128], True, True))
                    self.mm(None, pn, pairs, ["vn", "wsT"], flags=flags)
                    tb = tmpg[hq]
                    self.tt("dve", tb, ps[:], bias[:, hq * 512:(hq + 1) * 512], ALU.add, [pn, "bias"], [tmpgn[hq]])
                    self.tt("pool", gq[:, hq * 4:hq * 4 + 4, n * 128:(n + 1) * 128],
                            tb.rearrange("p (a b) -> p a b", a=4), us[:, hq * 4:hq * 4 + 4, n * 128:(n + 1) * 128], ALU.mult,
                            [tmpgn[hq]] + ["us%d" % m for m in range(hq * 4, hq * 4 + 4)],
                            ["a1_%d" % m for m in range(hq * 4, hq * 4 + 4)])
            self.out_proj(l, Wout, "gwout", gq, ["a1_%d" % m for m in range(8)])
            self.store_x(ti)

    def s5_prep(self):
        P = self.P
        d = self.dram
        c = self.s5c
        ARE, AIM, LDT, DT, MAG, ANG, SN, CS, AR, AI, T1, T2, T3, T4, FRE, FIM = [c[:, i, :] for i in range(16)]
        P.dma(ARE, d["are_c"], writes=["s5c"])
        P.dma(AIM, d["aim_c"], writes=["s5c"])
        P.dma(LDT, d["ldt_c"], writes=["s5c"])
        R = ["s5c"]
        self.act(DT, LDT, AF.Exp, R, R)
        self.tt("dve", T1, ARE, DT, ALU.mult, R, R)
        self.act(MAG, T1, AF.Exp, R, R)
        self.tt("dve", ANG, AIM, DT, ALU.mult, R, R)

        def sin_of(dst, src, shift):
            self.ts("dve", T2, src, shift, None, ALU.add, None, R, R)
            self.ts("dve", T3, T2, 1.0 / (2 * PI), None, ALU.mult, None, R, R)
            self.cp("dve", self.s5i[:], T3, R, ["s5i"])
            self.cp("dve", T3, self.s5i[:], ["s5i"], R)
            self.stt("dve", T2, T3, -2 * PI, T2, ALU.mult, ALU.add, R, R)
            self.ts("dve", T3, T2, PI, -2 * PI, ALU.is_gt, ALU.mult, R, R)
            self.tt("dve", T2, T2, T3, ALU.add, R, R)
            self.ts("dve", T3, T2, -PI, 2 * PI, ALU.is_lt, ALU.mult, R, R)
            self.tt("dve", T2, T2, T3, ALU.add, R, R)
            self.ts("dve", T2, T2, -3.141592, 3.141592, ALU.max, ALU.min, R, R)
            self.act(dst, T2, AF.Sin, R, R)

        sin_of(SN, ANG, 0.0)
        sin_of(CS, ANG, PI / 2)
        self.tt("dve", AR, MAG, CS, ALU.mult, R, R)
        self.tt("dve", AI, MAG, SN, ALU.mult, R, R)
        self.tt("dve", T1, ARE, ARE, ALU.mult, R, R)
        self.tt("dve", T2, AIM, AIM, ALU.mult, R, R)
        self.tt("dve", T1, T1, T2, ALU.add, R, R)
        P.op("dve", lambda e: e.reciprocal(out=T4, in_=T1), reads=R, writes=R)
        self.ts("dve", T3, AR, -1.0, None, ALU.add, None, R, R)
        self.tt("dve", T1, T3, ARE, ALU.mult, R, R)
        self.tt("dve", T2, AI, AIM, ALU.mult, R, R)
        self.tt("dve", T1, T1, T2, ALU.add, R, R)
        self.tt("dve", FRE, T1, T4, ALU.mult, R, R)
        self.tt("dve", T1, AI, ARE, ALU.mult, R, R)
        self.tt("dve", T2, T3, AIM, ALU.mult, R, R)
        self.tt("dve", T1, T1, T2, ALU.subtract, R, R)
        self.tt("dve", FIM, T1, T4, ALU.mult, R, R)
        pr, pi_, npi = self.pw[:, 0], self.pw[:, 1], self.pw[:, 2]
        W = ["pw"]
        self.cp("dve", pr[:, 0, :], CS, R, W)
        self.cp("dve", pi_[:, 0, :], SN, R, W)
        for k in range(1, 9):
            self.tt("dve", T1, pr[:, k - 1, :], pr[:, k - 1, :], ALU.mult, W + R, R)
            self.tt("dve", T2, pi_[:, k - 1, :], pi_[:, k - 1, :], ALU.mult, W + R, R)
            self.tt("dve", pr[:, k, :], T1, T2, ALU.subtract, R, W)
            self.tt("dve", T1, pr[:, k - 1, :], pi_[:, k - 1, :], ALU.mult, W + R, R)
            self.ts("dve", pi_[:, k, :], T1, 2.0, None, ALU.mult, None, R, W)
        self.ts("dve", npi, pi_, -1.0, None, ALU.mult, None, W, W)

    def s5_block(self, l):
        P = self.P
        d = self.dram
        P.barrier()
        self.s5_prep()
        Win = self.wview(self.WA, 0, 8, 1024)
        Wglu = self.wview(self.WA, 8192, 8, 1024)
        BW = self.WA[:, 16384:24576].rearrange("p (g r n) -> p g r n", g=32, r=2)
        CW = self.WA[:, 24576:32768].rearrange("p (g r n) -> p g r n", g=32, r=2)
        Wout = self.wview(self.WB, 0, 8, 1024)
        self.load_w("swin", Win, d["ssm_w_in"], 8, 1024)
        self.load_w("sglu", Wglu, d["ssm_glu_w"], 8, 1024)
        self.load_w("swout", Wout, d["ssm_w_out"], 8, 1024)
        c = self.s5c
        FRE, FIM = c[:, 14, :], c[:, 15, :]
        bwparts, cwparts = [], []
        for pc in range(8):
            psr, nr_ = self.next_ps()
            psi, ni_ = self.next_ps()
            for g4 in range(4):
                gp = pc * 4 + g4
                self.ts("dve", self.dg[:, 0, :], self.ident[:], FRE[:, gp:gp + 1], None, ALU.mult, None, ["s5c", "ident"], ["dg0"])
                self.ts("dve", self.dg[:, 1, :], self.ident[:], FIM[:, gp:gp + 1], None, ALU.mult, None, ["s5c", "ident"], ["dg1"])
                self.mm(psr[:, g4 * 128:(g4 + 1) * 128], nr_, [(self.ones_f[:], self.dg[:, 0, :])], ["dg0", "ones_f"],
                        flags=[(psr[:, g4 * 128:(g4 + 1) * 128], True, True)])
                self.mm(psi[:, g4 * 128:(g4 + 1) * 128], ni_, [(self.ones_f[:], self.dg[:, 1, :])], ["dg1", "ones_f"],
                        flags=[(psi[:, g4 * 128:(g4 + 1) * 128], True, True)])
            b1 = self.stage_rr
            b2 = (b1 + 1) % 3
            self.stage_rr = (b1 + 2) % 3
            s1, s2 = "stage%d" % b1, "stage%d" % b2
            bre = self.stage[:, b1, 0:512]
            bim = self.stage[:, b2, 0:512]
            P.dma(bre, d["Bre_z"][:, pc * 4:(pc + 1) * 4, :].rearrange("p g n -> p (g n)"), writes=[s1])
            P.dma(bim, d["Bim_z"][:, pc * 4:(pc + 1) * 4, :].rearrange("p g n -> p (g n)"), writes=[s2])
            t1, t2 = self.tmpn[:, 0, :], self.tmpn[:, 1, :]
            pn = "bw%d" % pc
            self.tt("dve", t1, psr[:], bre, ALU.mult, [nr_, s1], ["tmpn0"])
            self.tt("dve", t2, psi[:], bim, ALU.mult, [ni_, s2], ["tmpn1"])
            self.tt("dve", BW[:, pc * 4:(pc + 1) * 4, 0, :], t1.rearrange("p (g n) -> p g n", g=4),
                    t2.rearrange("p (g n) -> p g n", g=4), ALU.subtract, ["tmpn0", "tmpn1"], [pn + "r"])
            self.tt("dve", t1, psr[:], bim, ALU.mult, [nr_, s2], ["tmpn0"])
            self.tt("dve", t2, psi[:], bre, ALU.mult, [ni_, s1], ["tmpn1"])
            self.tt("dve", BW[:, pc * 4:(pc + 1) * 4, 1, :], t1.rearrange("p (g n) -> p g n", g=4),
                    t2.rearrange("p (g n) -> p g n", g=4), ALU.add, ["tmpn0", "tmpn1"], [pn + "i"])
            bwparts += [pn + "r", pn + "i"]
        for pc in range(8):
            for ri, key, sc in ((0, "Cre_z", 1.0), (1, "Cim_z", -1.0)):
                b1 = self.stage_rr
                self.stage_rr = (b1 + 1) % 3
                s1 = "stage%d" % b1
                P.dma(self.stage[:, b1, 0:512], d[key][:, pc * 4:(pc + 1) * 4, :].rearrange("p g n -> p (g n)"), writes=[s1])
                pn = "cwp%d_%d" % (pc, ri)
                self.act(CW[:, pc * 4:(pc + 1) * 4, ri, :], self.stage[:, b1, 0:512].rearrange("p (g n) -> p g n", g=4),
                         AF.Copy, [s1], [pn], scale=sc)
                cwparts.append(pn)
        K0 = self.KS0
        K1 = self.KS1
        SRI = self.WB[:, 14336:16384].bitcast(F32)
        BR = [K0[:, 0:512], K0[:, 1024:1536]]
        BI = [K0[:, 512:1024], K0[:, 1536:2048]]
        A2f = self.A2
        PRs = [K0[:, 2048:2560], A2f[:, 0:512]]
        PIs = [K0[:, 2560:3072], A2f[:, 512:1024]]
        TC = [K1[:, 0:512], K1[:, 1024:1536], A2f[:, 1024:1536]]
        TS = [K1[:, 512:1024], K1[:, 1536:2048], A2f[:, 1536:2048]]
        RR, RI = K1[:, 2048:2560], K1[:, 2560:3072]
        Dd = self.WB[:, 16384 + 4096:16384 + 4096 + 1024].rearrange("p (q n) -> p q n", q=8)
        sbf3 = self.WB[:, 16384 + 5120:16384 + 5120 + 1024].rearrange("p (r n) -> p r n", r=2)
        sbfs = [self.sbf[:, 0], self.sbf[:, 1], sbf3]
        for q_ in range(8):
            self.ts("dve", Dd[:, q_, :], self.ident[:], self.dsk[:, q_:q_ + 1], None, ALU.mult, None, ["ident", "dsk"], ["Dd"])
        SR, SI = SRI[:, 0:512], SRI[:, 512:1024]
        M1, M2 = self.rt[:, 0, :], self.rt[:, 1, :]
        scan_names = ["br0", "bi0", "br1", "bi1", "pr", "pi", "tc0", "ts0", "tc1", "ts1", "rr", "ri", "sr", "si"]
        self.memset("pool", K1, 0.0, ["tc0", "ts0", "tc1", "ts1", "rr", "ri", "stage0", "stage1", "stage2"])
        carn_all = ["car%d" % g for g in range(32)]
        self.memset("dve", self.car[:], 0.0, carn_all)
        uc, us_, uns = self.pw[:, 0], self.pw[:, 1], self.pw[:, 2]
        for gp in range(32):
            t = gp % 3
            tcn, tsn = "tc%d" % t, "ts%d" % t
            self.memset("dve", TC[t][:, 0:1], 1.0, [tcn])
            self.memset("dve", TS[t][:, 0:1], 0.0, [tsn])
            for k in range(9):
                n = 1 << k
                self.ts("dve", TC[t][:, n:2 * n], TC[t][:, 0:n], uc[:, k, gp:gp + 1], None, ALU.mult, None, [tcn, "pw"], [tcn])
                self.stt("dve", TC[t][:, n:2 * n], TS[t][:, 0:n], uns[:, k, gp:gp + 1], TC[t][:, n:2 * n], ALU.mult, ALU.add,
                         [tcn, tsn, "pw"], [tcn])
                self.ts("dve", TS[t][:, n:2 * n], TS[t][:, 0:n], uc[:, k, gp:gp + 1], None, ALU.mult, None, [tsn, "pw"], [tsn])
                self.stt("dve", TS[t][:, n:2 * n], TC[t][:, 0:n], us_[:, k, gp:gp + 1], TS[t][:, n:2 * n], ALU.mult, ALU.add,
                         [tcn, tsn, "pw"], [tsn])
            P.dma(self.tabC[gp], TC[t], reads=[tcn], writes=["tabC%d" % gp])
            P.dma(self.tabS[gp], TS[t], reads=[tsn], writes=["tabS%d" % gp])
        us = self.A2.rearrange("p (k n) -> p k n", k=8)
        yg = self.A3.rearrange("p (k n) -> p k n", k=8)
        ubf = self.A1[:, 0:8, :]
        ygb = self.A1[:, 8:16, :]
        qv = self.A1[:, 0:8, :]
        c = self.s5c
        MAG, SNt, CSt = c[:, 4, :], c[:, 6, :], c[:, 7, :]
        X1, X2, INITR, INITI = c[:, 10, :], c[:, 11, :], c[:, 12, :], c[:, 13, :]

        for ti in range(NT):
            self.load_x(ti)
            self.norm_stage(l, 0)
            hn = ["h%d" % k for k in range(8)]
            for m in range(8):
                ps, pn = self.next_ps()
                self.mm(ps[:], pn, [(Win[:, k, m * 128:(m + 1) * 128], self.h[:, k, :]) for k in range(8)],
                        self.wparts["swin"] + hn)
                self.cp("act" if m % 2 == 0 else "dve", ubf[:, m, :], ps[:], [pn], ["a1_%d" % m])
            self.tt("dve", X1, CSt, self.car[:, 0, :], ALU.mult, ["s5c"] + carn_all, ["x1"])
            self.tt("dve", X2, SNt, self.car[:, 1, :], ALU.mult, ["s5c"] + carn_all, ["x2"])
            self.tt("dve", INITR, X1, X2, ALU.subtract, ["x1", "x2"], ["initr"])
            self.tt("dve", X1, CSt, self.car[:, 1, :], ALU.mult, ["s5c"] + carn_all, ["x1"])
            self.tt("dve", X2, SNt, self.car[:, 0, :], ALU.mult, ["s5c"] + carn_all, ["x2"])
            self.tt("dve", INITI, X1, X2, ALU.add, ["x1", "x2"], ["initi"])

            def bu(gp, part):
                q = gp // 4
                s = gp % 2
                t = gp % 3
                pb = gp % 2
                s3 = gp % 3
                PRb, PIb = PRs[pb], PIs[pb]
                prn, pin = "pr%d" % pb, "pi%d" % pb
                brn, bin_, tcn, tsn = "br%d" % s, "bi%d" % s, "tc%d" % t, "ts%d" % t
                if part == 0:
                    psr, nr_ = self.next_ps()
                    psi, ni_ = self.next_ps()
                    self.mm(psr[:], nr_, [(BW[:, gp, 0, :], ubf[:, q, :])], bwparts + ["a1_%d" % q])
                    self.mm(psi[:], ni_, [(BW[:, gp, 1, :], ubf[:, q, :])], bwparts + ["a1_%d" % q])
                    self.cp("act", BR[s], psr[:], [nr_], [brn])
                    self.cp("act", BI[s], psi[:], [ni_], [bin_])
                    P.dma(TC[t], self.tabC[gp], reads=["tabC%d" % gp], writes=[tcn])
                    P.dma(TS[t], self.tabS[gp], reads=["tabS%d" % gp], writes=[tsn])
                    self.tt("pool", PRb, BR[s], TC[t], ALU.mult, [brn, tcn], [prn])
                    self.tt("pool", M1, BI[s], TS[t], ALU.mult, [bin_, tsn], ["rt0"])
                    self.tt("pool", PRb, PRb, M1, ALU.add, [prn, "rt0"], [prn])
                    self.tt("pool", PIb, BI[s], TC[t], ALU.mult, [bin_, tcn], [pin])
                    self.tt("pool", M1, BR[s], TS[t], ALU.mult, [brn, tsn], ["rt0"])
                    self.tt("pool", PIb, PIb, M1, ALU.subtract, [pin, "rt0"], [pin])
                    return
                rho = MAG[:, gp:gp + 1].broadcast_to([128, TT])
                P.op("dve", lambda e: e.tensor_tensor_scan(out=RR, data0=rho, data1=PRb, initial=INITR[:, gp:gp + 1],
                                                           op0=ALU.mult, op1=ALU.add),
                     reads=[prn, "s5c", "initr"], writes=["rr"])
                P.op("dve", lambda e: e.tensor_tensor_scan(out=RI, data0=rho, data1=PIb, initial=INITI[:, gp:gp + 1],
                                                           op0=ALU.mult, op1=ALU.add),
                     reads=[pin, "s5c", "initi"], writes=["ri"])
                self.tt("dve", SR, RR, TC[t], ALU.mult, ["rr", tcn], ["sr"])
                self.tt("dve", M2, RI, TS[t], ALU.mult, ["ri", tsn], ["rt1"])
                self.tt("dve", SR, SR, M2, ALU.subtract, ["sr", "rt1"], ["sr"])
                self.tt("dve", SI, RR, TS[t], ALU.mult, ["rr", tsn], ["si"])
                self.tt("dve", M2, RI, TC[t], ALU.mult, ["ri", tcn], ["rt1"])
                self.tt("dve", SI, SI, M2, ALU.add, ["si", "rt1"], ["si"])
                carn = "car%d" % gp
                self.cp("act", sbfs[s3][:, 0, :], SR, ["sr"], ["sbf%d" % s3])
                self.cp("act", sbfs[s3][:, 1, :], SI, ["si"], ["sbf%d" % s3])
                self.cp("act", self.car[:, 0, gp:gp + 1], SR[:, 511:512], ["sr"], [carn])
                self.cp("act", self.car[:, 1, gp:gp + 1], SI[:, 511:512], ["si"], [carn])

            def outp(gp):
                q = gp // 4
                s3 = gp % 3
                psY, nY = self.ps[6 + (q % 2)], "ps%d" % (6 + (q % 2))
                pairs = [(CW[:, gp, 0, :], sbfs[s3][:, 0, :]), (CW[:, gp, 1, :], sbfs[s3][:, 1, :])]
                flags = [(psY[:], False, False), (psY[:], False, gp % 4 == 3)]
                rd = cwparts + ["sbf%d" % s3]
                if gp % 4 == 0:
                    pairs = [(Dd[:, q, :], ubf[:, q, :])] + pairs
                    flags = [(psY[:], True, False)] + flags
                    rd = rd + ["Dd", "a1_%d" % q]
                self.mm(None, nY, pairs, rd, flags=flags)
                if gp % 4 == 3:
                    self.act(yg[:, q, :], psY[:], AF.Gelu_apprx_tanh, [nY], ["yg%d" % q])
                    self.cp("act", ygb[:, q, :], yg[:, q, :], ["yg%d" % q], ["a1_%d" % (8 + q)])

            for i in range(-1, 33):
                if 0 <= i + 1 < 32:
                    bu(i + 1, 0)
                if 0 <= i < 32:
                    bu(i, 1)
                if 0 <= i - 1 < 32:
                    outp(i - 1)
            ygn = ["a1_%d" % (8 + k) for k in range(8)]
            for mo in range(8):
                ps, pn = self.next_ps()
                self.mm(ps[:], pn, [(Wglu[:, k, mo * 128:(mo + 1) * 128], ygb[:, k, :]) for k in range(8)],
                        self.wparts["sglu"] + ygn)
                b = mo % 2
                self.act(self.rt[:, b, :], ps[:], AF.Sigmoid, [pn, "glub"], ["rt%d" % b], bias=self.glub[:, mo:mo + 1])
                self.tt("dve", qv[:, mo, :], yg[:, mo, :], self.rt[:, b, :], ALU.mult, ["yg%d" % mo, "rt%d" % b], ["a1_%d" % mo])
            self.out_proj(l, Wout, "swout", qv, ["a1_%d" % m for m in range(8)])
            self.store_x(ti)

    def build(self):
        nc = self.nc
        self.declare()
        with ExitStack() as st:
            self.alloc(st)
            self.P = Prog(nc, st)
            block = st.enter_context(nc.Block())
            self.prologue()
            for l in range(self.n_layers):
                self.ada_stage(l)
            for l in range(self.n_layers):
                kind = l % 3
                if kind == 0:
                    self.conv_block(l, l // 3, first=(l == 0))
                elif kind == 1:
                    self.s5_block(l)
                else:
                    self.sg_block(l)
                self.ffn_block(l, last=(l == self.n_layers - 1))
            self.P.finish()
            self.P.emit(block)
        return nc


def tT(v):
    v = np.asarray(v, np.float32)
    lead = v.shape[:-1]
    r = v.reshape(lead + (8, 128))
    return np.ascontiguousarray(np.moveaxis(r, -1, 0))


def host_layout(inp, b):
    f32 = np.float32
    m = {}
    m["x"] = np.ascontiguousarray(inp["x"][b], f32)
    m["cT"] = tT(inp["c"][b])
    m["ada_w"] = np.ascontiguousarray(inp["ada_w"], f32)
    ab = np.asarray(inp["ada_b"], f32).reshape(DEPTH, 48, 128)
    m["ada_bT"] = np.ascontiguousarray(ab.transpose(2, 0, 1))
    m["n1gT"] = tT(inp["norm1_g"])
    m["n2gT"] = tT(inp["norm2_g"])
    m["fgT"] = tT(inp["final_g"])
    m["ff_w1"] = np.ascontiguousarray(inp["ff_w1"], f32)
    m["ff_w2"] = np.ascontiguousarray(inp["ff_w2"], f32)
    m["conv_w_in"] = np.ascontiguousarray(inp["conv_w_in"], f32)
    m["conv_wT"] = tT(inp["conv_w"])
    m["conv_bT"] = tT(inp["conv_b"])
    m["conv_w_out"] = np.ascontiguousarray(inp["conv_w_out"], f32)
    m["ssm_w_in"] = np.ascontiguousarray(inp["ssm_w_in"][0], f32)
    m["ssm_glu_w"] = np.ascontiguousarray(inp["ssm_glu_w"][0], f32)
    m["ssm_w_out"] = np.ascontiguousarray(inp["ssm_w_out"][0], f32)
    m["glu_bT"] = tT(inp["ssm_glu_b"][0])
    m["dT"] = tT(inp["ssm_d"][0])
    def compact(a):
        a = np.asarray(a, f32).reshape(32, 2, 64)
        return np.ascontiguousarray(a.transpose(1, 2, 0).reshape(128, 32))
    m["are_c"] = compact(inp["ssm_a_re"][0])
    m["aim_c"] = compact(inp["ssm_a_im"][0])
    m["ldt_c"] = compact(np.broadcast_to(np.asarray(inp["ssm_log_dt"][0], f32)[:, None], (64, 64)))
    bre = np.asarray(inp["ssm_b_re"][0], f32)
    bim = np.asarray(inp["ssm_b_im"][0], f32)
    cre = np.asarray(inp["ssm_c_re"][0], f32)
    cim = np.asarray(inp["ssm_c_im"][0], f32)
    Bre_z = np.zeros((128, 32, 128), f32); Bim_z = np.zeros((128, 32, 128), f32)
    Cre_z = np.zeros((128, 32, 128), f32); Cim_z = np.zeros((128, 32, 128), f32)
    for g in range(64):
        gp, g2, g8 = g // 2, g % 2, g % 8
        Bre_z[g8 * 16:(g8 + 1) * 16, gp, g2 * 64:(g2 + 1) * 64] = bre[g].T
        Bim_z[g8 * 16:(g8 + 1) * 16, gp, g2 * 64:(g2 + 1) * 64] = bim[g].T
        Cre_z[g2 * 64:(g2 + 1) * 64, gp, g8 * 16:(g8 + 1) * 16] = cre[g].T
        Cim_z[g2 * 64:(g2 + 1) * 64, gp, g8 * 16:(g8 + 1) * 16] = cim[g].T
    m["Bre_z"], m["Bim_z"], m["Cre_z"], m["Cim_z"] = Bre_z, Bim_z, Cre_z, Cim_z
    m["sg_w_in"] = np.ascontiguousarray(inp["sg_w_in"][0], f32)
    m["sg_w_out"] = np.ascontiguousarray(inp["sg_w_out"][0], f32)
    m["sg_w_s"] = np.ascontiguousarray(inp["sg_w_s"][0], f32)
    m["sg_bias_rep"] = np.ascontiguousarray(np.broadcast_to(np.asarray(inp["sg_b_s"][0], f32)[None], (128, 8, 128)))
    m["sg_gain_rep"] = np.ascontiguousarray(np.broadcast_to(np.asarray(inp["sg_v_g"][0], f32)[None], (128, D)))
    m["ident"] = np.eye(128, dtype=f32)
    m["tril"] = np.tril(np.ones((128, 128), f32))
    return m


_NC_CACHE = {}


def kernel(_n_layers=DEPTH, **inputs):
    inp = {k: np.asarray(v) for k, v in inputs.items()}
    if _n_layers not in _NC_CACHE:
        _NC_CACHE[_n_layers] = Builder(_n_layers).build()
    nc = _NC_CACHE[_n_layers]
    in_maps = [host_layout(inp, b) for b in range(8)]
    res = run_bass_kernel_spmd(nc, in_maps, core_ids=list(range(8)))
    out = np.stack([np.asarray(r["out"], np.float32) for r in res.results], axis=0)
    return out
```

```python
import math
from contextlib import ExitStack

import numpy as np
import concourse.bass as bass
import concourse.mybir as mybir
from concourse.bass_utils import run_bass_kernel_spmd

F32 = mybir.dt.float32
BF16 = mybir.dt.bfloat16
I32 = mybir.dt.int32
AF = mybir.ActivationFunctionType
ALU = mybir.AluOpType
AX = mybir.AxisListType

D = 1024
L = 4096
DEPTH = 4
TT = 512
NT = L // TT
KC = 8
EPS = 1e-6
ENGS = ("pe", "act", "dve", "pool", "sp")
N_DMA_SEMS = 14
PI = math.pi


class Prog:
    def __init__(self, nc, stack):
        self.nc = nc
        self.stream = {e: [] for e in ENGS}
        self.count = {e: 0 for e in ENGS}
        self.sem = {e: stack.enter_context(nc.semaphore("s_" + e)) for e in ENGS if e != "sp"}
        self.dsem = [stack.enter_context(nc.semaphore("d%d" % i)) for i in range(N_DMA_SEMS)]
        self.dval = [0] * N_DMA_SEMS
        self.dnext = 0
        self.waited = {e: {} for e in ENGS}
        self.last_w = {}
        self.readers = {}

    def _deps(self, eng, reads, writes, same_engine_ok=False):
        evs = []
        for r in reads:
            ev = self.last_w.get(r)
            if ev is not None:
                evs.append((ev, False))
        for w in writes:
            ev = self.last_w.get(w)
            if ev is not None:
                evs.append((ev, False))
            for ev in self.readers.get(w, {}).values():
                evs.append((ev, True))
        for (ev, is_war) in evs:
            owner, key, sem, val = ev
            if owner == eng and (same_engine_ok or is_war):
                continue
            if self.waited[eng].get(key, 0) >= val:
                continue
            self.waited[eng][key] = val
            self.stream[eng].append(("wait", sem, val))

    def _record(self, rkey, ev, reads, writes):
        for w in writes:
            self.last_w[w] = ev
            self.readers[w] = {}
        for r in reads:
            if r in writes:
                continue
            self.readers.setdefault(r, {})[rkey] = ev

    def op(self, eng, fn, reads=(), writes=()):
        self.group(eng, [fn], reads, writes)

    def group(self, eng, fns, reads=(), writes=()):
        psr = [r for r in reads if r.startswith("ps") and r not in writes]
        if psr:
            writes = list(writes) + psr
        self._deps(eng, reads, writes, same_engine_ok=(eng == "pe"))
        for fn in fns[:-1]:
            self.stream[eng].append(("ins", fn, None))
        self.count[eng] += 1
        ev = (eng, eng, self.sem[eng], self.count[eng])
        self.stream[eng].append(("ins", fns[-1], self.sem[eng]))
        self._record(eng, ev, reads, writes)

    def dma(self, out, in_, reads=(), writes=(), eng="sp"):
        i = self.dnext
        self.dnext = (self.dnext + 1) % N_DMA_SEMS
        key = "dma%d" % i
        sem = self.dsem[i]
        if self.dval[i] > 0 and self.waited[eng].get(key, 0) < self.dval[i]:
            self.waited[eng][key] = self.dval[i]
            self.stream[eng].append(("wait", sem, self.dval[i]))
        self._deps(eng, reads, writes)
        self.dval[i] += 16
        ev = ("dmaq", key, sem, self.dval[i])
        self.stream[eng].append(("dma", out, in_, sem))
        self._record("dmaq_" + key, ev, reads, writes)

    def barrier(self):
        for e in ENGS:
            for o in ENGS:
                if o != e and o != "sp" and self.count[o] > 0 and self.waited[e].get(o, 0) < self.count[o]:
                    self.waited[e][o] = self.count[o]
                    self.stream[e].append(("wait", self.sem[o], self.count[o]))
            for i in range(N_DMA_SEMS):
                key = "dma%d" % i
                if self.dval[i] > 0 and self.waited[e].get(key, 0) < self.dval[i]:
                    self.waited[e][key] = self.dval[i]
                    self.stream[e].append(("wait", self.dsem[i], self.dval[i]))

    def finish(self):
        for i in range(N_DMA_SEMS):
            if self.dval[i] > 0:
                self.stream["sp"].append(("wait", self.dsem[i], self.dval[i]))
        for e in ENGS:
            if e != "sp" and self.count[e] > 0:
                self.stream["sp"].append(("wait", self.sem[e], self.count[e]))

    def emit(self, block):
        def run(engh, items):
            for it in items:
                if it[0] == "wait":
                    engh.wait_ge(it[1], it[2])
                elif it[0] == "ins":
                    ins = it[1](engh)
                    if it[2] is not None:
                        ins.then_inc(it[2], 1)
                else:
                    engh.dma_start(out=it[1], in_=it[2]).then_inc(it[3], 16)

        @block.sync
        def _(e):
            run(e, self.stream["sp"])

        @block.tensor
        def _(e):
            run(e, self.stream["pe"])

        @block.scalar
        def _(e):
            run(e, self.stream["act"])

        @block.vector
        def _(e):
            run(e, self.stream["dve"])

        @block.gpsimd
        def _(e):
            run(e, self.stream["pool"])


class Builder:
    def __init__(self, n_layers=DEPTH):
        self.n_layers = n_layers
        self.nc = bass.Bass("TRN2", target_bir_lowering=False)
        self.dram = {}
        self.rr = 0
        self.cast_rr = 0
        self.stage_rr = 0
        self.wparts = {}

    def din(self, name, shape):
        self.dram[name] = self.nc.dram_tensor(name, list(shape), F32, kind="ExternalInput").ap()

    def declare(self):
        din = self.din
        din("x", [L, D]); din("cT", [128, 8]); din("ada_w", [DEPTH, D, 6 * D]); din("ada_bT", [128, DEPTH, 48])
        din("n1gT", [128, DEPTH, 8]); din("n2gT", [128, DEPTH, 8]); din("fgT", [128, 8])
        din("ff_w1", [DEPTH, D, 4 * D]); din("ff_w2", [DEPTH, 4 * D, D])
        din("conv_w_in", [2, D, 3 * D]); din("conv_wT", [128, 2, 3, 8]); din("conv_bT", [128, 2, 8])
        din("conv_w_out", [2, D, D])
        din("ssm_w_in", [D, D]); din("ssm_glu_w", [D, D]); din("ssm_w_out", [D, D])
        din("glu_bT", [128, 8]); din("dT", [128, 8])
        din("are_c", [128, 32]); din("aim_c", [128, 32]); din("ldt_c", [128, 32])
        din("Bre_z", [128, 32, 128]); din("Bim_z", [128, 32, 128])
        din("Cre_z", [128, 32, 128]); din("Cim_z", [128, 32, 128])
        din("sg_w_in", [D, 2 * D]); din("sg_w_out", [D, D]); din("sg_w_s", [8, 128, 128])
        din("sg_bias_rep", [128, 8, 128]); din("sg_gain_rep", [128, D])
        din("ident", [128, 128]); din("tril", [128, 128])
        self.out = self.nc.dram_tensor("out", [L, D], F32, kind="ExternalOutput").ap()
        self.xT = self.nc.dram_tensor("xT_scr", [D, L], F32, kind="Internal").ap()
        self.tabC = self.nc.dram_tensor("tabC_scr", [32, 128, TT], F32, kind="Internal").ap()
        self.tabS = self.nc.dram_tensor("tabS_scr", [32, 128, TT], F32, kind="Internal").ap()

    def alloc(self, st):
        nc = self.nc

        def sb(name, shape, dt):
            return st.enter_context(nc.sbuf_tensor("sb_" + name, list(shape), dt))

        self.WA = sb("WA", [128, 32768], BF16)
        self.WB = sb("WB", [128, 32768], BF16)
        self.stage = sb("stage", [128, 3, 1024], F32)
        self.xt = sb("xt", [128, 8, TT], F32)
        self.h = sb("h", [128, 8, TT], BF16)
        self.sq = sb("sq", [128, 2, TT], BF16)
        self.tmpn = sb("tmpn", [128, 2, TT], F32)
        self.rt = sb("rt", [128, 2, TT], F32)
        self.rstd = sb("rstd", [128, TT], F32)
        self.A1 = sb("A1", [128, 16, TT], BF16)
        self.sbf = sb("sbf", [128, 2, 2, TT], BF16)
        self.ident = sb("ident", [128, 128], F32)
        self.tril = sb("tril", [128, 128], F32)
        self.ones_bf = sb("ones_bf", [128, 128], BF16)
        self.ones_f = sb("ones_f", [128, 128], F32)
        self.epsc = sb("epsc", [128, 1], F32)
        self.cT = sb("cT", [128, 8], F32)
        self.cab = sb("cab", [128, 8], BF16)
        self.modT = sb("modT", [128, DEPTH, 48], F32)
        self.adab = sb("adab", [128, DEPTH, 48], F32)
        self.n1g = sb("n1g", [128, DEPTH, 8], F32)
        self.n2g = sb("n2g", [128, DEPTH, 8], F32)
        self.fg = sb("fg", [128, 8], F32)
        self.aT = sb("aT", [128, DEPTH, 2, 8], F32)
        self.cw = sb("cw", [128, 2, 3, 8], F32)
        self.cb = sb("cb", [128, 2, 8], F32)
        self.zc = sb("zc", [128, 8, 2], F32)
        self.glub = sb("glub", [128, 8], F32)
        self.dsk = sb("dsk", [128, 8], F32)
        self.s5c = sb("s5c", [128, 16, 32], F32)
        self.s5i = sb("s5i", [128, 32], I32)
        self.pw = sb("pw", [128, 3, 9, 32], F32)
        self.car = sb("car", [128, 2, 32], F32)
        self.ssv = sb("ssv", [128, 4], F32)
        self.dg = sb("dg", [128, 2, 128], F32)
        self.WAf = self.WA[:, :].bitcast(F32)
        self.A2 = self.WB[:, 16384:24576].bitcast(F32)
        self.A3 = self.WB[:, 24576:32768].bitcast(F32)
        self.KS0 = self.WB[:, 8192:14336].bitcast(F32)
        self.KS1 = self.stage[:, :, :].rearrange("p a b -> p (a b)")
        self.rowtmp = self.rt[0:1, :, :].rearrange("p a b -> p (a b)")
        self.ps = [st.enter_context(nc.psum_tensor("ps%d" % i, [128, 512], F32)) for i in range(8)]

    def xc(self, buf, k):
        if buf == 0:
            return self.xt[:, k, :]
        if k < 6:
            return self.KS1[:, k * 512:(k + 1) * 512]
        return self.sbf[:, :, :, :].rearrange("p a b c -> p (a b c)").bitcast(F32)[:, (k - 6) * 512:(k - 5) * 512]

    def xn(self, buf, k):
        return ("xt%d" % k) if buf == 0 else ("xu%d" % k)

    def xalias(self, buf, k):
        if buf == 0:
            return []
        return ["stage%d" % (k // 2)] if k < 6 else ["sbf0", "sbf1"]

    def next_ps(self):
        i = self.rr
        self.rr = (self.rr + 1) % 6
        return self.ps[i], "ps%d" % i

    def mm(self, out_ps, psname, pairs, reads, flags=None):
        fns = []
        n = len(pairs)
        for i, (a, b) in enumerate(pairs):
            if flags is None:
                o, s0, s1 = out_ps, (i == 0), (i == n - 1)
            else:
                o, s0, s1 = flags[i]
            fns.append(lambda e, o=o, a=a, b=b, s0=s0, s1=s1: e.matmul(o, lhsT=a, rhs=b, start=s0, stop=s1))
        self.P.group("pe", fns, reads=reads, writes=[psname])

    def act(self, out, in_, func, reads, writes, bias=None, scale=1.0):
        if bias is None:
            fn = lambda e: e.activation(out=out, in_=in_, func=func, scale=scale)
        else:
            fn = lambda e: e.activation(out=out, in_=in_, func=func, bias=bias, scale=scale)
        self.P.op("act", fn, reads=reads, writes=writes)

    def stt(self, eng, out, in0, scalar, in1, op0, op1, reads, writes):
        self.P.op(eng, lambda e: e.scalar_tensor_tensor(out=out, in0=in0, scalar=scalar, in1=in1, op0=op0, op1=op1),
                  reads=reads, writes=writes)

    def tt(self, eng, out, in0, in1, op, reads, writes):
        self.P.op(eng, lambda e: e.tensor_tensor(out=out, in0=in0, in1=in1, op=op), reads=reads, writes=writes)

    def ts(self, eng, out, in0, s1, s2, op0, op1, reads, writes):
        if s2 is None:
            fn = lambda e: e.tensor_scalar(out=out, in0=in0, scalar1=s1, scalar2=None, op0=op0)
        else:
            fn = lambda e: e.tensor_scalar(out=out, in0=in0, scalar1=s1, scalar2=s2, op0=op0, op1=op1)
        self.P.op(eng, fn, reads=reads, writes=writes)

    def cp(self, eng, out, in_, reads, writes):
        if eng == "act":
            self.act(out, in_, AF.Copy, reads, writes)
        else:
            self.P.op(eng, lambda e: e.tensor_copy(out=out, in_=in_), reads=reads, writes=writes)

    def memset(self, eng, ap, val, writes):
        self.P.op(eng, lambda e: e.memset(ap, val), writes=writes)

    def load_w(self, name, dst3, src2, K, N, scale=None, xt_stage=True):
        parts = []
        bufs = [(self.stage[:, i, :], ["stage%d" % i]) for i in range(3)]
        if xt_stage:
            for i in range(4):
                bufs.append((self.xt[:, 2 * i:2 * i + 2, :].rearrange("p a b -> p (a b)"), ["xt%d" % (2 * i), "xt%d" % (2 * i + 1)]))
        for k in range(K):
            for c0 in range(0, N, 1024):
                w = min(1024, N - c0)
                self.wl_rr = (getattr(self, "wl_rr", 0) + 1) % len(bufs)
                sb_, snames = bufs[self.wl_rr]
                self.P.dma(sb_[:, 0:w], src2[k * 128:(k + 1) * 128, c0:c0 + w], writes=snames)
                pn = "%s_%d_%d" % (name, k, c0)
                eng = ("act", "dve")[self.cast_rr % 2]
                self.cast_rr += 1
                if scale is None:
                    self.cp(eng, dst3[:, k, c0:c0 + w], sb_[:, 0:w], snames, [pn])
                else:
                    self.act(dst3[:, k, c0:c0 + w], sb_[:, 0:w], AF.Copy, snames, [pn], scale=scale)
                parts.append(pn)
        self.wparts[name] = parts
        return parts

    def wview(self, buf, c0, K, N):
        return buf[:, c0:c0 + K * N].rearrange("p (k n) -> p k n", k=K)

    def prologue(self):
        P = self.P
        d = self.dram
        P.dma(self.ident[:], d["ident"], writes=["ident"])
        P.dma(self.tril[:], d["tril"], writes=["tril"])
        P.dma(self.cT[:], d["cT"], writes=["cT"])
        P.dma(self.adab[:], d["ada_bT"], writes=["adab"])
        P.dma(self.n1g[:], d["n1gT"], writes=["n1g"])
        P.dma(self.n2g[:], d["n2gT"], writes=["n2g"])
        P.dma(self.fg[:], d["fgT"], writes=["fg"])
        P.dma(self.cw[:], d["conv_wT"], writes=["cw"])
        P.dma(self.cb[:], d["conv_bT"], writes=["cb"])
        P.dma(self.glub[:], d["glu_bT"], writes=["glub"])
        P.dma(self.dsk[:], d["dT"], writes=["dsk"])
        self.memset("pool", self.ones_bf[:], 1.0, ["ones_bf"])
        self.memset("pool", self.ones_f[:], 1.0, ["ones_f"])
        self.memset("pool", self.epsc[:], EPS, ["epsc"])
        self.act(self.s5c[:, 0, 0:8], self.cT[:], AF.Sigmoid, ["cT"], ["sgc"])
        self.tt("dve", self.cab[:], self.cT[:], self.s5c[:, 0, 0:8], ALU.mult, ["cT", "sgc"], ["cab"])

    def ada_stage(self, l):
        P = self.P
        wtmp = self.A1
        for cbk in range(6):
            psa, na = self.next_ps()
            psb, nb = self.next_ps()
            for k in range(8):
                b = self.stage_rr
                self.stage_rr = (self.stage_rr + 1) % 3
                sname = "stage%d" % b
                P.dma(self.stage[:, b, :], self.dram["ada_w"][l, k * 128:(k + 1) * 128, cbk * 1024:(cbk + 1) * 1024],
                      writes=[sname])
                wb = (cbk * 8 + k) % 4
                wt = self.A1[:, 2 * wb:2 * wb + 2, :].rearrange("p a b -> p (a b)")
                wn = ["a1_%d" % (2 * wb), "a1_%d" % (2 * wb + 1)]
                eng = ("act", "dve")[self.cast_rr % 2]
                self.cast_rr += 1
                self.cp(eng, wt, self.stage[:, b, :], [sname], wn)
                lhs = self.cab[:, k:k + 1]
                self.P.group("pe", [
                    (lambda e, o=psa[0:1, :], a=lhs, r=wt[:, 0:512], s0=(k == 0), s1=(k == 7):
                     e.matmul(o, lhsT=a, rhs=r, start=s0, stop=s1)),
                    (lambda e, o=psb[0:1, :], a=lhs, r=wt[:, 512:1024], s0=(k == 0), s1=(k == 7):
                     e.matmul(o, lhsT=a, rhs=r, start=s0, stop=s1)),
                ], reads=wn + ["cab"], writes=[na, nb])
            self.act(self.rowtmp[0:1, 0:512], psa[0:1, :], AF.Copy, [na], ["rt0", "rt1"])
            self.act(self.rowtmp[0:1, 512:1024], psb[0:1, :], AF.Copy, [nb], ["rt0", "rt1"])
            pst, nt = self.next_ps()
            fns = []
            for m in range(8):
                fns.append(lambda e, o=pst[:, m:m + 1], a=self.rowtmp[0:1, m * 128:(m + 1) * 128], r=self.ones_f[0:1, 0:1]:
                           e.matmul(o, lhsT=a, rhs=r, start=True, stop=True))
            P.group("pe", fns, reads=["rt0", "rt1", "ones_f"], writes=[nt])
            self.tt("dve", self.modT[:, l, cbk * 8:(cbk + 1) * 8], pst[:, 0:8], self.adab[:, l, cbk * 8:(cbk + 1) * 8],
                    ALU.add, [nt, "adab"], ["modT%d" % l])
        mn = "modT%d" % l
        self.ts("dve", self.aT[:, l, 0, :], self.modT[:, l, 8:16], 1.0, None, ALU.add, None, [mn], ["aT%d" % l])
        self.tt("dve", self.aT[:, l, 0, :], self.aT[:, l, 0, :], self.n1g[:, l, :], ALU.mult, ["aT%d" % l, "n1g"], ["aT%d" % l])
        self.ts("dve", self.aT[:, l, 1, :], self.modT[:, l, 32:40], 1.0, None, ALU.add, None, [mn], ["aT%d" % l])
        self.tt("dve", self.aT[:, l, 1, :], self.aT[:, l, 1, :], self.n2g[:, l, :], ALU.mult, ["aT%d" % l, "n2g"], ["aT%d" % l])

    def load_x_first(self, ti):
        P = self.P
        for tb in range(4):
            b = self.stage_rr
            self.stage_rr = (self.stage_rr + 1) % 3
            sname = "stage%d" % b
            r0 = (ti * 4 + tb) * 128
            P.dma(self.stage[:, b, :], self.dram["x"][r0:r0 + 128, :], writes=[sname])
            for hf in range(2):
                ps, pn = self.next_ps()
                fns = []
                for kk in range(4):
                    k = hf * 4 + kk
                    fns.append(lambda e, o=ps[:, kk * 128:(kk + 1) * 128], i=self.stage[:, b, k * 128:(k + 1) * 128]:
                               e.transpose(o, i, self.ident[:]))
                P.group("pe", fns, reads=[sname, "ident"], writes=[pn])
                eng = "dve" if hf == 0 else "act"
                self.cp(eng, self.xt[:, hf * 4:hf * 4 + 4, tb * 128:(tb + 1) * 128],
                        ps[:, :].rearrange("p (a b) -> p a b", a=4), [pn], ["xt%d" % k for k in range(hf * 4, hf * 4 + 4)])

    def load_x(self, ti, buf=0):
        for k in range(8):
            self.P.dma(self.xc(buf, k), self.xT[k * 128:(k + 1) * 128, ti * TT:(ti + 1) * TT],
                       reads=["xT_%d_%d" % (k, ti)], writes=[self.xn(buf, k)] + self.xalias(buf, k))

    def store_x(self, ti, buf=0):
        for k in range(8):
            self.P.dma(self.xT[k * 128:(k + 1) * 128, ti * TT:(ti + 1) * TT], self.xc(buf, k),
                       reads=[self.xn(buf, k)], writes=["xT_%d_%d" % (k, ti)])

    def sumsq_rstd(self, buf=0):
        P = self.P
        pss, pn = self.ps[6], "ps6"
        for k in range(8):
            b = k % 2
            self.tt("pool", self.sq[:, b, :], self.xc(buf, k), self.xc(buf, k), ALU.mult, [self.xn(buf, k)], ["sq%d" % b])
            P.group("pe", [lambda e, b=b, k=k: e.matmul(pss[:], lhsT=self.ones_bf[:], rhs=self.sq[:, b, :],
                                                         start=(k == 0), stop=(k == 7))],
                    reads=["sq%d" % b, "ones_bf"], writes=[pn])
        self.act(self.rstd[:], pss[:], AF.Sqrt, [pn, "epsc"], ["rstd"], bias=self.epsc[:, 0:1], scale=1.0 / D)
        P.op("dve", lambda e: e.reciprocal(out=self.rstd[:], in_=self.rstd[:]), reads=["rstd"], writes=["rstd"])

    def norm_stage(self, l, which, buf=0):
        self.sumsq_rstd(buf)
        an = "aT%d" % l
        mn = "modT%d" % l
        sh0 = 0 if which == 0 else 24
        for k in range(8):
            b = k % 2
            self.stt("dve", self.tmpn[:, b, :], self.xc(buf, k), self.aT[:, l, which, k:k + 1], self.rstd[:],
                     ALU.mult, ALU.mult, [self.xn(buf, k), an, "rstd"], ["tmpn%d" % b])
            self.act(self.h[:, k, :], self.tmpn[:, b, :], AF.Identity, ["tmpn%d" % b, mn], ["h%d" % k],
                     bias=self.modT[:, l, sh0 + k:sh0 + k + 1])

    def final_stage(self, ti):
        P = self.P
        self.sumsq_rstd()
        for k in range(8):
            self.stt("dve", self.xt[:, k, :], self.xt[:, k, :], self.fg[:, k:k + 1], self.rstd[:],
                     ALU.mult, ALU.mult, ["xt%d" % k, "fg", "rstd"], ["xt%d" % k])
        for tb in range(4):
            b = self.stage_rr
            self.stage_rr = (self.stage_rr + 1) % 3
            sname = "stage%d" % b
            for hf in range(2):
                ps, pn = self.next_ps()
                fns = []
                for kk in range(4):
                    k = hf * 4 + kk
                    fns.append(lambda e, o=ps[:, kk * 128:(kk + 1) * 128], i=self.xt[:, k, tb * 128:(tb + 1) * 128]:
                               e.transpose(o, i, self.ident[:]))
                P.group("pe", fns, reads=["xt%d" % k for k in range(hf * 4, hf * 4 + 4)] + ["ident"], writes=[pn])
                eng = "dve" if hf == 0 else "act"
                self.cp(eng, self.stage[:, b, hf * 512:(hf + 1) * 512], ps[:, :], [pn], [sname])
            r0 = (ti * 4 + tb) * 128
            P.dma(self.out[r0:r0 + 128, :], self.stage[:, b, :], reads=[sname], writes=["out_%d" % r0])

    def resid_update(self, ps, pn, mo, gate_ap, gname, buf=0):
        self.stt("dve", self.xc(buf, mo), ps[:], gate_ap, self.xc(buf, mo), ALU.mult, ALU.add,
                 [pn, gname, self.xn(buf, mo)], [self.xn(buf, mo)])

    def out_proj(self, l, Wout, wname, src, srcnames):
        for mo in range(8):
            ps, pn = self.next_ps()
            self.mm(ps[:], pn, [(Wout[:, k, mo * 128:(mo + 1) * 128], src[:, k, :]) for k in range(8)],
                    reads=self.wparts[wname] + srcnames)
            self.resid_update(ps, pn, mo, self.modT[:, l, 16 + mo:17 + mo], "modT%d" % l)

    def ffn_block(self, l, last):
        self.P.barrier()
        W1 = self.wview(self.WA, 0, 8, 4096)
        W2 = self.wview(self.WB, 0, 32, 1024)
        self.load_w("w1", W1, self.dram["ff_w1"][l], 8, 4096)
        self.load_w("w2", W2, self.dram["ff_w2"][l], 32, 1024)
        r2 = self.A1
        dbl = not last
        hn = ["h%d" % k for k in range(8)]
        if dbl:
            self.load_x(0, 0)
            self.norm_stage(l, 1, 0)
        for ti in range(NT):
            cur = (ti % 2) if dbl else 0
            nxt = 1 - cur
            if dbl:
                if ti + 1 < NT:
                    self.load_x(ti + 1, nxt)
            else:
                self.load_x(ti)
                self.norm_stage(l, 1)
            for half in range(2):
                for jj in range(16):
                    j = half * 16 + jj
                    ps, pn = self.next_ps()
                    self.mm(ps[:], pn, [(W1[:, k, j * 128:(j + 1) * 128], self.h[:, k, :]) for k in range(8)],
                            reads=self.wparts["w1"] + hn)
                    b = jj % 2
                    self.act(self.rt[:, b, :], ps[:], AF.Relu, [pn], ["rt%d" % b])
                    eng = "dve" if jj % 2 == 0 else "pool"
                    self.tt(eng, r2[:, jj, :], self.rt[:, b, :], self.rt[:, b, :], ALU.mult, ["rt%d" % b], ["a1_%d" % jj])
                if dbl and half == 1 and ti + 1 < NT:
                    self.norm_stage(l, 1, nxt)
                for mo in range(8):
                    ps, pn = self.next_ps()
                    self.mm(ps[:], pn, [(W2[:, half * 16 + jj, mo * 128:(mo + 1) * 128], r2[:, jj, :]) for jj in range(16)],
                            reads=self.wparts["w2"] + ["a1_%d" % jj for jj in range(16)])
                    self.resid_update(ps, pn, mo, self.modT[:, l, 40 + mo:41 + mo], "modT%d" % l, cur)
            if last:
                self.final_stage(ti)
            else:
                self.store_x(ti, cur)

    def conv_block(self, l, j, first):
        self.P.barrier()
        Win = self.wview(self.WA, 0, 8, 3072)
        Wout = self.wview(self.WB, 0, 8, 1024)
        self.load_w("cwin", Win, self.dram["conv_w_in"][j], 8, 3072)
        self.load_w("cwout", Wout, self.dram["conv_w_out"][j], 8, 1024)
        A2 = self.A2
        cs = [A2[:, 0:512], A2[:, 512:1024]]
        zb = [A2[:, 1024:1538], A2[:, 1538:2052]]
        acc = [A2[:, 2052:2564], A2[:, 2564:3076]]
        bsb = [self.A3[:, 0:512], self.A3[:, 512:1024]]
        q = self.A1
        self.memset("pool", self.zc[:], 0.0, ["zc%d" % m for m in range(8)])
        for ti in range(NT):
            if first:
                self.load_x_first(ti)
            else:
                self.load_x(ti)
            self.norm_stage(l, 0)
            hn = ["h%d" % k for k in range(8)]
            wr = self.wparts["cwin"] + hn
            for m in range(8):
                b = m % 2
                psB, nB = self.next_ps()
                psC, nC = self.next_ps()
                psX, nX = self.next_ps()
                self.mm(psC[:], nC, [(Win[:, k, 1024 + m * 128:1024 + (m + 1) * 128], self.h[:, k, :]) for k in range(8)], wr)
                self.mm(psX[:], nX, [(Win[:, k, 2048 + m * 128:2048 + (m + 1) * 128], self.h[:, k, :]) for k in range(8)], wr)
                self.mm(psB[:], nB, [(Win[:, k, m * 128:(m + 1) * 128], self.h[:, k, :]) for k in range(8)], wr)
                self.act(cs[b], psC[:], AF.Copy, [nC], ["cs%d" % b])
                self.act(bsb[b], psB[:], AF.Copy, [nB], ["bsb%d" % b])
                self.cp("pool", zb[b][:, 0:2], self.zc[:, m, :], ["zc%d" % m], ["zb%d" % b])
                self.tt("dve", zb[b][:, 2:514], psX[:], cs[b], ALU.mult, [nX, "cs%d" % b], ["zb%d" % b])
                self.act(acc[b], zb[b][:, 2:514], AF.Identity, ["zb%d" % b, "cw", "cb"], ["acc%d" % b],
                         bias=self.cb[:, j, m:m + 1], scale=self.cw[:, j, 2, m:m + 1])
                self.stt("dve", acc[b], zb[b][:, 1:513], self.cw[:, j, 1, m:m + 1], acc[b], ALU.mult, ALU.add,
                         ["zb%d" % b, "cw", "acc%d" % b], ["acc%d" % b])
                self.stt("dve", acc[b], zb[b][:, 0:512], self.cw[:, j, 0, m:m + 1], acc[b], ALU.mult, ALU.add,
                         ["zb%d" % b, "cw", "acc%d" % b], ["acc%d" % b])
                self.cp("pool", self.zc[:, m, :], zb[b][:, 512:514], ["zb%d" % b], ["zc%d" % m])
                self.tt("pool", q[:, m, :], bsb[b], acc[b], ALU.mult, ["bsb%d" % b, "acc%d" % b], ["a1_%d" % m])
            self.out_proj(l, Wout, "cwout", q, ["a1_%d" % m for m in range(8)])
            self.store_x(ti)

    def sg_block(self, l):
        P = self.P
        P.barrier()
        Win = self.wview(self.WA, 0, 8, 2048)
        Wout = self.wview(self.WB, 0, 8, 1024)
        wsT = self.WB[:, 8192:9216].rearrange("p (h t) -> p h t", h=8)
        self.load_w("gwin", Win, self.dram["sg_w_in"], 8, 2048)
        self.load_w("gwout", Wout, self.dram["sg_w_out"], 8, 1024)
        A3 = self.A3
        vsb = A3[:, 0:1024]
        gain = A3[:, 1024:2048]
        bias = A3[:, 2048:3072]
        vsq = A3[:, 3072:3584]
        tmpg = [A3[:, 3584:4096], self.rt[:, 0, :]]
        tmpgn = ["tmpg0", "rt0"]
        vn = self.sbf[:, :, :, :].rearrange("p a b c -> p (a b c)")[:, 0:1024]
        us = self.A2.rearrange("p (k n) -> p k n", k=8)
        gq = self.A1
        P.dma(gain, self.dram["sg_gain_rep"], writes=["gain"])
        P.dma(bias, self.dram["sg_bias_rep"].rearrange("p h t -> p (h t)"), writes=["bias"])
        b = self.stage_rr
        self.stage_rr = (self.stage_rr + 1) % 3
        sname = "stage%d" % b
        stg = self.stage[:, b, :].rearrange("p (h s) -> p h s", h=8)
        P.dma(stg, self.dram["sg_w_s"].rearrange("h t s -> t h s"), writes=[sname])
        self.tt("pool", stg, stg, self.tril[:, :].unsqueeze(1).broadcast_to([128, 8, 128]), ALU.mult,
                [sname, "tril"], [sname])
        for hf in range(2):
            ps, pn = self.next_ps()
            fns = []
            for hh in range(4):
                fns.append(lambda e, o=ps[:, hh * 128:(hh + 1) * 128], i=stg[:, hf * 4 + hh, :]:
                           e.transpose(o, i, self.ident[:]))
            P.group("pe", fns, reads=[sname, "ident"], writes=[pn])
            self.cp("dve", wsT[:, hf * 4:hf * 4 + 4, :], ps[:, :].rearrange("p (a b) -> p a b", a=4), [pn], ["wsT"])
        for ti in range(NT):
            self.load_x(ti)
            self.norm_stage(l, 0)
            hn = ["h%d" % k for k in range(8)]
            wr = self.wparts["gwin"] + hn
            for m in range(8):
                ps, pn = self.next_ps()
                self.mm(ps[:], pn, [(Win[:, k, m * 128:(m + 1) * 128], self.h[:, k, :]) for k in range(8)], wr)
                self.act(us[:, m, :], ps[:], AF.Copy, [pn], ["us%d" % m])
            for n in range(4):
                for hf in range(2):
                    ps, pn = self.next_ps()
                    self.mm(ps[:], pn, [(self.h[:, k, n * 128:(n + 1) * 128], Win[:, k, 1024 + hf * 512:1024 + (hf + 1) * 512])
                                        for k in range(8)], wr)
                    self.act(vsb[:, hf * 512:(hf + 1) * 512], ps[:], AF.Copy, [pn], ["vsb%d" % hf])
                    self.tt("pool", vsq, vsb[:, hf * 512:(hf + 1) * 512], vsb[:, hf * 512:(hf + 1) * 512], ALU.mult,
                            ["vsb%d" % hf], ["vsq"])
                    P.op("dve", lambda e, hf=hf: e.reduce_sum(out=self.ssv[:, hf:hf + 1], in_=vsq, axis=AX.X),
                         reads=["vsq"], writes=["ssv"])
                self.tt("dve", self.ssv[:, 2:3], self.ssv[:, 0:1], self.ssv[:, 1:2], ALU.add, ["ssv"], ["ssv2"])
                self.act(self.ssv[:, 3:4], self.ssv[:, 2:3], AF.Sqrt, ["ssv2", "epsc"], ["rv"], bias=self.epsc[:, 0:1], scale=1.0 / D)
                P.op("dve", lambda e: e.reciprocal(out=self.ssv[:, 3:4], in_=self.ssv[:, 3:4]), reads=["rv"], writes=["rv"])
                self.stt("dve", vn, vsb, self.ssv[:, 3:4], gain, ALU.mult, ALU.mult, ["vsb0", "vsb1", "rv", "gain"], ["vn"])
                for hq in range(2):
                    ps, pn = self.next_ps()
                    pairs, flags = [], []
                    for hh in range(4):
                        hd = hq * 4 + hh
                        pairs.append((vn[:, hd * 128:(hd + 1) * 128], wsT[:, hd, :]))
                        flags.append((ps[:, hh * 128:(hh + 1) * 128], True, True))
                    self.mm(None, pn, pairs, ["vn", "wsT"], flags=flags)
                    tb = tmpg[hq]
                    self.tt("dve", tb, ps[:], bias[:, hq * 512:(hq + 1) * 512], ALU.add, [pn, "bias"], [tmpgn[hq]])
                    self.tt("pool", gq[:, hq * 4:hq * 4 + 4, n * 128:(n + 1) * 128],
                            tb.rearrange("p (a b) -> p a b", a=4), us[:, hq * 4:hq * 4 + 4, n * 128:(n + 1) * 128], ALU.mult,
                            [tmpgn[hq]] + ["us%d" % m for m in range(hq * 4, hq * 4 + 4)],
                            ["a1_%d" % m for m in range(hq * 4, hq * 4 + 4)])
            self.out_proj(l, Wout, "gwout", gq, ["a1_%d" % m for m in range(8)])
            self.store_x(ti)

    def s5_prep(self):
        P = self.P
        d = self.dram
        c = self.s5c
        ARE, AIM, LDT, DT, MAG, ANG, SN, CS, AR, AI, T1, T2, T3, T4, FRE, FIM = [c[:, i, :] for i in range(16)]
        P.dma(ARE, d["are_c"], writes=["s5c"])
        P.dma(AIM, d["aim_c"], writes=["s5c"])
        P.dma(LDT, d["ldt_c"], writes=["s5c"])
        R = ["s5c"]
        self.act(DT, LDT, AF.Exp, R, R)
        self.tt("dve", T1, ARE, DT, ALU.mult, R, R)
        self.act(MAG, T1, AF.Exp, R, R)
        self.tt("dve", ANG, AIM, DT, ALU.mult, R, R)

        def sin_of(dst, src, shift):
            self.ts("dve", T2, src, shift, None, ALU.add, None, R, R)
            self.ts("dve", T3, T2, 1.0 / (2 * PI), None, ALU.mult, None, R, R)
            self.cp("dve", self.s5i[:], T3, R, ["s5i"])
            self.cp("dve", T3, self.s5i[:], ["s5i"], R)
            self.stt("dve", T2, T3, -2 * PI, T2, ALU.mult, ALU.add, R, R)
            self.ts("dve", T3, T2, PI, -2 * PI, ALU.is_gt, ALU.mult, R, R)
            self.tt("dve", T2, T2, T3, ALU.add, R, R)
            self.ts("dve", T3, T2, -PI, 2 * PI, ALU.is_lt, ALU.mult, R, R)
            self.tt("dve", T2, T2, T3, ALU.add, R, R)
            self.ts("dve", T2, T2, -3.141592, 3.141592, ALU.max, ALU.min, R, R)
            self.act(dst, T2, AF.Sin, R, R)

        sin_of(SN, ANG, 0.0)
        sin_of(CS, ANG, PI / 2)
        self.tt("dve", AR, MAG, CS, ALU.mult, R, R)
        self.tt("dve", AI, MAG, SN, ALU.mult, R, R)
        self.tt("dve", T1, ARE, ARE, ALU.mult, R, R)
        self.tt("dve", T2, AIM, AIM, ALU.mult, R, R)
        self.tt("dve", T1, T1, T2, ALU.add, R, R)
        P.op("dve", lambda e: e.reciprocal(out=T4, in_=T1), reads=R, writes=R)
        self.ts("dve", T3, AR, -1.0, None, ALU.add, None, R, R)
        self.tt("dve", T1, T3, ARE, ALU.mult, R, R)
        self.tt("dve", T2, AI, AIM, ALU.mult, R, R)
        self.tt("dve", T1, T1, T2, ALU.add, R, R)
        self.tt("dve", FRE, T1, T4, ALU.mult, R, R)
        self.tt("dve", T1, AI, ARE, ALU.mult, R, R)
        self.tt("dve", T2, T3, AIM, ALU.mult, R, R)
        self.tt("dve", T1, T1, T2, ALU.subtract, R, R)
        self.tt("dve", FIM, T1, T4, ALU.mult, R, R)
        pr, pi_, npi = self.pw[:, 0], self.pw[:, 1], self.pw[:, 2]
        W = ["pw"]
        self.cp("dve", pr[:, 0, :], CS, R, W)
        self.cp("dve", pi_[:, 0, :], SN, R, W)
        for k in range(1, 9):
            self.tt("dve", T1, pr[:, k - 1, :], pr[:, k - 1, :], ALU.mult, W + R, R)
            self.tt("dve", T2, pi_[:, k - 1, :], pi_[:, k - 1, :], ALU.mult, W + R, R)
            self.tt("dve", pr[:, k, :], T1, T2, ALU.subtract, R, W)
            self.tt("dve", T1, pr[:, k - 1, :], pi_[:, k - 1, :], ALU.mult, W + R, R)
            self.ts("dve", pi_[:, k, :], T1, 2.0, None, ALU.mult, None, R, W)
        self.ts("dve", npi, pi_, -1.0, None, ALU.mult, None, W, W)

    def s5_block(self, l):
        P = self.P
        d = self.dram
        P.barrier()
        self.s5_prep()
        Win = self.wview(self.WA, 0, 8, 1024)
        Wglu = self.wview(self.WA, 8192, 8, 1024)
        BW = self.WA[:, 16384:24576].rearrange("p (g r n) -> p g r n", g=32, r=2)
        CW = self.WA[:, 24576:32768].rearrange("p (g r n) -> p g r n", g=32, r=2)
        Wout = self.wview(self.WB, 0, 8, 1024)
        self.load_w("swin", Win, d["ssm_w_in"], 8, 1024)
        self.load_w("sglu", Wglu, d["ssm_glu_w"], 8, 1024)
        self.load_w("swout", Wout, d["ssm_w_out"], 8, 1024)
        c = self.s5c
        FRE, FIM = c[:, 14, :], c[:, 15, :]
        bwparts, cwparts = [], []
        for pc in range(8):
            psr, nr_ = self.next_ps()
            psi, ni_ = self.next_ps()
            for g4 in range(4):
                gp = pc * 4 + g4
                self.ts("dve", self.dg[:, 0, :], self.ident[:], FRE[:, gp:gp + 1], None, ALU.mult, None, ["s5c", "ident"], ["dg0"])
                self.ts("dve", self.dg[:, 1, :], self.ident[:], FIM[:, gp:gp + 1], None, ALU.mult, None, ["s5c", "ident"], ["dg1"])
                self.mm(psr[:, g4 * 128:(g4 + 1) * 128], nr_, [(self.ones_f[:], self.dg[:, 0, :])], ["dg0", "ones_f"],
                        flags=[(psr[:, g4 * 128:(g4 + 1) * 128], True, True)])
                self.mm(psi[:, g4 * 128:(g4 + 1) * 128], ni_, [(self.ones_f[:], self.dg[:, 1, :])], ["dg1", "ones_f"],
                        flags=[(psi[:, g4 * 128:(g4 + 1) * 128], True, True)])
            b1 = self.stage_rr
            b2 = (b1 + 1) % 3
            self.stage_rr = (b1 + 2) % 3
            s1, s2 = "stage%d" % b1, "stage%d" % b2
            bre = self.stage[:, b1, 0:512]
            bim = self.stage[:, b2, 0:512]
            P.dma(bre, d["Bre_z"][:, pc * 4:(pc + 1) * 4, :].rearrange("p g n -> p (g n)"), writes=[s1])
            P.dma(bim, d["Bim_z"][:, pc * 4:(pc + 1) * 4, :].rearrange("p g n -> p (g n)"), writes=[s2])
            t1, t2 = self.tmpn[:, 0, :], self.tmpn[:, 1, :]
            pn = "bw%d" % pc
            self.tt("dve", t1, psr[:], bre, ALU.mult, [nr_, s1], ["tmpn0"])
            self.tt("dve", t2, psi[:], bim, ALU.mult, [ni_, s2], ["tmpn1"])
            self.tt("dve", BW[:, pc * 4:(pc + 1) * 4, 0, :], t1.rearrange("p (g n) -> p g n", g=4),
                    t2.rearrange("p (g n) -> p g n", g=4), ALU.subtract, ["tmpn0", "tmpn1"], [pn + "r"])
            self.tt("dve", t1, psr[:], bim, ALU.mult, [nr_, s2], ["tmpn0"])
            self.tt("dve", t2, psi[:], bre, ALU.mult, [ni_, s1], ["tmpn1"])
            self.tt("dve", BW[:, pc * 4:(pc + 1) * 4, 1, :], t1.rearrange("p (g n) -> p g n", g=4),
                    t2.rearrange("p (g n) -> p g n", g=4), ALU.add, ["tmpn0", "tmpn1"], [pn + "i"])
            bwparts += [pn + "r", pn + "i"]
        for pc in range(8):
            for ri, key, sc in ((0, "Cre_z", 1.0), (1, "Cim_z", -1.0)):
                b1 = self.stage_rr
                self.stage_rr = (b1 + 1) % 3
                s1 = "stage%d" % b1
                P.dma(self.stage[:, b1, 0:512], d[key][:, pc * 4:(pc + 1) * 4, :].rearrange("p g n -> p (g n)"), writes=[s1])
                pn = "cwp%d_%d" % (pc, ri)
                self.act(CW[:, pc * 4:(pc + 1) * 4, ri, :], self.stage[:, b1, 0:512].rearrange("p (g n) -> p g n", g=4),
                         AF.Copy, [s1], [pn], scale=sc)
                cwparts.append(pn)
        K0 = self.KS0
        K1 = self.KS1
        SRI = self.WB[:, 14336:16384].bitcast(F32)
        BR = [K0[:, 0:512], K0[:, 1024:1536]]
        BI = [K0[:, 512:1024], K0[:, 1536:2048]]
        A2f = self.A2
        PRs = [K0[:, 2048:2560], A2f[:, 0:512]]
        PIs = [K0[:, 2560:3072], A2f[:, 512:1024]]
        TC = [K1[:, 0:512], K1[:, 1024:1536], A2f[:, 1024:1536]]
        TS = [K1[:, 512:1024], K1[:, 1536:2048], A2f[:, 1536:2048]]
        RR, RI = K1[:, 2048:2560], K1[:, 2560:3072]
        Dd = self.WB[:, 16384 + 4096:16384 + 4096 + 1024].rearrange("p (q n) -> p q n", q=8)
        sbf3 = self.WB[:, 16384 + 5120:16384 + 5120 + 1024].rearrange("p (r n) -> p r n", r=2)
        sbfs = [self.sbf[:, 0], self.sbf[:, 1], sbf3]
        for q_ in range(8):
            self.ts("dve", Dd[:, q_, :], self.ident[:], self.dsk[:, q_:q_ + 1], None, ALU.mult, None, ["ident", "dsk"], ["Dd"])
        SR, SI = SRI[:, 0:512], SRI[:, 512:1024]
        M1, M2 = self.rt[:, 0, :], self.rt[:, 1, :]
        scan_names = ["br0", "bi0", "br1", "bi1", "pr", "pi", "tc0", "ts0", "tc1", "ts1", "rr", "ri", "sr", "si"]
        self.memset("pool", K1, 0.0, ["tc0", "ts0", "tc1", "ts1", "rr", "ri", "stage0", "stage1", "stage2"])
        carn_all = ["car%d" % g for g in range(32)]
        self.memset("dve", self.car[:], 0.0, carn_all)
        uc, us_, uns = self.pw[:, 0], self.pw[:, 1], self.pw[:, 2]
        for gp in range(32):
            t = gp % 3
            tcn, tsn = "tc%d" % t, "ts%d" % t
            self.memset("dve", TC[t][:, 0:1], 1.0, [tcn])
            self.memset("dve", TS[t][:, 0:1], 0.0, [tsn])
            for k in range(9):
                n = 1 << k
                self.ts("dve", TC[t][:, n:2 * n], TC[t][:, 0:n], uc[:, k, gp:gp + 1], None, ALU.mult, None, [tcn, "pw"], [tcn])
                self.stt("dve", TC[t][:, n:2 * n], TS[t][:, 0:n], uns[:, k, gp:gp + 1], TC[t][:, n:2 * n], ALU.mult, ALU.add,
                         [tcn, tsn, "pw"], [tcn])
                self.ts("dve", TS[t][:, n:2 * n], TS[t][:, 0:n], uc[:, k, gp:gp + 1], None, ALU.mult, None, [tsn, "pw"], [tsn])
                self.stt("dve", TS[t][:, n:2 * n], TC[t][:, 0:n], us_[:, k, gp:gp + 1], TS[t][:, n:2 * n], ALU.mult, ALU.add,
                         [tcn, tsn, "pw"], [tsn])
            P.dma(self.tabC[gp], TC[t], reads=[tcn], writes=["tabC%d" % gp])
            P.dma(self.tabS[gp], TS[t], reads=[tsn], writes=["tabS%d" % gp])
        us = self.A2.rearrange("p (k n) -> p k n", k=8)
        yg = self.A3.rearrange("p (k n) -> p k n", k=8)
        ubf = self.A1[:, 0:8, :]
        ygb = self.A1[:, 8:16, :]
        qv = self.A1[:, 0:8, :]
        c = self.s5c
        MAG, SNt, CSt = c[:, 4, :], c[:, 6, :], c[:, 7, :]
        X1, X2, INITR, INITI = c[:, 10, :], c[:, 11, :], c[:, 12, :], c[:, 13, :]

        for ti in range(NT):
            self.load_x(ti)
            self.norm_stage(l, 0)
            hn = ["h%d" % k for k in range(8)]
            for m in range(8):
                ps, pn = self.next_ps()
                self.mm(ps[:], pn, [(Win[:, k, m * 128:(m + 1) * 128], self.h[:, k, :]) for k in range(8)],
                        self.wparts["swin"] + hn)
                self.cp("act" if m % 2 == 0 else "dve", ubf[:, m, :], ps[:], [pn], ["a1_%d" % m])
            self.tt("dve", X1, CSt, self.car[:, 0, :], ALU.mult, ["s5c"] + carn_all, ["x1"])
            self.tt("dve", X2, SNt, self.car[:, 1, :], ALU.mult, ["s5c"] + carn_all, ["x2"])
            self.tt("dve", INITR, X1, X2, ALU.subtract, ["x1", "x2"], ["initr"])
            self.tt("dve", X1, CSt, self.car[:, 1, :], ALU.mult, ["s5c"] + carn_all, ["x1"])
            self.tt("dve", X2, SNt, self.car[:, 0, :], ALU.mult, ["s5c"] + carn_all, ["x2"])
            self.tt("dve", INITI, X1, X2, ALU.add, ["x1", "x2"], ["initi"])

            def bu(gp, part):
                q = gp // 4
                s = gp % 2
                t = gp % 3
                pb = gp % 2
                s3 = gp % 3
                PRb, PIb = PRs[pb], PIs[pb]
                prn, pin = "pr%d" % pb, "pi%d" % pb
                brn, bin_, tcn, tsn = "br%d" % s, "bi%d" % s, "tc%d" % t, "ts%d" % t
                if part == 0:
                    psr, nr_ = self.next_ps()
                    psi, ni_ = self.next_ps()
                    self.mm(psr[:], nr_, [(BW[:, gp, 0, :], ubf[:, q, :])], bwparts + ["a1_%d" % q])
                    self.mm(psi[:], ni_, [(BW[:, gp, 1, :], ubf[:, q, :])], bwparts + ["a1_%d" % q])
                    self.cp("act", BR[s], psr[:], [nr_], [brn])
                    self.cp("act", BI[s], psi[:], [ni_], [bin_])
                    P.dma(TC[t], self.tabC[gp], reads=["tabC%d" % gp], writes=[tcn])
                    P.dma(TS[t], self.tabS[gp], reads=["tabS%d" % gp], writes=[tsn])
                    self.tt("pool", PRb, BR[s], TC[t], ALU.mult, [brn, tcn], [prn])
                    self.tt("pool", M1, BI[s], TS[t], ALU.mult, [bin_, tsn], ["rt0"])
                    self.tt("pool", PRb, PRb, M1, ALU.add, [prn, "rt0"], [prn])
                    self.tt("pool", PIb, BI[s], TC[t], ALU.mult, [bin_, tcn], [pin])
                    self.tt("pool", M1, BR[s], TS[t], ALU.mult, [brn, tsn], ["rt0"])
                    self.tt("pool", PIb, PIb, M1, ALU.subtract, [pin, "rt0"], [pin])
                    return
                rho = MAG[:, gp:gp + 1].broadcast_to([128, TT])
                P.op("dve", lambda e: e.tensor_tensor_scan(out=RR, data0=rho, data1=PRb, initial=INITR[:, gp:gp + 1],
                                                           op0=ALU.mult, op1=ALU.add),
                     reads=[prn, "s5c", "initr"], writes=["rr"])
                P.op("dve", lambda e: e.tensor_tensor_scan(out=RI, data0=rho, data1=PIb, initial=INITI[:, gp:gp + 1],
                                                           op0=ALU.mult, op1=ALU.add),
                     reads=[pin, "s5c", "initi"], writes=["ri"])
                self.tt("dve", SR, RR, TC[t], ALU.mult, ["rr", tcn], ["sr"])
                self.tt("dve", M2, RI, TS[t], ALU.mult, ["ri", tsn], ["rt1"])
                self.tt("dve", SR, SR, M2, ALU.subtract, ["sr", "rt1"], ["sr"])
                self.tt("dve", SI, RR, TS[t], ALU.mult, ["rr", tsn], ["si"])
                self.tt("dve", M2, RI, TC[t], ALU.mult, ["ri", tcn], ["rt1"])
                self.tt("dve", SI, SI, M2, ALU.add, ["si", "rt1"], ["si"])
                carn = "car%d" % gp
                self.cp("act", sbfs[s3][:, 0, :], SR, ["sr"], ["sbf%d" % s3])
                self.cp("act", sbfs[s3][:, 1, :], SI, ["si"], ["sbf%d" % s3])
                self.cp("act", self.car[:, 0, gp:gp + 1], SR[:, 511:512], ["sr"], [carn])
                self.cp("act", self.car[:, 1, gp:gp + 1], SI[:, 511:512], ["si"], [carn])

            def outp(gp):
                q = gp // 4
                s3 = gp % 3
                psY, nY = self.ps[6 + (q % 2)], "ps%d" % (6 + (q % 2))
                pairs = [(CW[:, gp, 0, :], sbfs[s3][:, 0, :]), (CW[:, gp, 1, :], sbfs[s3][:, 1, :])]
                flags = [(psY[:], False, False), (psY[:], False, gp % 4 == 3)]
                rd = cwparts + ["sbf%d" % s3]
                if gp % 4 == 0:
                    pairs = [(Dd[:, q, :], ubf[:, q, :])] + pairs
                    flags = [(psY[:], True, False)] + flags
                    rd = rd + ["Dd", "a1_%d" % q]
                self.mm(None, nY, pairs, rd, flags=flags)
                if gp % 4 == 3:
                    self.act(yg[:, q, :], psY[:], AF.Gelu_apprx_tanh, [nY], ["yg%d" % q])
                    self.cp("act", ygb[:, q, :], yg[:, q, :], ["yg%d" % q], ["a1_%d" % (8 + q)])

            for i in range(-1, 33):
                if 0 <= i + 1 < 32:
                    bu(i + 1, 0)
                if 0 <= i < 32:
                    bu(i, 1)
                if 0 <= i - 1 < 32:
                    outp(i - 1)
            ygn = ["a1_%d" % (8 + k) for k in range(8)]
            for mo in range(8):
                ps, pn = self.next_ps()
                self.mm(ps[:], pn, [(Wglu[:, k, mo * 128:(mo + 1) * 128], ygb[:, k, :]) for k in range(8)],
                        self.wparts["sglu"] + ygn)
                b = mo % 2
                self.act(self.rt[:, b, :], ps[:], AF.Sigmoid, [pn, "glub"], ["rt%d" % b], bias=self.glub[:, mo:mo + 1])
                self.tt("dve", qv[:, mo, :], yg[:, mo, :], self.rt[:, b, :], ALU.mult, ["yg%d" % mo, "rt%d" % b], ["a1_%d" % mo])
            self.out_proj(l, Wout, "swout", qv, ["a1_%d" % m for m in range(8)])
            self.store_x(ti)

    def build(self):
        nc = self.nc
        self.declare()
        with ExitStack() as st:
            self.alloc(st)
            self.P = Prog(nc, st)
            block = st.enter_context(nc.Block())
            self.prologue()
            for l in range(self.n_layers):
                self.ada_stage(l)
            for l in range(self.n_layers):
                kind = l % 3
                if kind == 0:
                    self.conv_block(l, l // 3, first=(l == 0))
                elif kind == 1:
                    self.s5_block(l)
                else:
                    self.sg_block(l)
                self.ffn_block(l, last=(l == self.n_layers - 1))
            self.P.finish()
            self.P.emit(block)
        return nc


def tT(v):
    v = np.asarray(v, np.float32)
    lead = v.shape[:-1]
    r = v.reshape(lead + (8, 128))
    return np.ascontiguousarray(np.moveaxis(r, -1, 0))


def host_layout(inp, b):
    f32 = np.float32
    m = {}
    m["x"] = np.ascontiguousarray(inp["x"][b], f32)
    m["cT"] = tT(inp["c"][b])
    m["ada_w"] = np.ascontiguousarray(inp["ada_w"], f32)
    ab = np.asarray(inp["ada_b"], f32).reshape(DEPTH, 48, 128)
    m["ada_bT"] = np.ascontiguousarray(ab.transpose(2, 0, 1))
    m["n1gT"] = tT(inp["norm1_g"])
    m["n2gT"] = tT(inp["norm2_g"])
    m["fgT"] = tT(inp["final_g"])
    m["ff_w1"] = np.ascontiguousarray(inp["ff_w1"], f32)
    m["ff_w2"] = np.ascontiguousarray(inp["ff_w2"], f32)
    m["conv_w_in"] = np.ascontiguousarray(inp["conv_w_in"], f32)
    m["conv_wT"] = tT(inp["conv_w"])
    m["conv_bT"] = tT(inp["conv_b"])
    m["conv_w_out"] = np.ascontiguousarray(inp["conv_w_out"], f32)
    m["ssm_w_in"] = np.ascontiguousarray(inp["ssm_w_in"][0], f32)
    m["ssm_glu_w"] = np.ascontiguousarray(inp["ssm_glu_w"][0], f32)
    m["ssm_w_out"] = np.ascontiguousarray(inp["ssm_w_out"][0], f32)
    m["glu_bT"] = tT(inp["ssm_glu_b"][0])
    m["dT"] = tT(inp["ssm_d"][0])
    def compact(a):
        a = np.asarray(a, f32).reshape(32, 2, 64)
        return np.ascontiguousarray(a.transpose(1, 2, 0).reshape(128, 32))
    m["are_c"] = compact(inp["ssm_a_re"][0])
    m["aim_c"] = compact(inp["ssm_a_im"][0])
    m["ldt_c"] = compact(np.broadcast_to(np.asarray(inp["ssm_log_dt"][0], f32)[:, None], (64, 64)))
    bre = np.asarray(inp["ssm_b_re"][0], f32)
    bim = np.asarray(inp["ssm_b_im"][0], f32)
    cre = np.asarray(inp["ssm_c_re"][0], f32)
    cim = np.asarray(inp["ssm_c_im"][0], f32)
    Bre_z = np.zeros((128, 32, 128), f32); Bim_z = np.zeros((128, 32, 128), f32)
    Cre_z = np.zeros((128, 32, 128), f32); Cim_z = np.zeros((128, 32, 128), f32)
    for g in range(64):
        gp, g2, g8 = g // 2, g % 2, g % 8
        Bre_z[g8 * 16:(g8 + 1) * 16, gp, g2 * 64:(g2 + 1) * 64] = bre[g].T
        Bim_z[g8 * 16:(g8 + 1) * 16, gp, g2 * 64:(g2 + 1) * 64] = bim[g].T
        Cre_z[g2 * 64:(g2 + 1) * 64, gp, g8 * 16:(g8 + 1) * 16] = cre[g].T
        Cim_z[g2 * 64:(g2 + 1) * 64, gp, g8 * 16:(g8 + 1) * 16] = cim[g].T
    m["Bre_z"], m["Bim_z"], m["Cre_z"], m["Cim_z"] = Bre_z, Bim_z, Cre_z, Cim_z
    m["sg_w_in"] = np.ascontiguousarray(inp["sg_w_in"][0], f32)
    m["sg_w_out"] = np.ascontiguousarray(inp["sg_w_out"][0], f32)
    m["sg_w_s"] = np.ascontiguousarray(inp["sg_w_s"][0], f32)
    m["sg_bias_rep"] = np.ascontiguousarray(np.broadcast_to(np.asarray(inp["sg_b_s"][0], f32)[None], (128, 8, 128)))
    m["sg_gain_rep"] = np.ascontiguousarray(np.broadcast_to(np.asarray(inp["sg_v_g"][0], f32)[None], (128, D)))
    m["ident"] = np.eye(128, dtype=f32)
    m["tril"] = np.tril(np.ones((128, 128), f32))
    return m


_NC_CACHE = {}


def kernel(_n_layers=DEPTH, **inputs):
    inp = {k: np.asarray(v) for k, v in inputs.items()}
    if _n_layers not in _NC_CACHE:
        _NC_CACHE[_n_layers] = Builder(_n_layers).build()
    nc = _NC_CACHE[_n_layers]
    in_maps = [host_layout(inp, b) for b in range(8)]
    res = run_bass_kernel_spmd(nc, in_maps, core_ids=list(range(8)))
    out = np.stack([np.asarray(r["out"], np.float32) for r in res.results], axis=0)
    return out
```

```python
import math
from contextlib import ExitStack

import numpy as np
import concourse.bass as bass
import concourse.mybir as mybir
from concourse.bass_utils import run_bass_kernel_spmd

F32 = mybir.dt.float32
BF16 = mybir.dt.bfloat16
I32 = mybir.dt.int32
AF = mybir.ActivationFunctionType
ALU = mybir.AluOpType
AX = mybir.AxisListType

D = 1024
L = 4096
DEPTH = 4
TT = 512
NT = L // TT
KC = 8
EPS = 1e-6
ENGS = ("pe", "act", "dve", "pool", "sp")
N_DMA_SEMS = 14
PI = math.pi


class Prog:
    def __init__(self, nc, stack):
        self.nc = nc
        self.stream = {e: [] for e in ENGS}
        self.count = {e: 0 for e in ENGS}
        self.sem = {e: stack.enter_context(nc.semaphore("s_" + e)) for e in ENGS if e != "sp"}
        self.dsem = [stack.enter_context(nc.semaphore("d%d" % i)) for i in range(N_DMA_SEMS)]
        self.dval = [0] * N_DMA_SEMS
        self.dnext = 0
        self.waited = {e: {} for e in ENGS}
        self.last_w = {}
        self.readers = {}

    def _deps(self, eng, reads, writes, same_engine_ok=False):
        evs = []
        for r in reads:
            ev = self.last_w.get(r)
            if ev is not None:
                evs.append((ev, False))
        for w in writes:
            ev = self.last_w.get(w)
            if ev is not None:
                evs.append((ev, False))
            for ev in self.readers.get(w, {}).values():
                evs.append((ev, True))
        for (ev, is_war) in evs:
            owner, key, sem, val = ev
            if owner == eng and (same_engine_ok or is_war):
                continue
            if self.waited[eng].get(key, 0) >= val:
                continue
            self.waited[eng][key] = val
            self.stream[eng].append(("wait", sem, val))

    def _record(self, rkey, ev, reads, writes):
        for w in writes:
            self.last_w[w] = ev
            self.readers[w] = {}
        for r in reads:
            if r in writes:
                continue
            self.readers.setdefault(r, {})[rkey] = ev

    def op(self, eng, fn, reads=(), writes=()):
        self.group(eng, [fn], reads, writes)

    def group(self, eng, fns, reads=(), writes=()):
        psr = [r for r in reads if r.startswith("ps") and r not in writes]
        if psr:
            writes = list(writes) + psr
        self._deps(eng, reads, writes, same_engine_ok=(eng == "pe"))
        for fn in fns[:-1]:
            self.stream[eng].append(("ins", fn, None))
        self.count[eng] += 1
        ev = (eng, eng, self.sem[eng], self.count[eng])
        self.stream[eng].append(("ins", fns[-1], self.sem[eng]))
        self._record(eng, ev, reads, writes)

    def dma(self, out, in_, reads=(), writes=(), eng="sp"):
        i = self.dnext
        self.dnext = (self.dnext + 1) % N_DMA_SEMS
        key = "dma%d" % i
        sem = self.dsem[i]
        if self.dval[i] > 0 and self.waited[eng].get(key, 0) < self.dval[i]:
            self.waited[eng][key] = self.dval[i]
            self.stream[eng].append(("wait", sem, self.dval[i]))
        self._deps(eng, reads, writes)
        self.dval[i] += 16
        ev = ("dmaq", key, sem, self.dval[i])
        self.stream[eng].append(("dma", out, in_, sem))
        self._record("dmaq_" + key, ev, reads, writes)

    def barrier(self):
        for e in ENGS:
            for o in ENGS:
                if o != e and o != "sp" and self.count[o] > 0 and self.waited[e].get(o, 0) < self.count[o]:
                    self.waited[e][o] = self.count[o]
                    self.stream[e].append(("wait", self.sem[o], self.count[o]))
            for i in range(N_DMA_SEMS):
                key = "dma%d" % i
                if self.dval[i] > 0 and self.waited[e].get(key, 0) < self.dval[i]:
                    self.waited[e][key] = self.dval[i]
                    self.stream[e].append(("wait", self.dsem[i], self.dval[i]))

    def finish(self):
        for i in range(N_DMA_SEMS):
            if self.dval[i] > 0:
                self.stream["sp"].append(("wait", self.dsem[i], self.dval[i]))
        for e in ENGS:
            if e != "sp" and self.count[e] > 0:
                self.stream["sp"].append(("wait", self.sem[e], self.count[e]))

    def emit(self, block):
        def run(engh, items):
            for it in items:
                if it[0] == "wait":
                    engh.wait_ge(it[1], it[2])
                elif it[0] == "ins":
                    ins = it[1](engh)
                    if it[2] is not None:
                        ins.then_inc(it[2], 1)
                else:
                    engh.dma_start(out=it[1], in_=it[2]).then_inc(it[3], 16)

        @block.sync
        def _(e):
            run(e, self.stream["sp"])

        @block.tensor
        def _(e):
            run(e, self.stream["pe"])

        @block.scalar
        def _(e):
            run(e, self.stream["act"])

        @block.vector
        def _(e):
            run(e, self.stream["dve"])

        @block.gpsimd
        def _(e):
            run(e, self.stream["pool"])


class Builder:
    def __init__(self, n_layers=DEPTH):
        self.n_layers = n_layers
        self.nc = bass.Bass("TRN2", target_bir_lowering=False)
        self.dram = {}
        self.rr = 0
        self.cast_rr = 0
        self.stage_rr = 0
        self.wparts = {}

    def din(self, name, shape):
        self.dram[name] = self.nc.dram_tensor(name, list(shape), F32, kind="ExternalInput").ap()

    def declare(self):
        din = self.din
        din("x", [L, D]); din("cT", [128, 8]); din("ada_w", [DEPTH, D, 6 * D]); din("ada_bT", [128, DEPTH, 48])
        din("n1gT", [128, DEPTH, 8]); din("n2gT", [128, DEPTH, 8]); din("fgT", [128, 8])
        din("ff_w1", [DEPTH, D, 4 * D]); din("ff_w2", [DEPTH, 4 * D, D])
        din("conv_w_in", [2, D, 3 * D]); din("conv_wT", [128, 2, 3, 8]); din("conv_bT", [128, 2, 8])
        din("conv_w_out", [2, D, D])
        din("ssm_w_in", [D, D]); din("ssm_glu_w", [D, D]); din("ssm_w_out", [D, D])
        din("glu_bT", [128, 8]); din("dT", [128, 8])
        din("are_c", [128, 32]); din("aim_c", [128, 32]); din("ldt_c", [128, 32])
        din("Bre_z", [128, 32, 128]); din("Bim_z", [128, 32, 128])
        din("Cre_z", [128, 32, 128]); din("Cim_z", [128, 32, 128])
        din("sg_w_in", [D, 2 * D]); din("sg_w_out", [D, D]); din("sg_w_s", [8, 128, 128])
        din("sg_bias_rep", [128, 8, 128]); din("sg_gain_rep", [128, D])
        din("ident", [128, 128]); din("tril", [128, 128])
        self.out = self.nc.dram_tensor("out", [L, D], F32, kind="ExternalOutput").ap()
        self.xT = self.nc.dram_tensor("xT_scr", [D, L], F32, kind="Internal").ap()
        self.tabC = self.nc.dram_tensor("tabC_scr", [32, 128, TT], F32, kind="Internal").ap()
        self.tabS = self.nc.dram_tensor("tabS_scr", [32, 128, TT], F32, kind="Internal").ap()

    def alloc(self, st):
        nc = self.nc

        def sb(name, shape, dt):
            return st.enter_context(nc.sbuf_tensor("sb_" + name, list(shape), dt))

        self.WA = sb("WA", [128, 32768], BF16)
        self.WB = sb("WB", [128, 32768], BF16)
        self.stage = sb("stage", [128, 3, 1024], F32)
        self.xt = sb("xt", [128, 8, TT], F32)
        self.h = sb("h", [128, 8, TT], BF16)
        self.sq = sb("sq", [128, 2, TT], BF16)
        self.tmpn = sb("tmpn", [128, 2, TT], F32)
        self.rt = sb("rt", [128, 2, TT], F32)
        self.rstd = sb("rstd", [128, TT], F32)
        self.A1 = sb("A1", [128, 16, TT], BF16)
        self.sbf = sb("sbf", [128, 2, 2, TT], BF16)
        self.ident = sb("ident", [128, 128], F32)
        self.tril = sb("tril", [128, 128], F32)
        self.ones_bf = sb("ones_bf", [128, 128], BF16)
        self.ones_f = sb("ones_f", [128, 128], F32)
        self.epsc = sb("epsc", [128, 1], F32)
        self.cT = sb("cT", [128, 8], F32)
        self.cab = sb("cab", [128, 8], BF16)
        self.modT = sb("modT", [128, DEPTH, 48], F32)
        self.adab = sb("adab", [128, DEPTH, 48], F32)
        self.n1g = sb("n1g", [128, DEPTH, 8], F32)
        self.n2g = sb("n2g", [128, DEPTH, 8], F32)
        self.fg = sb("fg", [128, 8], F32)
        self.aT = sb("aT", [128, DEPTH, 2, 8], F32)
        self.cw = sb("cw", [128, 2, 3, 8], F32)
        self.cb = sb("cb", [128, 2, 8], F32)
        self.zc = sb("zc", [128, 8, 2], F32)
        self.glub = sb("glub", [128, 8], F32)
        self.dsk = sb("dsk", [128, 8], F32)
        self.s5c = sb("s5c", [128, 16, 32], F32)
        self.s5i = sb("s5i", [128, 32], I32)
        self.pw = sb("pw", [128, 3, 9, 32], F32)
        self.car = sb("car", [128, 2, 32], F32)
        self.ssv = sb("ssv", [128, 4], F32)
        self.dg = sb("dg", [128, 2, 128], F32)
        self.WAf = self.WA[:, :].bitcast(F32)
        self.A2 = self.WB[:, 16384:24576].bitcast(F32)
        self.A3 = self.WB[:, 24576:32768].bitcast(F32)
        self.KS0 = self.WB[:, 8192:14336].bitcast(F32)
        self.KS1 = self.stage[:, :, :].rearrange("p a b -> p (a b)")
        self.rowtmp = self.rt[0:1, :, :].rearrange("p a b -> p (a b)")
        self.ps = [st.enter_context(nc.psum_tensor("ps%d" % i, [128, 512], F32)) for i in range(8)]

    def xc(self, buf, k):
        if buf == 0:
            return self.xt[:, k, :]
        if k < 6:
            return self.KS1[:, k * 512:(k + 1) * 512]
        return self.sbf[:, :, :, :].rearrange("p a b c -> p (a b c)").bitcast(F32)[:, (k - 6) * 512:(k - 5) * 512]

    def xn(self, buf, k):
        return ("xt%d" % k) if buf == 0 else ("xu%d" % k)

    def xalias(self, buf, k):
        if buf == 0:
            return []
        return ["stage%d" % (k // 2)] if k < 6 else ["sbf0", "sbf1"]

    def next_ps(self):
        i = self.rr
        self.rr = (self.rr + 1) % 6
        return self.ps[i], "ps%d" % i

    def mm(self, out_ps, psname, pairs, reads, flags=None):
        fns = []
        n = len(pairs)
        for i, (a, b) in enumerate(pairs):
            if flags is None:
                o, s0, s1 = out_ps, (i == 0), (i == n - 1)
            else:
                o, s0, s1 = flags[i]
            fns.append(lambda e, o=o, a=a, b=b, s0=s0, s1=s1: e.matmul(o, lhsT=a, rhs=b, start=s0, stop=s1))
        self.P.group("pe", fns, reads=reads, writes=[psname])

    def act(self, out, in_, func, reads, writes, bias=None, scale=1.0):
        if bias is None:
            fn = lambda e: e.activation(out=out, in_=in_, func=func, scale=scale)
        else:
            fn = lambda e: e.activation(out=out, in_=in_, func=func, bias=bias, scale=scale)
        self.P.op("act", fn, reads=reads, writes=writes)

    def stt(self, eng, out, in0, scalar, in1, op0, op1, reads, writes):
        self.P.op(eng, lambda e: e.scalar_tensor_tensor(out=out, in0=in0, scalar=scalar, in1=in1, op0=op0, op1=op1),
                  reads=reads, writes=writes)

    def tt(self, eng, out, in0, in1, op, reads, writes):
        self.P.op(eng, lambda e: e.tensor_tensor(out=out, in0=in0, in1=in1, op=op), reads=reads, writes=writes)

    def ts(self, eng, out, in0, s1, s2, op0, op1, reads, writes):
        if s2 is None:
            fn = lambda e: e.tensor_scalar(out=out, in0=in0, scalar1=s1, scalar2=None, op0=op0)
        else:
            fn = lambda e: e.tensor_scalar(out=out, in0=in0, scalar1=s1, scalar2=s2, op0=op0, op1=op1)
        self.P.op(eng, fn, reads=reads, writes=writes)

    def cp(self, eng, out, in_, reads, writes):
        if eng == "act":
            self.act(out, in_, AF.Copy, reads, writes)
        else:
            self.P.op(eng, lambda e: e.tensor_copy(out=out, in_=in_), reads=reads, writes=writes)

    def memset(self, eng, ap, val, writes):
        self.P.op(eng, lambda e: e.memset(ap, val), writes=writes)

    def load_w(self, name, dst3, src2, K, N, scale=None, xt_stage=True):
        parts = []
        bufs = [(self.stage[:, i, :], ["stage%d" % i]) for i in range(3)]
        if xt_stage:
            for i in range(4):
                bufs.append((self.xt[:, 2 * i:2 * i + 2, :].rearrange("p a b -> p (a b)"), ["xt%d" % (2 * i), "xt%d" % (2 * i + 1)]))
        for k in range(K):
            for c0 in range(0, N, 1024):
                w = min(1024, N - c0)
                self.wl_rr = (getattr(self, "wl_rr", 0) + 1) % len(bufs)
                sb_, snames = bufs[self.wl_rr]
                self.P.dma(sb_[:, 0:w], src2[k * 128:(k + 1) * 128, c0:c0 + w], writes=snames)
                pn = "%s_%d_%d" % (name, k, c0)
                eng = ("act", "dve")[self.cast_rr % 2]
                self.cast_rr += 1
                if scale is None:
                    self.cp(eng, dst3[:, k, c0:c0 + w], sb_[:, 0:w], snames, [pn])
                else:
                    self.act(dst3[:, k, c0:c0 + w], sb_[:, 0:w], AF.Copy, snames, [pn], scale=scale)
                parts.append(pn)
        self.wparts[name] = parts
        return parts

    def wview(self, buf, c0, K, N):
        return buf[:, c0:c0 + K * N].rearrange("p (k n) -> p k n", k=K)

    def prologue(self):
        P = self.P
        d = self.dram
        P.dma(self.ident[:], d["ident"], writes=["ident"])
        P.dma(self.tril[:], d["tril"], writes=["tril"])
        P.dma(self.cT[:], d["cT"], writes=["cT"])
        P.dma(self.adab[:], d["ada_bT"], writes=["adab"])
        P.dma(self.n1g[:], d["n1gT"], writes=["n1g"])
        P.dma(self.n2g[:], d["n2gT"], writes=["n2g"])
        P.dma(self.fg[:], d["fgT"], writes=["fg"])
        P.dma(self.cw[:], d["conv_wT"], writes=["cw"])
        P.dma(self.cb[:], d["conv_bT"], writes=["cb"])
        P.dma(self.glub[:], d["glu_bT"], writes=["glub"])
        P.dma(self.dsk[:], d["dT"], writes=["dsk"])
        self.memset("pool", self.ones_bf[:], 1.0, ["ones_bf"])
        self.memset("pool", self.ones_f[:], 1.0, ["ones_f"])
        self.memset("pool", self.epsc[:], EPS, ["epsc"])
        self.act(self.s5c[:, 0, 0:8], self.cT[:], AF.Sigmoid, ["cT"], ["sgc"])
        self.tt("dve", self.cab[:], self.cT[:], self.s5c[:, 0, 0:8], ALU.mult, ["cT", "sgc"], ["cab"])

    def ada_stage(self, l):
        P = self.P
        wtmp = self.A1
        for cbk in range(6):
            psa, na = self.next_ps()
            psb, nb = self.next_ps()
            for k in range(8):
                b = self.stage_rr
                self.stage_rr = (self.stage_rr + 1) % 3
                sname = "stage%d" % b
                P.dma(self.stage[:, b, :], self.dram["ada_w"][l, k * 128:(k + 1) * 128, cbk * 1024:(cbk + 1) * 1024],
                      writes=[sname])
                wb = (cbk * 8 + k) % 4
                wt = self.A1[:, 2 * wb:2 * wb + 2, :].rearrange("p a b -> p (a b)")
                wn = ["a1_%d" % (2 * wb), "a1_%d" % (2 * wb + 1)]
                eng = ("act", "dve")[self.cast_rr % 2]
                self.cast_rr += 1
                self.cp(eng, wt, self.stage[:, b, :], [sname], wn)
                lhs = self.cab[:, k:k + 1]
                self.P.group("pe", [
                    (lambda e, o=psa[0:1, :], a=lhs, r=wt[:, 0:512], s0=(k == 0), s1=(k == 7):
                     e.matmul(o, lhsT=a, rhs=r, start=s0, stop=s1)),
                    (lambda e, o=psb[0:1, :], a=lhs, r=wt[:, 512:1024], s0=(k == 0), s1=(k == 7):
                     e.matmul(o, lhsT=a, rhs=r, start=s0, stop=s1)),
                ], reads=wn + ["cab"], writes=[na, nb])
            self.act(self.rowtmp[0:1, 0:512], psa[0:1, :], AF.Copy, [na], ["rt0", "rt1"])
            self.act(self.rowtmp[0:1, 512:1024], psb[0:1, :], AF.Copy, [nb], ["rt0", "rt1"])
            pst, nt = self.next_ps()
            fns = []
            for m in range(8):
                fns.append(lambda e, o=pst[:, m:m + 1], a=self.rowtmp[0:1, m * 128:(m + 1) * 128], r=self.ones_f[0:1, 0:1]:
                           e.matmul(o, lhsT=a, rhs=r, start=True, stop=True))
            P.group("pe", fns, reads=["rt0", "rt1", "ones_f"], writes=[nt])
            self.tt("dve", self.modT[:, l, cbk * 8:(cbk + 1) * 8], pst[:, 0:8], self.adab[:, l, cbk * 8:(cbk + 1) * 8],
                    ALU.add, [nt, "adab"], ["modT%d" % l])
        mn = "modT%d" % l
        self.ts("dve", self.aT[:, l, 0, :], self.modT[:, l, 8:16], 1.0, None, ALU.add, None, [mn], ["aT%d" % l])
        self.tt("dve", self.aT[:, l, 0, :], self.aT[:, l, 0, :], self.n1g[:, l, :], ALU.mult, ["aT%d" % l, "n1g"], ["aT%d" % l])
        self.ts("dve", self.aT[:, l, 1, :], self.modT[:, l, 32:40], 1.0, None, ALU.add, None, [mn], ["aT%d" % l])
        self.tt("dve", self.aT[:, l, 1, :], self.aT[:, l, 1, :], self.n2g[:, l, :], ALU.mult, ["aT%d" % l, "n2g"], ["aT%d" % l])

    def load_x_first(self, ti):
        P = self.P
        for tb in range(4):
            b = self.stage_rr
            self.stage_rr = (self.stage_rr + 1) % 3
            sname = "stage%d" % b
            r0 = (ti * 4 + tb) * 128
            P.dma(self.stage[:, b, :], self.dram["x"][r0:r0 + 128, :], writes=[sname])
            for hf in range(2):
                ps, pn = self.next_ps()
                fns = []
                for kk in range(4):
                    k = hf * 4 + kk
                    fns.append(lambda e, o=ps[:, kk * 128:(kk + 1) * 128], i=self.stage[:, b, k * 128:(k + 1) * 128]:
                               e.transpose(o, i, self.ident[:]))
                P.group("pe", fns, reads=[sname, "ident"], writes=[pn])
                eng = "dve" if hf == 0 else "act"
                self.cp(eng, self.xt[:, hf * 4:hf * 4 + 4, tb * 128:(tb + 1) * 128],
                        ps[:, :].rearrange("p (a b) -> p a b", a=4), [pn], ["xt%d" % k for k in range(hf * 4, hf * 4 + 4)])

    def load_x(self, ti, buf=0):
        for k in range(8):
            self.P.dma(self.xc(buf, k), self.xT[k * 128:(k + 1) * 128, ti * TT:(ti + 1) * TT],
                       reads=["xT_%d_%d" % (k, ti)], writes=[self.xn(buf, k)] + self.xalias(buf, k))

    def store_x(self, ti, buf=0):
        for k in range(8):
            self.P.dma(self.xT[k * 128:(k + 1) * 128, ti * TT:(ti + 1) * TT], self.xc(buf, k),
                       reads=[self.xn(buf, k)], writes=["xT_%d_%d" % (k, ti)])

    def sumsq_rstd(self, buf=0):
        P = self.P
        pss, pn = self.ps[6], "ps6"
        for k in range(8):
            b = k % 2
            self.tt("pool", self.sq[:, b, :], self.xc(buf, k), self.xc(buf, k), ALU.mult, [self.xn(buf, k)], ["sq%d" % b])
            P.group("pe", [lambda e, b=b, k=k: e.matmul(pss[:], lhsT=self.ones_bf[:], rhs=self.sq[:, b, :],
                                                         start=(k == 0), stop=(k == 7))],
                    reads=["sq%d" % b, "ones_bf"], writes=[pn])
        self.act(self.rstd[:], pss[:], AF.Sqrt, [pn, "epsc"], ["rstd"], bias=self.epsc[:, 0:1], scale=1.0 / D)
        P.op("dve", lambda e: e.reciprocal(out=self.rstd[:], in_=self.rstd[:]), reads=["rstd"], writes=["rstd"])

    def norm_stage(self, l, which, buf=0):
        self.sumsq_rstd(buf)
        an = "aT%d" % l
        mn = "modT%d" % l
        sh0 = 0 if which == 0 else 24
        for k in range(8):
            b = k % 2
            self.stt("dve", self.tmpn[:, b, :], self.xc(buf, k), self.aT[:, l, which, k:k + 1], self.rstd[:],
                     ALU.mult, ALU.mult, [self.xn(buf, k), an, "rstd"], ["tmpn%d" % b])
            self.act(self.h[:, k, :], self.tmpn[:, b, :], AF.Identity, ["tmpn%d" % b, mn], ["h%d" % k],
                     bias=self.modT[:, l, sh0 + k:sh0 + k + 1])

    def final_stage(self, ti, buf=0):
        P = self.P
        self.sumsq_rstd(buf)
        for k in range(8):
            self.stt("dve", self.xc(buf, k), self.xc(buf, k), self.fg[:, k:k + 1], self.rstd[:],
                     ALU.mult, ALU.mult, [self.xn(buf, k), "fg", "rstd"], [self.xn(buf, k)])
        for tb in range(4):
            sl = 4 * (tb % 2)
            ot = self.A1[:, sl:sl + 4, :].rearrange("p a b -> p (a b)").bitcast(F32)
            otn = ["a1_%d" % i for i in range(sl, sl + 4)]
            for hf in range(2):
                ps, pn = self.next_ps()
                fns = []
                for kk in range(4):
                    k = hf * 4 + kk
                    fns.append(lambda e, o=ps[:, kk * 128:(kk + 1) * 128], i=self.xc(buf, k)[:, tb * 128:(tb + 1) * 128]:
                               e.transpose(o, i, self.ident[:]))
                P.group("pe", fns, reads=[self.xn(buf, k) for k in range(hf * 4, hf * 4 + 4)] + ["ident"], writes=[pn])
                eng = "dve" if hf == 0 else "act"
                self.cp(eng, ot[:, hf * 512:(hf + 1) * 512], ps[:, :], [pn], otn)
            r0 = (ti * 4 + tb) * 128
            P.dma(self.out[r0:r0 + 128, :], ot, reads=otn, writes=["out_%d" % r0])

    def resid_update(self, ps, pn, mo, gate_ap, gname, buf=0):
        self.stt("dve", self.xc(buf, mo), ps[:], gate_ap, self.xc(buf, mo), ALU.mult, ALU.add,
                 [pn, gname, self.xn(buf, mo)], [self.xn(buf, mo)])

    def out_proj(self, l, Wout, wname, src, srcnames, buf=0):
        for mo in range(8):
            ps, pn = self.next_ps()
            self.mm(ps[:], pn, [(Wout[:, k, mo * 128:(mo + 1) * 128], src[:, k, :]) for k in range(8)],
                    reads=self.wparts[wname] + srcnames)
            self.resid_update(ps, pn, mo, self.modT[:, l, 16 + mo:17 + mo], "modT%d" % l, buf)

    def ffn_block(self, l, last):
        self.P.barrier()
        W1 = self.wview(self.WA, 0, 8, 4096)
        W2 = self.wview(self.WB, 0, 32, 1024)
        self.load_w("w1", W1, self.dram["ff_w1"][l], 8, 4096)
        self.load_w("w2", W2, self.dram["ff_w2"][l], 32, 1024)
        r2 = self.A1
        dbl = True
        hn = ["h%d" % k for k in range(8)]
        if dbl:
            self.load_x(0, 0)
            self.norm_stage(l, 1, 0)
        for ti in range(NT):
            cur = (ti % 2) if dbl else 0
            nxt = 1 - cur
            if dbl:
                if ti + 1 < NT:
                    self.load_x(ti + 1, nxt)
            else:
                self.load_x(ti)
                self.norm_stage(l, 1)
            for half in range(2):
                for jj in range(16):
                    j = half * 16 + jj
                    ps, pn = self.next_ps()
                    self.mm(ps[:], pn, [(W1[:, k, j * 128:(j + 1) * 128], self.h[:, k, :]) for k in range(8)],
                            reads=self.wparts["w1"] + hn)
                    b = jj % 2
                    self.act(self.rt[:, b, :], ps[:], AF.Relu, [pn], ["rt%d" % b])
                    eng = "dve" if jj % 2 == 0 else "pool"
                    self.tt(eng, r2[:, jj, :], self.rt[:, b, :], self.rt[:, b, :], ALU.mult, ["rt%d" % b], ["a1_%d" % jj])
                if dbl and half == 1 and ti + 1 < NT:
                    self.norm_stage(l, 1, nxt)
                for mo in range(8):
                    ps, pn = self.next_ps()
                    self.mm(ps[:], pn, [(W2[:, half * 16 + jj, mo * 128:(mo + 1) * 128], r2[:, jj, :]) for jj in range(16)],
                            reads=self.wparts["w2"] + ["a1_%d" % jj for jj in range(16)])
                    self.resid_update(ps, pn, mo, self.modT[:, l, 40 + mo:41 + mo], "modT%d" % l, cur)
            if last:
                self.final_stage(ti, cur)
            else:
                self.store_x(ti, cur)

    def conv_block(self, l, j, first):
        self.P.barrier()
        Win = self.wview(self.WA, 0, 8, 3072)
        Wout = self.wview(self.WB, 0, 8, 1024)
        self.load_w("cwin", Win, self.dram["conv_w_in"][j], 8, 3072)
        self.load_w("cwout", Wout, self.dram["conv_w_out"][j], 8, 1024)
        A2 = self.A2
        cs = [A2[:, 0:512], A2[:, 512:1024]]
        zb = [A2[:, 1024:1538], A2[:, 1538:2052]]
        acc = [A2[:, 2052:2564], A2[:, 2564:3076]]
        bsb = [self.A3[:, 0:512], self.A3[:, 512:1024]]
        q = self.A1
        self.memset("pool", self.zc[:], 0.0, ["zc%d" % m for m in range(8)])
        dbl = not first
        if dbl:
            self.load_x(0, 0)
            self.norm_stage(l, 0, 0)
        for ti in range(NT):
            cur = (ti % 2) if dbl else 0
            nxt = 1 - cur
            if dbl:
                if ti + 1 < NT:
                    self.load_x(ti + 1, nxt)
            else:
                self.load_x_first(ti)
                self.norm_stage(l, 0)
            hn = ["h%d" % k for k in range(8)]
            wr = self.wparts["cwin"] + hn
            for m in range(8):
                b = m % 2
                psB, nB = self.next_ps()
                psC, nC = self.next_ps()
                psX, nX = self.next_ps()
                self.mm(psC[:], nC, [(Win[:, k, 1024 + m * 128:1024 + (m + 1) * 128], self.h[:, k, :]) for k in range(8)], wr)
                self.mm(psX[:], nX, [(Win[:, k, 2048 + m * 128:2048 + (m + 1) * 128], self.h[:, k, :]) for k in range(8)], wr)
                self.mm(psB[:], nB, [(Win[:, k, m * 128:(m + 1) * 128], self.h[:, k, :]) for k in range(8)], wr)
                self.act(cs[b], psC[:], AF.Copy, [nC], ["cs%d" % b])
                self.act(bsb[b], psB[:], AF.Copy, [nB], ["bsb%d" % b])
                self.cp("pool", zb[b][:, 0:2], self.zc[:, m, :], ["zc%d" % m], ["zb%d" % b])
                self.tt("dve", zb[b][:, 2:514], psX[:], cs[b], ALU.mult, [nX, "cs%d" % b], ["zb%d" % b])
                self.act(acc[b], zb[b][:, 2:514], AF.Identity, ["zb%d" % b, "cw", "cb"], ["acc%d" % b],
                         bias=self.cb[:, j, m:m + 1], scale=self.cw[:, j, 2, m:m + 1])
                self.stt("dve", acc[b], zb[b][:, 1:513], self.cw[:, j, 1, m:m + 1], acc[b], ALU.mult, ALU.add,
                         ["zb%d" % b, "cw", "acc%d" % b], ["acc%d" % b])
                self.stt("dve", acc[b], zb[b][:, 0:512], self.cw[:, j, 0, m:m + 1], acc[b], ALU.mult, ALU.add,
                         ["zb%d" % b, "cw", "acc%d" % b], ["acc%d" % b])
                self.cp("pool", self.zc[:, m, :], zb[b][:, 512:514], ["zb%d" % b], ["zc%d" % m])
                self.tt("pool", q[:, m, :], bsb[b], acc[b], ALU.mult, ["bsb%d" % b, "acc%d" % b], ["a1_%d" % m])
            if dbl and ti + 1 < NT:
                self.norm_stage(l, 0, nxt)
            self.out_proj(l, Wout, "cwout", q, ["a1_%d" % m for m in range(8)], cur)
            self.store_x(ti, cur)

    def sg_block(self, l):
        P = self.P
        P.barrier()
        Win = self.wview(self.WA, 0, 8, 2048)
        Wout = self.wview(self.WB, 0, 8, 1024)
        wsT = self.WB[:, 8192:9216].rearrange("p (h t) -> p h t", h=8)
        self.load_w("gwin", Win, self.dram["sg_w_in"], 8, 2048)
        self.load_w("gwout", Wout, self.dram["sg_w_out"], 8, 1024)
        A3 = self.A3
        vsb = A3[:, 0:1024]
        gain = A3[:, 1024:2048]
        bias = A3[:, 2048:3072]
        vsq = A3[:, 3072:3584]
        tmpg = [A3[:, 3584:4096], self.rt[:, 0, :]]
        tmpgn = ["tmpg0", "rt0"]
        vn = self.sbf[:, :, :, :].rearrange("p a b c -> p (a b c)")[:, 0:1024]
        us = self.A2.rearrange("p (k n) -> p k n", k=8)
        gq = self.A1
        P.dma(gain, self.dram["sg_gain_rep"], writes=["gain"])
        P.dma(bias, self.dram["sg_bias_rep"].rearrange("p h t -> p (h t)"), writes=["bias"])
        b = self.stage_rr
        self.stage_rr = (self.stage_rr + 1) % 3
        sname = "stage%d" % b
        stg = self.stage[:, b, :].rearrange("p (h s) -> p h s", h=8)
        P.dma(stg, self.dram["sg_w_s"].rearrange("h t s -> t h s"), writes=[sname])
        self.tt("pool", stg, stg, self.tril[:, :].unsqueeze(1).broadcast_to([128, 8, 128]), ALU.mult,
                [sname, "tril"], [sname])
        for hf in range(2):
            ps, pn = self.next_ps()
            fns = []
            for hh in range(4):
                fns.append(lambda e, o=ps[:, hh * 128:(hh + 1) * 128], i=stg[:, hf * 4 + hh, :]:
                           e.transpose(o, i, self.ident[:]))
            P.group("pe", fns, reads=[sname, "ident"], writes=[pn])
            self.cp("dve", wsT[:, hf * 4:hf * 4 + 4, :], ps[:, :].rearrange("p (a b) -> p a b", a=4), [pn], ["wsT"])
        for ti in range(NT):
            self.load_x(ti)
            self.norm_stage(l, 0)
            hn = ["h%d" % k for k in range(8)]
            wr = self.wparts["gwin"] + hn
            for m in range(8):
                ps, pn = self.next_ps()
                self.mm(ps[:], pn, [(Win[:, k, m * 128:(m + 1) * 128], self.h[:, k, :]) for k in range(8)], wr)
                self.act(us[:, m, :], ps[:], AF.Copy, [pn], ["us%d" % m])
            for n in range(4):
                for hf in range(2):
                    ps, pn = self.next_ps()
                    self.mm(ps[:], pn, [(self.h[:, k, n * 128:(n + 1) * 128], Win[:, k, 1024 + hf * 512:1024 + (hf + 1) * 512])
                                        for k in range(8)], wr)
                    self.act(vsb[:, hf * 512:(hf + 1) * 512], ps[:], AF.Copy, [pn], ["vsb%d" % hf])
                    self.tt("pool", vsq, vsb[:, hf * 512:(hf + 1) * 512], vsb[:, hf * 512:(hf + 1) * 512], ALU.mult,
                            ["vsb%d" % hf], ["vsq"])
                    P.op("dve", lambda e, hf=hf: e.reduce_sum(out=self.ssv[:, hf:hf + 1], in_=vsq, axis=AX.X),
                         reads=["vsq"], writes=["ssv"])
                self.tt("dve", self.ssv[:, 2:3], self.ssv[:, 0:1], self.ssv[:, 1:2], ALU.add, ["ssv"], ["ssv2"])
                self.act(self.ssv[:, 3:4], self.ssv[:, 2:3], AF.Sqrt, ["ssv2", "epsc"], ["rv"], bias=self.epsc[:, 0:1], scale=1.0 / D)
                P.op("dve", lambda e: e.reciprocal(out=self.ssv[:, 3:4], in_=self.ssv[:, 3:4]), reads=["rv"], writes=["rv"])
                self.stt("dve", vn, vsb, self.ssv[:, 3:4], gain, ALU.mult, ALU.mult, ["vsb0", "vsb1", "rv", "gain"], ["vn"])
                for hq in range(2):
                    ps, pn = self.next_ps()
                    pairs, flags = [], []
                    for hh in range(4):
                        hd = hq * 4 + hh
                        pairs.append((vn[:, hd * 128:(hd + 1) * 128], wsT[:, hd, :]))
                        flags.append((ps[:, hh * 128:(hh + 1) * 128], True, True))
                    self.mm(None, pn, pairs, ["vn", "wsT"], flags=flags)
                    tb = tmpg[hq]
                    self.tt("dve", tb, ps[:], bias[:, hq * 512:(hq + 1) * 512], ALU.add, [pn, "bias"], [tmpgn[hq]])
                    self.tt("pool", gq[:, hq * 4:hq * 4 + 4, n * 128:(n + 1) * 128],
                            tb.rearrange("p (a b) -> p a b", a=4), us[:, hq * 4:hq * 4 + 4, n * 128:(n + 1) * 128], ALU.mult,
                            [tmpgn[hq]] + ["us%d" % m for m in range(hq * 4, hq * 4 + 4)],
                            ["a1_%d" % m for m in range(hq * 4, hq * 4 + 4)])
            self.out_proj(l, Wout, "gwout", gq, ["a1_%d" % m for m in range(8)])
            self.store_x(ti)

    def s5_prep(self):
        P = self.P
        d = self.dram
        c = self.s5c
        ARE, AIM, LDT, DT, MAG, ANG, SN, CS, AR, AI, T1, T2, T3, T4, FRE, FIM = [c[:, i, :] for i in range(16)]
        P.dma(ARE, d["are_c"], writes=["s5c"])
        P.dma(AIM, d["aim_c"], writes=["s5c"])
        P.dma(LDT, d["ldt_c"], writes=["s5c"])
        R = ["s5c"]
        self.act(DT, LDT, AF.Exp, R, R)
        self.tt("dve", T1, ARE, DT, ALU.mult, R, R)
        self.act(MAG, T1, AF.Exp, R, R)
        self.tt("dve", ANG, AIM, DT, ALU.mult, R, R)

        def sin_of(dst, src, shift):
            self.ts("dve", T2, src, shift, None, ALU.add, None, R, R)
            self.ts("dve", T3, T2, 1.0 / (2 * PI), None, ALU.mult, None, R, R)
            self.cp("dve", self.s5i[:], T3, R, ["s5i"])
            self.cp("dve", T3, self.s5i[:], ["s5i"], R)
            self.stt("dve", T2, T3, -2 * PI, T2, ALU.mult, ALU.add, R, R)
            self.ts("dve", T3, T2, PI, -2 * PI, ALU.is_gt, ALU.mult, R, R)
            self.tt("dve", T2, T2, T3, ALU.add, R, R)
            self.ts("dve", T3, T2, -PI, 2 * PI, ALU.is_lt, ALU.mult, R, R)
            self.tt("dve", T2, T2, T3, ALU.add, R, R)
            self.ts("dve", T2, T2, -3.141592, 3.141592, ALU.max, ALU.min, R, R)
            self.act(dst, T2, AF.Sin, R, R)

        sin_of(SN, ANG, 0.0)
        sin_of(CS, ANG, PI / 2)
        self.tt("dve", AR, MAG, CS, ALU.mult, R, R)
        self.tt("dve", AI, MAG, SN, ALU.mult, R, R)
        self.tt("dve", T1, ARE, ARE, ALU.mult, R, R)
        self.tt("dve", T2, AIM, AIM, ALU.mult, R, R)
        self.tt("dve", T1, T1, T2, ALU.add, R, R)
        P.op("dve", lambda e: e.reciprocal(out=T4, in_=T1), reads=R, writes=R)
        self.ts("dve", T3, AR, -1.0, None, ALU.add, None, R, R)
        self.tt("dve", T1, T3, ARE, ALU.mult, R, R)
        self.tt("dve", T2, AI, AIM, ALU.mult, R, R)
        self.tt("dve", T1, T1, T2, ALU.add, R, R)
        self.tt("dve", FRE, T1, T4, ALU.mult, R, R)
        self.tt("dve", T1, AI, ARE, ALU.mult, R, R)
        self.tt("dve", T2, T3, AIM, ALU.mult, R, R)
        self.tt("dve", T1, T1, T2, ALU.subtract, R, R)
        self.tt("dve", FIM, T1, T4, ALU.mult, R, R)
        pr, pi_, npi = self.pw[:, 0], self.pw[:, 1], self.pw[:, 2]
        W = ["pw"]
        self.cp("dve", pr[:, 0, :], CS, R, W)
        self.cp("dve", pi_[:, 0, :], SN, R, W)
        for k in range(1, 9):
            self.tt("dve", T1, pr[:, k - 1, :], pr[:, k - 1, :], ALU.mult, W + R, R)
            self.tt("dve", T2, pi_[:, k - 1, :], pi_[:, k - 1, :], ALU.mult, W + R, R)
            self.tt("dve", pr[:, k, :], T1, T2, ALU.subtract, R, W)
            self.tt("dve", T1, pr[:, k - 1, :], pi_[:, k - 1, :], ALU.mult, W + R, R)
            self.ts("dve", pi_[:, k, :], T1, 2.0, None, ALU.mult, None, R, W)
        self.ts("dve", npi, pi_, -1.0, None, ALU.mult, None, W, W)

    def s5_block(self, l):
        P = self.P
        d = self.dram
        P.barrier()
        self.s5_prep()
        Win = self.wview(self.WA, 0, 8, 1024)
        Wglu = self.wview(self.WA, 8192, 8, 1024)
        BW = self.WA[:, 16384:24576].rearrange("p (g r n) -> p g r n", g=32, r=2)
        CW = self.WA[:, 24576:32768].rearrange("p (g r n) -> p g r n", g=32, r=2)
        Wout = self.wview(self.WB, 0, 8, 1024)
        self.load_w("swin", Win, d["ssm_w_in"], 8, 1024)
        self.load_w("sglu", Wglu, d["ssm_glu_w"], 8, 1024)
        self.load_w("swout", Wout, d["ssm_w_out"], 8, 1024)
        c = self.s5c
        FRE, FIM = c[:, 14, :], c[:, 15, :]
        bwparts, cwparts = [], []
        for pc in range(8):
            psr, nr_ = self.next_ps()
            psi, ni_ = self.next_ps()
            for g4 in range(4):
                gp = pc * 4 + g4
                self.ts("dve", self.dg[:, 0, :], self.ident[:], FRE[:, gp:gp + 1], None, ALU.mult, None, ["s5c", "ident"], ["dg0"])
                self.ts("dve", self.dg[:, 1, :], self.ident[:], FIM[:, gp:gp + 1], None, ALU.mult, None, ["s5c", "ident"], ["dg1"])
                self.mm(psr[:, g4 * 128:(g4 + 1) * 128], nr_, [(self.ones_f[:], self.dg[:, 0, :])], ["dg0", "ones_f"],
                        flags=[(psr[:, g4 * 128:(g4 + 1) * 128], True, True)])
                self.mm(psi[:, g4 * 128:(g4 + 1) * 128], ni_, [(self.ones_f[:], self.dg[:, 1, :])], ["dg1", "ones_f"],
                        flags=[(psi[:, g4 * 128:(g4 + 1) * 128], True, True)])
            b1 = self.stage_rr
            b2 = (b1 + 1) % 3
            self.stage_rr = (b1 + 2) % 3
            s1, s2 = "stage%d" % b1, "stage%d" % b2
            bre = self.stage[:, b1, 0:512]
            bim = self.stage[:, b2, 0:512]
            P.dma(bre, d["Bre_z"][:, pc * 4:(pc + 1) * 4, :].rearrange("p g n -> p (g n)"), writes=[s1])
            P.dma(bim, d["Bim_z"][:, pc * 4:(pc + 1) * 4, :].rearrange("p g n -> p (g n)"), writes=[s2])
            t1, t2 = self.tmpn[:, 0, :], self.tmpn[:, 1, :]
            pn = "bw%d" % pc
            self.tt("dve", t1, psr[:], bre, ALU.mult, [nr_, s1], ["tmpn0"])
            self.tt("dve", t2, psi[:], bim, ALU.mult, [ni_, s2], ["tmpn1"])
            self.tt("dve", BW[:, pc * 4:(pc + 1) * 4, 0, :], t1.rearrange("p (g n) -> p g n", g=4),
                    t2.rearrange("p (g n) -> p g n", g=4), ALU.subtract, ["tmpn0", "tmpn1"], [pn + "r"])
            self.tt("dve", t1, psr[:], bim, ALU.mult, [nr_, s2], ["tmpn0"])
            self.tt("dve", t2, psi[:], bre, ALU.mult, [ni_, s1], ["tmpn1"])
            self.tt("dve", BW[:, pc * 4:(pc + 1) * 4, 1, :], t1.rearrange("p (g n) -> p g n", g=4),
                    t2.rearrange("p (g n) -> p g n", g=4), ALU.add, ["tmpn0", "tmpn1"], [pn + "i"])
            bwparts += [pn + "r", pn + "i"]
        for pc in range(8):
            for ri, key, sc in ((0, "Cre_z", 1.0), (1, "Cim_z", -1.0)):
                b1 = self.stage_rr
                self.stage_rr = (b1 + 1) % 3
                s1 = "stage%d" % b1
                P.dma(self.stage[:, b1, 0:512], d[key][:, pc * 4:(pc + 1) * 4, :].rearrange("p g n -> p (g n)"), writes=[s1])
                pn = "cwp%d_%d" % (pc, ri)
                self.act(CW[:, pc * 4:(pc + 1) * 4, ri, :], self.stage[:, b1, 0:512].rearrange("p (g n) -> p g n", g=4),
                         AF.Copy, [s1], [pn], scale=sc)
                cwparts.append(pn)
        K0 = self.KS0
        K1 = self.KS1
        SRI = self.WB[:, 14336:16384].bitcast(F32)
        BR = [K0[:, 0:512], K0[:, 1024:1536]]
        BI = [K0[:, 512:1024], K0[:, 1536:2048]]
        A2f = self.A2
        PRs = [K0[:, 2048:2560], A2f[:, 0:512]]
        PIs = [K0[:, 2560:3072], A2f[:, 512:1024]]
        TC = [K1[:, 0:512], K1[:, 1024:1536], A2f[:, 1024:1536]]
        TS = [K1[:, 512:1024], K1[:, 1536:2048], A2f[:, 1536:2048]]
        RR, RI = K1[:, 2048:2560], K1[:, 2560:3072]
        Dd = self.WB[:, 16384 + 4096:16384 + 4096 + 1024].rearrange("p (q n) -> p q n", q=8)
        sbf3 = self.WB[:, 16384 + 5120:16384 + 5120 + 1024].rearrange("p (r n) -> p r n", r=2)
        sbfs = [self.sbf[:, 0], self.sbf[:, 1], sbf3]
        for q_ in range(8):
            self.ts("dve", Dd[:, q_, :], self.ident[:], self.dsk[:, q_:q_ + 1], None, ALU.mult, None, ["ident", "dsk"], ["Dd"])
        SR, SI = SRI[:, 0:512], SRI[:, 512:1024]
        M1, M2 = self.rt[:, 0, :], self.rt[:, 1, :]
        scan_names = ["br0", "bi0", "br1", "bi1", "pr", "pi", "tc0", "ts0", "tc1", "ts1", "rr", "ri", "sr", "si"]
        self.memset("pool", K1, 0.0, ["tc0", "ts0", "tc1", "ts1", "rr", "ri", "stage0", "stage1", "stage2"])
        carn_all = ["car%d" % g for g in range(32)]
        self.memset("dve", self.car[:], 0.0, carn_all)
        uc, us_, uns = self.pw[:, 0], self.pw[:, 1], self.pw[:, 2]
        for gp in range(32):
            t = gp % 3
            tcn, tsn = "tc%d" % t, "ts%d" % t
            self.memset("dve", TC[t][:, 0:1], 1.0, [tcn])
            self.memset("dve", TS[t][:, 0:1], 0.0, [tsn])
            for k in range(9):
                n = 1 << k
                self.ts("dve", TC[t][:, n:2 * n], TC[t][:, 0:n], uc[:, k, gp:gp + 1], None, ALU.mult, None, [tcn, "pw"], [tcn])
                self.stt("dve", TC[t][:, n:2 * n], TS[t][:, 0:n], uns[:, k, gp:gp + 1], TC[t][:, n:2 * n], ALU.mult, ALU.add,
                         [tcn, tsn, "pw"], [tcn])
                self.ts("dve", TS[t][:, n:2 * n], TS[t][:, 0:n], uc[:, k, gp:gp + 1], None, ALU.mult, None, [tsn, "pw"], [tsn])
                self.stt("dve", TS[t][:, n:2 * n], TC[t][:, 0:n], us_[:, k, gp:gp + 1], TS[t][:, n:2 * n], ALU.mult, ALU.add,
                         [tcn, tsn, "pw"], [tsn])
            P.dma(self.tabC[gp], TC[t], reads=[tcn], writes=["tabC%d" % gp])
            P.dma(self.tabS[gp], TS[t], reads=[tsn], writes=["tabS%d" % gp])
        us = self.A2.rearrange("p (k n) -> p k n", k=8)
        yg = self.A3.rearrange("p (k n) -> p k n", k=8)
        ubf = self.A1[:, 0:8, :]
        ygb = self.A1[:, 8:16, :]
        qv = self.A1[:, 0:8, :]
        c = self.s5c
        MAG, SNt, CSt = c[:, 4, :], c[:, 6, :], c[:, 7, :]
        X1, X2, INITR, INITI = c[:, 10, :], c[:, 11, :], c[:, 12, :], c[:, 13, :]

        for ti in range(NT):
            self.load_x(ti)
            self.norm_stage(l, 0)
            hn = ["h%d" % k for k in range(8)]
            for m in range(8):
                ps, pn = self.next_ps()
                self.mm(ps[:], pn, [(Win[:, k, m * 128:(m + 1) * 128], self.h[:, k, :]) for k in range(8)],
                        self.wparts["swin"] + hn)
                self.cp("act" if m % 2 == 0 else "dve", ubf[:, m, :], ps[:], [pn], ["a1_%d" % m])
            self.tt("dve", X1, CSt, self.car[:, 0, :], ALU.mult, ["s5c"] + carn_all, ["x1"])
            self.tt("dve", X2, SNt, self.car[:, 1, :], ALU.mult, ["s5c"] + carn_all, ["x2"])
            self.tt("dve", INITR, X1, X2, ALU.subtract, ["x1", "x2"], ["initr"])
            self.tt("dve", X1, CSt, self.car[:, 1, :], ALU.mult, ["s5c"] + carn_all, ["x1"])
            self.tt("dve", X2, SNt, self.car[:, 0, :], ALU.mult, ["s5c"] + carn_all, ["x2"])
            self.tt("dve", INITI, X1, X2, ALU.add, ["x1", "x2"], ["initi"])

            def bu(gp, part):
                q = gp // 4
                s = gp % 2
                t = gp % 3
                pb = gp % 2
                s3 = gp % 3
                PRb, PIb = PRs[pb], PIs[pb]
                prn, pin = "pr%d" % pb, "pi%d" % pb
                brn, bin_, tcn, tsn = "br%d" % s, "bi%d" % s, "tc%d" % t, "ts%d" % t
                if part == 0:
                    psr, nr_ = self.next_ps()
                    psi, ni_ = self.next_ps()
                    self.mm(psr[:], nr_, [(BW[:, gp, 0, :], ubf[:, q, :])], bwparts + ["a1_%d" % q])
                    self.mm(psi[:], ni_, [(BW[:, gp, 1, :], ubf[:, q, :])], bwparts + ["a1_%d" % q])
                    self.cp("act", BR[s], psr[:], [nr_], [brn])
                    self.cp("act", BI[s], psi[:], [ni_], [bin_])
                    P.dma(TC[t], self.tabC[gp], reads=["tabC%d" % gp], writes=[tcn])
                    P.dma(TS[t], self.tabS[gp], reads=["tabS%d" % gp], writes=[tsn])
                    self.tt("dve", PRb, BR[s], TC[t], ALU.mult, [brn, tcn], [prn])
                    self.tt("dve", M1, BI[s], TS[t], ALU.mult, [bin_, tsn], ["rt0"])
                    self.tt("dve", PRb, PRb, M1, ALU.add, [prn, "rt0"], [prn])
                    self.tt("dve", PIb, BI[s], TC[t], ALU.mult, [bin_, tcn], [pin])
                    self.tt("dve", M1, BR[s], TS[t], ALU.mult, [brn, tsn], ["rt0"])
                    self.tt("dve", PIb, PIb, M1, ALU.subtract, [pin, "rt0"], [pin])
                    return
                rho = MAG[:, gp:gp + 1].broadcast_to([128, TT])
                P.op("dve", lambda e: e.tensor_tensor_scan(out=RR, data0=rho, data1=PRb, initial=INITR[:, gp:gp + 1],
                                                           op0=ALU.mult, op1=ALU.add),
                     reads=[prn, "s5c", "initr"], writes=["rr"])
                P.op("dve", lambda e: e.tensor_tensor_scan(out=RI, data0=rho, data1=PIb, initial=INITI[:, gp:gp + 1],
                                                           op0=ALU.mult, op1=ALU.add),
                     reads=[pin, "s5c", "initi"], writes=["ri"])
                self.tt("dve", SR, RR, TC[t], ALU.mult, ["rr", tcn], ["sr"])
                self.tt("dve", M2, RI, TS[t], ALU.mult, ["ri", tsn], ["rt1"])
                self.tt("dve", SR, SR, M2, ALU.subtract, ["sr", "rt1"], ["sr"])
                self.tt("dve", SI, RR, TS[t], ALU.mult, ["rr", tsn], ["si"])
                self.tt("dve", M2, RI, TC[t], ALU.mult, ["ri", tcn], ["rt1"])
                self.tt("dve", SI, SI, M2, ALU.add, ["si", "rt1"], ["si"])
                carn = "car%d" % gp
                self.cp("act", sbfs[s3][:, 0, :], SR, ["sr"], ["sbf%d" % s3])
                self.cp("act", sbfs[s3][:, 1, :], SI, ["si"], ["sbf%d" % s3])
                self.cp("act", self.car[:, 0, gp:gp + 1], SR[:, 511:512], ["sr"], [carn])
                self.cp("act", self.car[:, 1, gp:gp + 1], SI[:, 511:512], ["si"], [carn])

            def outp(gp):
                q = gp // 4
                s3 = gp % 3
                psY, nY = self.ps[6 + (q % 2)], "ps%d" % (6 + (q % 2))
                pairs = [(CW[:, gp, 0, :], sbfs[s3][:, 0, :]), (CW[:, gp, 1, :], sbfs[s3][:, 1, :])]
                flags = [(psY[:], False, False), (psY[:], False, gp % 4 == 3)]
                rd = cwparts + ["sbf%d" % s3]
                if gp % 4 == 0:
                    pairs = [(Dd[:, q, :], ubf[:, q, :])] + pairs
                    flags = [(psY[:], True, False)] + flags
                    rd = rd + ["Dd", "a1_%d" % q]
                self.mm(None, nY, pairs, rd, flags=flags)
                if gp % 4 == 3:
                    self.act(yg[:, q, :], psY[:], AF.Gelu_apprx_tanh, [nY], ["yg%d" % q])
                    self.cp("act", ygb[:, q, :], yg[:, q, :], ["yg%d" % q], ["a1_%d" % (8 + q)])

            for i in range(-1, 33):
                if 0 <= i + 1 < 32:
                    bu(i + 1, 0)
                if 0 <= i < 32:
                    bu(i, 1)
                if 0 <= i - 1 < 32:
                    outp(i - 1)
            ygn = ["a1_%d" % (8 + k) for k in range(8)]
            for mo in range(8):
                ps, pn = self.next_ps()
                self.mm(ps[:], pn, [(Wglu[:, k, mo * 128:(mo + 1) * 128], ygb[:, k, :]) for k in range(8)],
                        self.wparts["sglu"] + ygn)
                b = mo % 2
                self.act(self.rt[:, b, :], ps[:], AF.Sigmoid, [pn, "glub"], ["rt%d" % b], bias=self.glub[:, mo:mo + 1])
                self.tt("dve", qv[:, mo, :], yg[:, mo, :], self.rt[:, b, :], ALU.mult, ["yg%d" % mo, "rt%d" % b], ["a1_%d" % mo])
            self.out_proj(l, Wout, "swout", qv, ["a1_%d" % m for m in range(8)])
            self.store_x(ti)

    def build(self):
        nc = self.nc
        self.declare()
        with ExitStack() as st:
            self.alloc(st)
            self.P = Prog(nc, st)
            block = st.enter_context(nc.Block())
            self.prologue()
            for l in range(self.n_layers):
                self.ada_stage(l)
            for l in range(self.n_layers):
                kind = l % 3
                if kind == 0:
                    self.conv_block(l, l // 3, first=(l == 0))
                elif kind == 1:
                    self.s5_block(l)
                else:
                    self.sg_block(l)
                self.ffn_block(l, last=(l == self.n_layers - 1))
            self.P.finish()
            self.P.emit(block)
        return nc


def tT(v):
    v = np.asarray(v, np.float32)
    lead = v.shape[:-1]
    r = v.reshape(lead + (8, 128))
    return np.ascontiguousarray(np.moveaxis(r, -1, 0))


def host_layout(inp, b):
    f32 = np.float32
    m = {}
    m["x"] = np.ascontiguousarray(inp["x"][b], f32)
    m["cT"] = tT(inp["c"][b])
    m["ada_w"] = np.ascontiguousarray(inp["ada_w"], f32)
    ab = np.asarray(inp["ada_b"], f32).reshape(DEPTH, 48, 128)
    m["ada_bT"] = np.ascontiguousarray(ab.transpose(2, 0, 1))
    m["n1gT"] = tT(inp["norm1_g"])
    m["n2gT"] = tT(inp["norm2_g"])
    m["fgT"] = tT(inp["final_g"])
    m["ff_w1"] = np.ascontiguousarray(inp["ff_w1"], f32)
    m["ff_w2"] = np.ascontiguousarray(inp["ff_w2"], f32)
    m["conv_w_in"] = np.ascontiguousarray(inp["conv_w_in"], f32)
    m["conv_wT"] = tT(inp["conv_w"])
    m["conv_bT"] = tT(inp["conv_b"])
    m["conv_w_out"] = np.ascontiguousarray(inp["conv_w_out"], f32)
    m["ssm_w_in"] = np.ascontiguousarray(inp["ssm_w_in"][0], f32)
    m["ssm_glu_w"] = np.ascontiguousarray(inp["ssm_glu_w"][0], f32)
    m["ssm_w_out"] = np.ascontiguousarray(inp["ssm_w_out"][0], f32)
    m["glu_bT"] = tT(inp["ssm_glu_b"][0])
    m["dT"] = tT(inp["ssm_d"][0])
    def compact(a):
        a = np.asarray(a, f32).reshape(32, 2, 64)
        return np.ascontiguousarray(a.transpose(1, 2, 0).reshape(128, 32))
    m["are_c"] = compact(inp["ssm_a_re"][0])
    m["aim_c"] = compact(inp["ssm_a_im"][0])
    m["ldt_c"] = compact(np.broadcast_to(np.asarray(inp["ssm_log_dt"][0], f32)[:, None], (64, 64)))
    bre = np.asarray(inp["ssm_b_re"][0], f32)
    bim = np.asarray(inp["ssm_b_im"][0], f32)
    cre = np.asarray(inp["ssm_c_re"][0], f32)
    cim = np.asarray(inp["ssm_c_im"][0], f32)
    Bre_z = np.zeros((128, 32, 128), f32); Bim_z = np.zeros((128, 32, 128), f32)
    Cre_z = np.zeros((128, 32, 128), f32); Cim_z = np.zeros((128, 32, 128), f32)
    for g in range(64):
        gp, g2, g8 = g // 2, g % 2, g % 8
        Bre_z[g8 * 16:(g8 + 1) * 16, gp, g2 * 64:(g2 + 1) * 64] = bre[g].T
        Bim_z[g8 * 16:(g8 + 1) * 16, gp, g2 * 64:(g2 + 1) * 64] = bim[g].T
        Cre_z[g2 * 64:(g2 + 1) * 64, gp, g8 * 16:(g8 + 1) * 16] = cre[g].T
        Cim_z[g2 * 64:(g2 + 1) * 64, gp, g8 * 16:(g8 + 1) * 16] = cim[g].T
    m["Bre_z"], m["Bim_z"], m["Cre_z"], m["Cim_z"] = Bre_z, Bim_z, Cre_z, Cim_z
    m["sg_w_in"] = np.ascontiguousarray(inp["sg_w_in"][0], f32)
    m["sg_w_out"] = np.ascontiguousarray(inp["sg_w_out"][0], f32)
    m["sg_w_s"] = np.ascontiguousarray(inp["sg_w_s"][0], f32)
    m["sg_bias_rep"] = np.ascontiguousarray(np.broadcast_to(np.asarray(inp["sg_b_s"][0], f32)[None], (128, 8, 128)))
    m["sg_gain_rep"] = np.ascontiguousarray(np.broadcast_to(np.asarray(inp["sg_v_g"][0], f32)[None], (128, D)))
    m["ident"] = np.eye(128, dtype=f32)
    m["tril"] = np.tril(np.ones((128, 128), f32))
    return m


_NC_CACHE = {}


def kernel(_n_layers=DEPTH, **inputs):
    inp = {k: np.asarray(v) for k, v in inputs.items()}
    if _n_layers not in _NC_CACHE:
        _NC_CACHE[_n_layers] = Builder(_n_layers).build()
    nc = _NC_CACHE[_n_layers]
    in_maps = [host_layout(inp, b) for b in range(8)]
    res = run_bass_kernel_spmd(nc, in_maps, core_ids=list(range(8)))
    out = np.stack([np.asarray(r["out"], np.float32) for r in res.results], axis=0)
    return out
```

```python
import math
from contextlib import ExitStack

import numpy as np
import concourse.bass as bass
import concourse.mybir as mybir
from concourse.bass_utils import run_bass_kernel_spmd

F32 = mybir.dt.float32
BF16 = mybir.dt.bfloat16
I32 = mybir.dt.int32
AF = mybir.ActivationFunctionType
ALU = mybir.AluOpType
AX = mybir.AxisListType

D = 1024
L = 4096
DEPTH = 4
TT = 512
NT = L // TT
KC = 8
EPS = 1e-6
ENGS = ("pe", "act", "dve", "pool", "sp")
N_DMA_SEMS = 14
PI = math.pi


class Prog:
    def __init__(self, nc, stack):
        self.nc = nc
        self.stream = {e: [] for e in ENGS}
        self.count = {e: 0 for e in ENGS}
        self.sem = {e: stack.enter_context(nc.semaphore("s_" + e)) for e in ENGS if e != "sp"}
        self.dsem = [stack.enter_context(nc.semaphore("d%d" % i)) for i in range(N_DMA_SEMS)]
        self.dval = [0] * N_DMA_SEMS
        self.dnext = 0
        self.waited = {e: {} for e in ENGS}
        self.last_w = {}
        self.readers = {}

    def _deps(self, eng, reads, writes, same_engine_ok=False):
        evs = []
        for r in reads:
            ev = self.last_w.get(r)
            if ev is not None:
                evs.append((ev, False))
        for w in writes:
            ev = self.last_w.get(w)
            if ev is not None:
                evs.append((ev, False))
            for ev in self.readers.get(w, {}).values():
                evs.append((ev, True))
        for (ev, is_war) in evs:
            owner, key, sem, val = ev
            if owner == eng and (same_engine_ok or is_war):
                continue
            if self.waited[eng].get(key, 0) >= val:
                continue
            self.waited[eng][key] = val
            self.stream[eng].append(("wait", sem, val))

    def _record(self, rkey, ev, reads, writes):
        for w in writes:
            self.last_w[w] = ev
            self.readers[w] = {}
        for r in reads:
            if r in writes:
                continue
            self.readers.setdefault(r, {})[rkey] = ev

    def op(self, eng, fn, reads=(), writes=()):
        self.group(eng, [fn], reads, writes)

    def group(self, eng, fns, reads=(), writes=()):
        psr = [r for r in reads if r.startswith("ps") and r not in writes]
        if psr:
            writes = list(writes) + psr
        self._deps(eng, reads, writes, same_engine_ok=(eng == "pe"))
        for fn in fns[:-1]:
            self.stream[eng].append(("ins", fn, None))
        self.count[eng] += 1
        ev = (eng, eng, self.sem[eng], self.count[eng])
        self.stream[eng].append(("ins", fns[-1], self.sem[eng]))
        self._record(eng, ev, reads, writes)

    def dma(self, out, in_, reads=(), writes=(), eng="sp"):
        i = self.dnext
        self.dnext = (self.dnext + 1) % N_DMA_SEMS
        key = "dma%d" % i
        sem = self.dsem[i]
        if self.dval[i] > 0 and self.waited[eng].get(key, 0) < self.dval[i]:
            self.waited[eng][key] = self.dval[i]
            self.stream[eng].append(("wait", sem, self.dval[i]))
        self._deps(eng, reads, writes)
        self.dval[i] += 16
        ev = ("dmaq", key, sem, self.dval[i])
        self.stream[eng].append(("dma", out, in_, sem))
        self._record("dmaq_" + key, ev, reads, writes)

    def barrier(self):
        for e in ENGS:
            for o in ENGS:
                if o != e and o != "sp" and self.count[o] > 0 and self.waited[e].get(o, 0) < self.count[o]:
                    self.waited[e][o] = self.count[o]
                    self.stream[e].append(("wait", self.sem[o], self.count[o]))
            for i in range(N_DMA_SEMS):
                key = "dma%d" % i
                if self.dval[i] > 0 and self.waited[e].get(key, 0) < self.dval[i]:
                    self.waited[e][key] = self.dval[i]
                    self.stream[e].append(("wait", self.dsem[i], self.dval[i]))

    def finish(self):
        for i in range(N_DMA_SEMS):
            if self.dval[i] > 0:
                self.stream["sp"].append(("wait", self.dsem[i], self.dval[i]))
        for e in ENGS:
            if e != "sp" and self.count[e] > 0:
                self.stream["sp"].append(("wait", self.sem[e], self.count[e]))

    def emit(self, block):
        def run(engh, items):
            for it in items:
                if it[0] == "wait":
                    engh.wait_ge(it[1], it[2])
                elif it[0] == "ins":
                    ins = it[1](engh)
                    if it[2] is not None:
                        ins.then_inc(it[2], 1)
                else:
                    engh.dma_start(out=it[1], in_=it[2]).then_inc(it[3], 16)

        @block.sync
        def _(e):
            run(e, self.stream["sp"])

        @block.tensor
        def _(e):
            run(e, self.stream["pe"])

        @block.scalar
        def _(e):
            run(e, self.stream["act"])

        @block.vector
        def _(e):
            run(e, self.stream["dve"])

        @block.gpsimd
        def _(e):
            run(e, self.stream["pool"])


class Builder:
    def __init__(self, n_layers=DEPTH):
        self.n_layers = n_layers
        self.nc = bass.Bass("TRN2", target_bir_lowering=False)
        self.dram = {}
        self.rr = 0
        self.cast_rr = 0
        self.stage_rr = 0
        self.wparts = {}

    def din(self, name, shape):
        self.dram[name] = self.nc.dram_tensor(name, list(shape), F32, kind="ExternalInput").ap()

    def declare(self):
        din = self.din
        din("x", [L, D]); din("cT", [128, 8]); din("ada_w", [DEPTH, D, 6 * D]); din("ada_bT", [128, DEPTH, 48])
        din("n1gT", [128, DEPTH, 8]); din("n2gT", [128, DEPTH, 8]); din("fgT", [128, 8])
        din("ff_w1", [DEPTH, D, 4 * D]); din("ff_w2", [DEPTH, 4 * D, D])
        din("conv_w_in", [2, D, 3 * D]); din("conv_wT", [128, 2, 3, 8]); din("conv_bT", [128, 2, 8])
        din("conv_w_out", [2, D, D])
        din("ssm_w_in", [D, D]); din("ssm_glu_w", [D, D]); din("ssm_w_out", [D, D])
        din("glu_bT", [128, 8]); din("dT", [128, 8])
        din("are_c", [128, 32]); din("aim_c", [128, 32]); din("ldt_c", [128, 32])
        din("Bre_z", [128, 32, 128]); din("Bim_z", [128, 32, 128])
        din("Cre_z", [128, 32, 128]); din("Cim_z", [128, 32, 128])
        din("sg_w_in", [D, 2 * D]); din("sg_w_out", [D, D]); din("sg_w_s", [8, 128, 128])
        din("sg_bias_rep", [128, 8, 128]); din("sg_gain_rep", [128, D])
        din("ident", [128, 128]); din("tril", [128, 128])
        self.out = self.nc.dram_tensor("out", [L, D], F32, kind="ExternalOutput").ap()
        self.xT = self.nc.dram_tensor("xT_scr", [D, L], F32, kind="Internal").ap()
        self.tabC = self.nc.dram_tensor("tabC_scr", [32, 128, TT], F32, kind="Internal").ap()
        self.tabS = self.nc.dram_tensor("tabS_scr", [32, 128, TT], F32, kind="Internal").ap()

    def alloc(self, st):
        nc = self.nc

        def sb(name, shape, dt):
            return st.enter_context(nc.sbuf_tensor("sb_" + name, list(shape), dt))

        self.WA = sb("WA", [128, 32768], BF16)
        self.WB = sb("WB", [128, 32768], BF16)
        self.stage = sb("stage", [128, 3, 1024], F32)
        self.xt = sb("xt", [128, 8, TT], F32)
        self.h = sb("h", [128, 8, TT], BF16)
        self.sq = sb("sq", [128, 2, TT], BF16)
        self.tmpn = sb("tmpn", [128, 2, TT], F32)
        self.rt = sb("rt", [128, 2, TT], F32)
        self.rstd = sb("rstd", [128, TT], F32)
        self.A1 = sb("A1", [128, 16, TT], BF16)
        self.sbf = sb("sbf", [128, 2, 2, TT], BF16)
        self.ident = sb("ident", [128, 128], F32)
        self.tril = sb("tril", [128, 128], F32)
        self.ones_bf = sb("ones_bf", [128, 128], BF16)
        self.ones_f = sb("ones_f", [128, 128], F32)
        self.epsc = sb("epsc", [128, 1], F32)
        self.cT = sb("cT", [128, 8], F32)
        self.cab = sb("cab", [128, 8], BF16)
        self.modT = sb("modT", [128, DEPTH, 48], F32)
        self.adab = sb("adab", [128, DEPTH, 48], F32)
        self.n1g = sb("n1g", [128, DEPTH, 8], F32)
        self.n2g = sb("n2g", [128, DEPTH, 8], F32)
        self.fg = sb("fg", [128, 8], F32)
        self.aT = sb("aT", [128, DEPTH, 2, 8], F32)
        self.cw = sb("cw", [128, 2, 3, 8], F32)
        self.cb = sb("cb", [128, 2, 8], F32)
        self.zc = sb("zc", [128, 8, 2], F32)
        self.glub = sb("glub", [128, 8], F32)
        self.dsk = sb("dsk", [128, 8], F32)
        self.s5c = sb("s5c", [128, 16, 32], F32)
        self.s5i = sb("s5i", [128, 32], I32)
        self.pw = sb("pw", [128, 3, 9, 32], F32)
        self.car = sb("car", [128, 2, 32], F32)
        self.ssv = sb("ssv", [128, 4], F32)
        self.dg = sb("dg", [128, 2, 128], F32)
        self.WAf = self.WA[:, :].bitcast(F32)
        self.A2 = self.WB[:, 16384:24576].bitcast(F32)
        self.A3 = self.WB[:, 24576:32768].bitcast(F32)
        self.KS0 = self.WB[:, 8192:14336].bitcast(F32)
        self.KS1 = self.stage[:, :, :].rearrange("p a b -> p (a b)")
        self.rowtmp = self.rt[0:1, :, :].rearrange("p a b -> p (a b)")
        self.ps = [st.enter_context(nc.psum_tensor("ps%d" % i, [128, 512], F32)) for i in range(8)]

    def xc(self, buf, k):
        if buf == 0:
            return self.xt[:, k, :]
        if k < 6:
            return self.KS1[:, k * 512:(k + 1) * 512]
        return self.sbf[:, :, :, :].rearrange("p a b c -> p (a b c)").bitcast(F32)[:, (k - 6) * 512:(k - 5) * 512]

    def xn(self, buf, k):
        return ("xt%d" % k) if buf == 0 else ("xu%d" % k)

    def xalias(self, buf, k):
        if buf == 0:
            return []
        return ["stage%d" % (k // 2)] if k < 6 else ["sbf0", "sbf1"]

    def next_ps(self):
        i = self.rr
        self.rr = (self.rr + 1) % 6
        return self.ps[i], "ps%d" % i

    def mm(self, out_ps, psname, pairs, reads, flags=None):
        fns = []
        n = len(pairs)
        for i, (a, b) in enumerate(pairs):
            if flags is None:
                o, s0, s1 = out_ps, (i == 0), (i == n - 1)
            else:
                o, s0, s1 = flags[i]
            fns.append(lambda e, o=o, a=a, b=b, s0=s0, s1=s1: e.matmul(o, lhsT=a, rhs=b, start=s0, stop=s1))
        self.P.group("pe", fns, reads=reads, writes=[psname])

    def act(self, out, in_, func, reads, writes, bias=None, scale=1.0):
        if bias is None:
            fn = lambda e: e.activation(out=out, in_=in_, func=func, scale=scale)
        else:
            fn = lambda e: e.activation(out=out, in_=in_, func=func, bias=bias, scale=scale)
        self.P.op("act", fn, reads=reads, writes=writes)

    def stt(self, eng, out, in0, scalar, in1, op0, op1, reads, writes):
        self.P.op(eng, lambda e: e.scalar_tensor_tensor(out=out, in0=in0, scalar=scalar, in1=in1, op0=op0, op1=op1),
                  reads=reads, writes=writes)

    def tt(self, eng, out, in0, in1, op, reads, writes):
        self.P.op(eng, lambda e: e.tensor_tensor(out=out, in0=in0, in1=in1, op=op), reads=reads, writes=writes)

    def ts(self, eng, out, in0, s1, s2, op0, op1, reads, writes):
        if s2 is None:
            fn = lambda e: e.tensor_scalar(out=out, in0=in0, scalar1=s1, scalar2=None, op0=op0)
        else:
            fn = lambda e: e.tensor_scalar(out=out, in0=in0, scalar1=s1, scalar2=s2, op0=op0, op1=op1)
        self.P.op(eng, fn, reads=reads, writes=writes)

    def cp(self, eng, out, in_, reads, writes):
        if eng == "act":
            self.act(out, in_, AF.Copy, reads, writes)
        else:
            self.P.op(eng, lambda e: e.tensor_copy(out=out, in_=in_), reads=reads, writes=writes)

    def memset(self, eng, ap, val, writes):
        self.P.op(eng, lambda e: e.memset(ap, val), writes=writes)

    def load_w(self, name, dst3, src2, K, N, scale=None, xt_stage=True):
        parts = []
        bufs = [(self.stage[:, i, :], ["stage%d" % i]) for i in range(3)]
        if xt_stage:
            for i in range(4):
                bufs.append((self.xt[:, 2 * i:2 * i + 2, :].rearrange("p a b -> p (a b)"), ["xt%d" % (2 * i), "xt%d" % (2 * i + 1)]))
        for k in range(K):
            for c0 in range(0, N, 1024):
                w = min(1024, N - c0)
                self.wl_rr = (getattr(self, "wl_rr", 0) + 1) % len(bufs)
                sb_, snames = bufs[self.wl_rr]
                self.P.dma(sb_[:, 0:w], src2[k * 128:(k + 1) * 128, c0:c0 + w], writes=snames)
                pn = "%s_%d_%d" % (name, k, c0)
                eng = ("act", "dve")[self.cast_rr % 2]
                self.cast_rr += 1
                if scale is None:
                    self.cp(eng, dst3[:, k, c0:c0 + w], sb_[:, 0:w], snames, [pn])
                else:
                    self.act(dst3[:, k, c0:c0 + w], sb_[:, 0:w], AF.Copy, snames, [pn], scale=scale)
                parts.append(pn)
        self.wparts[name] = parts
        return parts

    def wview(self, buf, c0, K, N):
        return buf[:, c0:c0 + K * N].rearrange("p (k n) -> p k n", k=K)

    def prologue(self):
        P = self.P
        d = self.dram
        P.dma(self.ident[:], d["ident"], writes=["ident"])
        P.dma(self.tril[:], d["tril"], writes=["tril"])
        P.dma(self.cT[:], d["cT"], writes=["cT"])
        P.dma(self.adab[:], d["ada_bT"], writes=["adab"])
        P.dma(self.n1g[:], d["n1gT"], writes=["n1g"])
        P.dma(self.n2g[:], d["n2gT"], writes=["n2g"])
        P.dma(self.fg[:], d["fgT"], writes=["fg"])
        P.dma(self.cw[:], d["conv_wT"], writes=["cw"])
        P.dma(self.cb[:], d["conv_bT"], writes=["cb"])
        P.dma(self.glub[:], d["glu_bT"], writes=["glub"])
        P.dma(self.dsk[:], d["dT"], writes=["dsk"])
        self.memset("pool", self.ones_bf[:], 1.0, ["ones_bf"])
        self.memset("pool", self.ones_f[:], 1.0, ["ones_f"])
        self.memset("pool", self.epsc[:], EPS, ["epsc"])
        self.act(self.s5c[:, 0, 0:8], self.cT[:], AF.Sigmoid, ["cT"], ["sgc"])
        self.tt("dve", self.cab[:], self.cT[:], self.s5c[:, 0, 0:8], ALU.mult, ["cT", "sgc"], ["cab"])

    def ada_stage(self, l):
        P = self.P
        wtmp = self.A1
        for cbk in range(6):
            psa, na = self.next_ps()
            psb, nb = self.next_ps()
            for k in range(8):
                b = self.stage_rr
                self.stage_rr = (self.stage_rr + 1) % 3
                sname = "stage%d" % b
                P.dma(self.stage[:, b, :], self.dram["ada_w"][l, k * 128:(k + 1) * 128, cbk * 1024:(cbk + 1) * 1024],
                      writes=[sname])
                wb = (cbk * 8 + k) % 4
                wt = self.A1[:, 2 * wb:2 * wb + 2, :].rearrange("p a b -> p (a b)")
                wn = ["a1_%d" % (2 * wb), "a1_%d" % (2 * wb + 1)]
                eng = ("act", "dve")[self.cast_rr % 2]
                self.cast_rr += 1
                self.cp(eng, wt, self.stage[:, b, :], [sname], wn)
                lhs = self.cab[:, k:k + 1]
                self.P.group("pe", [
                    (lambda e, o=psa[0:1, :], a=lhs, r=wt[:, 0:512], s0=(k == 0), s1=(k == 7):
                     e.matmul(o, lhsT=a, rhs=r, start=s0, stop=s1)),
                    (lambda e, o=psb[0:1, :], a=lhs, r=wt[:, 512:1024], s0=(k == 0), s1=(k == 7):
                     e.matmul(o, lhsT=a, rhs=r, start=s0, stop=s1)),
                ], reads=wn + ["cab"], writes=[na, nb])
            self.act(self.rowtmp[0:1, 0:512], psa[0:1, :], AF.Copy, [na], ["rt0", "rt1"])
            self.act(self.rowtmp[0:1, 512:1024], psb[0:1, :], AF.Copy, [nb], ["rt0", "rt1"])
            pst, nt = self.next_ps()
            fns = []
            for m in range(8):
                fns.append(lambda e, o=pst[:, m:m + 1], a=self.rowtmp[0:1, m * 128:(m + 1) * 128], r=self.ones_f[0:1, 0:1]:
                           e.matmul(o, lhsT=a, rhs=r, start=True, stop=True))
            P.group("pe", fns, reads=["rt0", "rt1", "ones_f"], writes=[nt])
            self.tt("dve", self.modT[:, l, cbk * 8:(cbk + 1) * 8], pst[:, 0:8], self.adab[:, l, cbk * 8:(cbk + 1) * 8],
                    ALU.add, [nt, "adab"], ["modT%d" % l])
        mn = "modT%d" % l
        self.ts("dve", self.aT[:, l, 0, :], self.modT[:, l, 8:16], 1.0, None, ALU.add, None, [mn], ["aT%d" % l])
        self.tt("dve", self.aT[:, l, 0, :], self.aT[:, l, 0, :], self.n1g[:, l, :], ALU.mult, ["aT%d" % l, "n1g"], ["aT%d" % l])
        self.ts("dve", self.aT[:, l, 1, :], self.modT[:, l, 32:40], 1.0, None, ALU.add, None, [mn], ["aT%d" % l])
        self.tt("dve", self.aT[:, l, 1, :], self.aT[:, l, 1, :], self.n2g[:, l, :], ALU.mult, ["aT%d" % l, "n2g"], ["aT%d" % l])

    def load_x_first(self, ti):
        P = self.P
        for tb in range(4):
            b = self.stage_rr
            self.stage_rr = (self.stage_rr + 1) % 3
            sname = "stage%d" % b
            r0 = (ti * 4 + tb) * 128
            P.dma(self.stage[:, b, :], self.dram["x"][r0:r0 + 128, :], writes=[sname])
            for hf in range(2):
                ps, pn = self.next_ps()
                fns = []
                for kk in range(4):
                    k = hf * 4 + kk
                    fns.append(lambda e, o=ps[:, kk * 128:(kk + 1) * 128], i=self.stage[:, b, k * 128:(k + 1) * 128]:
                               e.transpose(o, i, self.ident[:]))
                P.group("pe", fns, reads=[sname, "ident"], writes=[pn])
                eng = "dve" if hf == 0 else "act"
                self.cp(eng, self.xt[:, hf * 4:hf * 4 + 4, tb * 128:(tb + 1) * 128],
                        ps[:, :].rearrange("p (a b) -> p a b", a=4), [pn], ["xt%d" % k for k in range(hf * 4, hf * 4 + 4)])

    def load_x(self, ti, buf=0):
        for k in range(8):
            self.P.dma(self.xc(buf, k), self.xT[k * 128:(k + 1) * 128, ti * TT:(ti + 1) * TT],
                       reads=["xT_%d_%d" % (k, ti)], writes=[self.xn(buf, k)] + self.xalias(buf, k))

    def store_x(self, ti, buf=0):
        for k in range(8):
            self.P.dma(self.xT[k * 128:(k + 1) * 128, ti * TT:(ti + 1) * TT], self.xc(buf, k),
                       reads=[self.xn(buf, k)], writes=["xT_%d_%d" % (k, ti)])

    def sumsq_rstd(self, buf=0):
        P = self.P
        pss, pn = self.ps[6], "ps6"
        for k in range(8):
            b = k % 2
            self.tt("pool", self.sq[:, b, :], self.xc(buf, k), self.xc(buf, k), ALU.mult, [self.xn(buf, k)], ["sq%d" % b])
            P.group("pe", [lambda e, b=b, k=k: e.matmul(pss[:], lhsT=self.ones_bf[:], rhs=self.sq[:, b, :],
                                                         start=(k == 0), stop=(k == 7))],
                    reads=["sq%d" % b, "ones_bf"], writes=[pn])
        self.act(self.rstd[:], pss[:], AF.Sqrt, [pn, "epsc"], ["rstd"], bias=self.epsc[:, 0:1], scale=1.0 / D)
        P.op("dve", lambda e: e.reciprocal(out=self.rstd[:], in_=self.rstd[:]), reads=["rstd"], writes=["rstd"])

    def norm_stage(self, l, which, buf=0):
        self.sumsq_rstd(buf)
        an = "aT%d" % l
        mn = "modT%d" % l
        sh0 = 0 if which == 0 else 24
        for k in range(8):
            b = k % 2
            self.stt("dve", self.tmpn[:, b, :], self.xc(buf, k), self.aT[:, l, which, k:k + 1], self.rstd[:],
                     ALU.mult, ALU.mult, [self.xn(buf, k), an, "rstd"], ["tmpn%d" % b])
            self.act(self.h[:, k, :], self.tmpn[:, b, :], AF.Identity, ["tmpn%d" % b, mn], ["h%d" % k],
                     bias=self.modT[:, l, sh0 + k:sh0 + k + 1])

    def final_stage(self, ti, buf=0):
        P = self.P
        self.sumsq_rstd(buf)
        for k in range(8):
            self.stt("dve", self.xc(buf, k), self.xc(buf, k), self.fg[:, k:k + 1], self.rstd[:],
                     ALU.mult, ALU.mult, [self.xn(buf, k), "fg", "rstd"], [self.xn(buf, k)])
        for tb in range(4):
            sl = 4 * (tb % 2)
            ot = self.A1[:, sl:sl + 4, :].rearrange("p a b -> p (a b)").bitcast(F32)
            otn = ["a1_%d" % i for i in range(sl, sl + 4)]
            for hf in range(2):
                ps, pn = self.next_ps()
                fns = []
                for kk in range(4):
                    k = hf * 4 + kk
                    fns.append(lambda e, o=ps[:, kk * 128:(kk + 1) * 128], i=self.xc(buf, k)[:, tb * 128:(tb + 1) * 128]:
                               e.transpose(o, i, self.ident[:]))
                P.group("pe", fns, reads=[self.xn(buf, k) for k in range(hf * 4, hf * 4 + 4)] + ["ident"], writes=[pn])
                eng = "dve" if hf == 0 else "act"
                self.cp(eng, ot[:, hf * 512:(hf + 1) * 512], ps[:, :], [pn], otn)
            r0 = (ti * 4 + tb) * 128
            P.dma(self.out[r0:r0 + 128, :], ot, reads=otn, writes=["out_%d" % r0])

    def resid_update(self, ps, pn, mo, gate_ap, gname, buf=0):
        self.stt("dve", self.xc(buf, mo), ps[:], gate_ap, self.xc(buf, mo), ALU.mult, ALU.add,
                 [pn, gname, self.xn(buf, mo)], [self.xn(buf, mo)])

    def out_proj(self, l, Wout, wname, src, srcnames, buf=0):
        for mo in range(8):
            ps, pn = self.next_ps()
            self.mm(ps[:], pn, [(Wout[:, k, mo * 128:(mo + 1) * 128], src[:, k, :]) for k in range(8)],
                    reads=self.wparts[wname] + srcnames)
            self.resid_update(ps, pn, mo, self.modT[:, l, 16 + mo:17 + mo], "modT%d" % l, buf)

    def ffn_block(self, l, last):
        self.P.barrier()
        W1 = self.wview(self.WA, 0, 8, 4096)
        W2 = self.wview(self.WB, 0, 32, 1024)
        self.load_w("w1", W1, self.dram["ff_w1"][l], 8, 4096)
        self.load_w("w2", W2, self.dram["ff_w2"][l], 32, 1024)
        r2 = self.A1
        dbl = True
        hn = ["h%d" % k for k in range(8)]
        if dbl:
            self.load_x(0, 0)
            self.norm_stage(l, 1, 0)
        for ti in range(NT):
            cur = (ti % 2) if dbl else 0
            nxt = 1 - cur
            if dbl:
                if ti + 1 < NT:
                    self.load_x(ti + 1, nxt)
            else:
                self.load_x(ti)
                self.norm_stage(l, 1)
            for half in range(2):
                for jj in range(16):
                    j = half * 16 + jj
                    ps, pn = self.next_ps()
                    self.mm(ps[:], pn, [(W1[:, k, j * 128:(j + 1) * 128], self.h[:, k, :]) for k in range(8)],
                            reads=self.wparts["w1"] + hn)
                    b = jj % 2
                    self.act(self.rt[:, b, :], ps[:], AF.Relu, [pn], ["rt%d" % b])
                    eng = "dve" if jj % 2 == 0 else "pool"
                    self.tt(eng, r2[:, jj, :], self.rt[:, b, :], self.rt[:, b, :], ALU.mult, ["rt%d" % b], ["a1_%d" % jj])
                if dbl and half == 1 and ti + 1 < NT:
                    self.norm_stage(l, 1, nxt)
                for mo in range(8):
                    ps, pn = self.next_ps()
                    self.mm(ps[:], pn, [(W2[:, half * 16 + jj, mo * 128:(mo + 1) * 128], r2[:, jj, :]) for jj in range(16)],
                            reads=self.wparts["w2"] + ["a1_%d" % jj for jj in range(16)])
                    self.resid_update(ps, pn, mo, self.modT[:, l, 40 + mo:41 + mo], "modT%d" % l, cur)
            if last:
                self.final_stage(ti, cur)
            else:
                self.store_x(ti, cur)

    def conv_block(self, l, j, first):
        self.P.barrier()
        Win = self.wview(self.WA, 0, 8, 3072)
        Wout = self.wview(self.WB, 0, 8, 1024)
        self.load_w("cwin", Win, self.dram["conv_w_in"][j], 8, 3072)
        self.load_w("cwout", Wout, self.dram["conv_w_out"][j], 8, 1024)
        A2 = self.A2
        cs = [A2[:, 0:512], A2[:, 512:1024]]
        zb = [A2[:, 1024:1538], A2[:, 1538:2052]]
        acc = [A2[:, 2052:2564], A2[:, 2564:3076]]
        bsb = [self.A3[:, 0:512], self.A3[:, 512:1024]]
        q = self.A1
        self.memset("pool", self.zc[:], 0.0, ["zc%d" % m for m in range(8)])
        dbl = not first
        if dbl:
            self.load_x(0, 0)
            self.norm_stage(l, 0, 0)
        for ti in range(NT):
            cur = (ti % 2) if dbl else 0
            nxt = 1 - cur
            if dbl:
                if ti + 1 < NT:
                    self.load_x(ti + 1, nxt)
            else:
                self.load_x_first(ti)
                self.norm_stage(l, 0)
            hn = ["h%d" % k for k in range(8)]
            wr = self.wparts["cwin"] + hn
            for m in range(8):
                b = m % 2
                psB, nB = self.next_ps()
                psC, nC = self.next_ps()
                psX, nX = self.next_ps()
                self.mm(psC[:], nC, [(Win[:, k, 1024 + m * 128:1024 + (m + 1) * 128], self.h[:, k, :]) for k in range(8)], wr)
                self.mm(psX[:], nX, [(Win[:, k, 2048 + m * 128:2048 + (m + 1) * 128], self.h[:, k, :]) for k in range(8)], wr)
                self.mm(psB[:], nB, [(Win[:, k, m * 128:(m + 1) * 128], self.h[:, k, :]) for k in range(8)], wr)
                self.act(cs[b], psC[:], AF.Copy, [nC], ["cs%d" % b])
                self.act(bsb[b], psB[:], AF.Copy, [nB], ["bsb%d" % b])
                self.cp("pool", zb[b][:, 0:2], self.zc[:, m, :], ["zc%d" % m], ["zb%d" % b])
                self.tt("dve", zb[b][:, 2:514], psX[:], cs[b], ALU.mult, [nX, "cs%d" % b], ["zb%d" % b])
                self.act(acc[b], zb[b][:, 2:514], AF.Identity, ["zb%d" % b, "cw", "cb"], ["acc%d" % b],
                         bias=self.cb[:, j, m:m + 1], scale=self.cw[:, j, 2, m:m + 1])
                self.stt("dve", acc[b], zb[b][:, 1:513], self.cw[:, j, 1, m:m + 1], acc[b], ALU.mult, ALU.add,
                         ["zb%d" % b, "cw", "acc%d" % b], ["acc%d" % b])
                self.stt("dve", acc[b], zb[b][:, 0:512], self.cw[:, j, 0, m:m + 1], acc[b], ALU.mult, ALU.add,
                         ["zb%d" % b, "cw", "acc%d" % b], ["acc%d" % b])
                self.cp("pool", self.zc[:, m, :], zb[b][:, 512:514], ["zb%d" % b], ["zc%d" % m])
                self.tt("pool", q[:, m, :], bsb[b], acc[b], ALU.mult, ["bsb%d" % b, "acc%d" % b], ["a1_%d" % m])
            if dbl and ti + 1 < NT:
                self.norm_stage(l, 0, nxt)
            self.out_proj(l, Wout, "cwout", q, ["a1_%d" % m for m in range(8)], cur)
            self.store_x(ti, cur)

    def sg_block(self, l):
        P = self.P
        P.barrier()
        Win = self.wview(self.WA, 0, 8, 2048)
        Wout = self.wview(self.WB, 0, 8, 1024)
        wsT = self.WB[:, 8192:9216].rearrange("p (h t) -> p h t", h=8)
        self.load_w("gwin", Win, self.dram["sg_w_in"], 8, 2048)
        self.load_w("gwout", Wout, self.dram["sg_w_out"], 8, 1024)
        A3 = self.A3
        vsb = A3[:, 0:1024]
        gain = A3[:, 1024:2048]
        bias = A3[:, 2048:3072]
        vsq = A3[:, 3072:3584]
        tmpg = [A3[:, 3584:4096], self.rt[:, 0, :]]
        tmpgn = ["tmpg0", "rt0"]
        vn = self.sbf[:, :, :, :].rearrange("p a b c -> p (a b c)")[:, 0:1024]
        us = self.A2.rearrange("p (k n) -> p k n", k=8)
        gq = self.A1
        P.dma(gain, self.dram["sg_gain_rep"], writes=["gain"])
        P.dma(bias, self.dram["sg_bias_rep"].rearrange("p h t -> p (h t)"), writes=["bias"])
        b = self.stage_rr
        self.stage_rr = (self.stage_rr + 1) % 3
        sname = "stage%d" % b
        stg = self.stage[:, b, :].rearrange("p (h s) -> p h s", h=8)
        P.dma(stg, self.dram["sg_w_s"].rearrange("h t s -> t h s"), writes=[sname])
        self.tt("pool", stg, stg, self.tril[:, :].unsqueeze(1).broadcast_to([128, 8, 128]), ALU.mult,
                [sname, "tril"], [sname])
        for hf in range(2):
            ps, pn = self.next_ps()
            fns = []
            for hh in range(4):
                fns.append(lambda e, o=ps[:, hh * 128:(hh + 1) * 128], i=stg[:, hf * 4 + hh, :]:
                           e.transpose(o, i, self.ident[:]))
            P.group("pe", fns, reads=[sname, "ident"], writes=[pn])
            self.cp("dve", wsT[:, hf * 4:hf * 4 + 4, :], ps[:, :].rearrange("p (a b) -> p a b", a=4), [pn], ["wsT"])
        for ti in range(NT):
            self.load_x(ti)
            self.norm_stage(l, 0)
            hn = ["h%d" % k for k in range(8)]
            wr = self.wparts["gwin"] + hn
            for m in range(8):
                ps, pn = self.next_ps()
                self.mm(ps[:], pn, [(Win[:, k, m * 128:(m + 1) * 128], self.h[:, k, :]) for k in range(8)], wr)
                self.act(us[:, m, :], ps[:], AF.Copy, [pn], ["us%d" % m])
            for n in range(4):
                for hf in range(2):
                    ps, pn = self.next_ps()
                    self.mm(ps[:], pn, [(self.h[:, k, n * 128:(n + 1) * 128], Win[:, k, 1024 + hf * 512:1024 + (hf + 1) * 512])
                                        for k in range(8)], wr)
                    self.act(vsb[:, hf * 512:(hf + 1) * 512], ps[:], AF.Copy, [pn], ["vsb%d" % hf])
                    self.tt("pool", vsq, vsb[:, hf * 512:(hf + 1) * 512], vsb[:, hf * 512:(hf + 1) * 512], ALU.mult,
                            ["vsb%d" % hf], ["vsq"])
                    P.op("dve", lambda e, hf=hf: e.reduce_sum(out=self.ssv[:, hf:hf + 1], in_=vsq, axis=AX.X),
                         reads=["vsq"], writes=["ssv"])
                self.tt("dve", self.ssv[:, 2:3], self.ssv[:, 0:1], self.ssv[:, 1:2], ALU.add, ["ssv"], ["ssv2"])
                self.act(self.ssv[:, 3:4], self.ssv[:, 2:3], AF.Sqrt, ["ssv2", "epsc"], ["rv"], bias=self.epsc[:, 0:1], scale=1.0 / D)
                P.op("dve", lambda e: e.reciprocal(out=self.ssv[:, 3:4], in_=self.ssv[:, 3:4]), reads=["rv"], writes=["rv"])
                self.stt("dve", vn, vsb, self.ssv[:, 3:4], gain, ALU.mult, ALU.mult, ["vsb0", "vsb1", "rv", "gain"], ["vn"])
                for hq in range(2):
                    ps, pn = self.next_ps()
                    pairs, flags = [], []
                    for hh in range(4):
                        hd = hq * 4 + hh
                        pairs.append((vn[:, hd * 128:(hd + 1) * 128], wsT[:, hd, :]))
                        flags.append((ps[:, hh * 128:(hh + 1) * 128], True, True))
                    self.mm(None, pn, pairs, ["vn", "wsT"], flags=flags)
                    tb = tmpg[hq]
                    self.tt("dve", tb, ps[:], bias[:, hq * 512:(hq + 1) * 512], ALU.add, [pn, "bias"], [tmpgn[hq]])
                    self.tt("pool", gq[:, hq * 4:hq * 4 + 4, n * 128:(n + 1) * 128],
                            tb.rearrange("p (a b) -> p a b", a=4), us[:, hq * 4:hq * 4 + 4, n * 128:(n + 1) * 128], ALU.mult,
                            [tmpgn[hq]] + ["us%d" % m for m in range(hq * 4, hq * 4 + 4)],
                            ["a1_%d" % m for m in range(hq * 4, hq * 4 + 4)])
            self.out_proj(l, Wout, "gwout", gq, ["a1_%d" % m for m in range(8)])
            self.store_x(ti)

    def s5_prep(self):
        P = self.P
        d = self.dram
        c = self.s5c
        ARE, AIM, LDT, DT, MAG, ANG, SN, CS, AR, AI, T1, T2, T3, T4, FRE, FIM = [c[:, i, :] for i in range(16)]
        P.dma(ARE, d["are_c"], writes=["s5c"])
        P.dma(AIM, d["aim_c"], writes=["s5c"])
        P.dma(LDT, d["ldt_c"], writes=["s5c"])
        R = ["s5c"]
        self.act(DT, LDT, AF.Exp, R, R)
        self.tt("dve", T1, ARE, DT, ALU.mult, R, R)
        self.act(MAG, T1, AF.Exp, R, R)
        self.tt("dve", ANG, AIM, DT, ALU.mult, R, R)

        def sin_of(dst, src, shift):
            self.ts("dve", T2, src, shift, None, ALU.add, None, R, R)
            self.ts("dve", T3, T2, 1.0 / (2 * PI), None, ALU.mult, None, R, R)
            self.cp("dve", self.s5i[:], T3, R, ["s5i"])
            self.cp("dve", T3, self.s5i[:], ["s5i"], R)
            self.stt("dve", T2, T3, -2 * PI, T2, ALU.mult, ALU.add, R, R)
            self.ts("dve", T3, T2, PI, -2 * PI, ALU.is_gt, ALU.mult, R, R)
            self.tt("dve", T2, T2, T3, ALU.add, R, R)
            self.ts("dve", T3, T2, -PI, 2 * PI, ALU.is_lt, ALU.mult, R, R)
            self.tt("dve", T2, T2, T3, ALU.add, R, R)
            self.ts("dve", T2, T2, -3.141592, 3.141592, ALU.max, ALU.min, R, R)
            self.act(dst, T2, AF.Sin, R, R)

        sin_of(SN, ANG, 0.0)
        sin_of(CS, ANG, PI / 2)
        self.tt("dve", AR, MAG, CS, ALU.mult, R, R)
        self.tt("dve", AI, MAG, SN, ALU.mult, R, R)
        self.tt("dve", T1, ARE, ARE, ALU.mult, R, R)
        self.tt("dve", T2, AIM, AIM, ALU.mult, R, R)
        self.tt("dve", T1, T1, T2, ALU.add, R, R)
        P.op("dve", lambda e: e.reciprocal(out=T4, in_=T1), reads=R, writes=R)
        self.ts("dve", T3, AR, -1.0, None, ALU.add, None, R, R)
        self.tt("dve", T1, T3, ARE, ALU.mult, R, R)
        self.tt("dve", T2, AI, AIM, ALU.mult, R, R)
        self.tt("dve", T1, T1, T2, ALU.add, R, R)
        self.tt("dve", FRE, T1, T4, ALU.mult, R, R)
        self.tt("dve", T1, AI, ARE, ALU.mult, R, R)
        self.tt("dve", T2, T3, AIM, ALU.mult, R, R)
        self.tt("dve", T1, T1, T2, ALU.subtract, R, R)
        self.tt("dve", FIM, T1, T4, ALU.mult, R, R)
        pr, pi_, npi = self.pw[:, 0], self.pw[:, 1], self.pw[:, 2]
        W = ["pw"]
        self.cp("dve", pr[:, 0, :], CS, R, W)
        self.cp("dve", pi_[:, 0, :], SN, R, W)
        for k in range(1, 9):
            self.tt("dve", T1, pr[:, k - 1, :], pr[:, k - 1, :], ALU.mult, W + R, R)
            self.tt("dve", T2, pi_[:, k - 1, :], pi_[:, k - 1, :], ALU.mult, W + R, R)
            self.tt("dve", pr[:, k, :], T1, T2, ALU.subtract, R, W)
            self.tt("dve", T1, pr[:, k - 1, :], pi_[:, k - 1, :], ALU.mult, W + R, R)
            self.ts("dve", pi_[:, k, :], T1, 2.0, None, ALU.mult, None, R, W)
        self.ts("dve", npi, pi_, -1.0, None, ALU.mult, None, W, W)

    def s5_block(self, l):
        P = self.P
        d = self.dram
        P.barrier()
        self.s5_prep()
        Win = self.wview(self.WA, 0, 8, 1024)
        Wglu = self.wview(self.WA, 8192, 8, 1024)
        BW = self.WA[:, 16384:24576].rearrange("p (g r n) -> p g r n", g=32, r=2)
        CW = self.WA[:, 24576:32768].rearrange("p (g r n) -> p g r n", g=32, r=2)
        Wout = self.wview(self.WB, 0, 8, 1024)
        self.load_w("swin", Win, d["ssm_w_in"], 8, 1024)
        self.load_w("sglu", Wglu, d["ssm_glu_w"], 8, 1024)
        self.load_w("swout", Wout, d["ssm_w_out"], 8, 1024)
        c = self.s5c
        FRE, FIM = c[:, 14, :], c[:, 15, :]
        bwparts, cwparts = [], []
        for pc in range(8):
            psr, nr_ = self.next_ps()
            psi, ni_ = self.next_ps()
            for g4 in range(4):
                gp = pc * 4 + g4
                self.ts("dve", self.dg[:, 0, :], self.ident[:], FRE[:, gp:gp + 1], None, ALU.mult, None, ["s5c", "ident"], ["dg0"])
                self.ts("dve", self.dg[:, 1, :], self.ident[:], FIM[:, gp:gp + 1], None, ALU.mult, None, ["s5c", "ident"], ["dg1"])
                self.mm(psr[:, g4 * 128:(g4 + 1) * 128], nr_, [(self.ones_f[:], self.dg[:, 0, :])], ["dg0", "ones_f"],
                        flags=[(psr[:, g4 * 128:(g4 + 1) * 128], True, True)])
                self.mm(psi[:, g4 * 128:(g4 + 1) * 128], ni_, [(self.ones_f[:], self.dg[:, 1, :])], ["dg1", "ones_f"],
                        flags=[(psi[:, g4 * 128:(g4 + 1) * 128], True, True)])
            b1 = self.stage_rr
            b2 = (b1 + 1) % 3
            self.stage_rr = (b1 + 2) % 3
            s1, s2 = "stage%d" % b1, "stage%d" % b2
            bre = self.stage[:, b1, 0:512]
            bim = self.stage[:, b2, 0:512]
            P.dma(bre, d["Bre_z"][:, pc * 4:(pc + 1) * 4, :].rearrange("p g n -> p (g n)"), writes=[s1])
            P.dma(bim, d["Bim_z"][:, pc * 4:(pc + 1) * 4, :].rearrange("p g n -> p (g n)"), writes=[s2])
            t1, t2 = self.tmpn[:, 0, :], self.tmpn[:, 1, :]
            pn = "bw%d" % pc
            self.tt("dve", t1, psr[:], bre, ALU.mult, [nr_, s1], ["tmpn0"])
            self.tt("dve", t2, psi[:], bim, ALU.mult, [ni_, s2], ["tmpn1"])
            self.tt("dve", BW[:, pc * 4:(pc + 1) * 4, 0, :], t1.rearrange("p (g n) -> p g n", g=4),
                    t2.rearrange("p (g n) -> p g n", g=4), ALU.subtract, ["tmpn0", "tmpn1"], [pn + "r"])
            self.tt("dve", t1, psr[:], bim, ALU.mult, [nr_, s2], ["tmpn0"])
            self.tt("dve", t2, psi[:], bre, ALU.mult, [ni_, s1], ["tmpn1"])
            self.tt("dve", BW[:, pc * 4:(pc + 1) * 4, 1, :], t1.rearrange("p (g n) -> p g n", g=4),
                    t2.rearrange("p (g n) -> p g n", g=4), ALU.add, ["tmpn0", "tmpn1"], [pn + "i"])
            bwparts += [pn + "r", pn + "i"]
        for pc in range(8):
            for ri, key, sc in ((0, "Cre_z", 1.0), (1, "Cim_z", -1.0)):
                b1 = self.stage_rr
                self.stage_rr = (b1 + 1) % 3
                s1 = "stage%d" % b1
                P.dma(self.stage[:, b1, 0:512], d[key][:, pc * 4:(pc + 1) * 4, :].rearrange("p g n -> p (g n)"), writes=[s1])
                pn = "cwp%d_%d" % (pc, ri)
                self.act(CW[:, pc * 4:(pc + 1) * 4, ri, :], self.stage[:, b1, 0:512].rearrange("p (g n) -> p g n", g=4),
                         AF.Copy, [s1], [pn], scale=sc)
                cwparts.append(pn)
        K0 = self.KS0
        K1 = self.KS1
        SRI = self.WB[:, 14336:16384].bitcast(F32)
        BR = [K0[:, 0:512], K0[:, 1024:1536]]
        BI = [K0[:, 512:1024], K0[:, 1536:2048]]
        A2f = self.A2
        PRs = [K0[:, 2048:2560], A2f[:, 0:512]]
        PIs = [K0[:, 2560:3072], A2f[:, 512:1024]]
        TC = [K1[:, 0:512], K1[:, 1024:1536], A2f[:, 1024:1536]]
        TS = [K1[:, 512:1024], K1[:, 1536:2048], A2f[:, 1536:2048]]
        RR, RI = K1[:, 2048:2560], K1[:, 2560:3072]
        Dd = self.WB[:, 16384 + 4096:16384 + 4096 + 1024].rearrange("p (q n) -> p q n", q=8)
        sbf3 = self.WB[:, 16384 + 5120:16384 + 5120 + 1024].rearrange("p (r n) -> p r n", r=2)
        sbfs = [self.sbf[:, 0], self.sbf[:, 1], sbf3]
        for q_ in range(8):
            self.ts("dve", Dd[:, q_, :], self.ident[:], self.dsk[:, q_:q_ + 1], None, ALU.mult, None, ["ident", "dsk"], ["Dd"])
        SR, SI = SRI[:, 0:512], SRI[:, 512:1024]
        M1, M2 = self.rt[:, 0, :], self.rt[:, 1, :]
        scan_names = ["br0", "bi0", "br1", "bi1", "pr", "pi", "tc0", "ts0", "tc1", "ts1", "rr", "ri", "sr", "si"]
        self.memset("pool", K1, 0.0, ["tc0", "ts0", "tc1", "ts1", "rr", "ri", "stage0", "stage1", "stage2"])
        carn_all = ["car%d" % g for g in range(32)]
        self.memset("dve", self.car[:], 0.0, carn_all)
        uc, us_, uns = self.pw[:, 0], self.pw[:, 1], self.pw[:, 2]
        for gp in range(32):
            t = gp % 3
            tcn, tsn = "tc%d" % t, "ts%d" % t
            self.memset("dve", TC[t][:, 0:1], 1.0, [tcn])
            self.memset("dve", TS[t][:, 0:1], 0.0, [tsn])
            for k in range(9):
                n = 1 << k
                self.ts("dve", TC[t][:, n:2 * n], TC[t][:, 0:n], uc[:, k, gp:gp + 1], None, ALU.mult, None, [tcn, "pw"], [tcn])
                self.stt("dve", TC[t][:, n:2 * n], TS[t][:, 0:n], uns[:, k, gp:gp + 1], TC[t][:, n:2 * n], ALU.mult, ALU.add,
                         [tcn, tsn, "pw"], [tcn])
                self.ts("dve", TS[t][:, n:2 * n], TS[t][:, 0:n], uc[:, k, gp:gp + 1], None, ALU.mult, None, [tsn, "pw"], [tsn])
                self.stt("dve", TS[t][:, n:2 * n], TC[t][:, 0:n], us_[:, k, gp:gp + 1], TS[t][:, n:2 * n], ALU.mult, ALU.add,
                         [tcn, tsn, "pw"], [tsn])
            P.dma(self.tabC[gp], TC[t], reads=[tcn], writes=["tabC%d" % gp])
            P.dma(self.tabS[gp], TS[t], reads=[tsn], writes=["tabS%d" % gp])
        us = self.A2.rearrange("p (k n) -> p k n", k=8)
        yg = self.A3.rearrange("p (k n) -> p k n", k=8)
        ubf = self.A1[:, 0:8, :]
        ygb = self.A1[:, 8:16, :]
        qv = self.A1[:, 0:8, :]
        c = self.s5c
        MAG, SNt, CSt = c[:, 4, :], c[:, 6, :], c[:, 7, :]
        X1, X2, INITR, INITI = c[:, 10, :], c[:, 11, :], c[:, 12, :], c[:, 13, :]

        for ti in range(NT):
            self.load_x(ti)
            self.norm_stage(l, 0)
            hn = ["h%d" % k for k in range(8)]
            for m in range(8):
                ps, pn = self.next_ps()
                self.mm(ps[:], pn, [(Win[:, k, m * 128:(m + 1) * 128], self.h[:, k, :]) for k in range(8)],
                        self.wparts["swin"] + hn)
                self.cp("act" if m % 2 == 0 else "dve", ubf[:, m, :], ps[:], [pn], ["a1_%d" % m])
            self.tt("dve", X1, CSt, self.car[:, 0, :], ALU.mult, ["s5c"] + carn_all, ["x1"])
            self.tt("dve", X2, SNt, self.car[:, 1, :], ALU.mult, ["s5c"] + carn_all, ["x2"])
            self.tt("dve", INITR, X1, X2, ALU.subtract, ["x1", "x2"], ["initr"])
            self.tt("dve", X1, CSt, self.car[:, 1, :], ALU.mult, ["s5c"] + carn_all, ["x1"])
            self.tt("dve", X2, SNt, self.car[:, 0, :], ALU.mult, ["s5c"] + carn_all, ["x2"])
            self.tt("dve", INITI, X1, X2, ALU.add, ["x1", "x2"], ["initi"])

            def bu(gp, part):
                q = gp // 4
                s = gp % 2
                t = gp % 3
                pb = gp % 2
                s3 = gp % 3
                PRb, PIb = PRs[pb], PIs[pb]
                prn, pin = "pr%d" % pb, "pi%d" % pb
                brn, bin_, tcn, tsn = "br%d" % s, "bi%d" % s, "tc%d" % t, "ts%d" % t
                if part == 0:
                    psr, nr_ = self.next_ps()
                    psi, ni_ = self.next_ps()
                    self.mm(psr[:], nr_, [(BW[:, gp, 0, :], ubf[:, q, :])], bwparts + ["a1_%d" % q])
                    self.mm(psi[:], ni_, [(BW[:, gp, 1, :], ubf[:, q, :])], bwparts + ["a1_%d" % q])
                    self.cp("act", BR[s], psr[:], [nr_], [brn])
                    self.cp("act", BI[s], psi[:], [ni_], [bin_])
                    P.dma(TC[t], self.tabC[gp], reads=["tabC%d" % gp], writes=[tcn])
                    P.dma(TS[t], self.tabS[gp], reads=["tabS%d" % gp], writes=[tsn])
                    self.tt("dve", PRb, BR[s], TC[t], ALU.mult, [brn, tcn], [prn])
                    self.tt("dve", M1, BI[s], TS[t], ALU.mult, [bin_, tsn], ["rt0"])
                    self.tt("dve", PRb, PRb, M1, ALU.add, [prn, "rt0"], [prn])
                    self.tt("dve", PIb, BI[s], TC[t], ALU.mult, [bin_, tcn], [pin])
                    self.tt("dve", M1, BR[s], TS[t], ALU.mult, [brn, tsn], ["rt0"])
                    self.tt("dve", PIb, PIb, M1, ALU.subtract, [pin, "rt0"], [pin])
                    return
                if part == 2:
                    carn = "car%d" % gp
                    self.cp("act", sbfs[s3][:, 0, :], SR, ["sr"], ["sbf%d" % s3])
                    self.cp("act", sbfs[s3][:, 1, :], SI, ["si"], ["sbf%d" % s3])
                    self.cp("act", self.car[:, 0, gp:gp + 1], SR[:, 511:512], ["sr"], [carn])
                    self.cp("act", self.car[:, 1, gp:gp + 1], SI[:, 511:512], ["si"], [carn])
                    return
                rho = MAG[:, gp:gp + 1].broadcast_to([128, TT])
                P.op("dve", lambda e: e.tensor_tensor_scan(out=RR, data0=rho, data1=PRb, initial=INITR[:, gp:gp + 1],
                                                           op0=ALU.mult, op1=ALU.add),
                     reads=[prn, "s5c", "initr"], writes=["rr"])
                P.op("dve", lambda e: e.tensor_tensor_scan(out=RI, data0=rho, data1=PIb, initial=INITI[:, gp:gp + 1],
                                                           op0=ALU.mult, op1=ALU.add),
                     reads=[pin, "s5c", "initi"], writes=["ri"])
                self.tt("dve", SR, RR, TC[t], ALU.mult, ["rr", tcn], ["sr"])
                self.tt("dve", M2, RI, TS[t], ALU.mult, ["ri", tsn], ["rt1"])
                self.tt("dve", SR, SR, M2, ALU.subtract, ["sr", "rt1"], ["sr"])
                self.tt("dve", SI, RR, TS[t], ALU.mult, ["rr", tsn], ["si"])
                self.tt("dve", M2, RI, TC[t], ALU.mult, ["ri", tcn], ["rt1"])
                self.tt("dve", SI, SI, M2, ALU.add, ["si", "rt1"], ["si"])

            def outp(gp):
                q = gp // 4
                s3 = gp % 3
                psY, nY = self.ps[6 + (q % 2)], "ps%d" % (6 + (q % 2))
                pairs = [(CW[:, gp, 0, :], sbfs[s3][:, 0, :]), (CW[:, gp, 1, :], sbfs[s3][:, 1, :])]
                flags = [(psY[:], False, False), (psY[:], False, gp % 4 == 3)]
                rd = cwparts + ["sbf%d" % s3]
                if gp % 4 == 0:
                    pairs = [(Dd[:, q, :], ubf[:, q, :])] + pairs
                    flags = [(psY[:], True, False)] + flags
                    rd = rd + ["Dd", "a1_%d" % q]
                self.mm(None, nY, pairs, rd, flags=flags)
                if gp % 4 == 3:
                    self.act(yg[:, q, :], psY[:], AF.Gelu_apprx_tanh, [nY], ["yg%d" % q])
                    self.cp("act", ygb[:, q, :], yg[:, q, :], ["yg%d" % q], ["a1_%d" % (8 + q)])

            for i in range(-2, 33):
                if 0 <= i + 2 < 32:
                    bu(i + 2, 0)
                if 0 <= i < 32:
                    bu(i, 2)
                if 0 <= i + 1 < 32:
                    bu(i + 1, 1)
                if 0 <= i < 32:
                    outp(i)
            ygn = ["a1_%d" % (8 + k) for k in range(8)]
            for mo in range(8):
                ps, pn = self.next_ps()
                self.mm(ps[:], pn, [(Wglu[:, k, mo * 128:(mo + 1) * 128], ygb[:, k, :]) for k in range(8)],
                        self.wparts["sglu"] + ygn)
                b = mo % 2
                self.act(self.rt[:, b, :], ps[:], AF.Sigmoid, [pn, "glub"], ["rt%d" % b], bias=self.glub[:, mo:mo + 1])
                self.tt("dve", qv[:, mo, :], yg[:, mo, :], self.rt[:, b, :], ALU.mult, ["yg%d" % mo, "rt%d" % b], ["a1_%d" % mo])
            self.out_proj(l, Wout, "swout", qv, ["a1_%d" % m for m in range(8)])
            self.store_x(ti)

    def build(self):
        nc = self.nc
        self.declare()
        with ExitStack() as st:
            self.alloc(st)
            self.P = Prog(nc, st)
            block = st.enter_context(nc.Block())
            self.prologue()
            for l in range(self.n_layers):
                self.ada_stage(l)
            for l in range(self.n_layers):
                kind = l % 3
                if kind == 0:
                    self.conv_block(l, l // 3, first=(l == 0))
                elif kind == 1:
                    self.s5_block(l)
                else:
                    self.sg_block(l)
                self.ffn_block(l, last=(l == self.n_layers - 1))
            self.P.finish()
            self.P.emit(block)
        return nc


def tT(v):
    v = np.asarray(v, np.float32)
    lead = v.shape[:-1]
    r = v.reshape(lead + (8, 128))
    return np.ascontiguousarray(np.moveaxis(r, -1, 0))


def host_layout(inp, b):
    f32 = np.float32
    m = {}
    m["x"] = np.ascontiguousarray(inp["x"][b], f32)
    m["cT"] = tT(inp["c"][b])
    m["ada_w"] = np.ascontiguousarray(inp["ada_w"], f32)
    ab = np.asarray(inp["ada_b"], f32).reshape(DEPTH, 48, 128)
    m["ada_bT"] = np.ascontiguousarray(ab.transpose(2, 0, 1))
    m["n1gT"] = tT(inp["norm1_g"])
    m["n2gT"] = tT(inp["norm2_g"])
    m["fgT"] = tT(inp["final_g"])
    m["ff_w1"] = np.ascontiguousarray(inp["ff_w1"], f32)
    m["ff_w2"] = np.ascontiguousarray(inp["ff_w2"], f32)
    m["conv_w_in"] = np.ascontiguousarray(inp["conv_w_in"], f32)
    m["conv_wT"] = tT(inp["conv_w"])
    m["conv_bT"] = tT(inp["conv_b"])
    m["conv_w_out"] = np.ascontiguousarray(inp["conv_w_out"], f32)
    m["ssm_w_in"] = np.ascontiguousarray(inp["ssm_w_in"][0], f32)
    m["ssm_glu_w"] = np.ascontiguousarray(inp["ssm_glu_w"][0], f32)
    m["ssm_w_out"] = np.ascontiguousarray(inp["ssm_w_out"][0], f32)
    m["glu_bT"] = tT(inp["ssm_glu_b"][0])
    m["dT"] = tT(inp["ssm_d"][0])
    def compact(a):
        a = np.asarray(a, f32).reshape(32, 2, 64)
        return np.ascontiguousarray(a.transpose(1, 2, 0).reshape(128, 32))
    m["are_c"] = compact(inp["ssm_a_re"][0])
    m["aim_c"] = compact(inp["ssm_a_im"][0])
    m["ldt_c"] = compact(np.broadcast_to(np.asarray(inp["ssm_log_dt"][0], f32)[:, None], (64, 64)))
    bre = np.asarray(inp["ssm_b_re"][0], f32)
    bim = np.asarray(inp["ssm_b_im"][0], f32)
    cre = np.asarray(inp["ssm_c_re"][0], f32)
    cim = np.asarray(inp["ssm_c_im"][0], f32)
    Bre_z = np.zeros((128, 32, 128), f32); Bim_z = np.zeros((128, 32, 128), f32)
    Cre_z = np.zeros((128, 32, 128), f32); Cim_z = np.zeros((128, 32, 128), f32)
    for g in range(64):
        gp, g2, g8 = g // 2, g % 2, g % 8
        Bre_z[g8 * 16:(g8 + 1) * 16, gp, g2 * 64:(g2 + 1) * 64] = bre[g].T
        Bim_z[g8 * 16:(g8 + 1) * 16, gp, g2 * 64:(g2 + 1) * 64] = bim[g].T
        Cre_z[g2 * 64:(g2 + 1) * 64, gp, g8 * 16:(g8 + 1) * 16] = cre[g].T
        Cim_z[g2 * 64:(g2 + 1) * 64, gp, g8 * 16:(g8 + 1) * 16] = cim[g].T
    m["Bre_z"], m["Bim_z"], m["Cre_z"], m["Cim_z"] = Bre_z, Bim_z, Cre_z, Cim_z
    m["sg_w_in"] = np.ascontiguousarray(inp["sg_w_in"][0], f32)
    m["sg_w_out"] = np.ascontiguousarray(inp["sg_w_out"][0], f32)
    m["sg_w_s"] = np.ascontiguousarray(inp["sg_w_s"][0], f32)
    m["sg_bias_rep"] = np.ascontiguousarray(np.broadcast_to(np.asarray(inp["sg_b_s"][0], f32)[None], (128, 8, 128)))
    m["sg_gain_rep"] = np.ascontiguousarray(np.broadcast_to(np.asarray(inp["sg_v_g"][0], f32)[None], (128, D)))
    m["ident"] = np.eye(128, dtype=f32)
    m["tril"] = np.tril(np.ones((128, 128), f32))
    return m


_NC_CACHE = {}


def kernel(_n_layers=DEPTH, **inputs):
    inp = {k: np.asarray(v) for k, v in inputs.items()}
    if _n_layers not in _NC_CACHE:
        _NC_CACHE[_n_layers] = Builder(_n_layers).build()
    nc = _NC_CACHE[_n_layers]
    in_maps = [host_layout(inp, b) for b in range(8)]
    res = run_bass_kernel_spmd(nc, in_maps, core_ids=list(range(8)))
    out = np.stack([np.asarray(r["out"], np.float32) for r in res.results], axis=0)
    return out
```

```python
import math
from contextlib import ExitStack

import numpy as np
import concourse.bass as bass
import concourse.mybir as mybir
from concourse.bass_utils import run_bass_kernel_spmd

F32 = mybir.dt.float32
BF16 = mybir.dt.bfloat16
I32 = mybir.dt.int32
AF = mybir.ActivationFunctionType
ALU = mybir.AluOpType
AX = mybir.AxisListType

D = 1024
L = 4096
DEPTH = 4
TT = 512
NT = L // TT
KC = 8
EPS = 1e-6
ENGS = ("pe", "act", "dve", "pool", "sp")
N_DMA_SEMS = 14
PI = math.pi


class Prog:
    def __init__(self, nc, stack):
        self.nc = nc
        self.stream = {e: [] for e in ENGS}
        self.count = {e: 0 for e in ENGS}
        self.sem = {e: stack.enter_context(nc.semaphore("s_" + e)) for e in ENGS if e != "sp"}
        self.dsem = [stack.enter_context(nc.semaphore("d%d" % i)) for i in range(N_DMA_SEMS)]
        self.dval = [0] * N_DMA_SEMS
        self.dnext = 0
        self.waited = {e: {} for e in ENGS}
        self.last_w = {}
        self.readers = {}

    def _deps(self, eng, reads, writes, same_engine_ok=False):
        evs = []
        for r in reads:
            ev = self.last_w.get(r)
            if ev is not None:
                evs.append((ev, False))
        for w in writes:
            ev = self.last_w.get(w)
            if ev is not None:
                evs.append((ev, False))
            for ev in self.readers.get(w, {}).values():
                evs.append((ev, True))
        for (ev, is_war) in evs:
            owner, key, sem, val = ev
            if owner == eng and (same_engine_ok or is_war):
                continue
            if self.waited[eng].get(key, 0) >= val:
                continue
            self.waited[eng][key] = val
            self.stream[eng].append(("wait", sem, val))

    def _record(self, rkey, ev, reads, writes):
        for w in writes:
            self.last_w[w] = ev
            self.readers[w] = {}
        for r in reads:
            if r in writes:
                continue
            self.readers.setdefault(r, {})[rkey] = ev

    def op(self, eng, fn, reads=(), writes=()):
        self.group(eng, [fn], reads, writes)

    def group(self, eng, fns, reads=(), writes=()):
        psr = [r for r in reads if r.startswith("ps") and r not in writes]
        if psr:
            writes = list(writes) + psr
        self._deps(eng, reads, writes, same_engine_ok=(eng == "pe"))
        for fn in fns[:-1]:
            self.stream[eng].append(("ins", fn, None))
        self.count[eng] += 1
        ev = (eng, eng, self.sem[eng], self.count[eng])
        self.stream[eng].append(("ins", fns[-1], self.sem[eng]))
        self._record(eng, ev, reads, writes)

    def dma(self, out, in_, reads=(), writes=(), eng="sp"):
        i = self.dnext
        self.dnext = (self.dnext + 1) % N_DMA_SEMS
        key = "dma%d" % i
        sem = self.dsem[i]
        if self.dval[i] > 0 and self.waited[eng].get(key, 0) < self.dval[i]:
            self.waited[eng][key] = self.dval[i]
            self.stream[eng].append(("wait", sem, self.dval[i]))
        self._deps(eng, reads, writes)
        self.dval[i] += 16
        ev = ("dmaq", key, sem, self.dval[i])
        self.stream[eng].append(("dma", out, in_, sem))
        self._record("dmaq_" + key, ev, reads, writes)

    def barrier(self):
        for e in ENGS:
            for o in ENGS:
                if o != e and o != "sp" and self.count[o] > 0 and self.waited[e].get(o, 0) < self.count[o]:
                    self.waited[e][o] = self.count[o]
                    self.stream[e].append(("wait", self.sem[o], self.count[o]))
            for i in range(N_DMA_SEMS):
                key = "dma%d" % i
                if self.dval[i] > 0 and self.waited[e].get(key, 0) < self.dval[i]:
                    self.waited[e][key] = self.dval[i]
                    self.stream[e].append(("wait", self.dsem[i], self.dval[i]))

    def finish(self):
        for i in range(N_DMA_SEMS):
            if self.dval[i] > 0:
                self.stream["sp"].append(("wait", self.dsem[i], self.dval[i]))
        for e in ENGS:
            if e != "sp" and self.count[e] > 0:
                self.stream["sp"].append(("wait", self.sem[e], self.count[e]))

    def emit(self, block):
        def run(engh, items):
            for it in items:
                if it[0] == "wait":
                    engh.wait_ge(it[1], it[2])
                elif it[0] == "ins":
                    ins = it[1](engh)
                    if it[2] is not None:
                        ins.then_inc(it[2], 1)
                else:
                    engh.dma_start(out=it[1], in_=it[2]).then_inc(it[3], 16)

        @block.sync
        def _(e):
            run(e, self.stream["sp"])

        @block.tensor
        def _(e):
            run(e, self.stream["pe"])

        @block.scalar
        def _(e):
            run(e, self.stream["act"])

        @block.vector
        def _(e):
            run(e, self.stream["dve"])

        @block.gpsimd
        def _(e):
            run(e, self.stream["pool"])


class Builder:
    def __init__(self, n_layers=DEPTH):
        self.n_layers = n_layers
        self.nc = bass.Bass("TRN2", target_bir_lowering=False)
        self.dram = {}
        self.rr = 0
        self.cast_rr = 0
        self.stage_rr = 0
        self.wparts = {}

    def din(self, name, shape):
        self.dram[name] = self.nc.dram_tensor(name, list(shape), F32, kind="ExternalInput").ap()

    def declare(self):
        din = self.din
        din("x", [L, D]); din("cT", [128, 8]); din("ada_w", [DEPTH, D, 6 * D]); din("ada_bT", [128, DEPTH, 48])
        din("n1gT", [128, DEPTH, 8]); din("n2gT", [128, DEPTH, 8]); din("fgT", [128, 8])
        din("ff_w1", [DEPTH, D, 4 * D]); din("ff_w2", [DEPTH, 4 * D, D])
        din("conv_w_in", [2, D, 3 * D]); din("conv_wT", [128, 2, 3, 8]); din("conv_bT", [128, 2, 8])
        din("conv_w_out", [2, D, D])
        din("ssm_w_in", [D, D]); din("ssm_glu_w", [D, D]); din("ssm_w_out", [D, D])
        din("glu_bT", [128, 8]); din("dT", [128, 8])
        din("are_c", [128, 32]); din("aim_c", [128, 32]); din("ldt_c", [128, 32])
        din("Bre_z", [128, 32, 128]); din("Bim_z", [128, 32, 128])
        din("Cre_z", [128, 32, 128]); din("Cim_z", [128, 32, 128])
        din("sg_w_in", [D, 2 * D]); din("sg_w_out", [D, D]); din("sg_w_s", [8, 128, 128])
        din("sg_bias_rep", [128, 8, 128]); din("sg_gain_rep", [128, D])
        din("ident", [128, 128]); din("tril", [128, 128])
        self.out = self.nc.dram_tensor("out", [L, D], F32, kind="ExternalOutput").ap()
        self.xT = self.nc.dram_tensor("xT_scr", [D, L], F32, kind="Internal").ap()
        self.tabC = self.nc.dram_tensor("tabC_scr", [32, 128, TT], F32, kind="Internal").ap()
        self.tabS = self.nc.dram_tensor("tabS_scr", [32, 128, TT], F32, kind="Internal").ap()

    def alloc(self, st):
        nc = self.nc

        def sb(name, shape, dt):
            return st.enter_context(nc.sbuf_tensor("sb_" + name, list(shape), dt))

        self.WA = sb("WA", [128, 32768], BF16)
        self.WB = sb("WB", [128, 32768], BF16)
        self.stage = sb("stage", [128, 3, 1024], F32)
        self.xt = sb("xt", [128, 8, TT], F32)
        self.h = sb("h", [128, 8, TT], BF16)
        self.sq = sb("sq", [128, 2, TT], BF16)
        self.tmpn = sb("tmpn", [128, 2, TT], F32)
        self.rt = sb("rt", [128, 2, TT], F32)
        self.rstd = sb("rstd", [128, TT], F32)
        self.A1 = sb("A1", [128, 16, TT], BF16)
        self.sbf = sb("sbf", [128, 2, 2, TT], BF16)
        self.ident = sb("ident", [128, 128], F32)
        self.tril = sb("tril", [128, 128], F32)
        self.ones_bf = sb("ones_bf", [128, 128], BF16)
        self.ones_f = sb("ones_f", [128, 128], F32)
        self.epsc = sb("epsc", [128, 1], F32)
        self.cT = sb("cT", [128, 8], F32)
        self.cab = sb("cab", [128, 8], BF16)
        self.modT = sb("modT", [128, DEPTH, 48], F32)
        self.adab = sb("adab", [128, DEPTH, 48], F32)
        self.n1g = sb("n1g", [128, DEPTH, 8], F32)
        self.n2g = sb("n2g", [128, DEPTH, 8], F32)
        self.fg = sb("fg", [128, 8], F32)
        self.aT = sb("aT", [128, DEPTH, 2, 8], F32)
        self.cw = sb("cw", [128, 2, 3, 8], F32)
        self.cb = sb("cb", [128, 2, 8], F32)
        self.zc = sb("zc", [128, 8, 2], F32)
        self.glub = sb("glub", [128, 8], F32)
        self.dsk = sb("dsk", [128, 8], F32)
        self.s5c = sb("s5c", [128, 16, 32], F32)
        self.s5i = sb("s5i", [128, 32], I32)
        self.pw = sb("pw", [128, 3, 9, 32], F32)
        self.car = sb("car", [128, 2, 32], F32)
        self.ssv = sb("ssv", [128, 8], F32)
        self.dg = sb("dg", [128, 2, 128], F32)
        self.WAf = self.WA[:, :].bitcast(F32)
        self.A2 = self.WB[:, 16384:24576].bitcast(F32)
        self.A3 = self.WB[:, 24576:32768].bitcast(F32)
        self.KS0 = self.WB[:, 8192:14336].bitcast(F32)
        self.KS1 = self.stage[:, :, :].rearrange("p a b -> p (a b)")
        self.rowtmp = self.rt[0:1, :, :].rearrange("p a b -> p (a b)")
        self.ps = [st.enter_context(nc.psum_tensor("ps%d" % i, [128, 512], F32)) for i in range(8)]

    def xc(self, buf, k):
        if buf == 0:
            return self.xt[:, k, :]
        if k < 6:
            return self.KS1[:, k * 512:(k + 1) * 512]
        return self.sbf[:, :, :, :].rearrange("p a b c -> p (a b c)").bitcast(F32)[:, (k - 6) * 512:(k - 5) * 512]

    def xn(self, buf, k):
        return ("xt%d" % k) if buf == 0 else ("xu%d" % k)

    def xalias(self, buf, k):
        if buf == 0:
            return []
        return ["stage%d" % (k // 2)] if k < 6 else ["sbf0", "sbf1"]

    def next_ps(self):
        i = self.rr
        self.rr = (self.rr + 1) % 6
        return self.ps[i], "ps%d" % i

    def mm(self, out_ps, psname, pairs, reads, flags=None):
        fns = []
        n = len(pairs)
        for i, (a, b) in enumerate(pairs):
            if flags is None:
                o, s0, s1 = out_ps, (i == 0), (i == n - 1)
            else:
                o, s0, s1 = flags[i]
            fns.append(lambda e, o=o, a=a, b=b, s0=s0, s1=s1: e.matmul(o, lhsT=a, rhs=b, start=s0, stop=s1))
        self.P.group("pe", fns, reads=reads, writes=[psname])

    def act(self, out, in_, func, reads, writes, bias=None, scale=1.0):
        if bias is None:
            fn = lambda e: e.activation(out=out, in_=in_, func=func, scale=scale)
        else:
            fn = lambda e: e.activation(out=out, in_=in_, func=func, bias=bias, scale=scale)
        self.P.op("act", fn, reads=reads, writes=writes)

    def stt(self, eng, out, in0, scalar, in1, op0, op1, reads, writes):
        self.P.op(eng, lambda e: e.scalar_tensor_tensor(out=out, in0=in0, scalar=scalar, in1=in1, op0=op0, op1=op1),
                  reads=reads, writes=writes)

    def tt(self, eng, out, in0, in1, op, reads, writes):
        self.P.op(eng, lambda e: e.tensor_tensor(out=out, in0=in0, in1=in1, op=op), reads=reads, writes=writes)

    def ts(self, eng, out, in0, s1, s2, op0, op1, reads, writes):
        if s2 is None:
            fn = lambda e: e.tensor_scalar(out=out, in0=in0, scalar1=s1, scalar2=None, op0=op0)
        else:
            fn = lambda e: e.tensor_scalar(out=out, in0=in0, scalar1=s1, scalar2=s2, op0=op0, op1=op1)
        self.P.op(eng, fn, reads=reads, writes=writes)

    def cp(self, eng, out, in_, reads, writes):
        if eng == "act":
            self.act(out, in_, AF.Copy, reads, writes)
        else:
            self.P.op(eng, lambda e: e.tensor_copy(out=out, in_=in_), reads=reads, writes=writes)

    def memset(self, eng, ap, val, writes):
        self.P.op(eng, lambda e: e.memset(ap, val), writes=writes)

    def load_w(self, name, dst3, src2, K, N, scale=None, xt_stage=True):
        parts = []
        bufs = [(self.stage[:, i, :], ["stage%d" % i]) for i in range(3)]
        if xt_stage:
            for i in range(4):
                bufs.append((self.xt[:, 2 * i:2 * i + 2, :].rearrange("p a b -> p (a b)"), ["xt%d" % (2 * i), "xt%d" % (2 * i + 1)]))
        for k in range(K):
            for c0 in range(0, N, 1024):
                w = min(1024, N - c0)
                self.wl_rr = (getattr(self, "wl_rr", 0) + 1) % len(bufs)
                sb_, snames = bufs[self.wl_rr]
                self.P.dma(sb_[:, 0:w], src2[k * 128:(k + 1) * 128, c0:c0 + w], writes=snames)
                pn = "%s_%d_%d" % (name, k, c0)
                eng = ("act", "dve")[self.cast_rr % 2]
                self.cast_rr += 1
                if scale is None:
                    self.cp(eng, dst3[:, k, c0:c0 + w], sb_[:, 0:w], snames, [pn])
                else:
                    self.act(dst3[:, k, c0:c0 + w], sb_[:, 0:w], AF.Copy, snames, [pn], scale=scale)
                parts.append(pn)
        self.wparts[name] = parts
        return parts

    def wview(self, buf, c0, K, N):
        return buf[:, c0:c0 + K * N].rearrange("p (k n) -> p k n", k=K)

    def prologue(self):
        P = self.P
        d = self.dram
        P.dma(self.ident[:], d["ident"], writes=["ident"])
        P.dma(self.tril[:], d["tril"], writes=["tril"])
        P.dma(self.cT[:], d["cT"], writes=["cT"])
        P.dma(self.adab[:], d["ada_bT"], writes=["adab"])
        P.dma(self.n1g[:], d["n1gT"], writes=["n1g"])
        P.dma(self.n2g[:], d["n2gT"], writes=["n2g"])
        P.dma(self.fg[:], d["fgT"], writes=["fg"])
        P.dma(self.cw[:], d["conv_wT"], writes=["cw"])
        P.dma(self.cb[:], d["conv_bT"], writes=["cb"])
        P.dma(self.glub[:], d["glu_bT"], writes=["glub"])
        P.dma(self.dsk[:], d["dT"], writes=["dsk"])
        self.memset("pool", self.ones_bf[:], 1.0, ["ones_bf"])
        self.memset("pool", self.ones_f[:], 1.0, ["ones_f"])
        self.memset("pool", self.epsc[:], EPS, ["epsc"])
        self.act(self.s5c[:, 0, 0:8], self.cT[:], AF.Sigmoid, ["cT"], ["sgc"])
        self.tt("dve", self.cab[:], self.cT[:], self.s5c[:, 0, 0:8], ALU.mult, ["cT", "sgc"], ["cab"])

    def ada_stage(self, l):
        P = self.P
        wtmp = self.A1
        for cbk in range(6):
            psa, na = self.next_ps()
            psb, nb = self.next_ps()
            for k in range(8):
                self.ada_rr = (getattr(self, "ada_rr", 0) + 1) % 7
                if self.ada_rr < 3:
                    stg_ap, snames = self.stage[:, self.ada_rr, :], ["stage%d" % self.ada_rr]
                else:
                    i2 = self.ada_rr - 3
                    stg_ap = self.xt[:, 2 * i2:2 * i2 + 2, :].rearrange("p a b -> p (a b)")
                    snames = ["xt%d" % (2 * i2), "xt%d" % (2 * i2 + 1)]
                P.dma(stg_ap, self.dram["ada_w"][l, k * 128:(k + 1) * 128, cbk * 1024:(cbk + 1) * 1024],
                      writes=snames)
                wb = (cbk * 8 + k) % 4
                wt = self.A1[:, 2 * wb:2 * wb + 2, :].rearrange("p a b -> p (a b)")
                wn = ["a1_%d" % (2 * wb), "a1_%d" % (2 * wb + 1)]
                eng = ("act", "dve")[self.cast_rr % 2]
                self.cast_rr += 1
                self.cp(eng, wt, stg_ap, snames, wn)
                lhs = self.cab[:, k:k + 1]
                self.P.group("pe", [
                    (lambda e, o=psa[0:1, :], a=lhs, r=wt[:, 0:512], s0=(k == 0), s1=(k == 7):
                     e.matmul(o, lhsT=a, rhs=r, start=s0, stop=s1)),
                    (lambda e, o=psb[0:1, :], a=lhs, r=wt[:, 512:1024], s0=(k == 0), s1=(k == 7):
                     e.matmul(o, lhsT=a, rhs=r, start=s0, stop=s1)),
                ], reads=wn + ["cab"], writes=[na, nb])
            self.act(self.rowtmp[0:1, 0:512], psa[0:1, :], AF.Copy, [na], ["rt0", "rt1"])
            self.act(self.rowtmp[0:1, 512:1024], psb[0:1, :], AF.Copy, [nb], ["rt0", "rt1"])
            pst, nt = self.next_ps()
            fns = []
            for m in range(8):
                fns.append(lambda e, o=pst[:, m:m + 1], a=self.rowtmp[0:1, m * 128:(m + 1) * 128], r=self.ones_f[0:1, 0:1]:
                           e.matmul(o, lhsT=a, rhs=r, start=True, stop=True))
            P.group("pe", fns, reads=["rt0", "rt1", "ones_f"], writes=[nt])
            self.tt("dve", self.modT[:, l, cbk * 8:(cbk + 1) * 8], pst[:, 0:8], self.adab[:, l, cbk * 8:(cbk + 1) * 8],
                    ALU.add, [nt, "adab"], ["modT%d" % l])
        mn = "modT%d" % l
        self.ts("dve", self.aT[:, l, 0, :], self.modT[:, l, 8:16], 1.0, None, ALU.add, None, [mn], ["aT%d" % l])
        self.tt("dve", self.aT[:, l, 0, :], self.aT[:, l, 0, :], self.n1g[:, l, :], ALU.mult, ["aT%d" % l, "n1g"], ["aT%d" % l])
        self.ts("dve", self.aT[:, l, 1, :], self.modT[:, l, 32:40], 1.0, None, ALU.add, None, [mn], ["aT%d" % l])
        self.tt("dve", self.aT[:, l, 1, :], self.aT[:, l, 1, :], self.n2g[:, l, :], ALU.mult, ["aT%d" % l, "n2g"], ["aT%d" % l])

    def load_x_first(self, ti):
        P = self.P
        for tb in range(4):
            b = self.stage_rr
            self.stage_rr = (self.stage_rr + 1) % 3
            sname = "stage%d" % b
            r0 = (ti * 4 + tb) * 128
            P.dma(self.stage[:, b, :], self.dram["x"][r0:r0 + 128, :], writes=[sname])
            for hf in range(2):
                ps, pn = self.next_ps()
                fns = []
                for kk in range(4):
                    k = hf * 4 + kk
                    fns.append(lambda e, o=ps[:, kk * 128:(kk + 1) * 128], i=self.stage[:, b, k * 128:(k + 1) * 128]:
                               e.transpose(o, i, self.ident[:]))
                P.group("pe", fns, reads=[sname, "ident"], writes=[pn])
                eng = "dve" if hf == 0 else "act"
                self.cp(eng, self.xt[:, hf * 4:hf * 4 + 4, tb * 128:(tb + 1) * 128],
                        ps[:, :].rearrange("p (a b) -> p a b", a=4), [pn], ["xt%d" % k for k in range(hf * 4, hf * 4 + 4)])

    def load_x(self, ti, buf=0):
        for k in range(8):
            self.P.dma(self.xc(buf, k), self.xT[k * 128:(k + 1) * 128, ti * TT:(ti + 1) * TT],
                       reads=["xT_%d_%d" % (k, ti)], writes=[self.xn(buf, k)] + self.xalias(buf, k))

    def store_x(self, ti, buf=0):
        for k in range(8):
            self.P.dma(self.xT[k * 128:(k + 1) * 128, ti * TT:(ti + 1) * TT], self.xc(buf, k),
                       reads=[self.xn(buf, k)], writes=["xT_%d_%d" % (k, ti)])

    def sumsq_rstd(self, buf=0):
        P = self.P
        pss, pn = self.ps[6], "ps6"
        for k in range(8):
            b = k % 2
            self.tt("pool", self.sq[:, b, :], self.xc(buf, k), self.xc(buf, k), ALU.mult, [self.xn(buf, k)], ["sq%d" % b])
            P.group("pe", [lambda e, b=b, k=k: e.matmul(pss[:], lhsT=self.ones_bf[:], rhs=self.sq[:, b, :],
                                                         start=(k == 0), stop=(k == 7))],
                    reads=["sq%d" % b, "ones_bf"], writes=[pn])
        self.act(self.rstd[:], pss[:], AF.Sqrt, [pn, "epsc"], ["rstd"], bias=self.epsc[:, 0:1], scale=1.0 / D)
        P.op("dve", lambda e: e.reciprocal(out=self.rstd[:], in_=self.rstd[:]), reads=["rstd"], writes=["rstd"])

    def norm_stage(self, l, which, buf=0):
        self.sumsq_rstd(buf)
        an = "aT%d" % l
        mn = "modT%d" % l
        sh0 = 0 if which == 0 else 24
        for k in range(8):
            b = k % 2
            self.stt("dve", self.tmpn[:, b, :], self.xc(buf, k), self.aT[:, l, which, k:k + 1], self.rstd[:],
                     ALU.mult, ALU.mult, [self.xn(buf, k), an, "rstd"], ["tmpn%d" % b])
            self.act(self.h[:, k, :], self.tmpn[:, b, :], AF.Identity, ["tmpn%d" % b, mn], ["h%d" % k],
                     bias=self.modT[:, l, sh0 + k:sh0 + k + 1])

    def final_stage(self, ti, buf=0):
        P = self.P
        self.sumsq_rstd(buf)
        for k in range(8):
            self.stt("dve", self.xc(buf, k), self.xc(buf, k), self.fg[:, k:k + 1], self.rstd[:],
                     ALU.mult, ALU.mult, [self.xn(buf, k), "fg", "rstd"], [self.xn(buf, k)])
        for tb in range(4):
            sl = 4 * (tb % 2)
            ot = self.A1[:, sl:sl + 4, :].rearrange("p a b -> p (a b)").bitcast(F32)
            otn = ["a1_%d" % i for i in range(sl, sl + 4)]
            for hf in range(2):
                ps, pn = self.next_ps()
                fns = []
                for kk in range(4):
                    k = hf * 4 + kk
                    fns.append(lambda e, o=ps[:, kk * 128:(kk + 1) * 128], i=self.xc(buf, k)[:, tb * 128:(tb + 1) * 128]:
                               e.transpose(o, i, self.ident[:]))
                P.group("pe", fns, reads=[self.xn(buf, k) for k in range(hf * 4, hf * 4 + 4)] + ["ident"], writes=[pn])
                eng = "dve" if hf == 0 else "act"
                self.cp(eng, ot[:, hf * 512:(hf + 1) * 512], ps[:, :], [pn], otn)
            r0 = (ti * 4 + tb) * 128
            P.dma(self.out[r0:r0 + 128, :], ot, reads=otn, writes=["out_%d" % r0])

    def resid_update(self, ps, pn, mo, gate_ap, gname, buf=0):
        self.stt("dve", self.xc(buf, mo), ps[:], gate_ap, self.xc(buf, mo), ALU.mult, ALU.add,
                 [pn, gname, self.xn(buf, mo)], [self.xn(buf, mo)])

    def out_proj(self, l, Wout, wname, src, srcnames, buf=0):
        for mo in range(8):
            ps, pn = self.next_ps()
            self.mm(ps[:], pn, [(Wout[:, k, mo * 128:(mo + 1) * 128], src[:, k, :]) for k in range(8)],
                    reads=self.wparts[wname] + srcnames)
            self.resid_update(ps, pn, mo, self.modT[:, l, 16 + mo:17 + mo], "modT%d" % l, buf)

    def ffn_block(self, l, last):
        self.P.barrier()
        W1 = self.wview(self.WA, 0, 8, 4096)
        W2 = self.wview(self.WB, 0, 32, 1024)
        self.load_w("w1", W1, self.dram["ff_w1"][l], 8, 4096)
        self.load_w("w2", W2, self.dram["ff_w2"][l], 32, 1024)
        r2 = self.A1
        dbl = True
        hn = ["h%d" % k for k in range(8)]
        if dbl:
            self.load_x(0, 0)
            self.norm_stage(l, 1, 0)
        for ti in range(NT):
            cur = (ti % 2) if dbl else 0
            nxt = 1 - cur
            if dbl:
                if ti + 1 < NT:
                    self.load_x(ti + 1, nxt)
            else:
                self.load_x(ti)
                self.norm_stage(l, 1)
            for half in range(2):
                for jj in range(16):
                    j = half * 16 + jj
                    ps, pn = self.next_ps()
                    self.mm(ps[:], pn, [(W1[:, k, j * 128:(j + 1) * 128], self.h[:, k, :]) for k in range(8)],
                            reads=self.wparts["w1"] + hn)
                    b = jj % 2
                    self.act(self.rt[:, b, :], ps[:], AF.Relu, [pn], ["rt%d" % b])
                    eng = "dve" if jj % 2 == 0 else "pool"
                    self.tt(eng, r2[:, jj, :], self.rt[:, b, :], self.rt[:, b, :], ALU.mult, ["rt%d" % b], ["a1_%d" % jj])
                if dbl and half == 1 and ti + 1 < NT:
                    self.norm_stage(l, 1, nxt)
                for mo in range(8):
                    ps, pn = self.next_ps()
                    self.mm(ps[:], pn, [(W2[:, half * 16 + jj, mo * 128:(mo + 1) * 128], r2[:, jj, :]) for jj in range(16)],
                            reads=self.wparts["w2"] + ["a1_%d" % jj for jj in range(16)])
                    self.resid_update(ps, pn, mo, self.modT[:, l, 40 + mo:41 + mo], "modT%d" % l, cur)
            if last:
                self.final_stage(ti, cur)
            else:
                self.store_x(ti, cur)

    def conv_block(self, l, j, first):
        self.P.barrier()
        Win = self.wview(self.WA, 0, 8, 3072)
        Wout = self.wview(self.WB, 0, 8, 1024)
        self.load_w("cwin", Win, self.dram["conv_w_in"][j], 8, 3072)
        self.load_w("cwout", Wout, self.dram["conv_w_out"][j], 8, 1024)
        A2 = self.A2
        cs = [A2[:, 0:512], A2[:, 512:1024]]
        zb = [A2[:, 1024:1538], A2[:, 1538:2052]]
        acc = [A2[:, 2052:2564], A2[:, 2564:3076]]
        bsb = [self.A3[:, 0:512], self.A3[:, 512:1024]]
        q = self.A1
        self.memset("pool", self.zc[:], 0.0, ["zc%d" % m for m in range(8)])
        dbl = not first
        if dbl:
            self.load_x(0, 0)
            self.norm_stage(l, 0, 0)
        for ti in range(NT):
            cur = (ti % 2) if dbl else 0
            nxt = 1 - cur
            if dbl:
                if ti + 1 < NT:
                    self.load_x(ti + 1, nxt)
            else:
                self.load_x_first(ti)
                self.norm_stage(l, 0)
            hn = ["h%d" % k for k in range(8)]
            wr = self.wparts["cwin"] + hn
            for m in range(8):
                b = m % 2
                psB, nB = self.next_ps()
                psC, nC = self.next_ps()
                psX, nX = self.next_ps()
                self.mm(psC[:], nC, [(Win[:, k, 1024 + m * 128:1024 + (m + 1) * 128], self.h[:, k, :]) for k in range(8)], wr)
                self.mm(psX[:], nX, [(Win[:, k, 2048 + m * 128:2048 + (m + 1) * 128], self.h[:, k, :]) for k in range(8)], wr)
                self.mm(psB[:], nB, [(Win[:, k, m * 128:(m + 1) * 128], self.h[:, k, :]) for k in range(8)], wr)
                self.act(cs[b], psC[:], AF.Copy, [nC], ["cs%d" % b])
                self.act(bsb[b], psB[:], AF.Copy, [nB], ["bsb%d" % b])
                self.cp("pool", zb[b][:, 0:2], self.zc[:, m, :], ["zc%d" % m], ["zb%d" % b])
                self.tt("dve", zb[b][:, 2:514], psX[:], cs[b], ALU.mult, [nX, "cs%d" % b], ["zb%d" % b])
                self.act(acc[b], zb[b][:, 2:514], AF.Identity, ["zb%d" % b, "cw", "cb"], ["acc%d" % b],
                         bias=self.cb[:, j, m:m + 1], scale=self.cw[:, j, 2, m:m + 1])
                self.stt("dve", acc[b], zb[b][:, 1:513], self.cw[:, j, 1, m:m + 1], acc[b], ALU.mult, ALU.add,
                         ["zb%d" % b, "cw", "acc%d" % b], ["acc%d" % b])
                self.stt("dve", acc[b], zb[b][:, 0:512], self.cw[:, j, 0, m:m + 1], acc[b], ALU.mult, ALU.add,
                         ["zb%d" % b, "cw", "acc%d" % b], ["acc%d" % b])
                self.cp("pool", self.zc[:, m, :], zb[b][:, 512:514], ["zb%d" % b], ["zc%d" % m])
                self.tt("pool", q[:, m, :], bsb[b], acc[b], ALU.mult, ["bsb%d" % b, "acc%d" % b], ["a1_%d" % m])
            if dbl and ti + 1 < NT:
                self.norm_stage(l, 0, nxt)
            self.out_proj(l, Wout, "cwout", q, ["a1_%d" % m for m in range(8)], cur)
            self.store_x(ti, cur)

    def sg_block(self, l):
        P = self.P
        P.barrier()
        Win = self.wview(self.WA, 0, 8, 2048)
        Wout = self.wview(self.WB, 0, 8, 1024)
        wsT = self.WB[:, 8192:9216].rearrange("p (h t) -> p h t", h=8)
        self.load_w("gwin", Win, self.dram["sg_w_in"], 8, 2048)
        self.load_w("gwout", Wout, self.dram["sg_w_out"], 8, 1024)
        A3 = self.A3
        vsb = A3[:, 0:1024]
        gain = A3[:, 1024:2048]
        bias = A3[:, 2048:3072]
        vsq = A3[:, 3072:3584]
        tmpg = [A3[:, 3584:4096], self.rt[:, 0, :]]
        tmpgn = ["tmpg0", "rt0"]
        vn = self.sbf[:, :, :, :].rearrange("p a b c -> p (a b c)")[:, 0:1024]
        us = self.A2.rearrange("p (k n) -> p k n", k=8)
        gq = self.A1
        P.dma(gain, self.dram["sg_gain_rep"], writes=["gain"])
        P.dma(bias, self.dram["sg_bias_rep"].rearrange("p h t -> p (h t)"), writes=["bias"])
        b = self.stage_rr
        self.stage_rr = (self.stage_rr + 1) % 3
        sname = "stage%d" % b
        stg = self.stage[:, b, :].rearrange("p (h s) -> p h s", h=8)
        P.dma(stg, self.dram["sg_w_s"].rearrange("h t s -> t h s"), writes=[sname])
        self.tt("pool", stg, stg, self.tril[:, :].unsqueeze(1).broadcast_to([128, 8, 128]), ALU.mult,
                [sname, "tril"], [sname])
        for hf in range(2):
            ps, pn = self.next_ps()
            fns = []
            for hh in range(4):
                fns.append(lambda e, o=ps[:, hh * 128:(hh + 1) * 128], i=stg[:, hf * 4 + hh, :]:
                           e.transpose(o, i, self.ident[:]))
            P.group("pe", fns, reads=[sname, "ident"], writes=[pn])
            self.cp("dve", wsT[:, hf * 4:hf * 4 + 4, :], ps[:, :].rearrange("p (a b) -> p a b", a=4), [pn], ["wsT"])
        WAf2 = self.WA[:, 16384:32768].bitcast(F32)
        vsbs = [WAf2[:, 0:1024], WAf2[:, 1024:2048]]
        vsqs = [WAf2[:, 2048:2560], WAf2[:, 2560:3072]]
        vns = [self.WA[:, 16384 + 6144:16384 + 7168], self.WA[:, 16384 + 7168:16384 + 8192]]
        self.load_x(0, 0)
        self.norm_stage(l, 0, 0)
        hn = ["h%d" % k for k in range(8)]
        wr = self.wparts["gwin"] + hn
        for ti in range(NT):
            cur = ti % 2
            nxt = 1 - cur
            if ti + 1 < NT:
                self.load_x(ti + 1, nxt)
            for m in range(8):
                ps, pn = self.next_ps()
                self.mm(ps[:], pn, [(Win[:, k, m * 128:(m + 1) * 128], self.h[:, k, :]) for k in range(8)], wr)
                self.act(us[:, m, :], ps[:], AF.Copy, [pn], ["us%d" % m])

            def stageA(n):
                p = n % 2
                for hf in range(2):
                    ps, pn = self.next_ps()
                    self.mm(ps[:], pn, [(self.h[:, k, n * 128:(n + 1) * 128], Win[:, k, 1024 + hf * 512:1024 + (hf + 1) * 512])
                                        for k in range(8)], wr)
                    self.act(vsbs[p][:, hf * 512:(hf + 1) * 512], ps[:], AF.Copy, [pn], ["vsb%d_%d" % (p, hf)])
                    self.tt("pool", vsqs[p], vsbs[p][:, hf * 512:(hf + 1) * 512], vsbs[p][:, hf * 512:(hf + 1) * 512], ALU.mult,
                            ["vsb%d_%d" % (p, hf)], ["vsq%d" % p])
                    P.op("dve", lambda e, hf=hf, p=p: e.reduce_sum(out=self.ssv[:, 4 * p + hf:4 * p + hf + 1], in_=vsqs[p], axis=AX.X),
                         reads=["vsq%d" % p], writes=["ssv%d" % p])

            def stageB(n):
                p = n % 2
                c0 = 4 * p
                self.tt("dve", self.ssv[:, c0 + 2:c0 + 3], self.ssv[:, c0:c0 + 1], self.ssv[:, c0 + 1:c0 + 2], ALU.add,
                        ["ssv%d" % p], ["ssvs%d" % p])
                self.act(self.ssv[:, c0 + 3:c0 + 4], self.ssv[:, c0 + 2:c0 + 3], AF.Sqrt, ["ssvs%d" % p, "epsc"], ["rv%d" % p],
                         bias=self.epsc[:, 0:1], scale=1.0 / D)
                P.op("dve", lambda e: e.reciprocal(out=self.ssv[:, c0 + 3:c0 + 4], in_=self.ssv[:, c0 + 3:c0 + 4]),
                     reads=["rv%d" % p], writes=["rv%d" % p])
                self.stt("dve", vns[p], vsbs[p], self.ssv[:, c0 + 3:c0 + 4], gain, ALU.mult, ALU.mult,
                         ["vsb%d_0" % p, "vsb%d_1" % p, "rv%d" % p, "gain"], ["vn%d" % p])
                for hq in range(2):
                    ps, pn = self.next_ps()
                    pairs, flags = [], []
                    for hh in range(4):
                        hd = hq * 4 + hh
                        pairs.append((vns[p][:, hd * 128:(hd + 1) * 128], wsT[:, hd, :]))
                        flags.append((ps[:, hh * 128:(hh + 1) * 128], True, True))
                    self.mm(None, pn, pairs, ["vn%d" % p, "wsT"], flags=flags)
                    tb = tmpg[hq]
                    self.tt("dve", tb, ps[:], bias[:, hq * 512:(hq + 1) * 512], ALU.add, [pn, "bias"], [tmpgn[hq]])
                    self.tt("pool", gq[:, hq * 4:hq * 4 + 4, n * 128:(n + 1) * 128],
                            tb.rearrange("p (a b) -> p a b", a=4), us[:, hq * 4:hq * 4 + 4, n * 128:(n + 1) * 128], ALU.mult,
                            [tmpgn[hq]] + ["us%d" % m for m in range(hq * 4, hq * 4 + 4)],
                            ["a1_%d" % m for m in range(hq * 4, hq * 4 + 4)])

            stageA(0)
            for n in range(4):
                if n + 1 < 4:
                    stageA(n + 1)
                stageB(n)
            if ti + 1 < NT:
                self.norm_stage(l, 0, nxt)
            self.out_proj(l, Wout, "gwout", gq, ["a1_%d" % m for m in range(8)], cur)
            self.store_x(ti, cur)

    def s5_prep(self):
        P = self.P
        d = self.dram
        c = self.s5c
        ARE, AIM, LDT, DT, MAG, ANG, SN, CS, AR, AI, T1, T2, T3, T4, FRE, FIM = [c[:, i, :] for i in range(16)]
        P.dma(ARE, d["are_c"], writes=["s5c"])
        P.dma(AIM, d["aim_c"], writes=["s5c"])
        P.dma(LDT, d["ldt_c"], writes=["s5c"])
        R = ["s5c"]
        self.act(DT, LDT, AF.Exp, R, R)
        self.tt("dve", T1, ARE, DT, ALU.mult, R, R)
        self.act(MAG, T1, AF.Exp, R, R)
        self.tt("dve", ANG, AIM, DT, ALU.mult, R, R)

        def sin_of(dst, src, shift):
            self.ts("dve", T2, src, shift, None, ALU.add, None, R, R)
            self.ts("dve", T3, T2, 1.0 / (2 * PI), None, ALU.mult, None, R, R)
            self.cp("dve", self.s5i[:], T3, R, ["s5i"])
            self.cp("dve", T3, self.s5i[:], ["s5i"], R)
            self.stt("dve", T2, T3, -2 * PI, T2, ALU.mult, ALU.add, R, R)
            self.ts("dve", T3, T2, PI, -2 * PI, ALU.is_gt, ALU.mult, R, R)
            self.tt("dve", T2, T2, T3, ALU.add, R, R)
            self.ts("dve", T3, T2, -PI, 2 * PI, ALU.is_lt, ALU.mult, R, R)
            self.tt("dve", T2, T2, T3, ALU.add, R, R)
            self.ts("dve", T2, T2, -3.141592, 3.141592, ALU.max, ALU.min, R, R)
            self.act(dst, T2, AF.Sin, R, R)

        sin_of(SN, ANG, 0.0)
        sin_of(CS, ANG, PI / 2)
        self.tt("dve", AR, MAG, CS, ALU.mult, R, R)
        self.tt("dve", AI, MAG, SN, ALU.mult, R, R)
        self.tt("dve", T1, ARE, ARE, ALU.mult, R, R)
        self.tt("dve", T2, AIM, AIM, ALU.mult, R, R)
        self.tt("dve", T1, T1, T2, ALU.add, R, R)
        P.op("dve", lambda e: e.reciprocal(out=T4, in_=T1), reads=R, writes=R)
        self.ts("dve", T3, AR, -1.0, None, ALU.add, None, R, R)
        self.tt("dve", T1, T3, ARE, ALU.mult, R, R)
        self.tt("dve", T2, AI, AIM, ALU.mult, R, R)
        self.tt("dve", T1, T1, T2, ALU.add, R, R)
        self.tt("dve", FRE, T1, T4, ALU.mult, R, R)
        self.tt("dve", T1, AI, ARE, ALU.mult, R, R)
        self.tt("dve", T2, T3, AIM, ALU.mult, R, R)
        self.tt("dve", T1, T1, T2, ALU.subtract, R, R)
        self.tt("dve", FIM, T1, T4, ALU.mult, R, R)
        pr, pi_, npi = self.pw[:, 0], self.pw[:, 1], self.pw[:, 2]
        W = ["pw"]
        self.cp("dve", pr[:, 0, :], CS, R, W)
        self.cp("dve", pi_[:, 0, :], SN, R, W)
        for k in range(1, 9):
            self.tt("dve", T1, pr[:, k - 1, :], pr[:, k - 1, :], ALU.mult, W + R, R)
            self.tt("dve", T2, pi_[:, k - 1, :], pi_[:, k - 1, :], ALU.mult, W + R, R)
            self.tt("dve", pr[:, k, :], T1, T2, ALU.subtract, R, W)
            self.tt("dve", T1, pr[:, k - 1, :], pi_[:, k - 1, :], ALU.mult, W + R, R)
            self.ts("dve", pi_[:, k, :], T1, 2.0, None, ALU.mult, None, R, W)
        self.ts("dve", npi, pi_, -1.0, None, ALU.mult, None, W, W)

    def s5_block(self, l):
        P = self.P
        d = self.dram
        P.barrier()
        self.s5_prep()
        Win = self.wview(self.WA, 0, 8, 1024)
        Wglu = self.wview(self.WA, 8192, 8, 1024)
        BW = self.WA[:, 16384:24576].rearrange("p (g r n) -> p g r n", g=32, r=2)
        CW = self.WA[:, 24576:32768].rearrange("p (g r n) -> p g r n", g=32, r=2)
        Wout = self.wview(self.WB, 0, 8, 1024)
        self.load_w("swin", Win, d["ssm_w_in"], 8, 1024)
        self.load_w("sglu", Wglu, d["ssm_glu_w"], 8, 1024)
        self.load_w("swout", Wout, d["ssm_w_out"], 8, 1024)
        c = self.s5c
        FRE, FIM = c[:, 14, :], c[:, 15, :]
        bwparts, cwparts = [], []
        for pc in range(8):
            psr, nr_ = self.next_ps()
            psi, ni_ = self.next_ps()
            for g4 in range(4):
                gp = pc * 4 + g4
                self.ts("dve", self.dg[:, 0, :], self.ident[:], FRE[:, gp:gp + 1], None, ALU.mult, None, ["s5c", "ident"], ["dg0"])
                self.ts("dve", self.dg[:, 1, :], self.ident[:], FIM[:, gp:gp + 1], None, ALU.mult, None, ["s5c", "ident"], ["dg1"])
                self.mm(psr[:, g4 * 128:(g4 + 1) * 128], nr_, [(self.ones_f[:], self.dg[:, 0, :])], ["dg0", "ones_f"],
                        flags=[(psr[:, g4 * 128:(g4 + 1) * 128], True, True)])
                self.mm(psi[:, g4 * 128:(g4 + 1) * 128], ni_, [(self.ones_f[:], self.dg[:, 1, :])], ["dg1", "ones_f"],
                        flags=[(psi[:, g4 * 128:(g4 + 1) * 128], True, True)])
            b1 = self.stage_rr
            b2 = (b1 + 1) % 3
            self.stage_rr = (b1 + 2) % 3
            s1, s2 = "stage%d" % b1, "stage%d" % b2
            bre = self.stage[:, b1, 0:512]
            bim = self.stage[:, b2, 0:512]
            P.dma(bre, d["Bre_z"][:, pc * 4:(pc + 1) * 4, :].rearrange("p g n -> p (g n)"), writes=[s1])
            P.dma(bim, d["Bim_z"][:, pc * 4:(pc + 1) * 4, :].rearrange("p g n -> p (g n)"), writes=[s2])
            t1, t2 = self.tmpn[:, 0, :], self.tmpn[:, 1, :]
            pn = "bw%d" % pc
            self.tt("dve", t1, psr[:], bre, ALU.mult, [nr_, s1], ["tmpn0"])
            self.tt("dve", t2, psi[:], bim, ALU.mult, [ni_, s2], ["tmpn1"])
            self.tt("dve", BW[:, pc * 4:(pc + 1) * 4, 0, :], t1.rearrange("p (g n) -> p g n", g=4),
                    t2.rearrange("p (g n) -> p g n", g=4), ALU.subtract, ["tmpn0", "tmpn1"], [pn + "r"])
            self.tt("dve", t1, psr[:], bim, ALU.mult, [nr_, s2], ["tmpn0"])
            self.tt("dve", t2, psi[:], bre, ALU.mult, [ni_, s1], ["tmpn1"])
            self.tt("dve", BW[:, pc * 4:(pc + 1) * 4, 1, :], t1.rearrange("p (g n) -> p g n", g=4),
                    t2.rearrange("p (g n) -> p g n", g=4), ALU.add, ["tmpn0", "tmpn1"], [pn + "i"])
            bwparts += [pn + "r", pn + "i"]
        for pc in range(8):
            for ri, key, sc in ((0, "Cre_z", 1.0), (1, "Cim_z", -1.0)):
                b1 = self.stage_rr
                self.stage_rr = (b1 + 1) % 3
                s1 = "stage%d" % b1
                P.dma(self.stage[:, b1, 0:512], d[key][:, pc * 4:(pc + 1) * 4, :].rearrange("p g n -> p (g n)"), writes=[s1])
                pn = "cwp%d_%d" % (pc, ri)
                self.act(CW[:, pc * 4:(pc + 1) * 4, ri, :], self.stage[:, b1, 0:512].rearrange("p (g n) -> p g n", g=4),
                         AF.Copy, [s1], [pn], scale=sc)
                cwparts.append(pn)
        K0 = self.KS0
        K1 = self.KS1
        SRI = self.WB[:, 14336:16384].bitcast(F32)
        BR = [K0[:, 0:512], K0[:, 1024:1536]]
        BI = [K0[:, 512:1024], K0[:, 1536:2048]]
        A2f = self.A2
        PRs = [K0[:, 2048:2560], A2f[:, 0:512]]
        PIs = [K0[:, 2560:3072], A2f[:, 512:1024]]
        TC = [K1[:, 0:512], K1[:, 1024:1536], A2f[:, 1024:1536]]
        TS = [K1[:, 512:1024], K1[:, 1536:2048], A2f[:, 1536:2048]]
        RR, RI = K1[:, 2048:2560], K1[:, 2560:3072]
        Dd = self.WB[:, 16384 + 4096:16384 + 4096 + 1024].rearrange("p (q n) -> p q n", q=8)
        sbf3 = self.WB[:, 16384 + 5120:16384 + 5120 + 1024].rearrange("p (r n) -> p r n", r=2)
        sbfs = [self.sbf[:, 0], self.sbf[:, 1], sbf3]
        for q_ in range(8):
            self.ts("dve", Dd[:, q_, :], self.ident[:], self.dsk[:, q_:q_ + 1], None, ALU.mult, None, ["ident", "dsk"], ["Dd"])
        SR, SI = SRI[:, 0:512], SRI[:, 512:1024]
        M1, M2 = self.rt[:, 0, :], self.rt[:, 1, :]
        scan_names = ["br0", "bi0", "br1", "bi1", "pr", "pi", "tc0", "ts0", "tc1", "ts1", "rr", "ri", "sr", "si"]
        self.memset("pool", K1, 0.0, ["tc0", "ts0", "tc1", "ts1", "rr", "ri", "stage0", "stage1", "stage2"])
        carn_all = ["car%d" % g for g in range(32)]
        self.memset("dve", self.car[:], 0.0, carn_all)
        uc, us_, uns = self.pw[:, 0], self.pw[:, 1], self.pw[:, 2]
        for gp in range(32):
            t = gp % 3
            tcn, tsn = "tc%d" % t, "ts%d" % t
            self.memset("dve", TC[t][:, 0:1], 1.0, [tcn])
            self.memset("dve", TS[t][:, 0:1], 0.0, [tsn])
            for k in range(9):
                n = 1 << k
                self.ts("dve", TC[t][:, n:2 * n], TC[t][:, 0:n], uc[:, k, gp:gp + 1], None, ALU.mult, None, [tcn, "pw"], [tcn])
                self.stt("dve", TC[t][:, n:2 * n], TS[t][:, 0:n], uns[:, k, gp:gp + 1], TC[t][:, n:2 * n], ALU.mult, ALU.add,
                         [tcn, tsn, "pw"], [tcn])
                self.ts("dve", TS[t][:, n:2 * n], TS[t][:, 0:n], uc[:, k, gp:gp + 1], None, ALU.mult, None, [tsn, "pw"], [tsn])
                self.stt("dve", TS[t][:, n:2 * n], TC[t][:, 0:n], us_[:, k, gp:gp + 1], TS[t][:, n:2 * n], ALU.mult, ALU.add,
                         [tcn, tsn, "pw"], [tsn])
            P.dma(self.tabC[gp], TC[t], reads=[tcn], writes=["tabC%d" % gp])
            P.dma(self.tabS[gp], TS[t], reads=[tsn], writes=["tabS%d" % gp])
        us = self.A2.rearrange("p (k n) -> p k n", k=8)
        yg = self.A3.rearrange("p (k n) -> p k n", k=8)
        ubf = self.A1[:, 0:8, :]
        ygb = self.A1[:, 8:16, :]
        qv = self.A1[:, 0:8, :]
        c = self.s5c
        MAG, SNt, CSt = c[:, 4, :], c[:, 6, :], c[:, 7, :]
        X1, X2, INITR, INITI = c[:, 10, :], c[:, 11, :], c[:, 12, :], c[:, 13, :]

        for ti in range(NT):
            self.load_x(ti)
            self.norm_stage(l, 0)
            hn = ["h%d" % k for k in range(8)]
            for m in range(8):
                ps, pn = self.next_ps()
                self.mm(ps[:], pn, [(Win[:, k, m * 128:(m + 1) * 128], self.h[:, k, :]) for k in range(8)],
                        self.wparts["swin"] + hn)
                self.cp("act" if m % 2 == 0 else "dve", ubf[:, m, :], ps[:], [pn], ["a1_%d" % m])
            self.tt("dve", X1, CSt, self.car[:, 0, :], ALU.mult, ["s5c"] + carn_all, ["x1"])
            self.tt("dve", X2, SNt, self.car[:, 1, :], ALU.mult, ["s5c"] + carn_all, ["x2"])
            self.tt("dve", INITR, X1, X2, ALU.subtract, ["x1", "x2"], ["initr"])
            self.tt("dve", X1, CSt, self.car[:, 1, :], ALU.mult, ["s5c"] + carn_all, ["x1"])
            self.tt("dve", X2, SNt, self.car[:, 0, :], ALU.mult, ["s5c"] + carn_all, ["x2"])
            self.tt("dve", INITI, X1, X2, ALU.add, ["x1", "x2"], ["initi"])

            def bu(gp, part):
                q = gp // 4
                s = gp % 2
                t = gp % 3
                pb = gp % 2
                s3 = gp % 3
                PRb, PIb = PRs[pb], PIs[pb]
                prn, pin = "pr%d" % pb, "pi%d" % pb
                brn, bin_, tcn, tsn = "br%d" % s, "bi%d" % s, "tc%d" % t, "ts%d" % t
                if part == 0:
                    psr, nr_ = self.next_ps()
                    psi, ni_ = self.next_ps()
                    self.mm(psr[:], nr_, [(BW[:, gp, 0, :], ubf[:, q, :])], bwparts + ["a1_%d" % q])
                    self.mm(psi[:], ni_, [(BW[:, gp, 1, :], ubf[:, q, :])], bwparts + ["a1_%d" % q])
                    self.cp("act", BR[s], psr[:], [nr_], [brn])
                    self.cp("act", BI[s], psi[:], [ni_], [bin_])
                    P.dma(TC[t], self.tabC[gp], reads=["tabC%d" % gp], writes=[tcn])
                    P.dma(TS[t], self.tabS[gp], reads=["tabS%d" % gp], writes=[tsn])
                    self.tt("dve", PRb, BR[s], TC[t], ALU.mult, [brn, tcn], [prn])
                    self.tt("dve", M1, BI[s], TS[t], ALU.mult, [bin_, tsn], ["rt0"])
                    self.tt("dve", PRb, PRb, M1, ALU.add, [prn, "rt0"], [prn])
                    self.tt("dve", PIb, BI[s], TC[t], ALU.mult, [bin_, tcn], [pin])
                    self.tt("dve", M1, BR[s], TS[t], ALU.mult, [brn, tsn], ["rt0"])
                    self.tt("dve", PIb, PIb, M1, ALU.subtract, [pin, "rt0"], [pin])
                    return
                if part == 2:
                    carn = "car%d" % gp
                    self.cp("act", sbfs[s3][:, 0, :], SR, ["sr"], ["sbf%d" % s3])
                    self.cp("act", sbfs[s3][:, 1, :], SI, ["si"], ["sbf%d" % s3])
                    self.cp("act", self.car[:, 0, gp:gp + 1], SR[:, 511:512], ["sr"], [carn])
                    self.cp("act", self.car[:, 1, gp:gp + 1], SI[:, 511:512], ["si"], [carn])
                    return
                rho = MAG[:, gp:gp + 1].broadcast_to([128, TT])
                P.op("dve", lambda e: e.tensor_tensor_scan(out=RR, data0=rho, data1=PRb, initial=INITR[:, gp:gp + 1],
                                                           op0=ALU.mult, op1=ALU.add),
                     reads=[prn, "s5c", "initr"], writes=["rr"])
                P.op("dve", lambda e: e.tensor_tensor_scan(out=RI, data0=rho, data1=PIb, initial=INITI[:, gp:gp + 1],
                                                           op0=ALU.mult, op1=ALU.add),
                     reads=[pin, "s5c", "initi"], writes=["ri"])
                self.tt("dve", SR, RR, TC[t], ALU.mult, ["rr", tcn], ["sr"])
                self.tt("dve", M2, RI, TS[t], ALU.mult, ["ri", tsn], ["rt1"])
                self.tt("dve", SR, SR, M2, ALU.subtract, ["sr", "rt1"], ["sr"])
                self.tt("dve", SI, RR, TS[t], ALU.mult, ["rr", tsn], ["si"])
                self.tt("dve", M2, RI, TC[t], ALU.mult, ["ri", tcn], ["rt1"])
                self.tt("dve", SI, SI, M2, ALU.add, ["si", "rt1"], ["si"])

            def outp(gp):
                q = gp // 4
                s3 = gp % 3
                psY, nY = self.ps[6 + (q % 2)], "ps%d" % (6 + (q % 2))
                pairs = [(CW[:, gp, 0, :], sbfs[s3][:, 0, :]), (CW[:, gp, 1, :], sbfs[s3][:, 1, :])]
                flags = [(psY[:], False, False), (psY[:], False, gp % 4 == 3)]
                rd = cwparts + ["sbf%d" % s3]
                if gp % 4 == 0:
                    pairs = [(Dd[:, q, :], ubf[:, q, :])] + pairs
                    flags = [(psY[:], True, False)] + flags
                    rd = rd + ["Dd", "a1_%d" % q]
                self.mm(None, nY, pairs, rd, flags=flags)
                if gp % 4 == 3:
                    self.act(yg[:, q, :], psY[:], AF.Gelu_apprx_tanh, [nY], ["yg%d" % q])
                    self.cp("act", ygb[:, q, :], yg[:, q, :], ["yg%d" % q], ["a1_%d" % (8 + q)])

            for i in range(-2, 33):
                if 0 <= i + 2 < 32:
                    bu(i + 2, 0)
                if 0 <= i < 32:
                    bu(i, 2)
                if 0 <= i + 1 < 32:
                    bu(i + 1, 1)
                if 0 <= i < 32:
                    outp(i)
            ygn = ["a1_%d" % (8 + k) for k in range(8)]
            for mo in range(8):
                ps, pn = self.next_ps()
                self.mm(ps[:], pn, [(Wglu[:, k, mo * 128:(mo + 1) * 128], ygb[:, k, :]) for k in range(8)],
                        self.wparts["sglu"] + ygn)
                b = mo % 2
                self.act(self.rt[:, b, :], ps[:], AF.Sigmoid, [pn, "glub"], ["rt%d" % b], bias=self.glub[:, mo:mo + 1])
                self.tt("dve", qv[:, mo, :], yg[:, mo, :], self.rt[:, b, :], ALU.mult, ["yg%d" % mo, "rt%d" % b], ["a1_%d" % mo])
            self.out_proj(l, Wout, "swout", qv, ["a1_%d" % m for m in range(8)])
            self.store_x(ti)

    def build(self):
        nc = self.nc
        self.declare()
        with ExitStack() as st:
            self.alloc(st)
            self.P = Prog(nc, st)
            block = st.enter_context(nc.Block())
            self.prologue()
            for l in range(self.n_layers):
                self.ada_stage(l)
            for l in range(self.n_layers):
                kind = l % 3
                if kind == 0:
                    self.conv_block(l, l // 3, first=(l == 0))
                elif kind == 1:
                    self.s5_block(l)
                else:
                    self.sg_block(l)
                self.ffn_block(l, last=(l == self.n_layers - 1))
            self.P.finish()
            self.P.emit(block)
        return nc


def tT(v):
    v = np.asarray(v, np.float32)
    lead = v.shape[:-1]
    r = v.reshape(lead + (8, 128))
    return np.ascontiguousarray(np.moveaxis(r, -1, 0))


def host_layout(inp, b):
    f32 = np.float32
    m = {}
    m["x"] = np.ascontiguousarray(inp["x"][b], f32)
    m["cT"] = tT(inp["c"][b])
    m["ada_w"] = np.ascontiguousarray(inp["ada_w"], f32)
    ab = np.asarray(inp["ada_b"], f32).reshape(DEPTH, 48, 128)
    m["ada_bT"] = np.ascontiguousarray(ab.transpose(2, 0, 1))
    m["n1gT"] = tT(inp["norm1_g"])
    m["n2gT"] = tT(inp["norm2_g"])
    m["fgT"] = tT(inp["final_g"])
    m["ff_w1"] = np.ascontiguousarray(inp["ff_w1"], f32)
    m["ff_w2"] = np.ascontiguousarray(inp["ff_w2"], f32)
    m["conv_w_in"] = np.ascontiguousarray(inp["conv_w_in"], f32)
    m["conv_wT"] = tT(inp["conv_w"])
    m["conv_bT"] = tT(inp["conv_b"])
    m["conv_w_out"] = np.ascontiguousarray(inp["conv_w_out"], f32)
    m["ssm_w_in"] = np.ascontiguousarray(inp["ssm_w_in"][0], f32)
    m["ssm_glu_w"] = np.ascontiguousarray(inp["ssm_glu_w"][0], f32)
    m["ssm_w_out"] = np.ascontiguousarray(inp["ssm_w_out"][0], f32)
    m["glu_bT"] = tT(inp["ssm_glu_b"][0])
    m["dT"] = tT(inp["ssm_d"][0])
    def compact(a):
        a = np.asarray(a, f32).reshape(32, 2, 64)
        return np.ascontiguousarray(a.transpose(1, 2, 0).reshape(128, 32))
    m["are_c"] = compact(inp["ssm_a_re"][0])
    m["aim_c"] = compact(inp["ssm_a_im"][0])
    m["ldt_c"] = compact(np.broadcast_to(np.asarray(inp["ssm_log_dt"][0], f32)[:, None], (64, 64)))
    bre = np.asarray(inp["ssm_b_re"][0], f32)
    bim = np.asarray(inp["ssm_b_im"][0], f32)
    cre = np.asarray(inp["ssm_c_re"][0], f32)
    cim = np.asarray(inp["ssm_c_im"][0], f32)
    Bre_z = np.zeros((128, 32, 128), f32); Bim_z = np.zeros((128, 32, 128), f32)
    Cre_z = np.zeros((128, 32, 128), f32); Cim_z = np.zeros((128, 32, 128), f32)
    for g in range(64):
        gp, g2, g8 = g // 2, g % 2, g % 8
        Bre_z[g8 * 16:(g8 + 1) * 16, gp, g2 * 64:(g2 + 1) * 64] = bre[g].T
        Bim_z[g8 * 16:(g8 + 1) * 16, gp, g2 * 64:(g2 + 1) * 64] = bim[g].T
        Cre_z[g2 * 64:(g2 + 1) * 64, gp, g8 * 16:(g8 + 1) * 16] = cre[g].T
        Cim_z[g2 * 64:(g2 + 1) * 64, gp, g8 * 16:(g8 + 1) * 16] = cim[g].T
    m["Bre_z"], m["Bim_z"], m["Cre_z"], m["Cim_z"] = Bre_z, Bim_z, Cre_z, Cim_z
    m["sg_w_in"] = np.ascontiguousarray(inp["sg_w_in"][0], f32)
    m["sg_w_out"] = np.ascontiguousarray(inp["sg_w_out"][0], f32)
    m["sg_w_s"] = np.ascontiguousarray(inp["sg_w_s"][0], f32)
    m["sg_bias_rep"] = np.ascontiguousarray(np.broadcast_to(np.asarray(inp["sg_b_s"][0], f32)[None], (128, 8, 128)))
    m["sg_gain_rep"] = np.ascontiguousarray(np.broadcast_to(np.asarray(inp["sg_v_g"][0], f32)[None], (128, D)))
    m["ident"] = np.eye(128, dtype=f32)
    m["tril"] = np.tril(np.ones((128, 128), f32))
    return m


_NC_CACHE = {}


def kernel(_n_layers=DEPTH, **inputs):
    inp = {k: np.asarray(v) for k, v in inputs.items()}
    if _n_layers not in _NC_CACHE:
        _NC_CACHE[_n_layers] = Builder(_n_layers).build()
    nc = _NC_CACHE[_n_layers]
    in_maps = [host_layout(inp, b) for b in range(8)]
    res = run_bass_kernel_spmd(nc, in_maps, core_ids=list(range(8)))
    out = np.stack([np.asarray(r["out"], np.float32) for r in res.results], axis=0)
    return out
```

```python
import math
from contextlib import ExitStack

import numpy as np
import concourse.bass as bass
import concourse.mybir as mybir
from concourse.bass_utils import run_bass_kernel_spmd

F32 = mybir.dt.float32
BF16 = mybir.dt.bfloat16
I32 = mybir.dt.int32
AF = mybir.ActivationFunctionType
ALU = mybir.AluOpType
AX = mybir.AxisListType

D = 1024
L = 4096
DEPTH = 4
TT = 512
NT = L // TT
KC = 8
EPS = 1e-6
ENGS = ("pe", "act", "dve", "pool", "sp")
N_DMA_SEMS = 14
PI = math.pi


class Prog:
    def __init__(self, nc, stack):
        self.nc = nc
        self.stream = {e: [] for e in ENGS}
        self.count = {e: 0 for e in ENGS}
        self.sem = {e: stack.enter_context(nc.semaphore("s_" + e)) for e in ENGS if e != "sp"}
        self.dsem = [stack.enter_context(nc.semaphore("d%d" % i)) for i in range(N_DMA_SEMS)]
        self.dval = [0] * N_DMA_SEMS
        self.dnext = 0
        self.waited = {e: {} for e in ENGS}
        self.last_w = {}
        self.readers = {}

    def _deps(self, eng, reads, writes, same_engine_ok=False):
        evs = []
        for r in reads:
            ev = self.last_w.get(r)
            if ev is not None:
                evs.append((ev, False))
        for w in writes:
            ev = self.last_w.get(w)
            if ev is not None:
                evs.append((ev, False))
            for ev in self.readers.get(w, {}).values():
                evs.append((ev, True))
        for (ev, is_war) in evs:
            owner, key, sem, val = ev
            if owner == eng and (same_engine_ok or is_war):
                continue
            if self.waited[eng].get(key, 0) >= val:
                continue
            self.waited[eng][key] = val
            self.stream[eng].append(("wait", sem, val))

    def _record(self, rkey, ev, reads, writes):
        for w in writes:
            self.last_w[w] = ev
            self.readers[w] = {}
        for r in reads:
            if r in writes:
                continue
            self.readers.setdefault(r, {})[rkey] = ev

    def op(self, eng, fn, reads=(), writes=()):
        self.group(eng, [fn], reads, writes)

    def group(self, eng, fns, reads=(), writes=()):
        psr = [r for r in reads if r.startswith("ps") and r not in writes]
        if psr:
            writes = list(writes) + psr
        self._deps(eng, reads, writes, same_engine_ok=(eng == "pe"))
        for fn in fns[:-1]:
            self.stream[eng].append(("ins", fn, None))
        self.count[eng] += 1
        ev = (eng, eng, self.sem[eng], self.count[eng])
        self.stream[eng].append(("ins", fns[-1], self.sem[eng]))
        self._record(eng, ev, reads, writes)

    def dma(self, out, in_, reads=(), writes=(), eng="sp"):
        i = self.dnext
        self.dnext = (self.dnext + 1) % N_DMA_SEMS
        key = "dma%d" % i
        sem = self.dsem[i]
        if self.dval[i] > 0 and self.waited[eng].get(key, 0) < self.dval[i]:
            self.waited[eng][key] = self.dval[i]
            self.stream[eng].append(("wait", sem, self.dval[i]))
        self._deps(eng, reads, writes)
        self.dval[i] += 16
        ev = ("dmaq", key, sem, self.dval[i])
        self.stream[eng].append(("dma", out, in_, sem))
        self._record("dmaq_" + key, ev, reads, writes)

    def barrier(self):
        for e in ENGS:
            for o in ENGS:
                if o != e and o != "sp" and self.count[o] > 0 and self.waited[e].get(o, 0) < self.count[o]:
                    self.waited[e][o] = self.count[o]
                    self.stream[e].append(("wait", self.sem[o], self.count[o]))
            for i in range(N_DMA_SEMS):
                key = "dma%d" % i
                if self.dval[i] > 0 and self.waited[e].get(key, 0) < self.dval[i]:
                    self.waited[e][key] = self.dval[i]
                    self.stream[e].append(("wait", self.dsem[i], self.dval[i]))

    def finish(self):
        for i in range(N_DMA_SEMS):
            if self.dval[i] > 0:
                self.stream["sp"].append(("wait", self.dsem[i], self.dval[i]))
        for e in ENGS:
            if e != "sp" and self.count[e] > 0:
                self.stream["sp"].append(("wait", self.sem[e], self.count[e]))

    def emit(self, block):
        def run(engh, items):
            for it in items:
                if it[0] == "wait":
                    engh.wait_ge(it[1], it[2])
                elif it[0] == "ins":
                    ins = it[1](engh)
                    if it[2] is not None:
                        ins.then_inc(it[2], 1)
                else:
                    engh.dma_start(out=it[1], in_=it[2]).then_inc(it[3], 16)

        @block.sync
        def _(e):
            run(e, self.stream["sp"])

        @block.tensor
        def _(e):
            run(e, self.stream["pe"])

        @block.scalar
        def _(e):
            run(e, self.stream["act"])

        @block.vector
        def _(e):
            run(e, self.stream["dve"])

        @block.gpsimd
        def _(e):
            run(e, self.stream["pool"])


class Builder:
    def __init__(self, n_layers=DEPTH):
        self.n_layers = n_layers
        self.nc = bass.Bass("TRN2", target_bir_lowering=False)
        self.dram = {}
        self.rr = 0
        self.cast_rr = 0
        self.stage_rr = 0
        self.wparts = {}

    def din(self, name, shape):
        self.dram[name] = self.nc.dram_tensor(name, list(shape), F32, kind="ExternalInput").ap()

    def declare(self):
        din = self.din
        din("x", [L, D]); din("cT", [128, 8]); din("ada_w", [DEPTH, D, 6 * D]); din("ada_bT", [128, DEPTH, 48])
        din("n1gT", [128, DEPTH, 8]); din("n2gT", [128, DEPTH, 8]); din("fgT", [128, 8])
        din("ff_w1", [DEPTH, D, 4 * D]); din("ff_w2", [DEPTH, 4 * D, D])
        din("conv_w_in", [2, D, 3 * D]); din("conv_wT", [128, 2, 3, 8]); din("conv_bT", [128, 2, 8])
        din("conv_w_out", [2, D, D])
        din("ssm_w_in", [D, D]); din("ssm_glu_w", [D, D]); din("ssm_w_out", [D, D])
        din("glu_bT", [128, 8]); din("dT", [128, 8])
        din("are_c", [128, 32]); din("aim_c", [128, 32]); din("ldt_c", [128, 32])
        din("Bre_z", [128, 32, 128]); din("Bim_z", [128, 32, 128])
        din("Cre_z", [128, 32, 128]); din("Cim_z", [128, 32, 128])
        din("sg_w_in", [D, 2 * D]); din("sg_w_out", [D, D]); din("sg_w_s", [8, 128, 128])
        din("sg_bias_rep", [128, 8, 128]); din("sg_gain_rep", [128, D])
        din("ident", [128, 128]); din("tril", [128, 128])
        self.out = self.nc.dram_tensor("out", [L, D], F32, kind="ExternalOutput").ap()
        self.xT = self.nc.dram_tensor("xT_scr", [D, L], F32, kind="Internal").ap()
        self.tabC = self.nc.dram_tensor("tabC_scr", [32, 128, TT], F32, kind="Internal").ap()
        self.tabS = self.nc.dram_tensor("tabS_scr", [32, 128, TT], F32, kind="Internal").ap()

    def alloc(self, st):
        nc = self.nc

        def sb(name, shape, dt):
            return st.enter_context(nc.sbuf_tensor("sb_" + name, list(shape), dt))

        self.WA = sb("WA", [128, 32768], BF16)
        self.WB = sb("WB", [128, 32768], BF16)
        self.stage = sb("stage", [128, 3, 1024], F32)
        self.xt = sb("xt", [128, 8, TT], F32)
        self.h = sb("h", [128, 8, TT], BF16)
        self.sq = sb("sq", [128, 2, TT], BF16)
        self.tmpn = sb("tmpn", [128, 2, TT], F32)
        self.rt = sb("rt", [128, 2, TT], F32)
        self.rstd = sb("rstd", [128, TT], F32)
        self.A1 = sb("A1", [128, 16, TT], BF16)
        self.sbf = sb("sbf", [128, 2, 2, TT], BF16)
        self.ident = sb("ident", [128, 128], F32)
        self.tril = sb("tril", [128, 128], F32)
        self.ones_bf = sb("ones_bf", [128, 128], BF16)
        self.ones_f = sb("ones_f", [128, 128], F32)
        self.epsc = sb("epsc", [128, 1], F32)
        self.cT = sb("cT", [128, 8], F32)
        self.cab = sb("cab", [128, 8], BF16)
        self.modT = sb("modT", [128, DEPTH, 48], F32)
        self.adab = sb("adab", [128, DEPTH, 48], F32)
        self.n1g = sb("n1g", [128, DEPTH, 8], F32)
        self.n2g = sb("n2g", [128, DEPTH, 8], F32)
        self.fg = sb("fg", [128, 8], F32)
        self.aT = sb("aT", [128, DEPTH, 2, 8], F32)
        self.cw = sb("cw", [128, 2, 3, 8], F32)
        self.cb = sb("cb", [128, 2, 8], F32)
        self.zc = sb("zc", [128, 8, 2], F32)
        self.glub = sb("glub", [128, 8], F32)
        self.dsk = sb("dsk", [128, 8], F32)
        self.s5c = sb("s5c", [128, 16, 32], F32)
        self.s5i = sb("s5i", [128, 32], I32)
        self.pw = sb("pw", [128, 3, 10, 32], F32)
        self.car = sb("car", [128, 2, 32], F32)
        self.ssv = sb("ssv", [128, 8], F32)
        self.dg = sb("dg", [128, 2, 128], F32)
        self.WAf = self.WA[:, :].bitcast(F32)
        self.A2 = self.WB[:, 16384:24576].bitcast(F32)
        self.A3 = self.WB[:, 24576:32768].bitcast(F32)
        self.KS0 = self.WB[:, 8192:14336].bitcast(F32)
        self.KS1 = self.stage[:, :, :].rearrange("p a b -> p (a b)")
        self.rowtmp = self.rt[0:1, :, :].rearrange("p a b -> p (a b)")
        self.ps = [st.enter_context(nc.psum_tensor("ps%d" % i, [128, 512], F32)) for i in range(8)]

    def xc(self, buf, k):
        if buf == 0:
            return self.xt[:, k, :]
        if k < 6:
            return self.KS1[:, k * 512:(k + 1) * 512]
        return self.sbf[:, :, :, :].rearrange("p a b c -> p (a b c)").bitcast(F32)[:, (k - 6) * 512:(k - 5) * 512]

    def xn(self, buf, k):
        return ("xt%d" % k) if buf == 0 else ("xu%d" % k)

    def xalias(self, buf, k):
        if buf == 0:
            return []
        return ["stage%d" % (k // 2)] if k < 6 else ["sbf0", "sbf1"]

    def next_ps(self):
        i = self.rr
        self.rr = (self.rr + 1) % 6
        return self.ps[i], "ps%d" % i

    def mm(self, out_ps, psname, pairs, reads, flags=None):
        fns = []
        n = len(pairs)
        for i, (a, b) in enumerate(pairs):
            if flags is None:
                o, s0, s1 = out_ps, (i == 0), (i == n - 1)
            else:
                o, s0, s1 = flags[i]
            fns.append(lambda e, o=o, a=a, b=b, s0=s0, s1=s1: e.matmul(o, lhsT=a, rhs=b, start=s0, stop=s1))
        self.P.group("pe", fns, reads=reads, writes=[psname])

    def act(self, out, in_, func, reads, writes, bias=None, scale=1.0):
        if bias is None:
            fn = lambda e: e.activation(out=out, in_=in_, func=func, scale=scale)
        else:
            fn = lambda e: e.activation(out=out, in_=in_, func=func, bias=bias, scale=scale)
        self.P.op("act", fn, reads=reads, writes=writes)

    def stt(self, eng, out, in0, scalar, in1, op0, op1, reads, writes):
        self.P.op(eng, lambda e: e.scalar_tensor_tensor(out=out, in0=in0, scalar=scalar, in1=in1, op0=op0, op1=op1),
                  reads=reads, writes=writes)

    def tt(self, eng, out, in0, in1, op, reads, writes):
        self.P.op(eng, lambda e: e.tensor_tensor(out=out, in0=in0, in1=in1, op=op), reads=reads, writes=writes)

    def ts(self, eng, out, in0, s1, s2, op0, op1, reads, writes):
        if s2 is None:
            fn = lambda e: e.tensor_scalar(out=out, in0=in0, scalar1=s1, scalar2=None, op0=op0)
        else:
            fn = lambda e: e.tensor_scalar(out=out, in0=in0, scalar1=s1, scalar2=s2, op0=op0, op1=op1)
        self.P.op(eng, fn, reads=reads, writes=writes)

    def cp(self, eng, out, in_, reads, writes):
        if eng == "act":
            self.act(out, in_, AF.Copy, reads, writes)
        else:
            self.P.op(eng, lambda e: e.tensor_copy(out=out, in_=in_), reads=reads, writes=writes)

    def memset(self, eng, ap, val, writes):
        self.P.op(eng, lambda e: e.memset(ap, val), writes=writes)

    def load_w(self, name, dst3, src2, K, N, scale=None, xt_stage=True):
        parts = []
        bufs = [(self.stage[:, i, :], ["stage%d" % i]) for i in range(3)]
        if xt_stage:
            for i in range(4):
                bufs.append((self.xt[:, 2 * i:2 * i + 2, :].rearrange("p a b -> p (a b)"), ["xt%d" % (2 * i), "xt%d" % (2 * i + 1)]))
        for k in range(K):
            for c0 in range(0, N, 1024):
                w = min(1024, N - c0)
                self.wl_rr = (getattr(self, "wl_rr", 0) + 1) % len(bufs)
                sb_, snames = bufs[self.wl_rr]
                self.P.dma(sb_[:, 0:w], src2[k * 128:(k + 1) * 128, c0:c0 + w], writes=snames)
                pn = "%s_%d_%d" % (name, k, c0)
                eng = ("act", "dve")[self.cast_rr % 2]
                self.cast_rr += 1
                if scale is None:
                    self.cp(eng, dst3[:, k, c0:c0 + w], sb_[:, 0:w], snames, [pn])
                else:
                    self.act(dst3[:, k, c0:c0 + w], sb_[:, 0:w], AF.Copy, snames, [pn], scale=scale)
                parts.append(pn)
        self.wparts[name] = parts
        return parts

    def wview(self, buf, c0, K, N):
        return buf[:, c0:c0 + K * N].rearrange("p (k n) -> p k n", k=K)

    def prologue(self):
        P = self.P
        d = self.dram
        P.dma(self.ident[:], d["ident"], writes=["ident"])
        P.dma(self.tril[:], d["tril"], writes=["tril"])
        P.dma(self.cT[:], d["cT"], writes=["cT"])
        P.dma(self.adab[:], d["ada_bT"], writes=["adab"])
        P.dma(self.n1g[:], d["n1gT"], writes=["n1g"])
        P.dma(self.n2g[:], d["n2gT"], writes=["n2g"])
        P.dma(self.fg[:], d["fgT"], writes=["fg"])
        P.dma(self.cw[:], d["conv_wT"], writes=["cw"])
        P.dma(self.cb[:], d["conv_bT"], writes=["cb"])
        P.dma(self.glub[:], d["glu_bT"], writes=["glub"])
        P.dma(self.dsk[:], d["dT"], writes=["dsk"])
        self.memset("pool", self.ones_bf[:], 1.0, ["ones_bf"])
        self.memset("pool", self.ones_f[:], 1.0, ["ones_f"])
        self.memset("pool", self.epsc[:], EPS, ["epsc"])
        self.act(self.s5c[:, 0, 0:8], self.cT[:], AF.Sigmoid, ["cT"], ["sgc"])
        self.tt("dve", self.cab[:], self.cT[:], self.s5c[:, 0, 0:8], ALU.mult, ["cT", "sgc"], ["cab"])

    def ada_stage(self, l):
        P = self.P
        wtmp = self.A1
        for cbk in range(6):
            psa, na = self.next_ps()
            psb, nb = self.next_ps()
            for k in range(8):
                self.ada_rr = (getattr(self, "ada_rr", 0) + 1) % 7
                if self.ada_rr < 3:
                    stg_ap, snames = self.stage[:, self.ada_rr, :], ["stage%d" % self.ada_rr]
                else:
                    i2 = self.ada_rr - 3
                    stg_ap = self.xt[:, 2 * i2:2 * i2 + 2, :].rearrange("p a b -> p (a b)")
                    snames = ["xt%d" % (2 * i2), "xt%d" % (2 * i2 + 1)]
                P.dma(stg_ap, self.dram["ada_w"][l, k * 128:(k + 1) * 128, cbk * 1024:(cbk + 1) * 1024],
                      writes=snames)
                wb = (cbk * 8 + k) % 4
                wt = self.A1[:, 2 * wb:2 * wb + 2, :].rearrange("p a b -> p (a b)")
                wn = ["a1_%d" % (2 * wb), "a1_%d" % (2 * wb + 1)]
                eng = ("act", "dve")[self.cast_rr % 2]
                self.cast_rr += 1
                self.cp(eng, wt, stg_ap, snames, wn)
                lhs = self.cab[:, k:k + 1]
                self.P.group("pe", [
                    (lambda e, o=psa[0:1, :], a=lhs, r=wt[:, 0:512], s0=(k == 0), s1=(k == 7):
                     e.matmul(o, lhsT=a, rhs=r, start=s0, stop=s1)),
                    (lambda e, o=psb[0:1, :], a=lhs, r=wt[:, 512:1024], s0=(k == 0), s1=(k == 7):
                     e.matmul(o, lhsT=a, rhs=r, start=s0, stop=s1)),
                ], reads=wn + ["cab"], writes=[na, nb])
            self.act(self.rowtmp[0:1, 0:512], psa[0:1, :], AF.Copy, [na], ["rt0", "rt1"])
            self.act(self.rowtmp[0:1, 512:1024], psb[0:1, :], AF.Copy, [nb], ["rt0", "rt1"])
            pst, nt = self.next_ps()
            fns = []
            for m in range(8):
                fns.append(lambda e, o=pst[:, m:m + 1], a=self.rowtmp[0:1, m * 128:(m + 1) * 128], r=self.ones_f[0:1, 0:1]:
                           e.matmul(o, lhsT=a, rhs=r, start=True, stop=True))
            P.group("pe", fns, reads=["rt0", "rt1", "ones_f"], writes=[nt])
            self.tt("dve", self.modT[:, l, cbk * 8:(cbk + 1) * 8], pst[:, 0:8], self.adab[:, l, cbk * 8:(cbk + 1) * 8],
                    ALU.add, [nt, "adab"], ["modT%d" % l])
        mn = "modT%d" % l
        self.ts("dve", self.aT[:, l, 0, :], self.modT[:, l, 8:16], 1.0, None, ALU.add, None, [mn], ["aT%d" % l])
        self.tt("dve", self.aT[:, l, 0, :], self.aT[:, l, 0, :], self.n1g[:, l, :], ALU.mult, ["aT%d" % l, "n1g"], ["aT%d" % l])
        self.ts("dve", self.aT[:, l, 1, :], self.modT[:, l, 32:40], 1.0, None, ALU.add, None, [mn], ["aT%d" % l])
        self.tt("dve", self.aT[:, l, 1, :], self.aT[:, l, 1, :], self.n2g[:, l, :], ALU.mult, ["aT%d" % l, "n2g"], ["aT%d" % l])

    def load_x_first(self, ti):
        P = self.P
        for tb in range(4):
            b = self.stage_rr
            self.stage_rr = (self.stage_rr + 1) % 3
            sname = "stage%d" % b
            r0 = (ti * 4 + tb) * 128
            P.dma(self.stage[:, b, :], self.dram["x"][r0:r0 + 128, :], writes=[sname])
            for hf in range(2):
                ps, pn = self.next_ps()
                fns = []
                for kk in range(4):
                    k = hf * 4 + kk
                    fns.append(lambda e, o=ps[:, kk * 128:(kk + 1) * 128], i=self.stage[:, b, k * 128:(k + 1) * 128]:
                               e.transpose(o, i, self.ident[:]))
                P.group("pe", fns, reads=[sname, "ident"], writes=[pn])
                eng = "dve" if hf == 0 else "act"
                self.cp(eng, self.xt[:, hf * 4:hf * 4 + 4, tb * 128:(tb + 1) * 128],
                        ps[:, :].rearrange("p (a b) -> p a b", a=4), [pn], ["xt%d" % k for k in range(hf * 4, hf * 4 + 4)])

    def load_x(self, ti, buf=0):
        for k in range(8):
            self.P.dma(self.xc(buf, k), self.xT[k * 128:(k + 1) * 128, ti * TT:(ti + 1) * TT],
                       reads=["xT_%d_%d" % (k, ti)], writes=[self.xn(buf, k)] + self.xalias(buf, k))

    def store_x(self, ti, buf=0):
        for k in range(8):
            self.P.dma(self.xT[k * 128:(k + 1) * 128, ti * TT:(ti + 1) * TT], self.xc(buf, k),
                       reads=[self.xn(buf, k)], writes=["xT_%d_%d" % (k, ti)])

    def sumsq_rstd(self, buf=0):
        P = self.P
        pss, pn = self.ps[6], "ps6"
        for k in range(8):
            b = k % 2
            self.tt("pool", self.sq[:, b, :], self.xc(buf, k), self.xc(buf, k), ALU.mult, [self.xn(buf, k)], ["sq%d" % b])
            P.group("pe", [lambda e, b=b, k=k: e.matmul(pss[:], lhsT=self.ones_bf[:], rhs=self.sq[:, b, :],
                                                         start=(k == 0), stop=(k == 7))],
                    reads=["sq%d" % b, "ones_bf"], writes=[pn])
        self.act(self.rstd[:], pss[:], AF.Sqrt, [pn, "epsc"], ["rstd"], bias=self.epsc[:, 0:1], scale=1.0 / D)
        P.op("dve", lambda e: e.reciprocal(out=self.rstd[:], in_=self.rstd[:]), reads=["rstd"], writes=["rstd"])

    def norm_stage(self, l, which, buf=0):
        self.sumsq_rstd(buf)
        an = "aT%d" % l
        mn = "modT%d" % l
        sh0 = 0 if which == 0 else 24
        for k in range(8):
            b = k % 2
            self.stt("dve", self.tmpn[:, b, :], self.xc(buf, k), self.aT[:, l, which, k:k + 1], self.rstd[:],
                     ALU.mult, ALU.mult, [self.xn(buf, k), an, "rstd"], ["tmpn%d" % b])
            self.act(self.h[:, k, :], self.tmpn[:, b, :], AF.Identity, ["tmpn%d" % b, mn], ["h%d" % k],
                     bias=self.modT[:, l, sh0 + k:sh0 + k + 1])

    def final_stage(self, ti, buf=0):
        P = self.P
        self.sumsq_rstd(buf)
        for k in range(8):
            self.stt("dve", self.xc(buf, k), self.xc(buf, k), self.fg[:, k:k + 1], self.rstd[:],
                     ALU.mult, ALU.mult, [self.xn(buf, k), "fg", "rstd"], [self.xn(buf, k)])
        for tb in range(4):
            sl = 4 * (tb % 2)
            ot = self.A1[:, sl:sl + 4, :].rearrange("p a b -> p (a b)").bitcast(F32)
            otn = ["a1_%d" % i for i in range(sl, sl + 4)]
            for hf in range(2):
                ps, pn = self.next_ps()
                fns = []
                for kk in range(4):
                    k = hf * 4 + kk
                    fns.append(lambda e, o=ps[:, kk * 128:(kk + 1) * 128], i=self.xc(buf, k)[:, tb * 128:(tb + 1) * 128]:
                               e.transpose(o, i, self.ident[:]))
                P.group("pe", fns, reads=[self.xn(buf, k) for k in range(hf * 4, hf * 4 + 4)] + ["ident"], writes=[pn])
                eng = "dve" if hf == 0 else "act"
                self.cp(eng, ot[:, hf * 512:(hf + 1) * 512], ps[:, :], [pn], otn)
            r0 = (ti * 4 + tb) * 128
            P.dma(self.out[r0:r0 + 128, :], ot, reads=otn, writes=["out_%d" % r0])

    def resid_update(self, ps, pn, mo, gate_ap, gname, buf=0):
        self.stt("dve", self.xc(buf, mo), ps[:], gate_ap, self.xc(buf, mo), ALU.mult, ALU.add,
                 [pn, gname, self.xn(buf, mo)], [self.xn(buf, mo)])

    def out_proj(self, l, Wout, wname, src, srcnames, buf=0):
        for mo in range(8):
            ps, pn = self.next_ps()
            self.mm(ps[:], pn, [(Wout[:, k, mo * 128:(mo + 1) * 128], src[:, k, :]) for k in range(8)],
                    reads=self.wparts[wname] + srcnames)
            self.resid_update(ps, pn, mo, self.modT[:, l, 16 + mo:17 + mo], "modT%d" % l, buf)

    def ffn_block(self, l, last):
        self.P.barrier()
        W1 = self.wview(self.WA, 0, 8, 4096)
        W2 = self.wview(self.WB, 0, 32, 1024)
        self.load_w("w1", W1, self.dram["ff_w1"][l], 8, 4096)
        self.load_w("w2", W2, self.dram["ff_w2"][l], 32, 1024)
        r2 = self.A1
        dbl = True
        hn = ["h%d" % k for k in range(8)]
        if dbl:
            self.load_x(0, 0)
            self.norm_stage(l, 1, 0)
        for ti in range(NT):
            cur = (ti % 2) if dbl else 0
            nxt = 1 - cur
            if dbl:
                if ti + 1 < NT:
                    self.load_x(ti + 1, nxt)
            else:
                self.load_x(ti)
                self.norm_stage(l, 1)
            for half in range(2):
                for jj in range(16):
                    j = half * 16 + jj
                    ps, pn = self.next_ps()
                    self.mm(ps[:], pn, [(W1[:, k, j * 128:(j + 1) * 128], self.h[:, k, :]) for k in range(8)],
                            reads=self.wparts["w1"] + hn)
                    b = jj % 2
                    self.act(self.rt[:, b, :], ps[:], AF.Relu, [pn], ["rt%d" % b])
                    eng = "dve" if jj % 2 == 0 else "pool"
                    self.tt(eng, r2[:, jj, :], self.rt[:, b, :], self.rt[:, b, :], ALU.mult, ["rt%d" % b], ["a1_%d" % jj])
                if dbl and half == 1 and ti + 1 < NT:
                    self.norm_stage(l, 1, nxt)
                for mo in range(8):
                    ps, pn = self.next_ps()
                    self.mm(ps[:], pn, [(W2[:, half * 16 + jj, mo * 128:(mo + 1) * 128], r2[:, jj, :]) for jj in range(16)],
                            reads=self.wparts["w2"] + ["a1_%d" % jj for jj in range(16)])
                    self.resid_update(ps, pn, mo, self.modT[:, l, 40 + mo:41 + mo], "modT%d" % l, cur)
            if last:
                self.final_stage(ti, cur)
            else:
                self.store_x(ti, cur)

    def conv_block(self, l, j, first):
        self.P.barrier()
        Win = self.wview(self.WA, 0, 8, 3072)
        Wout = self.wview(self.WB, 0, 8, 1024)
        self.load_w("cwin", Win, self.dram["conv_w_in"][j], 8, 3072)
        self.load_w("cwout", Wout, self.dram["conv_w_out"][j], 8, 1024)
        A2 = self.A2
        cs = [A2[:, 0:512], A2[:, 512:1024]]
        zb = [A2[:, 1024:1538], A2[:, 1538:2052]]
        acc = [A2[:, 2052:2564], A2[:, 2564:3076]]
        bsb = [self.A3[:, 0:512], self.A3[:, 512:1024]]
        q = self.A1
        self.memset("pool", self.zc[:], 0.0, ["zc%d" % m for m in range(8)])
        dbl = not first
        if dbl:
            self.load_x(0, 0)
            self.norm_stage(l, 0, 0)
        for ti in range(NT):
            cur = (ti % 2) if dbl else 0
            nxt = 1 - cur
            if dbl:
                if ti + 1 < NT:
                    self.load_x(ti + 1, nxt)
            else:
                self.load_x_first(ti)
                self.norm_stage(l, 0)
            hn = ["h%d" % k for k in range(8)]
            wr = self.wparts["cwin"] + hn
            for m in range(8):
                b = m % 2
                psB, nB = self.next_ps()
                psC, nC = self.next_ps()
                psX, nX = self.next_ps()
                self.mm(psC[:], nC, [(Win[:, k, 1024 + m * 128:1024 + (m + 1) * 128], self.h[:, k, :]) for k in range(8)], wr)
                self.mm(psX[:], nX, [(Win[:, k, 2048 + m * 128:2048 + (m + 1) * 128], self.h[:, k, :]) for k in range(8)], wr)
                self.mm(psB[:], nB, [(Win[:, k, m * 128:(m + 1) * 128], self.h[:, k, :]) for k in range(8)], wr)
                self.act(cs[b], psC[:], AF.Copy, [nC], ["cs%d" % b])
                self.act(bsb[b], psB[:], AF.Copy, [nB], ["bsb%d" % b])
                self.cp("pool", zb[b][:, 0:2], self.zc[:, m, :], ["zc%d" % m], ["zb%d" % b])
                self.tt("dve", zb[b][:, 2:514], psX[:], cs[b], ALU.mult, [nX, "cs%d" % b], ["zb%d" % b])
                self.act(acc[b], zb[b][:, 2:514], AF.Identity, ["zb%d" % b, "cw", "cb"], ["acc%d" % b],
                         bias=self.cb[:, j, m:m + 1], scale=self.cw[:, j, 2, m:m + 1])
                self.stt("dve", acc[b], zb[b][:, 1:513], self.cw[:, j, 1, m:m + 1], acc[b], ALU.mult, ALU.add,
                         ["zb%d" % b, "cw", "acc%d" % b], ["acc%d" % b])
                self.stt("dve", acc[b], zb[b][:, 0:512], self.cw[:, j, 0, m:m + 1], acc[b], ALU.mult, ALU.add,
                         ["zb%d" % b, "cw", "acc%d" % b], ["acc%d" % b])
                self.cp("pool", self.zc[:, m, :], zb[b][:, 512:514], ["zb%d" % b], ["zc%d" % m])
                self.tt("pool", q[:, m, :], bsb[b], acc[b], ALU.mult, ["bsb%d" % b, "acc%d" % b], ["a1_%d" % m])
            if dbl and ti + 1 < NT:
                self.norm_stage(l, 0, nxt)
            self.out_proj(l, Wout, "cwout", q, ["a1_%d" % m for m in range(8)], cur)
            self.store_x(ti, cur)

    def sg_block(self, l):
        P = self.P
        P.barrier()
        Win = self.wview(self.WA, 0, 8, 2048)
        Wout = self.wview(self.WB, 0, 8, 1024)
        wsT = self.WB[:, 8192:9216].rearrange("p (h t) -> p h t", h=8)
        self.load_w("gwin", Win, self.dram["sg_w_in"], 8, 2048)
        self.load_w("gwout", Wout, self.dram["sg_w_out"], 8, 1024)
        A3 = self.A3
        vsb = A3[:, 0:1024]
        gain = A3[:, 1024:2048]
        bias = A3[:, 2048:3072]
        vsq = A3[:, 3072:3584]
        tmpg = [A3[:, 3584:4096], self.rt[:, 0, :]]
        tmpgn = ["tmpg0", "rt0"]
        vn = self.sbf[:, :, :, :].rearrange("p a b c -> p (a b c)")[:, 0:1024]
        us = self.A2.rearrange("p (k n) -> p k n", k=8)
        gq = self.A1
        P.dma(gain, self.dram["sg_gain_rep"], writes=["gain"])
        P.dma(bias, self.dram["sg_bias_rep"].rearrange("p h t -> p (h t)"), writes=["bias"])
        b = self.stage_rr
        self.stage_rr = (self.stage_rr + 1) % 3
        sname = "stage%d" % b
        stg = self.stage[:, b, :].rearrange("p (h s) -> p h s", h=8)
        P.dma(stg, self.dram["sg_w_s"].rearrange("h t s -> t h s"), writes=[sname])
        self.tt("pool", stg, stg, self.tril[:, :].unsqueeze(1).broadcast_to([128, 8, 128]), ALU.mult,
                [sname, "tril"], [sname])
        for hf in range(2):
            ps, pn = self.next_ps()
            fns = []
            for hh in range(4):
                fns.append(lambda e, o=ps[:, hh * 128:(hh + 1) * 128], i=stg[:, hf * 4 + hh, :]:
                           e.transpose(o, i, self.ident[:]))
            P.group("pe", fns, reads=[sname, "ident"], writes=[pn])
            self.cp("dve", wsT[:, hf * 4:hf * 4 + 4, :], ps[:, :].rearrange("p (a b) -> p a b", a=4), [pn], ["wsT"])
        WAf2 = self.WA[:, 16384:32768].bitcast(F32)
        vsbs = [WAf2[:, 0:1024], WAf2[:, 1024:2048]]
        vsqs = [WAf2[:, 2048:2560], WAf2[:, 2560:3072]]
        vns = [self.WA[:, 16384 + 6144:16384 + 7168], self.WA[:, 16384 + 7168:16384 + 8192]]
        self.load_x(0, 0)
        self.norm_stage(l, 0, 0)
        hn = ["h%d" % k for k in range(8)]
        wr = self.wparts["gwin"] + hn
        for ti in range(NT):
            cur = ti % 2
            nxt = 1 - cur
            if ti + 1 < NT:
                self.load_x(ti + 1, nxt)
            for m in range(8):
                ps, pn = self.next_ps()
                self.mm(ps[:], pn, [(Win[:, k, m * 128:(m + 1) * 128], self.h[:, k, :]) for k in range(8)], wr)
                self.act(us[:, m, :], ps[:], AF.Copy, [pn], ["us%d" % m])

            def stageA(n):
                p = n % 2
                for hf in range(2):
                    ps, pn = self.next_ps()
                    self.mm(ps[:], pn, [(self.h[:, k, n * 128:(n + 1) * 128], Win[:, k, 1024 + hf * 512:1024 + (hf + 1) * 512])
                                        for k in range(8)], wr)
                    self.act(vsbs[p][:, hf * 512:(hf + 1) * 512], ps[:], AF.Copy, [pn], ["vsb%d_%d" % (p, hf)])
                    self.tt("pool", vsqs[p], vsbs[p][:, hf * 512:(hf + 1) * 512], vsbs[p][:, hf * 512:(hf + 1) * 512], ALU.mult,
                            ["vsb%d_%d" % (p, hf)], ["vsq%d" % p])
                    P.op("dve", lambda e, hf=hf, p=p: e.reduce_sum(out=self.ssv[:, 4 * p + hf:4 * p + hf + 1], in_=vsqs[p], axis=AX.X),
                         reads=["vsq%d" % p], writes=["ssv%d" % p])

            def stageB(n):
                p = n % 2
                c0 = 4 * p
                self.tt("dve", self.ssv[:, c0 + 2:c0 + 3], self.ssv[:, c0:c0 + 1], self.ssv[:, c0 + 1:c0 + 2], ALU.add,
                        ["ssv%d" % p], ["ssvs%d" % p])
                self.act(self.ssv[:, c0 + 3:c0 + 4], self.ssv[:, c0 + 2:c0 + 3], AF.Sqrt, ["ssvs%d" % p, "epsc"], ["rv%d" % p],
                         bias=self.epsc[:, 0:1], scale=1.0 / D)
                P.op("dve", lambda e: e.reciprocal(out=self.ssv[:, c0 + 3:c0 + 4], in_=self.ssv[:, c0 + 3:c0 + 4]),
                     reads=["rv%d" % p], writes=["rv%d" % p])
                self.stt("dve", vns[p], vsbs[p], self.ssv[:, c0 + 3:c0 + 4], gain, ALU.mult, ALU.mult,
                         ["vsb%d_0" % p, "vsb%d_1" % p, "rv%d" % p, "gain"], ["vn%d" % p])
                for hq in range(2):
                    ps, pn = self.next_ps()
                    pairs, flags = [], []
                    for hh in range(4):
                        hd = hq * 4 + hh
                        pairs.append((vns[p][:, hd * 128:(hd + 1) * 128], wsT[:, hd, :]))
                        flags.append((ps[:, hh * 128:(hh + 1) * 128], True, True))
                    self.mm(None, pn, pairs, ["vn%d" % p, "wsT"], flags=flags)
                    tb = tmpg[hq]
                    self.tt("dve", tb, ps[:], bias[:, hq * 512:(hq + 1) * 512], ALU.add, [pn, "bias"], [tmpgn[hq]])
                    self.tt("pool", gq[:, hq * 4:hq * 4 + 4, n * 128:(n + 1) * 128],
                            tb.rearrange("p (a b) -> p a b", a=4), us[:, hq * 4:hq * 4 + 4, n * 128:(n + 1) * 128], ALU.mult,
                            [tmpgn[hq]] + ["us%d" % m for m in range(hq * 4, hq * 4 + 4)],
                            ["a1_%d" % m for m in range(hq * 4, hq * 4 + 4)])

            stageA(0)
            for n in range(4):
                if n + 1 < 4:
                    stageA(n + 1)
                stageB(n)
            if ti + 1 < NT:
                self.norm_stage(l, 0, nxt)
            self.out_proj(l, Wout, "gwout", gq, ["a1_%d" % m for m in range(8)], cur)
            self.store_x(ti, cur)

    def s5_prep(self):
        P = self.P
        d = self.dram
        c = self.s5c
        ARE, AIM, LDT, DT, MAG, ANG, SN, CS, AR, AI, T1, T2, T3, T4, FRE, FIM = [c[:, i, :] for i in range(16)]
        P.dma(ARE, d["are_c"], writes=["s5c"])
        P.dma(AIM, d["aim_c"], writes=["s5c"])
        P.dma(LDT, d["ldt_c"], writes=["s5c"])
        R = ["s5c"]
        self.act(DT, LDT, AF.Exp, R, R)
        self.tt("dve", T1, ARE, DT, ALU.mult, R, R)
        self.act(MAG, T1, AF.Exp, R, R)
        self.tt("dve", ANG, AIM, DT, ALU.mult, R, R)

        def sin_of(dst, src, shift):
            self.ts("dve", T2, src, shift, None, ALU.add, None, R, R)
            self.ts("dve", T3, T2, 1.0 / (2 * PI), None, ALU.mult, None, R, R)
            self.cp("dve", self.s5i[:], T3, R, ["s5i"])
            self.cp("dve", T3, self.s5i[:], ["s5i"], R)
            self.stt("dve", T2, T3, -2 * PI, T2, ALU.mult, ALU.add, R, R)
            self.ts("dve", T3, T2, PI, -2 * PI, ALU.is_gt, ALU.mult, R, R)
            self.tt("dve", T2, T2, T3, ALU.add, R, R)
            self.ts("dve", T3, T2, -PI, 2 * PI, ALU.is_lt, ALU.mult, R, R)
            self.tt("dve", T2, T2, T3, ALU.add, R, R)
            self.ts("dve", T2, T2, -3.141592, 3.141592, ALU.max, ALU.min, R, R)
            self.act(dst, T2, AF.Sin, R, R)

        sin_of(SN, ANG, 0.0)
        sin_of(CS, ANG, PI / 2)
        self.tt("dve", AR, MAG, CS, ALU.mult, R, R)
        self.tt("dve", AI, MAG, SN, ALU.mult, R, R)
        self.tt("dve", T1, ARE, ARE, ALU.mult, R, R)
        self.tt("dve", T2, AIM, AIM, ALU.mult, R, R)
        self.tt("dve", T1, T1, T2, ALU.add, R, R)
        P.op("dve", lambda e: e.reciprocal(out=T4, in_=T1), reads=R, writes=R)
        self.ts("dve", T3, AR, -1.0, None, ALU.add, None, R, R)
        self.tt("dve", T1, T3, ARE, ALU.mult, R, R)
        self.tt("dve", T2, AI, AIM, ALU.mult, R, R)
        self.tt("dve", T1, T1, T2, ALU.add, R, R)
        self.tt("dve", FRE, T1, T4, ALU.mult, R, R)
        self.tt("dve", T1, AI, ARE, ALU.mult, R, R)
        self.tt("dve", T2, T3, AIM, ALU.mult, R, R)
        self.tt("dve", T1, T1, T2, ALU.subtract, R, R)
        self.tt("dve", FIM, T1, T4, ALU.mult, R, R)
        pr, pi_, npi = self.pw[:, 0], self.pw[:, 1], self.pw[:, 2]
        W = ["pw"]
        self.cp("dve", pr[:, 0, :], CS, R, W)
        self.cp("dve", pi_[:, 0, :], SN, R, W)
        for k in range(1, 10):
            self.tt("dve", T1, pr[:, k - 1, :], pr[:, k - 1, :], ALU.mult, W + R, R)
            self.tt("dve", T2, pi_[:, k - 1, :], pi_[:, k - 1, :], ALU.mult, W + R, R)
            self.tt("dve", pr[:, k, :], T1, T2, ALU.subtract, R, W)
            self.tt("dve", T1, pr[:, k - 1, :], pi_[:, k - 1, :], ALU.mult, W + R, R)
            self.ts("dve", pi_[:, k, :], T1, 2.0, None, ALU.mult, None, R, W)
        self.ts("dve", npi, pi_, -1.0, None, ALU.mult, None, W, W)

    def s5_block(self, l):
        P = self.P
        d = self.dram
        P.barrier()
        self.s5_prep()
        Win = self.wview(self.WA, 0, 8, 1024)
        Wglu = self.wview(self.WA, 8192, 8, 1024)
        BW = self.WA[:, 16384:24576].rearrange("p (g r n) -> p g r n", g=32, r=2)
        CW = self.WA[:, 24576:32768].rearrange("p (g r n) -> p g r n", g=32, r=2)
        Wout = self.wview(self.WB, 0, 8, 1024)
        self.load_w("swin", Win, d["ssm_w_in"], 8, 1024)
        self.load_w("sglu", Wglu, d["ssm_glu_w"], 8, 1024)
        self.load_w("swout", Wout, d["ssm_w_out"], 8, 1024)
        c = self.s5c
        FRE, FIM = c[:, 14, :], c[:, 15, :]
        bwparts, cwparts = [], []
        for pc in range(8):
            psr, nr_ = self.next_ps()
            psi, ni_ = self.next_ps()
            for g4 in range(4):
                gp = pc * 4 + g4
                self.ts("dve", self.dg[:, 0, :], self.ident[:], FRE[:, gp:gp + 1], None, ALU.mult, None, ["s5c", "ident"], ["dg0"])
                self.ts("dve", self.dg[:, 1, :], self.ident[:], FIM[:, gp:gp + 1], None, ALU.mult, None, ["s5c", "ident"], ["dg1"])
                self.mm(psr[:, g4 * 128:(g4 + 1) * 128], nr_, [(self.ones_f[:], self.dg[:, 0, :])], ["dg0", "ones_f"],
                        flags=[(psr[:, g4 * 128:(g4 + 1) * 128], True, True)])
                self.mm(psi[:, g4 * 128:(g4 + 1) * 128], ni_, [(self.ones_f[:], self.dg[:, 1, :])], ["dg1", "ones_f"],
                        flags=[(psi[:, g4 * 128:(g4 + 1) * 128], True, True)])
            b1 = self.stage_rr
            b2 = (b1 + 1) % 3
            self.stage_rr = (b1 + 2) % 3
            s1, s2 = "stage%d" % b1, "stage%d" % b2
            bre = self.stage[:, b1, 0:512]
            bim = self.stage[:, b2, 0:512]
            P.dma(bre, d["Bre_z"][:, pc * 4:(pc + 1) * 4, :].rearrange("p g n -> p (g n)"), writes=[s1])
            P.dma(bim, d["Bim_z"][:, pc * 4:(pc + 1) * 4, :].rearrange("p g n -> p (g n)"), writes=[s2])
            t1, t2 = self.tmpn[:, 0, :], self.tmpn[:, 1, :]
            pn = "bw%d" % pc
            self.tt("dve", t1, psr[:], bre, ALU.mult, [nr_, s1], ["tmpn0"])
            self.tt("dve", t2, psi[:], bim, ALU.mult, [ni_, s2], ["tmpn1"])
            self.tt("dve", BW[:, pc * 4:(pc + 1) * 4, 0, :], t1.rearrange("p (g n) -> p g n", g=4),
                    t2.rearrange("p (g n) -> p g n", g=4), ALU.subtract, ["tmpn0", "tmpn1"], [pn + "r"])
            self.tt("dve", t1, psr[:], bim, ALU.mult, [nr_, s2], ["tmpn0"])
            self.tt("dve", t2, psi[:], bre, ALU.mult, [ni_, s1], ["tmpn1"])
            self.tt("dve", BW[:, pc * 4:(pc + 1) * 4, 1, :], t1.rearrange("p (g n) -> p g n", g=4),
                    t2.rearrange("p (g n) -> p g n", g=4), ALU.add, ["tmpn0", "tmpn1"], [pn + "i"])
            bwparts += [pn + "r", pn + "i"]
        for pc in range(8):
            for ri, key, sc in ((0, "Cre_z", 1.0), (1, "Cim_z", -1.0)):
                b1 = self.stage_rr
                self.stage_rr = (b1 + 1) % 3
                s1 = "stage%d" % b1
                P.dma(self.stage[:, b1, 0:512], d[key][:, pc * 4:(pc + 1) * 4, :].rearrange("p g n -> p (g n)"), writes=[s1])
                pn = "cwp%d_%d" % (pc, ri)
                self.act(CW[:, pc * 4:(pc + 1) * 4, ri, :], self.stage[:, b1, 0:512].rearrange("p (g n) -> p g n", g=4),
                         AF.Copy, [s1], [pn], scale=sc)
                cwparts.append(pn)
        K0 = self.KS0
        K1 = self.KS1
        SRI = self.WB[:, 14336:16384].bitcast(F32)
        BR = [K0[:, 0:512], K0[:, 1024:1536]]
        BI = [K0[:, 512:1024], K0[:, 1536:2048]]
        A2f = self.A2
        PRs = [K0[:, 2048:2560], A2f[:, 0:512]]
        PIs = [K0[:, 2560:3072], A2f[:, 512:1024]]
        TC = [K1[:, 0:512], K1[:, 1024:1536], A2f[:, 1024:1536]]
        TS = [K1[:, 512:1024], K1[:, 1536:2048], A2f[:, 1536:2048]]
        RR, RI = K1[:, 2048:2560], K1[:, 2560:3072]
        Dd = self.WB[:, 16384 + 4096:16384 + 4096 + 1024].rearrange("p (q n) -> p q n", q=8)
        sbf3 = self.WB[:, 16384 + 5120:16384 + 5120 + 1024].rearrange("p (r n) -> p r n", r=2)
        sbfs = [self.sbf[:, 0], self.sbf[:, 1], sbf3]
        Qs = [self.sbf[:, :, :, :].rearrange("p a b c -> p (a b) c"),
              self.WB[:, 14336:16384].rearrange("p (a c) -> p a c", a=4)]
        for q_ in range(8):
            self.ts("dve", Dd[:, q_, :], self.ident[:], self.dsk[:, q_:q_ + 1], None, ALU.mult, None, ["ident", "dsk"], ["Dd"])
        SR, SI = SRI[:, 0:512], SRI[:, 512:1024]
        M1, M2 = self.rt[:, 0, :], self.rt[:, 1, :]
        scan_names = ["br0", "bi0", "br1", "bi1", "pr", "pi", "tc0", "ts0", "tc1", "ts1", "rr", "ri", "sr", "si"]
        self.memset("pool", K1, 0.0, ["tc0", "ts0", "tc1", "ts1", "rr", "ri", "stage0", "stage1", "stage2"])
        carn_all = ["car%d" % g for g in range(32)]
        self.memset("dve", self.car[:], 0.0, carn_all)
        uc, us_, uns = self.pw[:, 0], self.pw[:, 1], self.pw[:, 2]
        for gp in range(32):
            t = gp % 3
            tcn, tsn = "tc%d" % t, "ts%d" % t
            self.memset("dve", TC[t][:, 0:1], 1.0, [tcn])
            self.memset("dve", TS[t][:, 0:1], 0.0, [tsn])
            for k in range(9):
                n = 1 << k
                self.ts("dve", TC[t][:, n:2 * n], TC[t][:, 0:n], uc[:, k, gp:gp + 1], None, ALU.mult, None, [tcn, "pw"], [tcn])
                self.stt("dve", TC[t][:, n:2 * n], TS[t][:, 0:n], uns[:, k, gp:gp + 1], TC[t][:, n:2 * n], ALU.mult, ALU.add,
                         [tcn, tsn, "pw"], [tcn])
                self.ts("dve", TS[t][:, n:2 * n], TS[t][:, 0:n], uc[:, k, gp:gp + 1], None, ALU.mult, None, [tsn, "pw"], [tsn])
                self.stt("dve", TS[t][:, n:2 * n], TC[t][:, 0:n], us_[:, k, gp:gp + 1], TS[t][:, n:2 * n], ALU.mult, ALU.add,
                         [tcn, tsn, "pw"], [tsn])
            P.dma(self.tabC[gp], TC[t], reads=[tcn], writes=["tabC%d" % gp])
            P.dma(self.tabS[gp], TS[t], reads=[tsn], writes=["tabS%d" % gp])
        us = self.A2.rearrange("p (k n) -> p k n", k=8)
        yg = self.A3.rearrange("p (k n) -> p k n", k=8)
        ubf = self.A1[:, 0:8, :]
        ygb = self.A1[:, 8:16, :]
        qv = self.A1[:, 0:8, :]
        c = self.s5c
        MAG, SNt, CSt = c[:, 4, :], c[:, 6, :], c[:, 7, :]
        X1, X2, INITR, INITI = c[:, 10, :], c[:, 11, :], c[:, 12, :], c[:, 13, :]

        for ti in range(NT):
            self.load_x(ti)
            self.norm_stage(l, 0)
            hn = ["h%d" % k for k in range(8)]
            for m in range(8):
                ps, pn = self.next_ps()
                self.mm(ps[:], pn, [(Win[:, k, m * 128:(m + 1) * 128], self.h[:, k, :]) for k in range(8)],
                        self.wparts["swin"] + hn)
                self.cp("act" if m % 2 == 0 else "dve", ubf[:, m, :], ps[:], [pn], ["a1_%d" % m])
            C512, S512 = self.pw[:, 0, 9, :], self.pw[:, 1, 9, :]
            self.tt("dve", X1, C512, self.car[:, 0, :], ALU.mult, ["pw"] + carn_all, ["x1"])
            self.tt("dve", X2, S512, self.car[:, 1, :], ALU.mult, ["pw"] + carn_all, ["x2"])
            self.tt("dve", INITR, X1, X2, ALU.subtract, ["x1", "x2"], ["initr"])
            self.tt("dve", X1, C512, self.car[:, 1, :], ALU.mult, ["pw"] + carn_all, ["x1"])
            self.tt("dve", X2, S512, self.car[:, 0, :], ALU.mult, ["pw"] + carn_all, ["x2"])
            self.tt("dve", INITI, X1, X2, ALU.add, ["x1", "x2"], ["initi"])

            def bu(gp, part):
                q = gp // 4
                s = gp % 2
                t = gp % 3
                pb = gp % 2
                s3 = gp % 3
                PRb, PIb = PRs[pb], PIs[pb]
                prn, pin = "pr%d" % pb, "pi%d" % pb
                brn, bin_, tcn, tsn = "br%d" % s, "bi%d" % s, "tc%d" % t, "ts%d" % t
                if part == 0:
                    psr, nr_ = self.next_ps()
                    psi, ni_ = self.next_ps()
                    self.mm(psr[:], nr_, [(BW[:, gp, 0, :], ubf[:, q, :])], bwparts + ["a1_%d" % q])
                    self.mm(psi[:], ni_, [(BW[:, gp, 1, :], ubf[:, q, :])], bwparts + ["a1_%d" % q])
                    self.cp("act", BR[s], psr[:], [nr_], [brn])
                    self.cp("act", BI[s], psi[:], [ni_], [bin_])
                    P.dma(TC[t], self.tabC[gp], reads=["tabC%d" % gp], writes=[tcn])
                    P.dma(TS[t], self.tabS[gp], reads=["tabS%d" % gp], writes=[tsn])
                    self.tt("dve", PRb, BR[s], TC[t], ALU.mult, [brn, tcn], [prn])
                    self.tt("dve", M1, BI[s], TS[t], ALU.mult, [bin_, tsn], ["rt0"])
                    self.tt("dve", PRb, PRb, M1, ALU.add, [prn, "rt0"], [prn])
                    self.tt("dve", PIb, BI[s], TC[t], ALU.mult, [bin_, tcn], [pin])
                    self.tt("dve", M1, BR[s], TS[t], ALU.mult, [brn, tsn], ["rt0"])
                    self.tt("dve", PIb, PIb, M1, ALU.subtract, [pin, "rt0"], [pin])
                    return
                if part == 2:
                    return
                rho = MAG[:, gp:gp + 1].broadcast_to([128, TT])
                P.op("dve", lambda e: e.tensor_tensor_scan(out=RR, data0=rho, data1=PRb, initial=INITR[:, gp:gp + 1],
                                                           op0=ALU.mult, op1=ALU.add),
                     reads=[prn, "s5c", "initr"], writes=["rr"])
                P.op("dve", lambda e: e.tensor_tensor_scan(out=RI, data0=rho, data1=PIb, initial=INITI[:, gp:gp + 1],
                                                           op0=ALU.mult, op1=ALU.add),
                     reads=[pin, "s5c", "initi"], writes=["ri"])
                qs = gp % 2
                Q = Qs[qs]
                qn = "q%d" % qs
                carn = "car%d" % gp
                self.cp("pool", self.car[:, 0, gp:gp + 1], RR[:, 511:512], ["rr"], [carn])
                self.cp("pool", self.car[:, 1, gp:gp + 1], RI[:, 511:512], ["ri"], [carn])
                self.tt("dve", Q[:, 0, :], RR, TC[t], ALU.mult, ["rr", tcn], [qn])
                self.stt("dve", Q[:, 1, :], RI, -1.0, TS[t], ALU.mult, ALU.mult, ["ri", tsn], [qn])
                self.tt("dve", Q[:, 2, :], RR, TS[t], ALU.mult, ["rr", tsn], [qn])
                self.tt("dve", Q[:, 3, :], RI, TC[t], ALU.mult, ["ri", tcn], [qn])

            def outp(gp):
                q = gp // 4
                s3 = gp % 3
                psY, nY = self.ps[6 + (q % 2)], "ps%d" % (6 + (q % 2))
                Q = Qs[gp % 2]
                pairs = [(CW[:, gp, 0, :], Q[:, 0, :]), (CW[:, gp, 0, :], Q[:, 1, :]),
                         (CW[:, gp, 1, :], Q[:, 2, :]), (CW[:, gp, 1, :], Q[:, 3, :])]
                flags = [(psY[:], False, False)] * 3 + [(psY[:], False, gp % 4 == 3)]
                rd = cwparts + ["q%d" % (gp % 2)]
                if gp % 4 == 0:
                    pairs = [(Dd[:, q, :], ubf[:, q, :])] + pairs
                    flags = [(psY[:], True, False)] + flags
                    rd = rd + ["Dd", "a1_%d" % q]
                self.mm(None, nY, pairs, rd, flags=flags)
                if gp % 4 == 3:
                    self.act(yg[:, q, :], psY[:], AF.Gelu_apprx_tanh, [nY], ["yg%d" % q])
                    self.cp("act", ygb[:, q, :], yg[:, q, :], ["yg%d" % q], ["a1_%d" % (8 + q)])

            for i in range(-2, 33):
                if 0 <= i + 2 < 32:
                    bu(i + 2, 0)
                if 0 <= i < 32:
                    bu(i, 2)
                if 0 <= i + 1 < 32:
                    bu(i + 1, 1)
                if 0 <= i < 32:
                    outp(i)
            ygn = ["a1_%d" % (8 + k) for k in range(8)]
            for mo in range(8):
                ps, pn = self.next_ps()
                self.mm(ps[:], pn, [(Wglu[:, k, mo * 128:(mo + 1) * 128], ygb[:, k, :]) for k in range(8)],
                        self.wparts["sglu"] + ygn)
                b = mo % 2
                self.act(self.rt[:, b, :], ps[:], AF.Sigmoid, [pn, "glub"], ["rt%d" % b], bias=self.glub[:, mo:mo + 1])
                self.tt("dve", qv[:, mo, :], yg[:, mo, :], self.rt[:, b, :], ALU.mult, ["yg%d" % mo, "rt%d" % b], ["a1_%d" % mo])
            self.out_proj(l, Wout, "swout", qv, ["a1_%d" % m for m in range(8)])
            self.store_x(ti)

    def build(self):
        nc = self.nc
        self.declare()
        with ExitStack() as st:
            self.alloc(st)
            self.P = Prog(nc, st)
            block = st.enter_context(nc.Block())
            self.prologue()
            for l in range(self.n_layers):
                self.ada_stage(l)
            for l in range(self.n_layers):
                kind = l % 3
                if kind == 0:
                    self.conv_block(l, l // 3, first=(l == 0))
                elif kind == 1:
                    self.s5_block(l)
                else:
                    self.sg_block(l)
                self.ffn_block(l, last=(l == self.n_layers - 1))
            self.P.finish()
            self.P.emit(block)
        return nc


def tT(v):
    v = np.asarray(v, np.float32)
    lead = v.shape[:-1]
    r = v.reshape(lead + (8, 128))
    return np.ascontiguousarray(np.moveaxis(r, -1, 0))


def host_layout(inp, b):
    f32 = np.float32
    m = {}
    m["x"] = np.ascontiguousarray(inp["x"][b], f32)
    m["cT"] = tT(inp["c"][b])
    m["ada_w"] = np.ascontiguousarray(inp["ada_w"], f32)
    ab = np.asarray(inp["ada_b"], f32).reshape(DEPTH, 48, 128)
    m["ada_bT"] = np.ascontiguousarray(ab.transpose(2, 0, 1))
    m["n1gT"] = tT(inp["norm1_g"])
    m["n2gT"] = tT(inp["norm2_g"])
    m["fgT"] = tT(inp["final_g"])
    m["ff_w1"] = np.ascontiguousarray(inp["ff_w1"], f32)
    m["ff_w2"] = np.ascontiguousarray(inp["ff_w2"], f32)
    m["conv_w_in"] = np.ascontiguousarray(inp["conv_w_in"], f32)
    m["conv_wT"] = tT(inp["conv_w"])
    m["conv_bT"] = tT(inp["conv_b"])
    m["conv_w_out"] = np.ascontiguousarray(inp["conv_w_out"], f32)
    m["ssm_w_in"] = np.ascontiguousarray(inp["ssm_w_in"][0], f32)
    m["ssm_glu_w"] = np.ascontiguousarray(inp["ssm_glu_w"][0], f32)
    m["ssm_w_out"] = np.ascontiguousarray(inp["ssm_w_out"][0], f32)
    m["glu_bT"] = tT(inp["ssm_glu_b"][0])
    m["dT"] = tT(inp["ssm_d"][0])
    def compact(a):
        a = np.asarray(a, f32).reshape(32, 2, 64)
        return np.ascontiguousarray(a.transpose(1, 2, 0).reshape(128, 32))
    m["are_c"] = compact(inp["ssm_a_re"][0])
    m["aim_c"] = compact(inp["ssm_a_im"][0])
    m["ldt_c"] = compact(np.broadcast_to(np.asarray(inp["ssm_log_dt"][0], f32)[:, None], (64, 64)))
    bre = np.asarray(inp["ssm_b_re"][0], f32)
    bim = np.asarray(inp["ssm_b_im"][0], f32)
    cre = np.asarray(inp["ssm_c_re"][0], f32)
    cim = np.asarray(inp["ssm_c_im"][0], f32)
    Bre_z = np.zeros((128, 32, 128), f32); Bim_z = np.zeros((128, 32, 128), f32)
    Cre_z = np.zeros((128, 32, 128), f32); Cim_z = np.zeros((128, 32, 128), f32)
    for g in range(64):
        gp, g2, g8 = g // 2, g % 2, g % 8
        Bre_z[g8 * 16:(g8 + 1) * 16, gp, g2 * 64:(g2 + 1) * 64] = bre[g].T
        Bim_z[g8 * 16:(g8 + 1) * 16, gp, g2 * 64:(g2 + 1) * 64] = bim[g].T
        Cre_z[g2 * 64:(g2 + 1) * 64, gp, g8 * 16:(g8 + 1) * 16] = cre[g].T
        Cim_z[g2 * 64:(g2 + 1) * 64, gp, g8 * 16:(g8 + 1) * 16] = cim[g].T
    m["Bre_z"], m["Bim_z"], m["Cre_z"], m["Cim_z"] = Bre_z, Bim_z, Cre_z, Cim_z
    m["sg_w_in"] = np.ascontiguousarray(inp["sg_w_in"][0], f32)
    m["sg_w_out"] = np.ascontiguousarray(inp["sg_w_out"][0], f32)
    m["sg_w_s"] = np.ascontiguousarray(inp["sg_w_s"][0], f32)
    m["sg_bias_rep"] = np.ascontiguousarray(np.broadcast_to(np.asarray(inp["sg_b_s"][0], f32)[None], (128, 8, 128)))
    m["sg_gain_rep"] = np.ascontiguousarray(np.broadcast_to(np.asarray(inp["sg_v_g"][0], f32)[None], (128, D)))
    m["ident"] = np.eye(128, dtype=f32)
    m["tril"] = np.tril(np.ones((128, 128), f32))
    return m


_NC_CACHE = {}


def kernel(_n_layers=DEPTH, **inputs):
    inp = {k: np.asarray(v) for k, v in inputs.items()}
    if _n_layers not in _NC_CACHE:
        _NC_CACHE[_n_layers] = Builder(_n_layers).build()
    nc = _NC_CACHE[_n_layers]
    in_maps = [host_layout(inp, b) for b in range(8)]
    res = run_bass_kernel_spmd(nc, in_maps, core_ids=list(range(8)))
    out = np.stack([np.asarray(r["out"], np.float32) for r in res.results], axis=0)
    return out
```
